# Optimizing a Trainium2 kernel written in Bass

```python
import jax, jax.numpy as jnp
from jax import lax
import numpy as np

D_MODEL = 1024
BATCH = 2
SEQ = 8192
DEPTH = 4

CHUNK = 64
Q_BLOCK = 128
N_BRANCHES = 3
BRANCH_WIDTH = 512
SB_HEADS = 8
SB_HEAD_DIM = BRANCH_WIDTH // SB_HEADS
CONV_CHANNELS = BRANCH_WIDTH
CONV_WIDTH = 31
GLA_HEADS = 4
GLA_KEY_DIM = BRANCH_WIDTH // 2
GLA_VALUE_DIM = BRANCH_WIDTH
GLA_HEAD_K = GLA_KEY_DIM // GLA_HEADS
GLA_HEAD_V = GLA_VALUE_DIM // GLA_HEADS
GLA_GATE_RANK = 16
GLA_GATE_TAU = 16.0
D_FF = 2816
NORM_EPS = 1e-6

IN_WIDTHS = (
    BRANCH_WIDTH, BRANCH_WIDTH, BRANCH_WIDTH,
    2 * CONV_CHANNELS,
    GLA_KEY_DIM, GLA_KEY_DIM, GLA_VALUE_DIM,
    GLA_VALUE_DIM,
    GLA_GATE_RANK,
    N_BRANCHES * D_MODEL,
)
IN_WIDTH = sum(IN_WIDTHS)

kernel_name = "hybrid_stickbreak_conformer_gla_trunk"


def rms_norm(x, g):
    xf = x.astype(jnp.float32)
    y = xf * lax.rsqrt(jnp.mean(xf * xf, axis=-1, keepdims=True) + NORM_EPS)
    return (y * g.astype(jnp.float32)).astype(x.dtype)


def swiglu_ffn(h, w_gate, w_up, w_down):
    return (jax.nn.silu(h @ w_gate) * (h @ w_up)) @ w_down


def stick_breaking_attention(q, k, v):
    b, nh, s, dh = q.shape
    nb = s // Q_BLOCK
    q_blocks = q.reshape(b, nh, nb, Q_BLOCK, dh).transpose(2, 0, 1, 3, 4)
    key_pos = jnp.arange(s)
    scale = dh ** -0.5

    def one_block(args):
        q_blk, blk = args
        z = jnp.einsum('bhqd,bhkd->bhqk', q_blk, k).astype(jnp.float32) * scale
        query_pos = blk * Q_BLOCK + jnp.arange(Q_BLOCK)
        earlier = key_pos[None, :] < query_pos[:, None]
        log_keep = jnp.where(earlier, jax.nn.log_sigmoid(-z), 0.0)
        log_stick = lax.cumsum(log_keep, axis=3, reverse=True) - log_keep
        w = jnp.where(earlier, jnp.exp(jax.nn.log_sigmoid(z) + log_stick), 0.0)
        return jnp.einsum('bhqk,bhkd->bhqd', w.astype(v.dtype), v)

    out = lax.map(one_block, (q_blocks, jnp.arange(nb)))
    return out.transpose(1, 2, 0, 3, 4).reshape(b, nh, s, dh)


def conformer_conv(u_glu, conv_w, conv_b, ln_g, ln_b):
    a, g = jnp.split(u_glu, 2, axis=-1)
    u = a * jax.nn.sigmoid(g)
    y = lax.conv_general_dilated(
        u, conv_w[:, None, :], window_strides=(1,),
        padding=[(CONV_WIDTH - 1, 0)],
        dimension_numbers=('NWC', 'WIO', 'NWC'),
        feature_group_count=CONV_CHANNELS) + conv_b
    yf = y.astype(jnp.float32)
    mu = jnp.mean(yf, axis=-1, keepdims=True)
    var = jnp.mean(jnp.square(yf - mu), axis=-1, keepdims=True)
    yn = (yf - mu) * lax.rsqrt(var + NORM_EPS) * ln_g.astype(jnp.float32) + ln_b.astype(jnp.float32)
    return jax.nn.silu(yn).astype(u.dtype)


def gla_chunked(q, k, v, log_alpha):
    b, s, nh, dk = q.shape
    dv = v.shape[-1]
    n = s // CHUNK

    def to_chunks(t):
        return t.reshape(b, n, CHUNK, nh, t.shape[-1]).transpose(1, 0, 2, 3, 4)

    la_c = to_chunks(log_alpha.astype(jnp.float32))
    decay_to_end = lax.cumsum(la_c, axis=2, reverse=True) - la_c
    chunk_decay = jnp.exp(jnp.sum(la_c, axis=2))
    k_dec = to_chunks(k).astype(jnp.float32) * jnp.exp(decay_to_end)
    q_c = to_chunks(q).astype(jnp.float32) * (dk ** -0.5)
    v_c = to_chunks(v).astype(jnp.float32)

    def step(state, xs):
        q_n, k_n, v_n, lam = xs
        state = lam[..., None] * state + jnp.einsum('bchk,bchv->bhkv', k_n, v_n)
        return state, jnp.einsum('bchk,bhkv->bchv', q_n, state)

    state0 = jnp.zeros((b, nh, dk, dv), jnp.float32)
    _, o = lax.scan(step, state0, (q_c, k_dec, v_c, chunk_decay))
    return o.transpose(1, 0, 2, 3, 4).reshape(b, s, nh, dv).astype(v.dtype)


def hybrid_mixer(h, w_in, conv_w, conv_b, conv_ln_g, conv_ln_b,
                 gla_w_alpha, gla_b_alpha, gla_norm_g, w_branch, w_out):
    b, s, _ = h.shape
    proj = h @ w_in
    offsets = [int(o) for o in np.cumsum(IN_WIDTHS)[:-1]]
    (sb_q, sb_k, sb_v, conv_in, gla_q, gla_k, gla_v, gla_r, gla_lr,
     gate_logits) = jnp.split(proj, offsets, axis=-1)

    def heads(t, nh):
        return t.reshape(b, s, nh, -1)

    sb_out = stick_breaking_attention(
        heads(sb_q, SB_HEADS).transpose(0, 2, 1, 3),
        heads(sb_k, SB_HEADS).transpose(0, 2, 1, 3),
        heads(sb_v, SB_HEADS).transpose(0, 2, 1, 3))
    sb_out = sb_out.transpose(0, 2, 1, 3).reshape(b, s, BRANCH_WIDTH)

    conv_out = conformer_conv(conv_in, conv_w, conv_b, conv_ln_g, conv_ln_b)

    log_alpha = jax.nn.log_sigmoid((gla_lr @ gla_w_alpha + gla_b_alpha).astype(jnp.float32)) / GLA_GATE_TAU
    gla_o = gla_chunked(heads(gla_q, GLA_HEADS), heads(gla_k, GLA_HEADS),
                        heads(gla_v, GLA_HEADS), heads(log_alpha, GLA_HEADS))
    gla_o = rms_norm(gla_o, gla_norm_g.reshape(GLA_HEADS, GLA_HEAD_V))
    gla_out = gla_o.reshape(b, s, GLA_VALUE_DIM) * jax.nn.silu(gla_r)

    branches = jnp.stack([sb_out, conv_out, gla_out], axis=2)
    branch_d = jnp.einsum('bsgw,gwd->bsgd', branches, w_branch)
    gates = jax.nn.sigmoid(gate_logits.reshape(b, s, N_BRANCHES, D_MODEL))
    merged = jnp.sum(gates * branch_d, axis=2)
    return merged @ w_out


def setup_inputs(seed: int = 0) -> dict:
    key = jax.random.key(seed)
    ks = jax.random.split(key, 20)

    def nrm(k, shape, scale):
        return jax.random.normal(k, shape, jnp.float32) * scale

    L, D = DEPTH, D_MODEL
    return {
        "x": nrm(ks[0], (BATCH, SEQ, D), 1.0),
        "norm_pre": 1.0 + nrm(ks[1], (L, 3, D), 0.05),
        "norm_post": 1.0 + nrm(ks[2], (L, 3, D), 0.05),
        "ffn1_w_gate": nrm(ks[3], (L, D, D_FF), D ** -0.5),
        "ffn1_w_up": nrm(ks[4], (L, D, D_FF), D ** -0.5),
        "ffn1_w_down": nrm(ks[5], (L, D_FF, D), D_FF ** -0.5),
        "ffn2_w_gate": nrm(ks[6], (L, D, D_FF), D ** -0.5),
        "ffn2_w_up": nrm(ks[7], (L, D, D_FF), D ** -0.5),
        "ffn2_w_down": nrm(ks[8], (L, D_FF, D), D_FF ** -0.5),
        "w_in": nrm(ks[9], (L, D, IN_WIDTH), D ** -0.5),
        "conv_w": nrm(ks[10], (L, CONV_WIDTH, CONV_CHANNELS), CONV_WIDTH ** -0.5),
        "conv_b": nrm(ks[11], (L, CONV_CHANNELS), 0.02),
        "conv_ln_g": 1.0 + nrm(ks[12], (L, CONV_CHANNELS), 0.05),
        "conv_ln_b": nrm(ks[13], (L, CONV_CHANNELS), 0.02),
        "gla_w_alpha": nrm(ks[14], (L, GLA_GATE_RANK, GLA_KEY_DIM), GLA_GATE_RANK ** -0.5),
        "gla_b_alpha": nrm(ks[15], (L, GLA_KEY_DIM), 0.02),
        "gla_norm_g": 1.0 + nrm(ks[16], (L, GLA_VALUE_DIM), 0.05),
        "w_branch": nrm(ks[17], (L, N_BRANCHES, BRANCH_WIDTH, D), BRANCH_WIDTH ** -0.5),
        "w_out": nrm(ks[18], (L, D, D), D ** -0.5),
    }


def reference(x, norm_pre, norm_post, ffn1_w_gate, ffn1_w_up, ffn1_w_down,
              ffn2_w_gate, ffn2_w_up, ffn2_w_down, w_in, conv_w, conv_b,
              conv_ln_g, conv_ln_b, gla_w_alpha, gla_b_alpha, gla_norm_g,
              w_branch, w_out):
    for l in range(DEPTH):
        h = rms_norm(x, norm_pre[l, 0])
        x = x + 0.5 * rms_norm(swiglu_ffn(h, ffn1_w_gate[l], ffn1_w_up[l], ffn1_w_down[l]), norm_post[l, 0])
        h = rms_norm(x, norm_pre[l, 1])
        m = hybrid_mixer(h, w_in[l], conv_w[l], conv_b[l], conv_ln_g[l], conv_ln_b[l],
                         gla_w_alpha[l], gla_b_alpha[l], gla_norm_g[l], w_branch[l], w_out[l])
        x = x + rms_norm(m, norm_post[l, 1])
        h = rms_norm(x, norm_pre[l, 2])
        x = x + 0.5 * rms_norm(swiglu_ffn(h, ffn2_w_gate[l], ffn2_w_up[l], ffn2_w_down[l]), norm_post[l, 2])
    return x
```

```python
import numpy as np
from contextlib import ExitStack
import ml_dtypes
import concourse.bass as bass
import concourse.mybir as mybir
from concourse.bass_utils import run_bass_kernel_spmd

F32 = mybir.dt.float32
BF16 = mybir.dt.bfloat16
AF = mybir.ActivationFunctionType
ALU = mybir.AluOpType

D = 1024
DFF = 2816
NF = DFF // 128
SEQ = 8192
BATCH = 2
DEPTH = 4
TPC = 2048
PASS = 1024
EPS = 1e-6
INW = 7184
NCORES = 8
POOL_ENG = "dve"

ENGS = ("pe", "act", "dve", "pool", "sp")
SEM_LIMIT = 28000


class Buf:
    __slots__ = ("name", "w", "r", "dsem")

    def __init__(self, name):
        self.name = name
        self.w = None
        self.r = {}
        self.dsem = None


class Sched:
    def __init__(self, nc, stack):
        self.nc = nc
        self.stack = stack
        self.sem = {}
        self.cnt = {}
        self.gen = {e: 0 for e in ENGS}
        self.seen = {e: {} for e in ENGS}
        self.q = {e: [] for e in ENGS}
        self.nsem = 0
        for e in ENGS:
            self._mksem((e, 0))
        self.n_inst = 0
        self.dma_rr = 0

    def _mksem(self, key):
        self.nsem += 1
        h = self.stack.enter_context(self.nc.semaphore("s%d" % self.nsem))
        self.sem[key] = h
        self.cnt[key] = 0
        return key

    def ekey(self, e):
        return (e, self.gen[e])

    def _waits(self, e, deps):
        out = []
        seen = self.seen[e]
        for (k, v) in deps:
            if k[0] == e:
                continue
            if seen.get(k, 0) < v:
                seen[k] = v
                out.append((k, v))
        return out

    @staticmethod
    def _deps(reads, writes):
        deps = []
        for b in reads:
            if b.w is not None:
                deps.append(b.w)
        for b in writes:
            if b.w is not None:
                deps.append(b.w)
            deps.extend(b.r.items())
        return deps

    def op(self, e, fns, reads=(), writes=()):
        if not isinstance(fns, (list, tuple)):
            fns = [fns]
        waits = self._waits(e, self._deps(reads, writes))
        k = self.ekey(e)
        if self.cnt[k] >= SEM_LIMIT:
            self.gen[e] += 1
            k = self._mksem(self.ekey(e))
        self.cnt[k] += 1
        ev = (k, self.cnt[k])
        for b in reads:
            if b.r.get(k, 0) < ev[1]:
                b.r[k] = ev[1]
        for b in writes:
            b.w = ev
            b.r = {}
        self.q[e].append((waits, fns, k, 1))
        self.n_inst += len(fns)
        return ev

    def dma(self, qe, fn, reads=(), writes=(), sembuf=None, outs=()):
        waits = self._waits(qe, self._deps(reads, writes))
        tgt = sembuf if sembuf is not None else (outs[0] if outs else (writes[0] if writes else reads[0]))
        if tgt.dsem is None:
            tgt.dsem = self._mksem(("dma", tgt.name))
        semkey = tgt.dsem
        self.cnt[semkey] += 16
        ev = (semkey, self.cnt[semkey])
        for b in reads:
            if b.r.get(semkey, 0) < ev[1]:
                b.r[semkey] = ev[1]
        for b in writes:
            b.w = ev
            b.r = {}
        for b in outs:
            b.w = ev
        self.q[qe].append((waits, [fn], semkey, 16))
        self.n_inst += 1
        return ev

    def wait_all_on(self, e, bufs):
        deps = [b.w for b in bufs if b.w is not None]
        waits = self._waits(e, deps)
        self.q[e].append((waits, [], None, 0))

    def emit(self):
        nc = self.nc
        with nc.Block() as block:
            def mk(e):
                def body(engobj):
                    for (waits, fns, k, inc) in self.q[e]:
                        for (wk, wv) in waits:
                            engobj.wait_ge(self.sem[wk], wv)
                        n = len(fns)
                        for i, fn in enumerate(fns):
                            ins = fn(engobj)
                            if i == n - 1:
                                ins.then_inc(self.sem[k], inc)
                return body
            block.tensor(mk("pe"))
            block.scalar(mk("act"))
            block.vector(mk("dve"))
            block.gpsimd(mk("pool"))
            block.sync(mk("sp"))


class Tl:
    __slots__ = ("t", "b")

    def __init__(self, t, b):
        self.t = t
        self.b = b


class Ctx:
    def __init__(self, nc, stack):
        self.nc = nc
        self.st = stack
        self.S = Sched(nc, stack)
        self.uid = 0
        self.ps = []
        self.psi = 0
        self.outs = []

    def name(self, p):
        self.uid += 1
        return "%s_%d" % (p, self.uid)

    def sb(self, shape, dt, name="t"):
        n = self.name(name)
        t = self.st.enter_context(self.nc.sbuf_tensor(n, list(shape), dt))
        return Tl(t, Buf(n))

    def pool(self, n, shape, dt, name="p"):
        return Rot([self.sb(shape, dt, name) for _ in range(n)])

    def init_psum(self, n=8):
        for i in range(n):
            nm = self.name("ps")
            t = self.st.enter_context(self.nc.psum_tensor(nm, [128, 512], F32))
            self.ps.append(Tl(t, Buf(nm)))

    def psum(self):
        p = self.ps[self.psi % len(self.ps)]
        self.psi += 1
        return p

    def dram_buf(self, name):
        return Buf(name)


class Rot:
    def __init__(self, items):
        self.items = items
        self.i = 0

    def next(self):
        it = self.items[self.i % len(self.items)]
        self.i += 1
        return it


def mm_group(K, ps, pairs, reads):
    n = len(pairs)

    def mk(i, l, r, o):
        return lambda e: e.matmul(o, l, r, start=(i == 0), stop=(i == n - 1))
    fns = [mk(i, l, r, o) for i, (l, r, o) in enumerate(pairs)]
    K.S.op("pe", fns, reads=reads, writes=[ps.b])


def rstd_from_sq(K, C, sq, sq_reads, nk, ncols, inv_n, extra=None):
    ps = K.psum()
    pairs = [(C["ones_bf"].t[:, :], sq.t[:, k, 0:ncols], ps.t[:, 0:ncols]) for k in range(nk)]
    mm_group(K, ps, pairs, reads=[C["ones_bf"].b, sq.b])
    r = C["rstd_pool"].next()
    K.S.op("act", lambda e: e.activation(out=r.t[:, 0:ncols], in_=ps.t[:, 0:ncols], func=AF.Sqrt,
                                         bias=C["eps_col"].t[:, 0:1], scale=inv_n),
           reads=[ps.b, C["eps_col"].b], writes=[r.b])
    K.S.op("dve", lambda e: e.reciprocal(out=r.t[:, 0:ncols], in_=r.t[:, 0:ncols]), reads=[r.b], writes=[r.b])
    return r


def pre_norm(K, C, x, xb, g, gcol0, h, hb, s):
    sl = slice(s * 512, (s + 1) * 512)
    sq = C["sq_pool"].next()
    K.S.op("act", lambda e: e.activation(out=sq.t[:, :, :], in_=x.t[:, :, sl], func=AF.Square),
           reads=[xb], writes=[sq.b])
    r = rstd_from_sq(K, C, sq, None, 8, 512, 1.0 / D)
    fns = [(lambda e, k=k: e.scalar_tensor_tensor(
        out=h.t[:, k, sl], in0=x.t[:, k, sl], scalar=g.t[:, gcol0 + k:gcol0 + k + 1], in1=r.t[:, :],
        op0=ALU.mult, op1=ALU.mult)) for k in range(8)]
    K.S.op("dve", fns, reads=[xb, r.b, g.b], writes=[hb])


def post_norm_residual(K, C, y, yb, g, gcol0, alpha, x, xb, s):
    sl = slice(s * 512, (s + 1) * 512)
    sq = C["sq_pool"].next()
    K.S.op("act", lambda e: e.activation(out=sq.t[:, :, :], in_=y.t[:, :, sl], func=AF.Square),
           reads=[yb], writes=[sq.b])
    r = rstd_from_sq(K, C, sq, None, 8, 512, 1.0 / D)
    for k in range(8):
        tmp = C["tmp_pool"].next()
        K.S.op("dve", lambda e, k=k, tmp=tmp: e.scalar_tensor_tensor(
            out=tmp.t[:, :], in0=y.t[:, k, sl], scalar=g.t[:, gcol0 + k:gcol0 + k + 1], in1=r.t[:, :],
            op0=ALU.mult, op1=ALU.mult), reads=[yb, r.b, g.b], writes=[tmp.b])
        K.S.op("dve", lambda e, k=k, tmp=tmp: e.scalar_tensor_tensor(
            out=x.t[:, k, sl], in0=tmp.t[:, :], scalar=float(alpha), in1=x.t[:, k, sl],
            op0=ALU.mult, op1=ALU.add), reads=[tmp.b, xb], writes=[xb])


def load_w(K, slot, src_ap, q="pool"):
    K.S.dma(q, lambda e: e.dma_start(out=slot.t[:], in_=src_ap), writes=[slot.b])


def ffn(K, C, h, hb, wg_ap, wu_ap, wd_ap, y, yb):
    a = C["a"]
    nsub = PASS // 512
    for f in range(NF):
        wg = C["wgu_pool"].next()
        load_w(K, wg, wg_ap[f].rearrange("p (k j) -> p k j", j=128))
        wu = C["wgu_pool"].next()
        load_w(K, wu, wu_ap[f].rearrange("p (k j) -> p k j", j=128))
        for s in range(nsub):
            sl = slice(s * 512, (s + 1) * 512)
            pg = K.psum()
            mm_group(K, pg, [(wg.t[:, k, :], h.t[:, k, sl], pg.t[:, :]) for k in range(8)], reads=[wg.b, hb])
            pu = K.psum()
            mm_group(K, pu, [(wu.t[:, k, :], h.t[:, k, sl], pu.t[:, :]) for k in range(8)], reads=[wu.b, hb])
            sg = C["tmp_pool"].next()
            K.S.op("act", lambda e, sg=sg, pg=pg: e.activation(out=sg.t[:, :], in_=pg.t[:, :], func=AF.Silu),
                   reads=[pg.b], writes=[sg.b])
            ab = a.b[f * nsub + s]
            K.S.op("dve", lambda e, sg=sg, pu=pu, f=f, sl=sl: e.tensor_tensor(
                out=a.t[:, f, sl], in0=sg.t[:, :], in1=pu.t[:, :], op=ALU.mult),
                reads=[sg.b, pu.b], writes=[ab])
    for d in range(8):
        wd = C["wd_pool"].next()
        load_w(K, wd, wd_ap[d].rearrange("p (c j) -> p c j", j=128))
        for s in range(nsub):
            sl = slice(s * 512, (s + 1) * 512)
            py = K.psum()
            mm_group(K, py, [(wd.t[:, c, :], a.t[:, c, sl], py.t[:, :]) for c in range(NF)],
                     reads=[wd.b] + [a.b[c * nsub + s] for c in range(NF)])
            K.S.op("act", lambda e, py=py, d=d, sl=sl: e.activation(out=y.t[:, d, sl], in_=py.t[:, :], func=AF.Copy),
                   reads=[py.b], writes=[yb])


class ATl:
    def __init__(self, t, bufs):
        self.t = t
        self.b = bufs


def common_consts(K):
    C = {}
    ones_bf = K.sb([128, 128], BF16, "ones_bf")
    K.S.op("dve", lambda e: e.memset(ones_bf.t[:, :], 1.0), writes=[ones_bf.b])
    C["ones_bf"] = ones_bf
    eps = K.sb([128, 1], F32, "eps")
    K.S.op("dve", lambda e: e.memset(eps.t[:, :], EPS), writes=[eps.b])
    C["eps_col"] = eps
    C["rstd_pool"] = K.pool(2, [128, 512], F32, "rstd")
    C["sq_pool"] = K.pool(1, [128, 8, 512], BF16, "sq")
    C["tmp_pool"] = K.pool(3, [128, 512], F32, "tmp")
    return C


def ffn_bufs(K, C):
    nsub = PASS // 512
    at = K.sb([128, NF, PASS], BF16, "a")
    C["a"] = ATl(at.t, [Buf("a%d" % i) for i in range(NF * nsub)])
    C["wgu_pool"] = K.pool(4, [128, 8, 128], BF16, "wgu")
    C["wd_pool"] = K.pool(2, [128, NF, 128], BF16, "wd")


OQ, OK_, OV, OCA, OCG, OGQ, OGK, OGV, OGR, OLR, OGT = 0, 512, 1024, 1536, 2048, 2560, 2816, 3072, 3584, 4096, 4112
FM_CHUNKS = ([("q", i, OQ + 128 * i) for i in range(4)] + [("k", i, OK_ + 128 * i) for i in range(4)] +
             [("a", i, OCA + 128 * i) for i in range(4)] + [("g", i, OCG + 128 * i) for i in range(4)] +
             [("gq", i, OGQ + 128 * i) for i in range(2)] + [("r", i, OGR + 128 * i) for i in range(4)] +
             [("lr", 0, OLR)])
NFM = len(FM_CHUNKS)
SB_QT, SB_KT, SB_V, SB_GQT, SB_GV = 0, 128 * TPC, 256 * TPC, 384 * TPC, 448 * TPC
SB_LEN = 576 * TPC
SF_UT, SF_GK, SF_LA = 0, 128 * TPC, 192 * TPC
SF_LEN = 256 * TPC


def build_A():
    nc = bass.Bass("TRN2", target_bir_lowering=False)
    dt = nc.dram_tensor
    xT = dt("xT", [D, TPC], F32, kind="ExternalInput").ap()
    wg = dt("wg", [NF, 128, 8 * 128], F32, kind="ExternalInput").ap()
    wu = dt("wu", [NF, 128, 8 * 128], F32, kind="ExternalInput").ap()
    wd = dt("wd", [8, 128, NF * 128], F32, kind="ExternalInput").ap()
    wfm = dt("wfm", [NFM, 128, 8 * 128], F32, kind="ExternalInput").ap()
    wtm = dt("wtm", [128, 8 * 1280], F32, kind="ExternalInput").ap()
    walpha = dt("walpha", [16, 256], F32, kind="ExternalInput").ap()
    balpha = dt("balpha", [1, 256], F32, kind="ExternalInput").ap()
    gains = dt("gains", [128, 24], F32, kind="ExternalInput").ap()
    x1T = dt("x1T", [D, TPC], F32, kind="ExternalOutput").ap()
    send_bf = dt("send_bf", [4, SB_LEN], BF16, kind="ExternalOutput").ap()
    send_f = dt("send_f", [4, SF_LEN], F32, kind="ExternalOutput").ap()
    silur = dt("silur", [512, TPC], F32, kind="ExternalOutput").ap()
    with ExitStack() as st:
        K = Ctx(nc, st)
        K.init_psum(8)
        C = common_consts(K)
        ffn_bufs(K, C)
        emit_A(K, C, dict(xT=xT, wg=wg, wu=wu, wd=wd, wfm=wfm, wtm=wtm, walpha=walpha, balpha=balpha,
                          gains=gains, x1T=x1T, send_bf=send_bf, send_f=send_f, silur=silur))
        K.S.wait_all_on("sp", K.outs)
        K.S.emit()
    return nc


def emit_A(K, C, io):
    S = K.S
    nsub = PASS // 512
    g = K.sb([128, 24], F32, "gains")
    S.dma("sp", lambda e: e.dma_start(out=g.t[:, :], in_=io["gains"]), writes=[g.b])
    wtm = K.sb([128, 8, 1280], BF16, "wtm")
    load_w(K, wtm, io["wtm"].rearrange("p (k j) -> p k j", j=1280))
    wal = K.sb([16, 256], F32, "walpha")
    S.dma("sp", lambda e: e.dma_start(out=wal.t[:, :], in_=io["walpha"]), writes=[wal.b])
    bal = K.sb([1, 256], F32, "balpha")
    S.dma("sp", lambda e: e.dma_start(out=bal.t[:, :], in_=io["balpha"]), writes=[bal.b])
    ones_f = K.sb([1, 128], F32, "ones_f")
    S.op("dve", lambda e: e.memset(ones_f.t[:, :], 1.0), writes=[ones_f.b])

    x = K.sb([128, 8, PASS], F32, "x")
    h = K.sb([128, 8, PASS], BF16, "h")
    y = K.sb([128, 8, PASS], F32, "y")
    lrT = K.sb([16, PASS], F32, "lrT")
    wfm_pool = K.pool(2, [128, 8, 128], BF16, "wfm")
    st_bf = K.pool(4, [128, 512], BF16, "st_bf")
    st_f = K.pool(3, [128, 512], F32, "st_f")
    st_la = K.pool(2, [128, 256], F32, "st_la")
    xv = io["xT"].rearrange("(k p) t -> p k t", p=128)
    x1v = io["x1T"].rearrange("(k p) t -> p k t", p=128)
    sbf, sf = io["send_bf"], io["send_f"]
    o_x1, o_sbf, o_sf, o_sr = K.dram_buf("o_x1"), K.dram_buf("o_sbf"), K.dram_buf("o_sf"), K.dram_buf("o_sr")
    K.outs += [o_x1, o_sbf, o_sf, o_sr]

    def fm_view(buf_ap, p, off, nfeat):
        return buf_ap[p, off:off + nfeat * TPC].rearrange("(f t) -> f t", t=TPC)

    def tm_view(buf_ap, off, nfeat):
        return buf_ap[:, off:off + nfeat * TPC].rearrange("p (t f) -> t p f", f=nfeat)

    for ps_ in range(TPC // PASS):
        t0 = ps_ * PASS
        S.dma("sp", lambda e, t0=t0: e.dma_start(out=x.t[:, :, :], in_=xv[:, :, t0:t0 + PASS]), writes=[x.b])
        for s in range(nsub):
            pre_norm(K, C, x, x.b, g, 0, h, h.b, s)
        ffn(K, C, h, h.b, io["wg"], io["wu"], io["wd"], y, y.b)
        for s in range(nsub):
            post_norm_residual(K, C, y, y.b, g, 16, 0.5, x, x.b, s)
        S.dma("sp", lambda e, t0=t0: e.dma_start(out=x1v[:, :, t0:t0 + PASS], in_=x.t[:, :, :]),
              reads=[x.b], outs=[o_x1])
        for s in range(nsub):
            pre_norm(K, C, x, x.b, g, 8, h, h.b, s)
        for ci, (kind, i, col) in enumerate(FM_CHUNKS):
            w = wfm_pool.next()
            load_w(K, w, io["wfm"][ci].rearrange("p (k j) -> p k j", j=128))
            for s in range(nsub):
                sl = slice(s * 512, (s + 1) * 512)
                tsl = slice(t0 + s * 512, t0 + (s + 1) * 512)
                pp = K.psum()
                mm_group(K, pp, [(w.t[:, k, :], h.t[:, k, sl], pp.t[:, :]) for k in range(8)], reads=[w.b, h.b])
                if kind in ("q", "k"):
                    o = st_bf.next()
                    sc = 0.125 if kind == "q" else 1.0
                    S.op("act", lambda e, o=o, pp=pp, sc=sc: e.activation(out=o.t[:, :], in_=pp.t[:, :], func=AF.Copy, scale=sc),
                         reads=[pp.b], writes=[o.b])
                    off = SB_QT if kind == "q" else SB_KT
                    dst = fm_view(sbf, i, off, 128)[:, tsl]
                    S.dma("sp", lambda e, o=o, dst=dst: e.dma_start(out=dst, in_=o.t[:, :]), reads=[o.b], outs=[o_sbf])
                elif kind == "a":
                    S.op("act", lambda e, pp=pp, i=i, sl=sl: e.activation(out=y.t[:, i, sl], in_=pp.t[:, :], func=AF.Copy),
                         reads=[pp.b], writes=[y.b])
                elif kind == "g":
                    sg = C["tmp_pool"].next()
                    S.op("act", lambda e, sg=sg, pp=pp: e.activation(out=sg.t[:, :], in_=pp.t[:, :], func=AF.Sigmoid),
                         reads=[pp.b], writes=[sg.b])
                    o = st_f.next()
                    S.op("dve", lambda e, o=o, sg=sg, i=i, sl=sl: e.tensor_tensor(out=o.t[:, :], in0=y.t[:, i, sl], in1=sg.t[:, :], op=ALU.mult),
                         reads=[sg.b, y.b], writes=[o.b])
                    dst = fm_view(sf, i, SF_UT, 128)[:, tsl]
                    S.dma("sp", lambda e, o=o, dst=dst: e.dma_start(out=dst, in_=o.t[:, :]), reads=[o.b], outs=[o_sf])
                elif kind == "gq":
                    o = st_bf.next()
                    S.op("act", lambda e, o=o, pp=pp: e.activation(out=o.t[:, :], in_=pp.t[:, :], func=AF.Copy, scale=0.125),
                         reads=[pp.b], writes=[o.b])
                    for half in range(2):
                        dst = fm_view(sbf, 2 * i + half, SB_GQT, 64)[:, tsl]
                        S.dma("sp", lambda e, o=o, dst=dst, half=half: e.dma_start(out=dst, in_=o.t[64 * half:64 * half + 64, :]),
                              reads=[o.b], outs=[o_sbf])
                elif kind == "r":
                    o = st_f.next()
                    S.op("act", lambda e, o=o, pp=pp: e.activation(out=o.t[:, :], in_=pp.t[:, :], func=AF.Silu),
                         reads=[pp.b], writes=[o.b])
                    dst = io["silur"][128 * i:128 * i + 128, tsl]
                    S.dma("sp", lambda e, o=o, dst=dst: e.dma_start(out=dst, in_=o.t[:, :]), reads=[o.b], outs=[o_sr])
                elif kind == "lr":
                    S.op("act", lambda e, pp=pp, sl=sl: e.activation(out=lrT.t[:, sl], in_=pp.t[0:16, :], func=AF.Copy),
                         reads=[pp.b], writes=[lrT.b])
        vdst = tm_view(sbf, SB_V, 128)
        gvdst = tm_view(sbf, SB_GV, 128)
        gkdst = tm_view(sf, SF_GK, 64)
        ladst = tm_view(sf, SF_LA, 64)
        for tb in range(PASS // 128):
            bsl = slice(tb * 128, (tb + 1) * 128)
            gsl = slice(t0 + tb * 128, t0 + (tb + 1) * 128)
            pp = K.psum()
            mm_group(K, pp, [(h.t[:, k, bsl], wtm.t[:, k, 0:512], pp.t[:, :]) for k in range(8)], reads=[wtm.b, h.b])
            o = st_bf.next()
            S.op("act", lambda e, o=o, pp=pp: e.activation(out=o.t[:, :], in_=pp.t[:, :], func=AF.Copy), reads=[pp.b], writes=[o.b])
            S.dma("sp", lambda e, o=o, gsl=gsl: e.dma_start(out=vdst[gsl], in_=o.t[:, :].rearrange("t (p f) -> t p f", f=128)),
                  reads=[o.b], outs=[o_sbf])
            pp = K.psum()
            mm_group(K, pp, [(h.t[:, k, bsl], wtm.t[:, k, 768:1280], pp.t[:, :]) for k in range(8)], reads=[wtm.b, h.b])
            o = st_bf.next()
            S.op("dve", lambda e, o=o, pp=pp: e.tensor_copy(out=o.t[:, :], in_=pp.t[:, :]), reads=[pp.b], writes=[o.b])
            S.dma("sp", lambda e, o=o, gsl=gsl: e.dma_start(out=gvdst[gsl], in_=o.t[:, :].rearrange("t (p f) -> t p f", f=128)),
                  reads=[o.b], outs=[o_sbf])
            pp = K.psum()
            mm_group(K, pp, [(h.t[:, k, bsl], wtm.t[:, k, 512:768], pp.t[:, 0:256]) for k in range(8)], reads=[wtm.b, h.b])
            o = st_la.next()
            S.op("act", lambda e, o=o, pp=pp: e.activation(out=o.t[:, :], in_=pp.t[:, 0:256], func=AF.Copy), reads=[pp.b], writes=[o.b])
            S.dma("sp", lambda e, o=o, gsl=gsl: e.dma_start(out=gkdst[gsl], in_=o.t[:, :].rearrange("t (p f) -> t p f", f=64)),
                  reads=[o.b], outs=[o_sf])
            pp = K.psum()
            mm_group(K, pp, [(lrT.t[:, bsl], wal.t[:, :], pp.t[:, 0:256]), (ones_f.t[:, :], bal.t[:, :], pp.t[:, 0:256])],
                     reads=[lrT.b, wal.b, ones_f.b, bal.b])
            o = st_la.next()
            S.op("act", lambda e, o=o, pp=pp: e.activation(out=o.t[:, :], in_=pp.t[:, 0:256], func=AF.Exp, scale=-1.0), reads=[pp.b], writes=[o.b])
            S.op("act", lambda e, o=o: e.activation(out=o.t[:, :], in_=o.t[:, :], func=AF.Ln, bias=1.0), reads=[o.b], writes=[o.b])
            S.op("dve", lambda e, o=o: e.tensor_scalar(out=o.t[:, :], in0=o.t[:, :], scalar1=-1.0 / 16.0, scalar2=None, op0=ALU.mult),
                 reads=[o.b], writes=[o.b])
            S.dma("sp", lambda e, o=o, gsl=gsl: e.dma_start(out=ladst[gsl], in_=o.t[:, :].rearrange("t (p f) -> t p f", f=64)),
                  reads=[o.b], outs=[o_sf])


def _c(a):
    return np.ascontiguousarray(a)


def lay_fm_w(W, ncols_chunks):
    Kc = W.shape[0] // 128
    n = W.shape[1] // 128
    return _c(W.reshape(Kc, 128, n, 128).transpose(2, 1, 0, 3).reshape(n, 128, Kc * 128))


def lay_pk(v):
    return _c(v.reshape(-1, 128).T)


def prep_A_weights(inp, l):
    w_in = inp["w_in"][l]
    cols = []
    for (kind, i, col) in FM_CHUNKS:
        blk = np.zeros((D, 128), np.float32)
        n = 16 if kind == "lr" else 128
        blk[:, :n] = w_in[:, col:col + n]
        cols.append(blk)
    wfm = lay_fm_w(np.concatenate(cols, axis=1), NFM)
    tm = np.concatenate([w_in[:, OV:OV + 512], w_in[:, OGK:OGK + 256], w_in[:, OGV:OGV + 512]], axis=1)
    wtm = _c(tm.reshape(8, 128, 1280).transpose(1, 0, 2).reshape(128, 8 * 1280))
    gains = np.concatenate([lay_pk(inp["norm_pre"][l, 0]), lay_pk(inp["norm_pre"][l, 1]), lay_pk(inp["norm_post"][l, 0])], axis=1)
    return dict(wg=lay_fm_w(inp["ffn1_w_gate"][l], NF), wu=lay_fm_w(inp["ffn1_w_up"][l], NF),
                wd=lay_fm_w(inp["ffn1_w_down"][l], 8), wfm=wfm, wtm=wtm,
                walpha=_c(inp["gla_w_alpha"][l]), balpha=_c(inp["gla_b_alpha"][l][None, :]), gains=_c(gains))


RB_LEN = 128 * TPC
RF_CY, RF_GN = 0, 128 * TPC
RF_LEN = 256 * TPC


def build_C(dbg=0):
    nc = bass.Bass("TRN2", target_bir_lowering=False)
    dt = nc.dram_tensor
    io = dict(
        x1T=dt("x1T", [D, TPC], F32, kind="ExternalInput").ap(),
        recv_bf=dt("recv_bf", [4, RB_LEN], BF16, kind="ExternalInput").ap(),
        recv_f=dt("recv_f", [4, RF_LEN], F32, kind="ExternalInput").ap(),
        silur=dt("silur", [512, TPC], F32, kind="ExternalInput").ap(),
        wgt=dt("wgt", [24, 128, 8 * 128], F32, kind="ExternalInput").ap(),
        wbr=dt("wbr", [24, 128, 4 * 128], F32, kind="ExternalInput").ap(),
        wo=dt("wo", [8, 128, 8 * 128], F32, kind="ExternalInput").ap(),
        wg=dt("wg", [NF, 128, 8 * 128], F32, kind="ExternalInput").ap(),
        wu=dt("wu", [NF, 128, 8 * 128], F32, kind="ExternalInput").ap(),
        wd=dt("wd", [8, 128, NF * 128], F32, kind="ExternalInput").ap(),
        gains=dt("gains", [128, 40], F32, kind="ExternalInput").ap(),
        x3T=dt("x3T", [D, TPC], F32, kind="ExternalOutput").ap(),
    )
    with ExitStack() as st:
        K = Ctx(nc, st)
        K.init_psum(8)
        C = common_consts(K)
        ffn_bufs(K, C)
        emit_C(K, C, io, dbg)
        K.S.wait_all_on("sp", K.outs)
        K.S.emit()
    return nc


def emit_C(K, C, io, dbg=0):
    S = K.S
    nsub = PASS // 512
    T = TPC
    g = K.sb([128, 40], F32, "gainsC")
    S.dma("sp", lambda e: e.dma_start(out=g.t[:, :], in_=io["gains"]), writes=[g.b])
    x = K.sb([128, 8, PASS], F32, "x")
    h = K.sb([128, 8, PASS], BF16, "h")
    y = K.sb([128, 8, PASS], F32, "y")
    a = C["a"]
    sr_pool = K.pool(1, [128, 4, 512], F32, "sr")
    wgt_pool = K.pool(2, [128, 8, 128], BF16, "wgt")
    wbr_pool = K.pool(2, [128, 4, 128], BF16, "wbr")
    acc = [K.sb([128, 512], F32, "acc") for _ in range(nsub)]
    ln_mean = K.sb([128, 512], F32, "ln_mean")
    ln_msq = K.sb([128, 512], F32, "ln_msq")
    xv = io["x1T"].rearrange("(k p) t -> p k t", p=128)
    ov = io["x3T"].rearrange("(k p) t -> p k t", p=128)
    sbo_v = io["recv_bf"][:, 0:128 * T].rearrange("p (f t) -> f p t", t=T)
    cy_v = io["recv_f"][:, RF_CY:RF_CY + 128 * T].rearrange("p (f t) -> f p t", t=T)
    gn_v = io["recv_f"][:, RF_GN:RF_GN + 128 * T].rearrange("p (f t) -> f p t", t=T)
    sr_v = io["silur"].rearrange("(c f) t -> f c t", f=128)
    o_x3 = K.dram_buf("o_x3")
    K.outs.append(o_x3)

    def ab(f0, f1, s):
        return [a.b[f * nsub + s] for f in range(f0, f1)]

    for ps_ in range(TPC // PASS):
        t0 = ps_ * PASS
        S.dma("sp", lambda e, t0=t0: e.dma_start(out=x.t[:, :, :], in_=xv[:, :, t0:t0 + PASS]), writes=[x.b])
        for s in range(nsub):
            pre_norm(K, C, x, x.b, g, 0, h, h.b, s)
        for s in range(nsub):
            sl = slice(s * 512, (s + 1) * 512)
            tsl = slice(t0 + s * 512, t0 + (s + 1) * 512)
            S.dma("sp", lambda e, sl=sl, tsl=tsl: e.dma_start(out=y.t[:, 0:4, sl], in_=cy_v[:, :, tsl]), writes=[y.b])
            S.dma("sp", lambda e, sl=sl, tsl=tsl: e.dma_start(out=y.t[:, 4:8, sl], in_=gn_v[:, :, tsl]), writes=[y.b])
            sr = sr_pool.next()
            S.dma("sp", lambda e, sr=sr, tsl=tsl: e.dma_start(out=sr.t[:, :, :], in_=sr_v[:, :, tsl]), writes=[sr.b])
            S.dma("sp", lambda e, sl=sl, tsl=tsl: e.dma_start(out=a.t[:, 8:12, sl], in_=sbo_v[:, :, tsl]), writes=ab(8, 12, s))
            sq = C["sq_pool"].next()
            S.op("act", lambda e, sq=sq, sl=sl: e.activation(out=sq.t[:, 0:4, :], in_=y.t[:, 0:4, sl], func=AF.Copy),
                 reads=[y.b], writes=[sq.b])
            S.op("act", lambda e, sq=sq, sl=sl: e.activation(out=sq.t[:, 4:8, :], in_=y.t[:, 0:4, sl], func=AF.Square),
                 reads=[y.b], writes=[sq.b])
            p1 = K.psum()
            mm_group(K, p1, [(C["ones_bf"].t[:, :], sq.t[:, c, :], p1.t[:, :]) for c in range(4)], reads=[C["ones_bf"].b, sq.b])
            p2 = K.psum()
            mm_group(K, p2, [(C["ones_bf"].t[:, :], sq.t[:, 4 + c, :], p2.t[:, :]) for c in range(4)], reads=[C["ones_bf"].b, sq.b])
            mean = ln_mean
            S.op("act", lambda e, mean=mean, p1=p1: e.activation(out=mean.t[:, :], in_=p1.t[:, :], func=AF.Copy, scale=1.0 / 512),
                 reads=[p1.b], writes=[mean.b])
            msq = ln_msq
            S.op("dve", lambda e, mean=mean, msq=msq: e.tensor_tensor(out=msq.t[:, :], in0=mean.t[:, :], in1=mean.t[:, :], op=ALU.mult),
                 reads=[mean.b], writes=[msq.b])
            S.op("dve", lambda e, msq=msq, p2=p2: e.scalar_tensor_tensor(out=msq.t[:, :], in0=p2.t[:, :], scalar=1.0 / 512, in1=msq.t[:, :],
                                                                       op0=ALU.mult, op1=ALU.subtract),
                 reads=[p2.b, msq.b], writes=[msq.b])
            rs = C["rstd_pool"].next()
            S.op("act", lambda e, rs=rs, msq=msq: e.activation(out=rs.t[:, :], in_=msq.t[:, :], func=AF.Sqrt, bias=C["eps_col"].t[:, 0:1], scale=1.0),
                 reads=[msq.b, C["eps_col"].b], writes=[rs.b])
            S.op("dve", lambda e, rs=rs: e.reciprocal(out=rs.t[:, :], in_=rs.t[:, :]), reads=[rs.b], writes=[rs.b])
            for c in range(4):
                t = C["tmp_pool"].next()
                S.op("dve", lambda e, t=t, c=c, sl=sl, mean=mean: e.tensor_tensor(out=t.t[:, :], in0=y.t[:, c, sl], in1=mean.t[:, :], op=ALU.subtract),
                     reads=[y.b, mean.b], writes=[t.b])
                S.op("dve", lambda e, t=t, rs=rs: e.tensor_tensor(out=t.t[:, :], in0=t.t[:, :], in1=rs.t[:, :], op=ALU.mult),
                     reads=[t.b, rs.b], writes=[t.b])
                S.op("act", lambda e, t=t, c=c, sl=sl: e.activation(out=a.t[:, c, sl], in_=t.t[:, :], func=AF.Silu,
                                                                      scale=g.t[:, 32 + c:33 + c], bias=g.t[:, 36 + c:37 + c]),
                     reads=[t.b, g.b], writes=ab(c, c + 1, s))
                S.op(POOL_ENG, lambda e, c=c, sl=sl, sr=sr: e.tensor_tensor(out=a.t[:, 4 + c, sl], in0=y.t[:, 4 + c, sl], in1=sr.t[:, c, :], op=ALU.mult),
                     reads=[y.b, sr.b], writes=ab(4 + c, 5 + c, s))
        base = {0: 8, 1: 0, 2: 4}
        for d in range(8):
            for gi in range(3):
                wb = wbr_pool.next()
                load_w(K, wb, io["wbr"][gi * 8 + d].rearrange("p (k j) -> p k j", j=128))
                wgt = wgt_pool.next()
                load_w(K, wgt, io["wgt"][gi * 8 + d].rearrange("p (k j) -> p k j", j=128))
                for s in range(nsub):
                    sl = slice(s * 512, (s + 1) * 512)
                    pb = K.psum()
                    mm_group(K, pb, [(wb.t[:, kc, :], a.t[:, base[gi] + kc, sl], pb.t[:, :]) for kc in range(4)],
                             reads=[wb.b] + ab(base[gi], base[gi] + 4, s))
                    pl = K.psum()
                    mm_group(K, pl, [(wgt.t[:, k, :], h.t[:, k, sl], pl.t[:, :]) for k in range(8)], reads=[wgt.b, h.b])
                    sig = C["tmp_pool"].next()
                    S.op("act", lambda e, sig=sig, pl=pl: e.activation(out=sig.t[:, :], in_=pl.t[:, :], func=AF.Sigmoid),
                         reads=[pl.b], writes=[sig.b])
                    if gi == 0:
                        S.op("dve", lambda e, sig=sig, pb=pb, s=s: e.tensor_tensor(out=acc[s].t[:, :], in0=sig.t[:, :], in1=pb.t[:, :], op=ALU.mult),
                             reads=[sig.b, pb.b], writes=[acc[s].b])
                    else:
                        S.op("dve", lambda e, sig=sig, pb=pb: e.tensor_tensor(out=sig.t[:, :], in0=sig.t[:, :], in1=pb.t[:, :], op=ALU.mult),
                             reads=[sig.b, pb.b], writes=[sig.b])
                        if gi == 1:
                            S.op(POOL_ENG, lambda e, sig=sig, s=s: e.tensor_tensor(out=acc[s].t[:, :], in0=acc[s].t[:, :], in1=sig.t[:, :], op=ALU.add),
                                 reads=[sig.b, acc[s].b], writes=[acc[s].b])
                        else:
                            S.op(POOL_ENG, lambda e, sig=sig, s=s, d=d, sl=sl: e.tensor_tensor(out=a.t[:, 12 + d, sl], in0=acc[s].t[:, :], in1=sig.t[:, :], op=ALU.add),
                                 reads=[sig.b, acc[s].b], writes=ab(12 + d, 13 + d, s))
        for do in range(8):
            wo = C["wgu_pool"].next()
            load_w(K, wo, io["wo"][do].rearrange("p (k j) -> p k j", j=128))
            for s in range(nsub):
                sl = slice(s * 512, (s + 1) * 512)
                pp = K.psum()
                mm_group(K, pp, [(wo.t[:, k, :], a.t[:, 12 + k, sl], pp.t[:, :]) for k in range(8)], reads=[wo.b] + ab(12, 20, s))
                S.op("act", lambda e, pp=pp, do=do, sl=sl: e.activation(out=y.t[:, do, sl], in_=pp.t[:, :], func=AF.Copy),
                     reads=[pp.b], writes=[y.b])
        if dbg == 2:
            S.dma("sp", lambda e, t0=t0: e.dma_start(out=ov[:, :, t0:t0 + PASS], in_=y.t[:, :, :]), reads=[y.b], outs=[o_x3])
            continue
        for s in range(nsub):
            post_norm_residual(K, C, y, y.b, g, 8, 1.0, x, x.b, s)
        if dbg == 1:
            S.dma("sp", lambda e, t0=t0: e.dma_start(out=ov[:, :, t0:t0 + PASS], in_=x.t[:, :, :]), reads=[x.b], outs=[o_x3])
            continue
        for s in range(nsub):
            pre_norm(K, C, x, x.b, g, 16, h, h.b, s)
        ffn(K, C, h, h.b, io["wg"], io["wu"], io["wd"], y, y.b)
        for s in range(nsub):
            post_norm_residual(K, C, y, y.b, g, 24, 0.5, x, x.b, s)
        S.dma("sp", lambda e, t0=t0: e.dma_start(out=ov[:, :, t0:t0 + PASS], in_=x.t[:, :, :]), reads=[x.b], outs=[o_x3])


def prep_C_weights(inp, l):
    w_in = inp["w_in"][l]
    wgt = lay_fm_w(w_in[:, OGT:OGT + 3072], 24)
    wb = inp["w_branch"][l]
    wbr = np.concatenate([lay_fm_w(wb[gi], 8) for gi in range(3)], axis=0)
    gains = np.concatenate([lay_pk(inp["norm_pre"][l, 1]), lay_pk(inp["norm_post"][l, 1]), lay_pk(inp["norm_pre"][l, 2]),
                            lay_pk(inp["norm_post"][l, 2]), lay_pk(inp["conv_ln_g"][l]), lay_pk(inp["conv_ln_b"][l])], axis=1)
    return dict(wgt=wgt, wbr=_c(wbr), wo=lay_fm_w(inp["w_out"][l], 8), wg=lay_fm_w(inp["ffn2_w_gate"][l], NF),
                wu=lay_fm_w(inp["ffn2_w_up"][l], NF), wd=lay_fm_w(inp["ffn2_w_down"][l], 8), gains=_c(gains))


NQT = SEQ // 512
NBLK = SEQ // 128
S2B_LEN = 128 * TPC
S2F_LEN = 256 * TPC


def build_B():
    nc = bass.Bass("TRN2", target_bir_lowering=False)
    dt = nc.dram_tensor
    io = dict(
        rbf=dt("rbf", [4, SB_LEN], BF16, kind="ExternalInput").ap(),
        rf=dt("rf", [4, SF_LEN], F32, kind="ExternalInput").ap(),
        bw=dt("bw", [128, 33], F32, kind="ExternalInput").ap(),
        s2bf=dt("s2bf", [4, S2B_LEN], BF16, kind="ExternalOutput").ap(),
        s2f=dt("s2f", [4, S2F_LEN], F32, kind="ExternalOutput").ap(),
    )
    with ExitStack() as st:
        K = Ctx(nc, st)
        K.init_psum(8)
        emit_B(K, io)
        K.S.wait_all_on("sp", K.outs)
        K.S.emit()
    return nc


def emit_B(K, io):
    S = K.S
    T = TPC
    rbf, rf = io["rbf"], io["rf"]
    o_bf, o_f = K.dram_buf("o_s2bf"), K.dram_buf("o_s2f")
    K.outs += [o_bf, o_f]
    ones_bf = K.sb([128, 128], BF16, "ones_bf")
    S.op("dve", lambda e: e.memset(ones_bf.t[:, :], 1.0), writes=[ones_bf.b])
    negones = K.sb([128, 1], BF16, "negones")
    S.op("dve", lambda e: e.memset(negones.t[:, :], -1.0), writes=[negones.b])
    eps = K.sb([128, 1], F32, "eps")
    S.op("dve", lambda e: e.memset(eps.t[:, :], EPS), writes=[eps.b])
    ntri = K.sb([128, 128], BF16, "ntri")
    S.op("pool", lambda e: e.memset(ntri.t[:, :], -1.0), writes=[ntri.b])
    S.op("pool", lambda e: e.affine_select(out=ntri.t[:, :], in_=ntri.t[:, :], pattern=[[-1, 128]], compare_op=ALU.is_ge,
                                            fill=0.0, base=0, channel_multiplier=1), reads=[ntri.b], writes=[ntri.b])
    masks = []
    for i in range(4):
        m = K.sb([128, 512], BF16, "mask")
        S.op("pool", lambda e, m=m: e.memset(m.t[:, :], 1.0), writes=[m.b])
        S.op("pool", lambda e, m=m, i=i: e.affine_select(out=m.t[:, :], in_=m.t[:, :], pattern=[[1, 512]], compare_op=ALU.is_gt,
                                                           fill=0.0, base=-128 * i, channel_multiplier=-1), reads=[m.b], writes=[m.b])
        masks.append(m)
    tris = K.sb([128, 128], BF16, "tris")
    S.op("pool", lambda e: e.memset(tris.t[:, :], 1.0), writes=[tris.b])
    S.op("pool", lambda e: e.affine_select(out=tris.t[:, :], in_=tris.t[:, :], pattern=[[-1, 128]], compare_op=ALU.is_gt,
                                            fill=0.0, base=0, channel_multiplier=1), reads=[tris.b], writes=[tris.b])
    S.op("pool", lambda e: e.memset(tris.t[64:128, 0:64], 0.0), reads=[tris.b], writes=[tris.b])
    cind = K.sb([128, 2], BF16, "cind")
    S.op("pool", lambda e: e.memset(cind.t[:, :], 0.0), writes=[cind.b])
    S.op("pool", lambda e: e.memset(cind.t[0:64, 0:1], 1.0), reads=[cind.b], writes=[cind.b])
    S.op("pool", lambda e: e.memset(cind.t[64:128, 1:2], 1.0), reads=[cind.b], writes=[cind.b])
    bw = K.sb([128, 33], F32, "bw")
    S.dma("sp", lambda e: e.dma_start(out=bw.t[:, :], in_=io["bw"]), writes=[bw.b])
    qT = K.sb([128, SEQ], BF16, "qT")
    kT = K.sb([128, SEQ], BF16, "kT")
    v = K.sb([128, NBLK, 128], BF16, "v")
    gqT = K.sb([64, SEQ], BF16, "gqT")
    gv = K.sb([128, NBLK, 128], BF16, "gv")
    gk = K.sb([128, NBLK, 64], F32, "gk")
    la = K.sb([128, NBLK, 64], F32, "la")
    u = K.sb([128, 32 + SEQ], F32, "u")
    S.op("pool", lambda e: e.memset(u.t[:, 0:32], 0.0), writes=[u.b])
    for j in range(4):
        ts = slice(j * T, (j + 1) * T)
        bs = slice(j * 16, (j + 1) * 16)
        S.dma("sp", lambda e, j=j, ts=ts: e.dma_start(out=kT.t[:, ts], in_=rbf[j, SB_KT:SB_KT + 128 * T].rearrange("(f t) -> f t", t=T)), writes=[kT.b])
        S.dma("sp", lambda e, j=j, ts=ts: e.dma_start(out=qT.t[:, ts], in_=rbf[j, SB_QT:SB_QT + 128 * T].rearrange("(f t) -> f t", t=T)), writes=[qT.b])
        S.dma("sp", lambda e, j=j, bs=bs: e.dma_start(out=v.t[:, bs, :], in_=rbf[j, SB_V:SB_V + 128 * T].rearrange("(k t f) -> t k f", t=128, f=128)), writes=[v.b])
        S.dma("sp", lambda e, j=j, ts=ts: e.dma_start(out=gqT.t[:, ts], in_=rbf[j, SB_GQT:SB_GQT + 64 * T].rearrange("(f t) -> f t", t=T)), writes=[gqT.b])
        S.dma("sp", lambda e, j=j, bs=bs: e.dma_start(out=gv.t[:, bs, :], in_=rbf[j, SB_GV:SB_GV + 128 * T].rearrange("(k t f) -> t k f", t=128, f=128)), writes=[gv.b])
        S.dma("sp", lambda e, j=j, bs=bs: e.dma_start(out=gk.t[:, bs, :], in_=rf[j, SF_GK:SF_GK + 64 * T].rearrange("(k t f) -> t k f", t=128, f=64)), writes=[gk.b])
        S.dma("sp", lambda e, j=j, bs=bs: e.dma_start(out=la.t[:, bs, :], in_=rf[j, SF_LA:SF_LA + 64 * T].rearrange("(k t f) -> t k f", t=128, f=64)), writes=[la.b])
        S.dma("sp", lambda e, j=j, ts=ts: e.dma_start(out=u.t[:, 32 + j * T:32 + (j + 1) * T], in_=rf[j, SF_UT:SF_UT + 128 * T].rearrange("(f t) -> f t", t=T)), writes=[u.b])
    ps_z = Rot(K.ps[0:3])
    ps_o = Rot(K.ps[3:5])
    ps_c = Rot(K.ps[5:7])
    ps_g = K.ps[7]
    e_pool = K.pool(2, [128, 512], F32, "e")
    sp_pool = K.pool(3, [128, 512], BF16, "sp")
    w_pool = K.pool(3, [128, 512], BF16, "w")
    cb_pool = K.pool(2, [1, 512], BF16, "carry")
    osb_pool = K.pool(2, [64, 512], BF16, "osb")

    def conv_gen():
        acc_pool = K.pool(2, [128, 512], F32, "cacc")
        for pc in range(SEQ // 512):
            acc = acc_pool.next()
            c0 = 2 + pc * 512
            S.op("dve", lambda e, acc=acc, c0=c0: e.tensor_scalar(out=acc.t[:, :], in0=u.t[:, c0:c0 + 512], scalar1=bw.t[:, 0:1], scalar2=bw.t[:, 31:32],
                                                                    op0=ALU.mult, op1=ALU.add), reads=[u.b, bw.b], writes=[acc.b])
            yield
            for k in range(1, 31):
                S.op("dve", lambda e, acc=acc, c0=c0, k=k: e.scalar_tensor_tensor(out=acc.t[:, :], in0=u.t[:, c0 + k:c0 + k + 512], scalar=bw.t[:, k:k + 1],
                                                                                  in1=acc.t[:, :], op0=ALU.mult, op1=ALU.add),
                     reads=[u.b, bw.b, acc.b], writes=[acc.b])
                yield
            dst = io["s2f"][pc // 4, 0:128 * T].rearrange("(f t) -> f t", t=T)[:, (pc % 4) * 512:(pc % 4 + 1) * 512]
            S.dma("sp", lambda e, acc=acc, dst=dst: e.dma_start(out=dst, in_=acc.t[:, :]), reads=[acc.b], outs=[o_f])
            yield

    def gla_gen():
        state = K.sb([64, 128], F32, "gstate")
        S.op("pool", lambda e: e.memset(state.t[:, :], 0.0), writes=[state.b])
        sbf_pool = K.pool(2, [64, 128], BF16, "gstbf")
        labf_pool = K.pool(2, [128, 64], BF16, "labf")
        ed_pool = K.pool(2, [128, 64], F32, "ged")
        kd_pool = K.pool(2, [128, 64], BF16, "gkd")
        lam_pool = K.pool(2, [64, 2], F32, "glam")
        osb_p = K.pool(2, [128, 128], F32, "gosb")
        sq_p = K.pool(2, [128, 128], BF16, "gsq")
        rs_p = K.pool(2, [128, 128], F32, "grs")
        gst_pool = K.pool(2, [128, 512], F32, "gnst")
        pg = ps_g
        r_dte = pg.t[:, 0:64]
        r_lam = pg.t[0:64, 64:66]
        r_st = pg.t[0:64, 128:256]
        r_o = pg.t[:, 256:384]
        r_ss = pg.t[:, 384:512]
        b_dte, b_lam, b_st, b_o, b_ss = Buf("g_dte"), Buf("g_lam"), Buf("g_st"), Buf("g_o"), Buf("g_ss")
        gst = None
        for blk in range(NBLK):
            labf = labf_pool.next()
            S.op("pool", lambda e, labf=labf, blk=blk: e.tensor_copy(out=labf.t[:, :], in_=la.t[:, blk, :]), reads=[la.b], writes=[labf.b])
            yield
            S.op("pe", [lambda e, labf=labf: e.matmul(r_dte, tris.t[:, :], labf.t[:, :], start=True, stop=True)], reads=[tris.b, labf.b], writes=[b_dte])
            S.op("pe", [lambda e, labf=labf: e.matmul(r_lam, labf.t[:, :], cind.t[:, :], start=True, stop=True)], reads=[cind.b, labf.b], writes=[b_lam])
            yield
            ed = ed_pool.next()
            S.op("act", lambda e, ed=ed: e.activation(out=ed.t[:, :], in_=r_dte, func=AF.Exp), reads=[b_dte], writes=[ed.b])
            lam = lam_pool.next()
            S.op("act", lambda e, lam=lam: e.activation(out=lam.t[:, :], in_=r_lam, func=AF.Exp), reads=[b_lam], writes=[lam.b])
            yield
            kd = kd_pool.next()
            S.op("dve", lambda e, kd=kd, ed=ed, blk=blk: e.tensor_tensor(out=kd.t[:, :], in0=gk.t[:, blk, :], in1=ed.t[:, :], op=ALU.mult),
                 reads=[gk.b, ed.b], writes=[kd.b])
            yield
            for c in range(2):
                cs = slice(64 * c, 64 * c + 64)
                S.op("pe", [lambda e, kd=kd, cs=cs, blk=blk: e.matmul(r_st, kd.t[cs, :], gv.t[cs, blk, :], start=True, stop=True)],
                     reads=[kd.b, gv.b], writes=[b_st])
                yield
                S.op("dve", lambda e, lam=lam, c=c: e.scalar_tensor_tensor(out=state.t[:, :], in0=state.t[:, :], scalar=lam.t[:, c:c + 1], in1=r_st,
                                                                          op0=ALU.mult, op1=ALU.add), reads=[state.b, lam.b, b_st], writes=[state.b])
                sbf = sbf_pool.next()
                S.op("dve", lambda e, sbf=sbf: e.tensor_copy(out=sbf.t[:, :], in_=state.t[:, :]), reads=[state.b], writes=[sbf.b])
                yield
                tk = slice(blk * 128 + 64 * c, blk * 128 + 64 * c + 64)
                S.op("pe", [lambda e, sbf=sbf, tk=tk, c=c: e.matmul(pg.t[:, 256 + 64 * c:256 + 64 * c + 64], sbf.t[:, :], gqT.t[:, tk], start=True, stop=True)],
                     reads=[sbf.b, gqT.b], writes=[b_o])
                yield
            osb = osb_p.next()
            S.op("dve", lambda e, osb=osb: e.tensor_copy(out=osb.t[:, :], in_=r_o), reads=[b_o], writes=[osb.b])
            sq = sq_p.next()
            S.op("pool", lambda e, osb=osb, sq=sq: e.tensor_tensor(out=sq.t[:, :], in0=osb.t[:, :], in1=osb.t[:, :], op=ALU.mult), reads=[osb.b], writes=[sq.b])
            yield
            S.op("pe", [lambda e, sq=sq: e.matmul(r_ss, ones_bf.t[:, :], sq.t[:, :], start=True, stop=True)], reads=[ones_bf.b, sq.b], writes=[b_ss])
            yield
            rs = rs_p.next()
            S.op("act", lambda e, rs=rs: e.activation(out=rs.t[:, :], in_=r_ss, func=AF.Ln, bias=eps.t[:, 0:1], scale=1.0 / 128), reads=[b_ss, eps.b], writes=[rs.b])
            S.op("act", lambda e, rs=rs: e.activation(out=rs.t[:, :], in_=rs.t[:, :], func=AF.Exp, scale=-0.5), reads=[rs.b], writes=[rs.b])
            yield
            if blk % 4 == 0:
                gst = gst_pool.next()
            cs4 = slice((blk % 4) * 128, (blk % 4 + 1) * 128)
            S.op("dve", lambda e, gst=gst, cs4=cs4, osb=osb, rs=rs: e.scalar_tensor_tensor(out=gst.t[:, cs4], in0=osb.t[:, :], scalar=bw.t[:, 32:33], in1=rs.t[:, :],
                                                                                        op0=ALU.mult, op1=ALU.mult), reads=[osb.b, rs.b, bw.b], writes=[gst.b])
            if blk % 4 == 3:
                pc = blk // 4
                dst = io["s2f"][pc // 4, 128 * T:256 * T].rearrange("(f t) -> f t", t=T)[:, (pc % 4) * 512:(pc % 4 + 1) * 512]
                S.dma("sp", lambda e, gst=gst, dst=dst: e.dma_start(out=dst, in_=gst.t[:, :]), reads=[gst.b], outs=[o_f])
            yield

    side = [conv_gen(), gla_gen()]

    def step_side():
        for gnr in list(side):
            try:
                next(gnr)
            except StopIteration:
                side.remove(gnr)

    tiles = []
    for qt in range(NQT):
        for hh in range(2):
            kbs = list(range(4 * qt + 3, -1, -1))
            for idx, kb in enumerate(kbs):
                tiles.append((qt, hh, kb, idx == 0, idx == len(kbs) - 1))
    stA = {}

    def stage_A(i):
        qt, hh, kb, first, last = tiles[i]
        hs = slice(64 * hh, 64 * hh + 64)
        qs = slice(qt * 512, (qt + 1) * 512)
        ks = slice(kb * 128, (kb + 1) * 128)
        pz = ps_z.next()
        S.op("pe", [lambda e: e.matmul(pz.t[:, :], kT.t[hs, ks], qT.t[hs, qs], start=True, stop=False)], reads=[kT.b, qT.b], writes=[pz.b])
        ee = e_pool.next()
        S.op("act", lambda e: e.activation(out=ee.t[:, :], in_=pz.t[:, :], func=AF.Exp), reads=[pz.b], writes=[ee.b])
        sp = sp_pool.next()
        S.op("act", lambda e: e.activation(out=sp.t[:, :], in_=ee.t[:, :], func=AF.Ln, bias=1.0), reads=[ee.b], writes=[sp.b])
        di = kb - 4 * qt
        if di >= 0:
            S.op("pool", lambda e: e.tensor_tensor(out=sp.t[:, :], in0=sp.t[:, :], in1=masks[di].t[:, :], op=ALU.mult), reads=[sp.b, masks[di].b], writes=[sp.b])
        stA[i] = (pz, sp)

    cur = {"po": None, "pc": None, "cb": None}

    def stage_B(i):
        qt, hh, kb, first, last = tiles[i]
        hs = slice(64 * hh, 64 * hh + 64)
        pz, sp = stA.pop(i)
        if first:
            cur["po"] = ps_o.next()
            cur["pc"] = ps_c.next()
        po, pc = cur["po"], cur["pc"]
        if first:
            S.op("pe", [lambda e: e.matmul(pz.t[:, :], ntri.t[:, :], sp.t[:, :], start=False, stop=True)], reads=[ntri.b, sp.b, pz.b], writes=[pz.b])
        else:
            cbo = cur["cb"]
            S.op("pe", [lambda e: e.matmul(pz.t[:, :], ntri.t[:, :], sp.t[:, :], start=False, stop=False),
                        lambda e: e.matmul(pz.t[:, :], ones_bf.t[0:1, :], cbo.t[0:1, :], start=False, stop=True)],
                 reads=[ntri.b, sp.b, pz.b, cbo.b, ones_bf.b], writes=[pz.b])
        w = w_pool.next()
        S.op("act", lambda e: e.activation(out=w.t[:, :], in_=pz.t[:, :], func=AF.Exp), reads=[pz.b], writes=[w.b])
        di = kb - 4 * qt
        if di >= 0:
            S.op("pool", lambda e: e.tensor_tensor(out=w.t[:, :], in0=w.t[:, :], in1=masks[di].t[:, :], op=ALU.mult), reads=[w.b, masks[di].b], writes=[w.b])
        S.op("pe", [lambda e: e.matmul(po.t[0:64, :], v.t[:, kb, hs], w.t[:, :], start=first, stop=last)], reads=[v.b, w.b], writes=[po.b])
        if not last:
            S.op("pe", [lambda e: e.matmul(pc.t[0:1, :], negones.t[:, :], sp.t[:, :], start=first, stop=False, skip_group_check=True)],
                 reads=[negones.b, sp.b], writes=[pc.b])
            cbn = cb_pool.next()
            S.op("dve", lambda e: e.tensor_copy(out=cbn.t[:, :], in_=pc.t[0:1, :]), reads=[pc.b], writes=[cbn.b])
            cur["cb"] = cbn
        else:
            osb = osb_pool.next()
            S.op("dve", lambda e: e.tensor_copy(out=osb.t[:, :], in_=po.t[0:64, :]), reads=[po.b], writes=[osb.b])
            dst = io["s2bf"][qt // 4, 0:128 * T].rearrange("(f t) -> f t", t=T)[64 * hh:64 * hh + 64, (qt % 4) * 512:(qt % 4 + 1) * 512]
            S.dma("sp", lambda e: e.dma_start(out=dst, in_=osb.t[:, :]), reads=[osb.b], outs=[o_bf])

    n = len(tiles)
    stage_A(0)
    for i in range(n):
        if i + 1 < n:
            stage_A(i + 1)
        stage_B(i)
        step_side()
    while side:
        step_side()


def prep_B_weights(inp, l, p):
    cw = inp["conv_w"][l][:, 128 * p:128 * p + 128].T
    cb = inp["conv_b"][l][128 * p:128 * p + 128][:, None]
    gg = inp["gla_norm_g"][l][128 * p:128 * p + 128][:, None]
    return _c(np.concatenate([cw, cb, gg], axis=1).astype(np.float32))


_PROGS = {}


def _prog(name):
    if name not in _PROGS:
        _PROGS[name] = {"A": build_A, "B": build_B, "C": build_C}[name]()
    return _PROGS[name]


def kernel(**inputs):
    inp = {k: np.asarray(v) for k, v in inputs.items()}
    x = inp["x"].astype(np.float32, copy=False)
    cores = list(range(NCORES))
    T = TPC
    xT = [_c(x[c // 4, (c % 4) * T:(c % 4 + 1) * T, :].T) for c in cores]
    for l in range(DEPTH):
        wa = prep_A_weights(inp, l)
        resA = run_bass_kernel_spmd(_prog("A"), [dict(wa, xT=xT[c]) for c in cores], core_ids=cores).results
        inB = []
        for c in cores:
            b, p = c // 4, c % 4
            rbf = _c(np.stack([np.asarray(resA[4 * b + j]["send_bf"])[p] for j in range(4)], axis=0))
            rf = _c(np.stack([np.asarray(resA[4 * b + j]["send_f"])[p] for j in range(4)], axis=0))
            inB.append(dict(rbf=rbf, rf=rf, bw=prep_B_weights(inp, l, p)))
        resB = run_bass_kernel_spmd(_prog("B"), inB, core_ids=cores).results
        wc = prep_C_weights(inp, l)
        inC = []
        for c in cores:
            b, j = c // 4, c % 4
            recv_bf = _c(np.stack([np.asarray(resB[4 * b + p]["s2bf"])[j] for p in range(4)], axis=0))
            recv_f = _c(np.stack([np.asarray(resB[4 * b + p]["s2f"])[j] for p in range(4)], axis=0))
            inC.append(dict(wc, x1T=np.asarray(resA[c]["x1T"]), recv_bf=recv_bf, recv_f=recv_f, silur=np.asarray(resA[c]["silur"])))
        resC = run_bass_kernel_spmd(_prog("C"), inC, core_ids=cores).results
        xT = [np.asarray(resC[c]["x3T"]) for c in cores]
    out = np.empty((BATCH, SEQ, D), np.float32)
    for c in cores:
        out[c // 4, (c % 4) * T:(c % 4 + 1) * T, :] = xT[c].T
    return out
```

```python
import numpy as np
from contextlib import ExitStack
import ml_dtypes
import concourse.bass as bass
import concourse.mybir as mybir
from concourse.bass_utils import run_bass_kernel_spmd

F32 = mybir.dt.float32
BF16 = mybir.dt.bfloat16
AF = mybir.ActivationFunctionType
ALU = mybir.AluOpType

D = 1024
DFF = 2816
NF = DFF // 128
SEQ = 8192
BATCH = 2
DEPTH = 4
TPC = 2048
PASS = 1024
EPS = 1e-6
INW = 7184
NCORES = 8
POOL_ENG = "dve"

ENGS = ("pe", "act", "dve", "pool", "sp")
SEM_LIMIT = 28000


class Buf:
    __slots__ = ("name", "w", "r", "dsem", "persist")

    def __init__(self, name, persist=False):
        self.name = name
        self.w = None
        self.r = {}
        self.dsem = None
        self.persist = persist


class Sched:
    def __init__(self, nc, stack):
        self.nc = nc
        self.stack = stack
        self.sem = {}
        self.cnt = {}
        self.gen = {e: 0 for e in ENGS}
        self.seen = {e: {} for e in ENGS}
        self.q = {e: [] for e in ENGS}
        self.nsem = 0
        for e in ENGS:
            self._mksem((e, 0))
        self.n_inst = 0
        self.dma_free = []
        self.dma_bufs = []
        self.dyn = {}
        self.cid_ap = None

    def _mksem(self, key):
        self.nsem += 1
        h = self.stack.enter_context(self.nc.semaphore("s%d" % self.nsem))
        self.sem[key] = h
        self.cnt[key] = 0
        return key

    def ekey(self, e):
        return (e, self.gen[e])

    def _waits(self, e, deps):
        out = []
        seen = self.seen[e]
        for (k, v) in deps:
            if k[0] == e:
                continue
            if seen.get(k, 0) < v:
                seen[k] = v
                out.append((k, v))
        return out

    @staticmethod
    def _deps(reads, writes):
        deps = []
        for b in reads:
            if b.w is not None:
                deps.append(b.w)
        for b in writes:
            if b.w is not None:
                deps.append(b.w)
            deps.extend(b.r.items())
        return deps

    def op(self, e, fns, reads=(), writes=()):
        if not isinstance(fns, (list, tuple)):
            fns = [fns]
        waits = self._waits(e, self._deps(reads, writes))
        k = self.ekey(e)
        if self.cnt[k] >= SEM_LIMIT:
            self.gen[e] += 1
            k = self._mksem(self.ekey(e))
        self.cnt[k] += 1
        ev = (k, self.cnt[k])
        for b in reads:
            if b.r.get(k, 0) < ev[1]:
                b.r[k] = ev[1]
        for b in writes:
            b.w = ev
            b.r = {}
        self.q[e].append((waits, fns, k, 1))
        self.n_inst += len(fns)
        return ev

    def dma(self, qe, fn, reads=(), writes=(), sembuf=None, outs=(), inc=16):
        deps = self._deps(reads, writes)
        for b in outs:
            deps.extend(b.r.items())
        waits = self._waits(qe, deps)
        tgt = sembuf if sembuf is not None else (outs[0] if outs else (writes[0] if writes else reads[0]))
        if tgt.dsem is None:
            if self.dma_free and not tgt.persist:
                tgt.dsem = self.dma_free.pop()
            else:
                tgt.dsem = self._mksem(("dma", tgt.name, self.nsem))
            if not tgt.persist:
                self.dma_bufs.append(tgt)
        semkey = tgt.dsem
        self.cnt[semkey] += inc
        ev = (semkey, self.cnt[semkey])
        for b in reads:
            if b.r.get(semkey, 0) < ev[1]:
                b.r[semkey] = ev[1]
        for b in writes:
            b.w = ev
            b.r = {}
        for b in outs:
            b.w = ev
        self.q[qe].append((waits, [fn], semkey, inc))
        self.n_inst += 1
        return ev

    def barrier(self):
        deps = [(k, v) for k, v in self.cnt.items() if v > 0 and k[0] != "sp"]
        waits = self._waits("sp", deps)
        k = self.ekey("sp")
        self.cnt[k] += 1
        ev = (k, self.cnt[k])
        self.q["sp"].append((waits, [lambda e: e.nop()], k, 1))
        for e in ENGS:
            if e != "sp":
                self.q[e].append((self._waits(e, [ev]), [], None, 0))
        for b in self.dma_bufs:
            if b.dsem is not None and self.cnt[b.dsem] < SEM_LIMIT:
                self.dma_free.append(b.dsem)
            b.dsem = None
        self.dma_bufs = []

    def wait_all_on(self, e, bufs):
        deps = [b.w for b in bufs if b.w is not None]
        waits = self._waits(e, deps)
        self.q[e].append((waits, [], None, 0))

    def emit(self):
        nc = self.nc
        with nc.Block() as block:
            def mk(e):
                def body(engobj):
                    if e == "sp" and self.cid_ap is not None:
                        with engobj.register("rank_reg") as rr:
                            engobj.reg_load(rr, self.cid_ap)
                            self.dyn["p"] = engobj.snap(rr)
                            run(engobj)
                    else:
                        run(engobj)

                def run(engobj):
                    for (waits, fns, k, inc) in self.q[e]:
                        for (wk, wv) in waits:
                            engobj.wait_ge(self.sem[wk], wv)
                        n = len(fns)
                        for i, fn in enumerate(fns):
                            ins = fn(engobj)
                            if i == n - 1:
                                ins.then_inc(self.sem[k], inc)
                return body
            block.tensor(mk("pe"))
            block.scalar(mk("act"))
            block.vector(mk("dve"))
            block.gpsimd(mk("pool"))
            block.sync(mk("sp"))


class Tl:
    __slots__ = ("t", "b")

    def __init__(self, t, b):
        self.t = t
        self.b = b


class Ctx:
    def __init__(self, nc, stack):
        self.nc = nc
        self.st = stack
        self.S = Sched(nc, stack)
        self.uid = 0
        self.ps = []
        self.psi = 0
        self.outs = []

    def name(self, p):
        self.uid += 1
        return "%s_%d" % (p, self.uid)

    ARENA_LO = 16512
    ARENA_HI = 229344

    def sb(self, shape, dt, name="t"):
        n = self.name(name)
        esz = 2 if dt == BF16 else 4
        nbytes = esz
        for d_ in shape[1:]:
            nbytes *= int(d_)
        off = (getattr(self, "arena_ptr", self.ARENA_LO) + 31) // 32 * 32
        assert off + nbytes <= self.ARENA_HI, "SBUF arena overflow at %s: need %d have %d" % (n, nbytes, self.ARENA_HI - off)
        self.arena_ptr = off + nbytes
        t = self.nc.alloc_sbuf_tensor_at(n, list(shape), dt, offset=off)
        return Tl(t, Buf(n))

    def arena_reset(self):
        self.arena_ptr = self.ARENA_LO

    def pool(self, n, shape, dt, name="p"):
        return Rot([self.sb(shape, dt, name) for _ in range(n)])

    def init_psum(self, n=8):
        for i in range(n):
            nm = self.name("ps")
            t = self.st.enter_context(self.nc.psum_tensor(nm, [128, 512], F32))
            self.ps.append(Tl(t, Buf(nm)))

    def psum(self):
        p = self.ps[self.psi % len(self.ps)]
        self.psi += 1
        return p

    def dram_buf(self, name):
        return Buf(name)


class Rot:
    def __init__(self, items):
        self.items = items
        self.i = 0

    def next(self):
        it = self.items[self.i % len(self.items)]
        self.i += 1
        return it


def mm_group(K, ps, pairs, reads):
    n = len(pairs)

    def mk(i, l, r, o):
        return lambda e: e.matmul(o, l, r, start=(i == 0), stop=(i == n - 1))
    fns = [mk(i, l, r, o) for i, (l, r, o) in enumerate(pairs)]
    K.S.op("pe", fns, reads=reads, writes=[ps.b])


def rstd_from_sq(K, C, sq, sq_reads, nk, ncols, inv_n, extra=None):
    ps = K.psum()
    pairs = [(C["ones_bf"].t[:, :], sq.t[:, k, 0:ncols], ps.t[:, 0:ncols]) for k in range(nk)]
    mm_group(K, ps, pairs, reads=[C["ones_bf"].b, sq.b])
    r = C["rstd_pool"].next()
    K.S.op("act", lambda e: e.activation(out=r.t[:, 0:ncols], in_=ps.t[:, 0:ncols], func=AF.Sqrt,
                                         bias=C["eps_col"].t[:, 0:1], scale=inv_n),
           reads=[ps.b, C["eps_col"].b], writes=[r.b])
    K.S.op("dve", lambda e: e.reciprocal(out=r.t[:, 0:ncols], in_=r.t[:, 0:ncols]), reads=[r.b], writes=[r.b])
    return r


def pre_norm(K, C, x, xb, g, gcol0, h, hb, s):
    sl = slice(s * 512, (s + 1) * 512)
    sq = C["sq_pool"].next()
    K.S.op("act", lambda e: e.activation(out=sq.t[:, :, :], in_=x.t[:, :, sl], func=AF.Square),
           reads=[xb], writes=[sq.b])
    r = rstd_from_sq(K, C, sq, None, 8, 512, 1.0 / D)
    fns = [(lambda e, k=k: e.scalar_tensor_tensor(
        out=h.t[:, k, sl], in0=x.t[:, k, sl], scalar=g.t[:, gcol0 + k:gcol0 + k + 1], in1=r.t[:, :],
        op0=ALU.mult, op1=ALU.mult)) for k in range(8)]
    K.S.op("dve", fns, reads=[xb, r.b, g.b], writes=[hb])


def post_norm_residual(K, C, y, yb, g, gcol0, alpha, x, xb, s):
    sl = slice(s * 512, (s + 1) * 512)
    sq = C["sq_pool"].next()
    K.S.op("act", lambda e: e.activation(out=sq.t[:, :, :], in_=y.t[:, :, sl], func=AF.Square),
           reads=[yb], writes=[sq.b])
    r = rstd_from_sq(K, C, sq, None, 8, 512, 1.0 / D)
    for k in range(8):
        tmp = C["tmp_pool"].next()
        K.S.op("dve", lambda e, k=k, tmp=tmp: e.scalar_tensor_tensor(
            out=tmp.t[:, :], in0=y.t[:, k, sl], scalar=g.t[:, gcol0 + k:gcol0 + k + 1], in1=r.t[:, :],
            op0=ALU.mult, op1=ALU.mult), reads=[yb, r.b, g.b], writes=[tmp.b])
        K.S.op("dve", lambda e, k=k, tmp=tmp: e.scalar_tensor_tensor(
            out=x.t[:, k, sl], in0=tmp.t[:, :], scalar=float(alpha), in1=x.t[:, k, sl],
            op0=ALU.mult, op1=ALU.add), reads=[tmp.b, xb], writes=[xb])


def load_w(K, slot, src_ap, q="pool"):
    K.S.dma(q, lambda e: e.dma_start(out=slot.t[:], in_=src_ap), writes=[slot.b])


def ffn(K, C, h, hb, wg_ap, wu_ap, wd_ap, y, yb):
    a = C["a"]
    nsub = PASS // 512
    for f in range(NF):
        wg = C["wgu_pool"].next()
        load_w(K, wg, wg_ap[f].rearrange("p (k j) -> p k j", j=128))
        wu = C["wgu_pool"].next()
        load_w(K, wu, wu_ap[f].rearrange("p (k j) -> p k j", j=128))
        for s in range(nsub):
            sl = slice(s * 512, (s + 1) * 512)
            pg = K.psum()
            mm_group(K, pg, [(wg.t[:, k, :], h.t[:, k, sl], pg.t[:, :]) for k in range(8)], reads=[wg.b, hb])
            pu = K.psum()
            mm_group(K, pu, [(wu.t[:, k, :], h.t[:, k, sl], pu.t[:, :]) for k in range(8)], reads=[wu.b, hb])
            sg = C["tmp_pool"].next()
            K.S.op("act", lambda e, sg=sg, pg=pg: e.activation(out=sg.t[:, :], in_=pg.t[:, :], func=AF.Silu),
                   reads=[pg.b], writes=[sg.b])
            ab = a.b[f * nsub + s]
            K.S.op("dve", lambda e, sg=sg, pu=pu, f=f, sl=sl: e.tensor_tensor(
                out=a.t[:, f, sl], in0=sg.t[:, :], in1=pu.t[:, :], op=ALU.mult),
                reads=[sg.b, pu.b], writes=[ab])
    for d in range(8):
        wd = C["wd_pool"].next()
        load_w(K, wd, wd_ap[d].rearrange("p (c j) -> p c j", j=128))
        for s in range(nsub):
            sl = slice(s * 512, (s + 1) * 512)
            py = K.psum()
            mm_group(K, py, [(wd.t[:, c, :], a.t[:, c, sl], py.t[:, :]) for c in range(NF)],
                     reads=[wd.b] + [a.b[c * nsub + s] for c in range(NF)])
            K.S.op("act", lambda e, py=py, d=d, sl=sl: e.activation(out=y.t[:, d, sl], in_=py.t[:, :], func=AF.Copy),
                   reads=[py.b], writes=[yb])


class ATl:
    def __init__(self, t, bufs):
        self.t = t
        self.b = bufs


def common_consts(K):
    C = {}
    ones_bf = K.sb([128, 128], BF16, "ones_bf")
    K.S.op("dve", lambda e: e.memset(ones_bf.t[:, :], 1.0), writes=[ones_bf.b])
    C["ones_bf"] = ones_bf
    eps = K.sb([128, 1], F32, "eps")
    K.S.op("dve", lambda e: e.memset(eps.t[:, :], EPS), writes=[eps.b])
    C["eps_col"] = eps
    C["rstd_pool"] = K.pool(2, [128, 512], F32, "rstd")
    C["sq_pool"] = K.pool(1, [128, 8, 512], BF16, "sq")
    C["tmp_pool"] = K.pool(3, [128, 512], F32, "tmp")
    return C


def ffn_bufs(K, C):
    nsub = PASS // 512
    at = K.sb([128, NF, PASS], BF16, "a")
    C["a"] = ATl(at.t, [Buf("a%d" % i) for i in range(NF * nsub)])
    C["wgu_pool"] = K.pool(4, [128, 8, 128], BF16, "wgu")
    C["wd_pool"] = K.pool(2, [128, NF, 128], BF16, "wd")


OQ, OK_, OV, OCA, OCG, OGQ, OGK, OGV, OGR, OLR, OGT = 0, 512, 1024, 1536, 2048, 2560, 2816, 3072, 3584, 4096, 4112
FM_CHUNKS = ([("q", i, OQ + 128 * i) for i in range(4)] + [("k", i, OK_ + 128 * i) for i in range(4)] +
             [("a", i, OCA + 128 * i) for i in range(4)] + [("g", i, OCG + 128 * i) for i in range(4)] +
             [("gq", i, OGQ + 128 * i) for i in range(2)] + [("r", i, OGR + 128 * i) for i in range(4)] +
             [("lr", 0, OLR)])
NFM = len(FM_CHUNKS)
SB_QT, SB_KT, SB_V, SB_GQT, SB_GV = 0, 128 * TPC, 256 * TPC, 384 * TPC, 448 * TPC
SB_LEN = 576 * TPC
SF_UT, SF_GK, SF_LA = 0, 128 * TPC, 192 * TPC
SF_LEN = 256 * TPC


class XB:
    def __init__(self, U, nunits):
        self.U, self.nunits = U, nunits
        self.NCH = nunits // U
        self.CL = U * TPC

    def shape(self, nrank=4):
        return [self.NCH, nrank, self.CL]

    def fm_pieces(self, ap3, d, off, f0, nf):
        u0 = off // TPC + f0
        u1 = u0 + nf
        out = []
        u = u0
        while u < u1:
            c = u // self.U
            ue = min(u1, (c + 1) * self.U)
            lo = (u - c * self.U) * TPC
            out.append((u - u0, ue - u0, ap3[c, d, lo:lo + (ue - u) * TPC].rearrange("(f t) -> f t", t=TPC)))
            u = ue
        return out

    def tm_all(self, ap3, off, nfeat, tok0, ntok):
        e0 = off + tok0 * nfeat
        c = e0 // self.CL
        lo = e0 - c * self.CL
        assert lo + ntok * nfeat <= self.CL
        return ap3[c, :, lo:lo + ntok * nfeat].rearrange("d (t f) -> t d f", f=nfeat)

    def tm_src(self, ap3, j, off, nfeat, tok0, ntok):
        e0 = off + tok0 * nfeat
        c = e0 // self.CL
        lo = e0 - c * self.CL
        assert lo + ntok * nfeat <= self.CL
        return ap3[c, j, lo:lo + ntok * nfeat].rearrange("(t f) -> t f", f=nfeat)


X1B = XB(64, 576)
X1F = XB(32, 256)
X2B = XB(64, 128)
X2F = XB(32, 256)


def build_A():
    nc = bass.Bass("TRN2", target_bir_lowering=False)
    dt = nc.dram_tensor
    xT = dt("xT", [D, TPC], F32, kind="ExternalInput").ap()
    wg = dt("wg", [NF, 128, 8 * 128], F32, kind="ExternalInput").ap()
    wu = dt("wu", [NF, 128, 8 * 128], F32, kind="ExternalInput").ap()
    wd = dt("wd", [8, 128, NF * 128], F32, kind="ExternalInput").ap()
    wfm = dt("wfm", [NFM, 128, 8 * 128], F32, kind="ExternalInput").ap()
    wtm = dt("wtm", [128, 8 * 1280], F32, kind="ExternalInput").ap()
    walpha = dt("walpha", [16, 256], F32, kind="ExternalInput").ap()
    balpha = dt("balpha", [1, 256], F32, kind="ExternalInput").ap()
    gains = dt("gains", [128, 24], F32, kind="ExternalInput").ap()
    x1T = dt("x1T", [D, TPC], F32, kind="ExternalOutput").ap()
    send_bf = dt("send_bf", [4, SB_LEN], BF16, kind="ExternalOutput").ap()
    send_f = dt("send_f", [4, SF_LEN], F32, kind="ExternalOutput").ap()
    silur = dt("silur", [512, TPC], F32, kind="ExternalOutput").ap()
    with ExitStack() as st:
        K = Ctx(nc, st)
        K.init_psum(8)
        C = common_consts(K)
        ffn_bufs(K, C)
        bb = dict(b_x1=Buf("o_x1", True), b_sbf=Buf("o_sbf", True), b_sf=Buf("o_sf", True), b_sr=Buf("o_sr", True))
        K.outs += list(bb.values())
        emit_A(K, C, dict(xT=xT, wg=wg, wu=wu, wd=wd, wfm=wfm, wtm=wtm, walpha=walpha, balpha=balpha,
                          gains=gains, x1T=x1T, send_bf=send_bf, send_f=send_f, silur=silur, **bb))
        K.S.wait_all_on("sp", K.outs)
        K.S.emit()
    return nc


def emit_A(K, C, io):
    S = K.S
    nsub = PASS // 512
    g = K.sb([128, 24], F32, "gains")
    S.dma("sp", lambda e: e.dma_start(out=g.t[:, :], in_=io["gains"]), writes=[g.b])
    wtm = K.sb([128, 8, 1280], BF16, "wtm")
    load_w(K, wtm, io["wtm"].rearrange("p (k j) -> p k j", j=1280))
    wal = K.sb([16, 256], F32, "walpha")
    S.dma("sp", lambda e: e.dma_start(out=wal.t[:, :], in_=io["walpha"]), writes=[wal.b])
    bal = K.sb([1, 256], F32, "balpha")
    S.dma("sp", lambda e: e.dma_start(out=bal.t[:, :], in_=io["balpha"]), writes=[bal.b])
    ones_f = K.sb([1, 128], F32, "ones_f")
    S.op("dve", lambda e: e.memset(ones_f.t[:, :], 1.0), writes=[ones_f.b])

    x = K.sb([128, 8, PASS], F32, "x")
    h = K.sb([128, 8, PASS], BF16, "h")
    y = K.sb([128, 8, PASS], F32, "y")
    lrT = K.sb([16, PASS], F32, "lrT")
    wfm_pool = K.pool(2, [128, 8, 128], BF16, "wfm")
    st_bf = K.pool(4, [128, 512], BF16, "st_bf")
    st_f = K.pool(3, [128, 512], F32, "st_f")
    st_la = K.pool(2, [128, 256], F32, "st_la")
    xv = io["xT"].rearrange("(k p) t -> p k t", p=128)
    x1v = io["x1T"].rearrange("(k p) t -> p k t", p=128)
    sbf, sf = io["send_bf"], io["send_f"]
    o_x1, o_sbf, o_sf, o_sr = io["b_x1"], io["b_sbf"], io["b_sf"], io["b_sr"]
    b_xin = io.get("b_xin")

    def fm_out(xb, ap3, p, off, o, tsl, outbuf, pf0=0, nf=128):
        for (fa, fb, dst) in xb.fm_pieces(ap3, p, off, 0, nf):
            S.dma("sp", lambda e, dst=dst, fa=fa, fb=fb: e.dma_start(out=dst[:, tsl], in_=o.t[pf0 + fa:pf0 + fb, :]),
                  reads=[o.b], outs=[outbuf])

    for ps_ in range(TPC // PASS):
        t0 = ps_ * PASS
        S.dma("sp", lambda e, t0=t0: e.dma_start(out=x.t[:, :, :], in_=xv[:, :, t0:t0 + PASS]), writes=[x.b],
              reads=[b_xin] if b_xin is not None else [])
        for s in range(nsub):
            pre_norm(K, C, x, x.b, g, 0, h, h.b, s)
        ffn(K, C, h, h.b, io["wg"], io["wu"], io["wd"], y, y.b)
        for s in range(nsub):
            post_norm_residual(K, C, y, y.b, g, 16, 0.5, x, x.b, s)
        S.dma("sp", lambda e, t0=t0: e.dma_start(out=x1v[:, :, t0:t0 + PASS], in_=x.t[:, :, :]),
              reads=[x.b], outs=[o_x1])
        for s in range(nsub):
            pre_norm(K, C, x, x.b, g, 8, h, h.b, s)
        for ci, (kind, i, col) in enumerate(FM_CHUNKS):
            w = wfm_pool.next()
            load_w(K, w, io["wfm"][ci].rearrange("p (k j) -> p k j", j=128))
            for s in range(nsub):
                sl = slice(s * 512, (s + 1) * 512)
                tsl = slice(t0 + s * 512, t0 + (s + 1) * 512)
                pp = K.psum()
                mm_group(K, pp, [(w.t[:, k, :], h.t[:, k, sl], pp.t[:, :]) for k in range(8)], reads=[w.b, h.b])
                if kind in ("q", "k"):
                    o = st_bf.next()
                    sc = 0.125 if kind == "q" else 1.0
                    S.op("act", lambda e, o=o, pp=pp, sc=sc: e.activation(out=o.t[:, :], in_=pp.t[:, :], func=AF.Copy, scale=sc),
                         reads=[pp.b], writes=[o.b])
                    off = SB_QT if kind == "q" else SB_KT
                    fm_out(X1B, sbf, i, off, o, tsl, o_sbf)
                elif kind == "a":
                    S.op("act", lambda e, pp=pp, i=i, sl=sl: e.activation(out=y.t[:, i, sl], in_=pp.t[:, :], func=AF.Copy),
                         reads=[pp.b], writes=[y.b])
                elif kind == "g":
                    sg = C["tmp_pool"].next()
                    S.op("act", lambda e, sg=sg, pp=pp: e.activation(out=sg.t[:, :], in_=pp.t[:, :], func=AF.Sigmoid),
                         reads=[pp.b], writes=[sg.b])
                    o = st_f.next()
                    S.op("dve", lambda e, o=o, sg=sg, i=i, sl=sl: e.tensor_tensor(out=o.t[:, :], in0=y.t[:, i, sl], in1=sg.t[:, :], op=ALU.mult),
                         reads=[sg.b, y.b], writes=[o.b])
                    fm_out(X1F, sf, i, SF_UT, o, tsl, o_sf)
                elif kind == "gq":
                    o = st_bf.next()
                    S.op("act", lambda e, o=o, pp=pp: e.activation(out=o.t[:, :], in_=pp.t[:, :], func=AF.Copy, scale=0.125),
                         reads=[pp.b], writes=[o.b])
                    for half in range(2):
                        fm_out(X1B, sbf, 2 * i + half, SB_GQT, o, tsl, o_sbf, pf0=64 * half, nf=64)
                elif kind == "r":
                    o = st_f.next()
                    S.op("act", lambda e, o=o, pp=pp: e.activation(out=o.t[:, :], in_=pp.t[:, :], func=AF.Silu),
                         reads=[pp.b], writes=[o.b])
                    dst = io["silur"][128 * i:128 * i + 128, tsl]
                    S.dma("sp", lambda e, o=o, dst=dst: e.dma_start(out=dst, in_=o.t[:, :]), reads=[o.b], outs=[o_sr])
                elif kind == "lr":
                    S.op("act", lambda e, pp=pp, sl=sl: e.activation(out=lrT.t[:, sl], in_=pp.t[0:16, :], func=AF.Copy),
                         reads=[pp.b], writes=[lrT.b])
        for tb in range(PASS // 128):
            bsl = slice(tb * 128, (tb + 1) * 128)
            gsl = slice(t0 + tb * 128, t0 + (tb + 1) * 128)
            pp = K.psum()
            mm_group(K, pp, [(h.t[:, k, bsl], wtm.t[:, k, 0:512], pp.t[:, :]) for k in range(8)], reads=[wtm.b, h.b])
            o = st_bf.next()
            S.op("act", lambda e, o=o, pp=pp: e.activation(out=o.t[:, :], in_=pp.t[:, :], func=AF.Copy), reads=[pp.b], writes=[o.b])
            S.dma("sp", lambda e, o=o, gsl=gsl: e.dma_start(out=X1B.tm_all(sbf, SB_V, 128, gsl.start, 128), in_=o.t[:, :].rearrange("t (p f) -> t p f", f=128)),
                  reads=[o.b], outs=[o_sbf])
            pp = K.psum()
            mm_group(K, pp, [(h.t[:, k, bsl], wtm.t[:, k, 768:1280], pp.t[:, :]) for k in range(8)], reads=[wtm.b, h.b])
            o = st_bf.next()
            S.op("dve", lambda e, o=o, pp=pp: e.tensor_copy(out=o.t[:, :], in_=pp.t[:, :]), reads=[pp.b], writes=[o.b])
            S.dma("sp", lambda e, o=o, gsl=gsl: e.dma_start(out=X1B.tm_all(sbf, SB_GV, 128, gsl.start, 128), in_=o.t[:, :].rearrange("t (p f) -> t p f", f=128)),
                  reads=[o.b], outs=[o_sbf])
            pp = K.psum()
            mm_group(K, pp, [(h.t[:, k, bsl], wtm.t[:, k, 512:768], pp.t[:, 0:256]) for k in range(8)], reads=[wtm.b, h.b])
            o = st_la.next()
            S.op("act", lambda e, o=o, pp=pp: e.activation(out=o.t[:, :], in_=pp.t[:, 0:256], func=AF.Copy), reads=[pp.b], writes=[o.b])
            S.dma("sp", lambda e, o=o, gsl=gsl: e.dma_start(out=X1F.tm_all(sf, SF_GK, 64, gsl.start, 128), in_=o.t[:, :].rearrange("t (p f) -> t p f", f=64)),
                  reads=[o.b], outs=[o_sf])
            pp = K.psum()
            mm_group(K, pp, [(lrT.t[:, bsl], wal.t[:, :], pp.t[:, 0:256]), (ones_f.t[:, :], bal.t[:, :], pp.t[:, 0:256])],
                     reads=[lrT.b, wal.b, ones_f.b, bal.b])
            o = st_la.next()
            S.op("act", lambda e, o=o, pp=pp: e.activation(out=o.t[:, :], in_=pp.t[:, 0:256], func=AF.Exp, scale=-1.0), reads=[pp.b], writes=[o.b])
            S.op("act", lambda e, o=o: e.activation(out=o.t[:, :], in_=o.t[:, :], func=AF.Ln, bias=1.0), reads=[o.b], writes=[o.b])
            S.op("dve", lambda e, o=o: e.tensor_scalar(out=o.t[:, :], in0=o.t[:, :], scalar1=-1.0 / 16.0, scalar2=None, op0=ALU.mult),
                 reads=[o.b], writes=[o.b])
            S.dma("sp", lambda e, o=o, gsl=gsl: e.dma_start(out=X1F.tm_all(sf, SF_LA, 64, gsl.start, 128), in_=o.t[:, :].rearrange("t (p f) -> t p f", f=64)),
                  reads=[o.b], outs=[o_sf])


def _c(a):
    return np.ascontiguousarray(a)


def lay_fm_w(W, ncols_chunks):
    Kc = W.shape[0] // 128
    n = W.shape[1] // 128
    return _c(W.reshape(Kc, 128, n, 128).transpose(2, 1, 0, 3).reshape(n, 128, Kc * 128))


def lay_pk(v):
    return _c(v.reshape(-1, 128).T)


def prep_A_weights(inp, l):
    w_in = inp["w_in"][l]
    cols = []
    for (kind, i, col) in FM_CHUNKS:
        blk = np.zeros((D, 128), np.float32)
        n = 16 if kind == "lr" else 128
        blk[:, :n] = w_in[:, col:col + n]
        cols.append(blk)
    wfm = lay_fm_w(np.concatenate(cols, axis=1), NFM)
    tm = np.concatenate([w_in[:, OV:OV + 512], w_in[:, OGK:OGK + 256], w_in[:, OGV:OGV + 512]], axis=1)
    wtm = _c(tm.reshape(8, 128, 1280).transpose(1, 0, 2).reshape(128, 8 * 1280))
    gains = np.concatenate([lay_pk(inp["norm_pre"][l, 0]), lay_pk(inp["norm_pre"][l, 1]), lay_pk(inp["norm_post"][l, 0])], axis=1)
    return dict(wg=lay_fm_w(inp["ffn1_w_gate"][l], NF), wu=lay_fm_w(inp["ffn1_w_up"][l], NF),
                wd=lay_fm_w(inp["ffn1_w_down"][l], 8), wfm=wfm, wtm=wtm,
                walpha=_c(inp["gla_w_alpha"][l]), balpha=_c(inp["gla_b_alpha"][l][None, :]), gains=_c(gains))


RB_LEN = 128 * TPC
RF_CY, RF_GN = 0, 128 * TPC
RF_LEN = 256 * TPC


def build_C(dbg=0):
    nc = bass.Bass("TRN2", target_bir_lowering=False)
    dt = nc.dram_tensor
    io = dict(
        x1T=dt("x1T", [D, TPC], F32, kind="ExternalInput").ap(),
        recv_bf=dt("recv_bf", [4, RB_LEN], BF16, kind="ExternalInput").ap(),
        recv_f=dt("recv_f", [4, RF_LEN], F32, kind="ExternalInput").ap(),
        silur=dt("silur", [512, TPC], F32, kind="ExternalInput").ap(),
        wgt=dt("wgt", [24, 128, 8 * 128], F32, kind="ExternalInput").ap(),
        wbr=dt("wbr", [24, 128, 4 * 128], F32, kind="ExternalInput").ap(),
        wo=dt("wo", [8, 128, 8 * 128], F32, kind="ExternalInput").ap(),
        wg=dt("wg", [NF, 128, 8 * 128], F32, kind="ExternalInput").ap(),
        wu=dt("wu", [NF, 128, 8 * 128], F32, kind="ExternalInput").ap(),
        wd=dt("wd", [8, 128, NF * 128], F32, kind="ExternalInput").ap(),
        gains=dt("gains", [128, 40], F32, kind="ExternalInput").ap(),
        x3T=dt("x3T", [D, TPC], F32, kind="ExternalOutput").ap(),
    )
    with ExitStack() as st:
        K = Ctx(nc, st)
        K.init_psum(8)
        C = common_consts(K)
        ffn_bufs(K, C)
        r2bf_, r2f_ = io["recv_bf"], io["recv_f"]
        io["recv_bf"] = lambda p, a, b: r2bf_[p:p + 1, a:b]
        io["recv_f"] = lambda p, a, b: r2f_[p:p + 1, a:b]
        io["b_x3"] = Buf("o_x3", True)
        K.outs.append(io["b_x3"])
        emit_C(K, C, io, dbg)
        K.S.wait_all_on("sp", K.outs)
        K.S.emit()
    return nc


def emit_C(K, C, io, dbg=0):
    S = K.S
    nsub = PASS // 512
    T = TPC
    g = K.sb([128, 40], F32, "gainsC")
    S.dma("sp", lambda e: e.dma_start(out=g.t[:, :], in_=io["gains"]), writes=[g.b])
    x = K.sb([128, 8, PASS], F32, "x")
    h = K.sb([128, 8, PASS], BF16, "h")
    y = K.sb([128, 8, PASS], F32, "y")
    a = C["a"]
    sr_pool = K.pool(1, [128, 4, 512], F32, "sr")
    wgt_pool = K.pool(2, [128, 8, 128], BF16, "wgt")
    wbr_pool = K.pool(2, [128, 4, 128], BF16, "wbr")
    acc = [K.sb([128, 512], F32, "acc") for _ in range(nsub)]
    ln_mean = K.sb([128, 512], F32, "ln_mean")
    ln_msq = K.sb([128, 512], F32, "ln_msq")
    xv = io["x1T"].rearrange("(k p) t -> p k t", p=128)
    ov = io["x3T"].rearrange("(k p) t -> p k t", p=128)
    r2bf, r2f = io["recv_bf"], io["recv_f"]
    sr_v = io["silur"].rearrange("(c f) t -> f c t", f=128)
    o_x3 = io["b_x3"]
    rd2 = [io[k] for k in ("b_r2bf", "b_r2f") if io.get(k) is not None]
    rdx = [io[k] for k in ("b_x1",) if io.get(k) is not None]
    rds = [io[k] for k in ("b_sr",) if io.get(k) is not None]

    def ab(f0, f1, s):
        return [a.b[f * nsub + s] for f in range(f0, f1)]

    for ps_ in range(TPC // PASS):
        t0 = ps_ * PASS
        S.dma("sp", lambda e, t0=t0: e.dma_start(out=x.t[:, :, :], in_=xv[:, :, t0:t0 + PASS]), writes=[x.b], reads=rdx)
        for s in range(nsub):
            pre_norm(K, C, x, x.b, g, 0, h, h.b, s)
        for s in range(nsub):
            sl = slice(s * 512, (s + 1) * 512)
            tsl = slice(t0 + s * 512, t0 + (s + 1) * 512)
            for p in range(4):
                for (fa, fb, sap) in X2F.fm_pieces(r2f, p, RF_CY, 0, 128):
                    S.dma("sp", lambda e, sl=sl, tsl=tsl, p=p, fa=fa, fb=fb, sap=sap: e.dma_start(out=y.t[fa:fb, p, sl], in_=sap[:, tsl]),
                          writes=[y.b], reads=rd2)
                for (fa, fb, sap) in X2F.fm_pieces(r2f, p, RF_GN, 0, 128):
                    S.dma("sp", lambda e, sl=sl, tsl=tsl, p=p, fa=fa, fb=fb, sap=sap: e.dma_start(out=y.t[fa:fb, 4 + p, sl], in_=sap[:, tsl]),
                          writes=[y.b], reads=rd2)
                for (fa, fb, sap) in X2B.fm_pieces(r2bf, p, 0, 0, 128):
                    S.dma("sp", lambda e, sl=sl, tsl=tsl, p=p, fa=fa, fb=fb, sap=sap: e.dma_start(out=a.t[fa:fb, 8 + p, sl], in_=sap[:, tsl]),
                          writes=ab(8 + p, 9 + p, s), reads=rd2)
            sr = sr_pool.next()
            S.dma("sp", lambda e, sr=sr, tsl=tsl: e.dma_start(out=sr.t[:, :, :], in_=sr_v[:, :, tsl]), writes=[sr.b], reads=rds)
            sq = C["sq_pool"].next()
            S.op("act", lambda e, sq=sq, sl=sl: e.activation(out=sq.t[:, 0:4, :], in_=y.t[:, 0:4, sl], func=AF.Copy),
                 reads=[y.b], writes=[sq.b])
            S.op("act", lambda e, sq=sq, sl=sl: e.activation(out=sq.t[:, 4:8, :], in_=y.t[:, 0:4, sl], func=AF.Square),
                 reads=[y.b], writes=[sq.b])
            p1 = K.psum()
            mm_group(K, p1, [(C["ones_bf"].t[:, :], sq.t[:, c, :], p1.t[:, :]) for c in range(4)], reads=[C["ones_bf"].b, sq.b])
            p2 = K.psum()
            mm_group(K, p2, [(C["ones_bf"].t[:, :], sq.t[:, 4 + c, :], p2.t[:, :]) for c in range(4)], reads=[C["ones_bf"].b, sq.b])
            mean = ln_mean
            S.op("act", lambda e, mean=mean, p1=p1: e.activation(out=mean.t[:, :], in_=p1.t[:, :], func=AF.Copy, scale=1.0 / 512),
                 reads=[p1.b], writes=[mean.b])
            msq = ln_msq
            S.op("dve", lambda e, mean=mean, msq=msq: e.tensor_tensor(out=msq.t[:, :], in0=mean.t[:, :], in1=mean.t[:, :], op=ALU.mult),
                 reads=[mean.b], writes=[msq.b])
            S.op("dve", lambda e, msq=msq, p2=p2: e.scalar_tensor_tensor(out=msq.t[:, :], in0=p2.t[:, :], scalar=1.0 / 512, in1=msq.t[:, :],
                                                                       op0=ALU.mult, op1=ALU.subtract),
                 reads=[p2.b, msq.b], writes=[msq.b])
            rs = C["rstd_pool"].next()
            S.op("act", lambda e, rs=rs, msq=msq: e.activation(out=rs.t[:, :], in_=msq.t[:, :], func=AF.Sqrt, bias=C["eps_col"].t[:, 0:1], scale=1.0),
                 reads=[msq.b, C["eps_col"].b], writes=[rs.b])
            S.op("dve", lambda e, rs=rs: e.reciprocal(out=rs.t[:, :], in_=rs.t[:, :]), reads=[rs.b], writes=[rs.b])
            for c in range(4):
                t = C["tmp_pool"].next()
                S.op("dve", lambda e, t=t, c=c, sl=sl, mean=mean: e.tensor_tensor(out=t.t[:, :], in0=y.t[:, c, sl], in1=mean.t[:, :], op=ALU.subtract),
                     reads=[y.b, mean.b], writes=[t.b])
                S.op("dve", lambda e, t=t, rs=rs: e.tensor_tensor(out=t.t[:, :], in0=t.t[:, :], in1=rs.t[:, :], op=ALU.mult),
                     reads=[t.b, rs.b], writes=[t.b])
                S.op("act", lambda e, t=t, c=c, sl=sl: e.activation(out=a.t[:, c, sl], in_=t.t[:, :], func=AF.Silu,
                                                                      scale=g.t[:, 32 + c:33 + c], bias=g.t[:, 36 + c:37 + c]),
                     reads=[t.b, g.b], writes=ab(c, c + 1, s))
                S.op(POOL_ENG, lambda e, c=c, sl=sl, sr=sr: e.tensor_tensor(out=a.t[:, 4 + c, sl], in0=y.t[:, 4 + c, sl], in1=sr.t[:, c, :], op=ALU.mult),
                     reads=[y.b, sr.b], writes=ab(4 + c, 5 + c, s))
        base = {0: 8, 1: 0, 2: 4}
        for d in range(8):
            for gi in range(3):
                wb = wbr_pool.next()
                load_w(K, wb, io["wbr"][gi * 8 + d].rearrange("p (k j) -> p k j", j=128))
                wgt = wgt_pool.next()
                load_w(K, wgt, io["wgt"][gi * 8 + d].rearrange("p (k j) -> p k j", j=128))
                for s in range(nsub):
                    sl = slice(s * 512, (s + 1) * 512)
                    pb = K.psum()
                    mm_group(K, pb, [(wb.t[:, kc, :], a.t[:, base[gi] + kc, sl], pb.t[:, :]) for kc in range(4)],
                             reads=[wb.b] + ab(base[gi], base[gi] + 4, s))
                    pl = K.psum()
                    mm_group(K, pl, [(wgt.t[:, k, :], h.t[:, k, sl], pl.t[:, :]) for k in range(8)], reads=[wgt.b, h.b])
                    sig = C["tmp_pool"].next()
                    S.op("act", lambda e, sig=sig, pl=pl: e.activation(out=sig.t[:, :], in_=pl.t[:, :], func=AF.Sigmoid),
                         reads=[pl.b], writes=[sig.b])
                    if gi == 0:
                        S.op("dve", lambda e, sig=sig, pb=pb, s=s: e.tensor_tensor(out=acc[s].t[:, :], in0=sig.t[:, :], in1=pb.t[:, :], op=ALU.mult),
                             reads=[sig.b, pb.b], writes=[acc[s].b])
                    else:
                        S.op("dve", lambda e, sig=sig, pb=pb: e.tensor_tensor(out=sig.t[:, :], in0=sig.t[:, :], in1=pb.t[:, :], op=ALU.mult),
                             reads=[sig.b, pb.b], writes=[sig.b])
                        if gi == 1:
                            S.op(POOL_ENG, lambda e, sig=sig, s=s: e.tensor_tensor(out=acc[s].t[:, :], in0=acc[s].t[:, :], in1=sig.t[:, :], op=ALU.add),
                                 reads=[sig.b, acc[s].b], writes=[acc[s].b])
                        else:
                            S.op(POOL_ENG, lambda e, sig=sig, s=s, d=d, sl=sl: e.tensor_tensor(out=a.t[:, 12 + d, sl], in0=acc[s].t[:, :], in1=sig.t[:, :], op=ALU.add),
                                 reads=[sig.b, acc[s].b], writes=ab(12 + d, 13 + d, s))
        for do in range(8):
            wo = C["wgu_pool"].next()
            load_w(K, wo, io["wo"][do].rearrange("p (k j) -> p k j", j=128))
            for s in range(nsub):
                sl = slice(s * 512, (s + 1) * 512)
                pp = K.psum()
                mm_group(K, pp, [(wo.t[:, k, :], a.t[:, 12 + k, sl], pp.t[:, :]) for k in range(8)], reads=[wo.b] + ab(12, 20, s))
                S.op("act", lambda e, pp=pp, do=do, sl=sl: e.activation(out=y.t[:, do, sl], in_=pp.t[:, :], func=AF.Copy),
                     reads=[pp.b], writes=[y.b])
        if dbg == 2:
            S.dma("sp", lambda e, t0=t0: e.dma_start(out=ov[:, :, t0:t0 + PASS], in_=y.t[:, :, :]), reads=[y.b], outs=[o_x3])
            continue
        for s in range(nsub):
            post_norm_residual(K, C, y, y.b, g, 8, 1.0, x, x.b, s)
        if dbg == 1:
            S.dma("sp", lambda e, t0=t0: e.dma_start(out=ov[:, :, t0:t0 + PASS], in_=x.t[:, :, :]), reads=[x.b], outs=[o_x3])
            continue
        for s in range(nsub):
            pre_norm(K, C, x, x.b, g, 16, h, h.b, s)
        ffn(K, C, h, h.b, io["wg"], io["wu"], io["wd"], y, y.b)
        for s in range(nsub):
            post_norm_residual(K, C, y, y.b, g, 24, 0.5, x, x.b, s)
        S.dma("sp", lambda e, t0=t0: e.dma_start(out=ov[:, :, t0:t0 + PASS], in_=x.t[:, :, :]), reads=[x.b], outs=[o_x3])


def prep_C_weights(inp, l):
    w_in = inp["w_in"][l]
    wgt = lay_fm_w(w_in[:, OGT:OGT + 3072], 24)
    wb = inp["w_branch"][l]
    wbr = np.concatenate([lay_fm_w(wb[gi], 8) for gi in range(3)], axis=0)
    gains = np.concatenate([lay_pk(inp["norm_pre"][l, 1]), lay_pk(inp["norm_post"][l, 1]), lay_pk(inp["norm_pre"][l, 2]),
                            lay_pk(inp["norm_post"][l, 2]), lay_pk(inp["conv_ln_g"][l]), lay_pk(inp["conv_ln_b"][l])], axis=1)
    return dict(wgt=wgt, wbr=_c(wbr), wo=lay_fm_w(inp["w_out"][l], 8), wg=lay_fm_w(inp["ffn2_w_gate"][l], NF),
                wu=lay_fm_w(inp["ffn2_w_up"][l], NF), wd=lay_fm_w(inp["ffn2_w_down"][l], 8), gains=_c(gains))


NQT = SEQ // 512
NBLK = SEQ // 128
S2B_LEN = 128 * TPC
S2F_LEN = 256 * TPC


def build_B():
    nc = bass.Bass("TRN2", target_bir_lowering=False)
    dt = nc.dram_tensor
    io = dict(
        rbf=dt("rbf", [4, SB_LEN], BF16, kind="ExternalInput").ap(),
        rf=dt("rf", [4, SF_LEN], F32, kind="ExternalInput").ap(),
        bw=dt("bw", [128, 33], F32, kind="ExternalInput").ap(),
        s2bf=dt("s2bf", [4, S2B_LEN], BF16, kind="ExternalOutput").ap(),
        s2f=dt("s2f", [4, S2F_LEN], F32, kind="ExternalOutput").ap(),
    )
    with ExitStack() as st:
        K = Ctx(nc, st)
        K.init_psum(8)
        rbf_, rf_ = io["rbf"], io["rf"]
        io["rbf"] = lambda j, a, b: rbf_[j:j + 1, a:b]
        io["rf"] = lambda j, a, b: rf_[j:j + 1, a:b]
        io["b_s2bf"], io["b_s2f"] = Buf("o_s2bf", True), Buf("o_s2f", True)
        K.outs += [io["b_s2bf"], io["b_s2f"]]
        emit_B(K, io)
        K.S.wait_all_on("sp", K.outs)
        K.S.emit()
    return nc


def emit_B(K, io):
    S = K.S
    T = TPC
    rbf, rf = io["rbf"], io["rf"]
    o_bf, o_f = io["b_s2bf"], io["b_s2f"]
    rdb = [io[k] for k in ("b_rbf", "b_rf") if io.get(k) is not None]
    ones_bf = K.sb([128, 128], BF16, "ones_bf")
    S.op("dve", lambda e: e.memset(ones_bf.t[:, :], 1.0), writes=[ones_bf.b])
    negones = K.sb([128, 1], BF16, "negones")
    S.op("dve", lambda e: e.memset(negones.t[:, :], -1.0), writes=[negones.b])
    eps = K.sb([128, 1], F32, "eps")
    S.op("dve", lambda e: e.memset(eps.t[:, :], EPS), writes=[eps.b])
    ntri = K.sb([128, 128], BF16, "ntri")
    S.op("pool", lambda e: e.memset(ntri.t[:, :], -1.0), writes=[ntri.b])
    S.op("pool", lambda e: e.affine_select(out=ntri.t[:, :], in_=ntri.t[:, :], pattern=[[-1, 128]], compare_op=ALU.is_ge,
                                            fill=0.0, base=0, channel_multiplier=1), reads=[ntri.b], writes=[ntri.b])
    masks = []
    for i in range(4):
        m = K.sb([128, 512], BF16, "mask")
        S.op("pool", lambda e, m=m: e.memset(m.t[:, :], 1.0), writes=[m.b])
        S.op("pool", lambda e, m=m, i=i: e.affine_select(out=m.t[:, :], in_=m.t[:, :], pattern=[[1, 512]], compare_op=ALU.is_gt,
                                                           fill=0.0, base=-128 * i, channel_multiplier=-1), reads=[m.b], writes=[m.b])
        masks.append(m)
    tris = K.sb([128, 128], BF16, "tris")
    S.op("pool", lambda e: e.memset(tris.t[:, :], 1.0), writes=[tris.b])
    S.op("pool", lambda e: e.affine_select(out=tris.t[:, :], in_=tris.t[:, :], pattern=[[-1, 128]], compare_op=ALU.is_gt,
                                            fill=0.0, base=0, channel_multiplier=1), reads=[tris.b], writes=[tris.b])
    S.op("pool", lambda e: e.memset(tris.t[64:128, 0:64], 0.0), reads=[tris.b], writes=[tris.b])
    cind = K.sb([128, 2], BF16, "cind")
    S.op("pool", lambda e: e.memset(cind.t[:, :], 0.0), writes=[cind.b])
    S.op("pool", lambda e: e.memset(cind.t[0:64, 0:1], 1.0), reads=[cind.b], writes=[cind.b])
    S.op("pool", lambda e: e.memset(cind.t[64:128, 1:2], 1.0), reads=[cind.b], writes=[cind.b])
    bw = K.sb([128, 33], F32, "bw")
    S.dma("sp", lambda e: e.dma_start(out=bw.t[:, :], in_=io["bw"]), writes=[bw.b])
    qT = K.sb([128, SEQ], BF16, "qT")
    kT = K.sb([128, SEQ], BF16, "kT")
    v = K.sb([128, NBLK, 128], BF16, "v")
    gqT = K.sb([64, SEQ], BF16, "gqT")
    gv = K.sb([128, NBLK, 128], BF16, "gv")
    gk = K.sb([128, NBLK, 64], F32, "gk")
    la = K.sb([128, NBLK, 64], F32, "la")
    u = K.sb([128, 32 + SEQ], F32, "u")
    S.op("pool", lambda e: e.memset(u.t[:, 0:32], 0.0), writes=[u.b])
    def ld(out_ap, in_ap, wbuf):
        S.dma("sp", lambda e: e.dma_start(out=out_ap, in_=in_ap), writes=[wbuf], reads=rdb)

    for j in range(4):
        ts = slice(j * T, (j + 1) * T)
        for (fa, fb, sap) in X1B.fm_pieces(rbf, j, SB_KT, 0, 128):
            ld(kT.t[fa:fb, ts], sap, kT.b)
        for (fa, fb, sap) in X1B.fm_pieces(rbf, j, SB_QT, 0, 128):
            ld(qT.t[fa:fb, ts], sap, qT.b)
        for (fa, fb, sap) in X1B.fm_pieces(rbf, j, SB_GQT, 0, 64):
            ld(gqT.t[fa:fb, ts], sap, gqT.b)
        for (fa, fb, sap) in X1F.fm_pieces(rf, j, SF_UT, 0, 128):
            ld(u.t[fa:fb, 32 + j * T:32 + (j + 1) * T], sap, u.b)
        for tok0 in (0, 1024):
            kb0 = j * 16 + tok0 // 128
            ld(v.t[:, kb0:kb0 + 8, :], X1B.tm_src(rbf, j, SB_V, 128, tok0, 1024).rearrange("(k t) f -> t k f", t=128), v.b)
            ld(gv.t[:, kb0:kb0 + 8, :], X1B.tm_src(rbf, j, SB_GV, 128, tok0, 1024).rearrange("(k t) f -> t k f", t=128), gv.b)
            ld(gk.t[:, kb0:kb0 + 8, :], X1F.tm_src(rf, j, SF_GK, 64, tok0, 1024).rearrange("(k t) f -> t k f", t=128), gk.b)
            ld(la.t[:, kb0:kb0 + 8, :], X1F.tm_src(rf, j, SF_LA, 64, tok0, 1024).rearrange("(k t) f -> t k f", t=128), la.b)
    ps_z = Rot(K.ps[0:3])
    ps_o = Rot(K.ps[3:5])
    ps_c = Rot(K.ps[5:7])
    ps_g = K.ps[7]
    e_pool = K.pool(2, [128, 512], F32, "e")
    sp_pool = K.pool(3, [128, 512], BF16, "sp")
    w_pool = K.pool(3, [128, 512], BF16, "w")
    cb_pool = K.pool(2, [1, 512], BF16, "carry")
    osb_pool = K.pool(2, [64, 512], BF16, "osb")

    def conv_gen():
        acc_pool = K.pool(2, [128, 512], F32, "cacc")
        for pc in range(SEQ // 512):
            acc = acc_pool.next()
            c0 = 2 + pc * 512
            S.op("dve", lambda e, acc=acc, c0=c0: e.tensor_scalar(out=acc.t[:, :], in0=u.t[:, c0:c0 + 512], scalar1=bw.t[:, 0:1], scalar2=bw.t[:, 31:32],
                                                                    op0=ALU.mult, op1=ALU.add), reads=[u.b, bw.b], writes=[acc.b])
            yield
            for k in range(1, 31):
                S.op("dve", lambda e, acc=acc, c0=c0, k=k: e.scalar_tensor_tensor(out=acc.t[:, :], in0=u.t[:, c0 + k:c0 + k + 512], scalar=bw.t[:, k:k + 1],
                                                                                  in1=acc.t[:, :], op0=ALU.mult, op1=ALU.add),
                     reads=[u.b, bw.b, acc.b], writes=[acc.b])
                yield
            cols = slice((pc % 4) * 512, (pc % 4 + 1) * 512)
            for (fa, fb, dst) in X2F.fm_pieces(io["s2f"], pc // 4, RF_CY, 0, 128):
                S.dma("sp", lambda e, acc=acc, dst=dst, fa=fa, fb=fb, cols=cols: e.dma_start(out=dst[:, cols], in_=acc.t[fa:fb, :]),
                      reads=[acc.b], outs=[o_f])
            yield

    def gla_gen():
        state = K.sb([64, 128], F32, "gstate")
        S.op("pool", lambda e: e.memset(state.t[:, :], 0.0), writes=[state.b])
        sbf_pool = K.pool(2, [64, 128], BF16, "gstbf")
        labf_pool = K.pool(2, [128, 64], BF16, "labf")
        ed_pool = K.pool(2, [128, 64], F32, "ged")
        kd_pool = K.pool(2, [128, 64], BF16, "gkd")
        lam_pool = K.pool(2, [64, 2], F32, "glam")
        osb_p = K.pool(2, [128, 128], F32, "gosb")
        sq_p = K.pool(2, [128, 128], BF16, "gsq")
        rs_p = K.pool(2, [128, 128], F32, "grs")
        gst_pool = K.pool(2, [128, 512], F32, "gnst")
        pg = ps_g
        r_dte = pg.t[:, 0:64]
        r_lam = pg.t[0:64, 64:66]
        r_st = pg.t[0:64, 128:256]
        r_o = pg.t[:, 256:384]
        r_ss = pg.t[:, 384:512]
        b_dte, b_lam, b_st, b_o, b_ss = Buf("g_dte"), Buf("g_lam"), Buf("g_st"), Buf("g_o"), Buf("g_ss")
        gst = None
        for blk in range(NBLK):
            labf = labf_pool.next()
            S.op("pool", lambda e, labf=labf, blk=blk: e.tensor_copy(out=labf.t[:, :], in_=la.t[:, blk, :]), reads=[la.b], writes=[labf.b])
            yield
            S.op("pe", [lambda e, labf=labf: e.matmul(r_dte, tris.t[:, :], labf.t[:, :], start=True, stop=True)], reads=[tris.b, labf.b], writes=[b_dte])
            S.op("pe", [lambda e, labf=labf: e.matmul(r_lam, labf.t[:, :], cind.t[:, :], start=True, stop=True)], reads=[cind.b, labf.b], writes=[b_lam])
            yield
            ed = ed_pool.next()
            S.op("act", lambda e, ed=ed: e.activation(out=ed.t[:, :], in_=r_dte, func=AF.Exp), reads=[b_dte], writes=[ed.b])
            lam = lam_pool.next()
            S.op("act", lambda e, lam=lam: e.activation(out=lam.t[:, :], in_=r_lam, func=AF.Exp), reads=[b_lam], writes=[lam.b])
            yield
            kd = kd_pool.next()
            S.op("dve", lambda e, kd=kd, ed=ed, blk=blk: e.tensor_tensor(out=kd.t[:, :], in0=gk.t[:, blk, :], in1=ed.t[:, :], op=ALU.mult),
                 reads=[gk.b, ed.b], writes=[kd.b])
            yield
            for c in range(2):
                cs = slice(64 * c, 64 * c + 64)
                S.op("pe", [lambda e, kd=kd, cs=cs, blk=blk: e.matmul(r_st, kd.t[cs, :], gv.t[cs, blk, :], start=True, stop=True)],
                     reads=[kd.b, gv.b], writes=[b_st])
                yield
                S.op("dve", lambda e, lam=lam, c=c: e.scalar_tensor_tensor(out=state.t[:, :], in0=state.t[:, :], scalar=lam.t[:, c:c + 1], in1=r_st,
                                                                          op0=ALU.mult, op1=ALU.add), reads=[state.b, lam.b, b_st], writes=[state.b])
                sbf = sbf_pool.next()
                S.op("dve", lambda e, sbf=sbf: e.tensor_copy(out=sbf.t[:, :], in_=state.t[:, :]), reads=[state.b], writes=[sbf.b])
                yield
                tk = slice(blk * 128 + 64 * c, blk * 128 + 64 * c + 64)
                S.op("pe", [lambda e, sbf=sbf, tk=tk, c=c: e.matmul(pg.t[:, 256 + 64 * c:256 + 64 * c + 64], sbf.t[:, :], gqT.t[:, tk], start=True, stop=True)],
                     reads=[sbf.b, gqT.b], writes=[b_o])
                yield
            osb = osb_p.next()
            S.op("dve", lambda e, osb=osb: e.tensor_copy(out=osb.t[:, :], in_=r_o), reads=[b_o], writes=[osb.b])
            sq = sq_p.next()
            S.op("pool", lambda e, osb=osb, sq=sq: e.tensor_tensor(out=sq.t[:, :], in0=osb.t[:, :], in1=osb.t[:, :], op=ALU.mult), reads=[osb.b], writes=[sq.b])
            yield
            S.op("pe", [lambda e, sq=sq: e.matmul(r_ss, ones_bf.t[:, :], sq.t[:, :], start=True, stop=True)], reads=[ones_bf.b, sq.b], writes=[b_ss])
            yield
            rs = rs_p.next()
            S.op("act", lambda e, rs=rs: e.activation(out=rs.t[:, :], in_=r_ss, func=AF.Ln, bias=eps.t[:, 0:1], scale=1.0 / 128), reads=[b_ss, eps.b], writes=[rs.b])
            S.op("act", lambda e, rs=rs: e.activation(out=rs.t[:, :], in_=rs.t[:, :], func=AF.Exp, scale=-0.5), reads=[rs.b], writes=[rs.b])
            yield
            if blk % 4 == 0:
                gst = gst_pool.next()
            cs4 = slice((blk % 4) * 128, (blk % 4 + 1) * 128)
            S.op("dve", lambda e, gst=gst, cs4=cs4, osb=osb, rs=rs: e.scalar_tensor_tensor(out=gst.t[:, cs4], in0=osb.t[:, :], scalar=bw.t[:, 32:33], in1=rs.t[:, :],
                                                                                        op0=ALU.mult, op1=ALU.mult), reads=[osb.b, rs.b, bw.b], writes=[gst.b])
            if blk % 4 == 3:
                pc = blk // 4
                cols = slice((pc % 4) * 512, (pc % 4 + 1) * 512)
                for (fa, fb, dst) in X2F.fm_pieces(io["s2f"], pc // 4, RF_GN, 0, 128):
                    S.dma("sp", lambda e, gst=gst, dst=dst, fa=fa, fb=fb, cols=cols: e.dma_start(out=dst[:, cols], in_=gst.t[fa:fb, :]),
                          reads=[gst.b], outs=[o_f])
            yield

    side = [conv_gen(), gla_gen()]

    def step_side():
        for gnr in list(side):
            try:
                next(gnr)
            except StopIteration:
                side.remove(gnr)

    tiles = []
    for qt in range(NQT):
        for hh in range(2):
            kbs = list(range(4 * qt + 3, -1, -1))
            for idx, kb in enumerate(kbs):
                tiles.append((qt, hh, kb, idx == 0, idx == len(kbs) - 1))
    stA = {}

    def stage_A(i):
        qt, hh, kb, first, last = tiles[i]
        hs = slice(64 * hh, 64 * hh + 64)
        qs = slice(qt * 512, (qt + 1) * 512)
        ks = slice(kb * 128, (kb + 1) * 128)
        pz = ps_z.next()
        S.op("pe", [lambda e: e.matmul(pz.t[:, :], kT.t[hs, ks], qT.t[hs, qs], start=True, stop=False)], reads=[kT.b, qT.b], writes=[pz.b])
        ee = e_pool.next()
        S.op("act", lambda e: e.activation(out=ee.t[:, :], in_=pz.t[:, :], func=AF.Exp), reads=[pz.b], writes=[ee.b])
        sp = sp_pool.next()
        S.op("act", lambda e: e.activation(out=sp.t[:, :], in_=ee.t[:, :], func=AF.Ln, bias=1.0), reads=[ee.b], writes=[sp.b])
        di = kb - 4 * qt
        if di >= 0:
            S.op("pool", lambda e: e.tensor_tensor(out=sp.t[:, :], in0=sp.t[:, :], in1=masks[di].t[:, :], op=ALU.mult), reads=[sp.b, masks[di].b], writes=[sp.b])
        stA[i] = (pz, sp)

    cur = {"po": None, "pc": None, "cb": None}

    def stage_B(i):
        qt, hh, kb, first, last = tiles[i]
        hs = slice(64 * hh, 64 * hh + 64)
        pz, sp = stA.pop(i)
        if first:
            cur["po"] = ps_o.next()
            cur["pc"] = ps_c.next()
        po, pc = cur["po"], cur["pc"]
        if first:
            S.op("pe", [lambda e: e.matmul(pz.t[:, :], ntri.t[:, :], sp.t[:, :], start=False, stop=True)], reads=[ntri.b, sp.b, pz.b], writes=[pz.b])
        else:
            cbo = cur["cb"]
            S.op("pe", [lambda e: e.matmul(pz.t[:, :], ntri.t[:, :], sp.t[:, :], start=False, stop=False),
                        lambda e: e.matmul(pz.t[:, :], ones_bf.t[0:1, :], cbo.t[0:1, :], start=False, stop=True)],
                 reads=[ntri.b, sp.b, pz.b, cbo.b, ones_bf.b], writes=[pz.b])
        w = w_pool.next()
        S.op("act", lambda e: e.activation(out=w.t[:, :], in_=pz.t[:, :], func=AF.Exp), reads=[pz.b], writes=[w.b])
        di = kb - 4 * qt
        if di >= 0:
            S.op("pool", lambda e: e.tensor_tensor(out=w.t[:, :], in0=w.t[:, :], in1=masks[di].t[:, :], op=ALU.mult), reads=[w.b, masks[di].b], writes=[w.b])
        S.op("pe", [lambda e: e.matmul(po.t[0:64, :], v.t[:, kb, hs], w.t[:, :], start=first, stop=last)], reads=[v.b, w.b], writes=[po.b])
        if not last:
            S.op("pe", [lambda e: e.matmul(pc.t[0:1, :], negones.t[:, :], sp.t[:, :], start=first, stop=False, skip_group_check=True)],
                 reads=[negones.b, sp.b], writes=[pc.b])
            cbn = cb_pool.next()
            S.op("dve", lambda e: e.tensor_copy(out=cbn.t[:, :], in_=pc.t[0:1, :]), reads=[pc.b], writes=[cbn.b])
            cur["cb"] = cbn
        else:
            osb = osb_pool.next()
            S.op("dve", lambda e: e.tensor_copy(out=osb.t[:, :], in_=po.t[0:64, :]), reads=[po.b], writes=[osb.b])
            cols = slice((qt % 4) * 512, (qt % 4 + 1) * 512)
            for (fa, fb, dst) in X2B.fm_pieces(io["s2bf"], qt // 4, 0, 64 * hh, 64):
                S.dma("sp", lambda e, dst=dst, fa=fa, fb=fb: e.dma_start(out=dst[:, cols], in_=osb.t[fa:fb, :]), reads=[osb.b], outs=[o_bf])

    n = len(tiles)
    stage_A(0)
    for i in range(n):
        if i + 1 < n:
            stage_A(i + 1)
        stage_B(i)
        step_side()
    while side:
        step_side()


def prep_B_weights(inp, l, p):
    cw = inp["conv_w"][l][:, 128 * p:128 * p + 128].T
    cb = inp["conv_b"][l][128 * p:128 * p + 128][:, None]
    gg = inp["gla_norm_g"][l][128 * p:128 * p + 128][:, None]
    return _c(np.concatenate([cw, cb, gg], axis=1).astype(np.float32))


I32 = mybir.dt.int32
GROUPS = [[0, 1, 2, 3], [4, 5, 6, 7]]


def build_fused(L=DEPTH):
    nc = bass.Bass("TRN2", target_bir_lowering=False)
    dt = nc.dram_tensor

    def ein(name, shape, d=F32):
        return dt(name, list(shape), d, kind="ExternalInput").ap()

    xT = ein("xT", [D, TPC])
    cid = ein("cid", [1, 1], I32)
    bw = ein("bw", [L, 128, 33])
    wg1, wu1, wd1 = ein("wg1", [L, NF, 128, 1024]), ein("wu1", [L, NF, 128, 1024]), ein("wd1", [L, 8, 128, NF * 128])
    wfm, wtm = ein("wfm", [L, NFM, 128, 1024]), ein("wtm", [L, 128, 8 * 1280])
    walpha, balpha, gainsA = ein("walpha", [L, 16, 256]), ein("balpha", [L, 1, 256]), ein("gainsA", [L, 128, 24])
    wgt, wbr, wo = ein("wgt", [L, 24, 128, 1024]), ein("wbr", [L, 24, 128, 512]), ein("wo", [L, 8, 128, 1024])
    wg2, wu2, wd2 = ein("wg2", [L, NF, 128, 1024]), ein("wu2", [L, NF, 128, 1024]), ein("wd2", [L, 8, 128, NF * 128])
    gainsC = ein("gainsC", [L, 128, 40])
    outT = dt("outT", [D, TPC], F32, kind="ExternalOutput").ap()
    x1T = dt("x1T_i", [D, TPC], F32).ap()
    xbuf = dt("xbuf_i", [D, TPC], F32).ap()
    silur = dt("silur_i", [512, TPC], F32).ap()
    send_bf = dt("send_bf_i", X1B.shape(), BF16).ap()
    send_f = dt("send_f_i", X1F.shape(), F32).ap()
    ag_bf = dt("ag_bf_i", X1B.shape(16), BF16).ap()
    ag_f = dt("ag_f_i", X1F.shape(16), F32).ap()
    s2bf = dt("s2bf_i", X2B.shape(), BF16).ap()
    s2f = dt("s2f_i", X2F.shape(), F32).ap()
    ag2_bf = dt("ag2_bf_i", X2B.shape(16), BF16).ap()
    ag2_f = dt("ag2_f_i", X2F.shape(16), F32).ap()
    st_bf = dt("st_bf_i", X1B.shape(), BF16).ap()
    st_f = dt("st_f_i", X1F.shape(), F32).ap()
    st2_bf = dt("st2_bf_i", X2B.shape(), BF16).ap()
    st2_f = dt("st2_f_i", X2F.shape(), F32).ap()
    P = {n: Buf(n, True) for n in ("x1", "sr", "sbf", "sf", "agbf", "agf", "s2bf", "s2f", "ag2bf", "ag2f", "xbuf", "out",
                                   "stbf", "stf", "st2bf", "st2f")}
    with ExitStack() as st:
        K = Ctx(nc, st)
        S = K.S
        S.cid_ap = cid[0:1, 0:1]
        K.init_psum(8)

        def stat(ap):
            return ap

        def pick(ap16, stage, bsrc, bdst):
            v4 = ap16.rearrange("c (r d) n -> d c r n", d=4)
            S.dma("sp", lambda e: e.dma_start(out=stage, in_=v4[bass.ds(S.dyn["p"], 1)].rearrange("o c r n -> (o c) r n")),
                  reads=[bsrc], writes=[bdst])

        def gather(src, dst, bsrc, bdst):
            for c in range(src.shape[0]):
                S.dma("pool", lambda e, c=c: e.collective_compute("AllGather", ALU.bypass, replica_groups=GROUPS,
                                                                  ins=[src[c].opt()], outs=[dst[c].opt()]),
                      reads=[bsrc], outs=[bdst], sembuf=bdst, inc=1)

        for l in range(L):
            last = (l == L - 1)
            K.arena_reset()
            C = common_consts(K)
            ffn_bufs(K, C)
            emit_A(K, C, dict(xT=xT if l == 0 else xbuf, b_xin=None if l == 0 else P["xbuf"],
                              wg=wg1[l], wu=wu1[l], wd=wd1[l], wfm=wfm[l], wtm=wtm[l], walpha=walpha[l], balpha=balpha[l],
                              gains=gainsA[l], x1T=x1T, send_bf=send_bf, send_f=send_f, silur=silur,
                              b_x1=P["x1"], b_sbf=P["sbf"], b_sf=P["sf"], b_sr=P["sr"]))
            S.barrier()
            gather(send_bf, ag_bf, P["sbf"], P["agbf"])
            gather(send_f, ag_f, P["sf"], P["agf"])
            pick(ag_bf, st_bf, P["agbf"], P["stbf"])
            pick(ag_f, st_f, P["agf"], P["stf"])
            K.arena_reset()
            emit_B(K, dict(rbf=stat(st_bf), rf=stat(st_f), bw=bw[l], s2bf=s2bf, s2f=s2f,
                           b_rbf=P["stbf"], b_rf=P["stf"], b_s2bf=P["s2bf"], b_s2f=P["s2f"]))
            S.barrier()
            gather(s2bf, ag2_bf, P["s2bf"], P["ag2bf"])
            gather(s2f, ag2_f, P["s2f"], P["ag2f"])
            pick(ag2_bf, st2_bf, P["ag2bf"], P["st2bf"])
            pick(ag2_f, st2_f, P["ag2f"], P["st2f"])
            K.arena_reset()
            C = common_consts(K)
            ffn_bufs(K, C)
            emit_C(K, C, dict(x1T=x1T, recv_bf=stat(st2_bf), recv_f=stat(st2_f), silur=silur,
                              wgt=wgt[l], wbr=wbr[l], wo=wo[l], wg=wg2[l], wu=wu2[l], wd=wd2[l], gains=gainsC[l],
                              x3T=outT if last else xbuf, b_x3=P["out"] if last else P["xbuf"],
                              b_x1=P["x1"], b_sr=P["sr"], b_r2bf=P["st2bf"], b_r2f=P["st2f"]))
            S.barrier()
        S.wait_all_on("sp", [P["out"]])
        S.emit()
    return nc, K


_FUSED = {}


def kernel(**inputs):
    inp = {k: np.asarray(v) for k, v in inputs.items()}
    x = inp["x"].astype(np.float32, copy=False)
    cores = list(range(NCORES))
    T = TPC
    L = DEPTH
    if "nc" not in _FUSED:
        _FUSED["nc"] = build_fused(L)[0]
    A = [prep_A_weights(inp, l) for l in range(L)]
    Cw = [prep_C_weights(inp, l) for l in range(L)]
    shared = dict(
        wg1=np.stack([a["wg"] for a in A]), wu1=np.stack([a["wu"] for a in A]), wd1=np.stack([a["wd"] for a in A]),
        wfm=np.stack([a["wfm"] for a in A]), wtm=np.stack([a["wtm"] for a in A]),
        walpha=np.stack([a["walpha"] for a in A]), balpha=np.stack([a["balpha"] for a in A]),
        gainsA=np.stack([a["gains"] for a in A]),
        wgt=np.stack([c["wgt"] for c in Cw]), wbr=np.stack([c["wbr"] for c in Cw]), wo=np.stack([c["wo"] for c in Cw]),
        wg2=np.stack([c["wg"] for c in Cw]), wu2=np.stack([c["wu"] for c in Cw]), wd2=np.stack([c["wd"] for c in Cw]),
        gainsC=np.stack([c["gains"] for c in Cw]))
    del A, Cw
    bws = [np.stack([prep_B_weights(inp, l, p) for l in range(L)]) for p in range(4)]
    in_maps = []
    for c in cores:
        m = dict(shared)
        m["xT"] = _c(x[c // 4, (c % 4) * T:(c % 4 + 1) * T, :].T)
        m["cid"] = np.array([[c % 4]], np.int32)
        m["bw"] = bws[c % 4]
        in_maps.append(m)
    res = run_bass_kernel_spmd(_FUSED["nc"], in_maps, core_ids=cores).results
    out = np.empty((BATCH, SEQ, D), np.float32)
    for c in cores:
        out[c // 4, (c % 4) * T:(c % 4 + 1) * T, :] = np.asarray(res[c]["outT"]).T
    return out
```

```python
import numpy as np
from contextlib import ExitStack
import ml_dtypes
import concourse.bass as bass
import concourse.mybir as mybir
from concourse.bass_utils import run_bass_kernel_spmd

F32 = mybir.dt.float32
BF16 = mybir.dt.bfloat16
AF = mybir.ActivationFunctionType
ALU = mybir.AluOpType

D = 1024
DFF = 2816
NF = DFF // 128
SEQ = 8192
BATCH = 2
DEPTH = 4
TPC = 2048
PASS = 1024
EPS = 1e-6
INW = 7184
NCORES = 8
POOL_ENG = "dve"

ENGS = ("pe", "act", "dve", "pool", "sp")
SEM_LIMIT = 28000


class Buf:
    __slots__ = ("name", "w", "r", "dsem", "persist")

    def __init__(self, name, persist=False):
        self.name = name
        self.w = None
        self.r = {}
        self.dsem = None
        self.persist = persist


class Sched:
    def __init__(self, nc, stack):
        self.nc = nc
        self.stack = stack
        self.sem = {}
        self.cnt = {}
        self.gen = {e: 0 for e in ENGS}
        self.seen = {e: {} for e in ENGS}
        self.q = {e: [] for e in ENGS}
        self.nsem = 0
        for e in ENGS:
            self._mksem((e, 0))
        self.n_inst = 0
        self.dma_free = []
        self.dma_bufs = []
        self.dyn = {}
        self.cid_ap = None

    def _mksem(self, key):
        self.nsem += 1
        h = self.stack.enter_context(self.nc.semaphore("s%d" % self.nsem))
        self.sem[key] = h
        self.cnt[key] = 0
        return key

    def ekey(self, e):
        return (e, self.gen[e])

    def _waits(self, e, deps):
        out = []
        seen = self.seen[e]
        for (k, v) in deps:
            if k[0] == e:
                continue
            if seen.get(k, 0) < v:
                seen[k] = v
                out.append((k, v))
        return out

    @staticmethod
    def _deps(reads, writes):
        deps = []
        for b in reads:
            if b.w is not None:
                deps.append(b.w)
        for b in writes:
            if b.w is not None:
                deps.append(b.w)
            deps.extend(b.r.items())
        return deps

    def op(self, e, fns, reads=(), writes=()):
        if not isinstance(fns, (list, tuple)):
            fns = [fns]
        waits = self._waits(e, self._deps(reads, writes))
        k = self.ekey(e)
        if self.cnt[k] >= SEM_LIMIT:
            self.gen[e] += 1
            k = self._mksem(self.ekey(e))
        self.cnt[k] += 1
        ev = (k, self.cnt[k])
        for b in reads:
            if b.r.get(k, 0) < ev[1]:
                b.r[k] = ev[1]
        for b in writes:
            b.w = ev
            b.r = {}
        self.q[e].append((waits, fns, k, 1))
        self.n_inst += len(fns)
        return ev

    def dma(self, qe, fn, reads=(), writes=(), sembuf=None, outs=(), inc=16):
        deps = self._deps(reads, writes)
        for b in outs:
            deps.extend(b.r.items())
        waits = self._waits(qe, deps)
        tgt = sembuf if sembuf is not None else (outs[0] if outs else (writes[0] if writes else reads[0]))
        if tgt.dsem is None:
            if self.dma_free and not tgt.persist:
                tgt.dsem = self.dma_free.pop()
            else:
                tgt.dsem = self._mksem(("dma", tgt.name, self.nsem))
            if not tgt.persist:
                self.dma_bufs.append(tgt)
        semkey = tgt.dsem
        self.cnt[semkey] += inc
        ev = (semkey, self.cnt[semkey])
        for b in reads:
            if b.r.get(semkey, 0) < ev[1]:
                b.r[semkey] = ev[1]
        for b in writes:
            b.w = ev
            b.r = {}
        for b in outs:
            b.w = ev
        self.q[qe].append((waits, [fn], semkey, inc))
        self.n_inst += 1
        return ev

    def barrier(self):
        deps = [(k, v) for k, v in self.cnt.items() if v > 0 and k[0] != "sp"]
        waits = self._waits("sp", deps)
        k = self.ekey("sp")
        self.cnt[k] += 1
        ev = (k, self.cnt[k])
        self.q["sp"].append((waits, [lambda e: e.nop()], k, 1))
        for e in ENGS:
            if e != "sp":
                self.q[e].append((self._waits(e, [ev]), [], None, 0))
        for b in self.dma_bufs:
            if b.dsem is not None and self.cnt[b.dsem] < SEM_LIMIT:
                self.dma_free.append(b.dsem)
            b.dsem = None
        self.dma_bufs = []

    def wait_all_on(self, e, bufs):
        deps = [b.w for b in bufs if b.w is not None]
        waits = self._waits(e, deps)
        self.q[e].append((waits, [], None, 0))

    def emit(self):
        nc = self.nc
        with nc.Block() as block:
            def mk(e):
                def body(engobj):
                    if e == "sp" and self.cid_ap is not None:
                        with engobj.register("rank_reg") as rr:
                            engobj.reg_load(rr, self.cid_ap)
                            self.dyn["p"] = engobj.snap(rr)
                            run(engobj)
                    else:
                        run(engobj)

                def run(engobj):
                    for (waits, fns, k, inc) in self.q[e]:
                        for (wk, wv) in waits:
                            engobj.wait_ge(self.sem[wk], wv)
                        n = len(fns)
                        for i, fn in enumerate(fns):
                            ins = fn(engobj)
                            if i == n - 1:
                                ins.then_inc(self.sem[k], inc)
                return body
            block.tensor(mk("pe"))
            block.scalar(mk("act"))
            block.vector(mk("dve"))
            block.gpsimd(mk("pool"))
            block.sync(mk("sp"))


class Tl:
    __slots__ = ("t", "b")

    def __init__(self, t, b):
        self.t = t
        self.b = b


class Ctx:
    def __init__(self, nc, stack):
        self.nc = nc
        self.st = stack
        self.S = Sched(nc, stack)
        self.uid = 0
        self.ps = []
        self.psi = 0
        self.outs = []

    def name(self, p):
        self.uid += 1
        return "%s_%d" % (p, self.uid)

    ARENA_LO = 16512
    ARENA_HI = 229344

    def sb(self, shape, dt, name="t"):
        n = self.name(name)
        esz = 2 if dt == BF16 else 4
        nbytes = esz
        for d_ in shape[1:]:
            nbytes *= int(d_)
        off = (getattr(self, "arena_ptr", self.ARENA_LO) + 31) // 32 * 32
        assert off + nbytes <= self.ARENA_HI, "SBUF arena overflow at %s: need %d have %d" % (n, nbytes, self.ARENA_HI - off)
        self.arena_ptr = off + nbytes
        t = self.nc.alloc_sbuf_tensor_at(n, list(shape), dt, offset=off)
        return Tl(t, Buf(n))

    def arena_reset(self):
        self.arena_ptr = self.ARENA_LO

    def pool(self, n, shape, dt, name="p"):
        return Rot([self.sb(shape, dt, name) for _ in range(n)])

    def init_psum(self, n=8):
        for i in range(n):
            nm = self.name("ps")
            t = self.st.enter_context(self.nc.psum_tensor(nm, [128, 512], F32))
            self.ps.append(Tl(t, Buf(nm)))

    def psum(self):
        p = self.ps[self.psi % len(self.ps)]
        self.psi += 1
        return p

    def dram_buf(self, name):
        return Buf(name)


class Rot:
    def __init__(self, items):
        self.items = items
        self.i = 0

    def next(self):
        it = self.items[self.i % len(self.items)]
        self.i += 1
        return it


def mm_group(K, ps, pairs, reads):
    n = len(pairs)

    def mk(i, l, r, o):
        return lambda e: e.matmul(o, l, r, start=(i == 0), stop=(i == n - 1))
    fns = [mk(i, l, r, o) for i, (l, r, o) in enumerate(pairs)]
    K.S.op("pe", fns, reads=reads, writes=[ps.b])


def rstd_from_sq(K, C, sq, sq_reads, nk, ncols, inv_n, extra=None):
    ps = K.psum()
    pairs = [(C["ones_bf"].t[:, :], sq.t[:, k, 0:ncols], ps.t[:, 0:ncols]) for k in range(nk)]
    mm_group(K, ps, pairs, reads=[C["ones_bf"].b, sq.b])
    r = C["rstd_pool"].next()
    K.S.op("act", lambda e: e.activation(out=r.t[:, 0:ncols], in_=ps.t[:, 0:ncols], func=AF.Sqrt,
                                         bias=C["eps_col"].t[:, 0:1], scale=inv_n),
           reads=[ps.b, C["eps_col"].b], writes=[r.b])
    K.S.op("dve", lambda e: e.reciprocal(out=r.t[:, 0:ncols], in_=r.t[:, 0:ncols]), reads=[r.b], writes=[r.b])
    return r


def pre_norm(K, C, x, xb, g, gcol0, h, hb, s):
    sl = slice(s * 512, (s + 1) * 512)
    sq = C["sq_pool"].next()
    K.S.op("act", lambda e: e.activation(out=sq.t[:, :, :], in_=x.t[:, :, sl], func=AF.Square),
           reads=[xb], writes=[sq.b])
    r = rstd_from_sq(K, C, sq, None, 8, 512, 1.0 / D)
    fns = [(lambda e, k=k: e.scalar_tensor_tensor(
        out=h.t[:, k, sl], in0=x.t[:, k, sl], scalar=g.t[:, gcol0 + k:gcol0 + k + 1], in1=r.t[:, :],
        op0=ALU.mult, op1=ALU.mult)) for k in range(8)]
    K.S.op("dve", fns, reads=[xb, r.b, g.b], writes=[hb])


def post_norm_residual(K, C, y, yb, g, gcol0, alpha, x, xb, s):
    sl = slice(s * 512, (s + 1) * 512)
    sq = C["sq_pool"].next()
    K.S.op("act", lambda e: e.activation(out=sq.t[:, :, :], in_=y.t[:, :, sl], func=AF.Square),
           reads=[yb], writes=[sq.b])
    r = rstd_from_sq(K, C, sq, None, 8, 512, 1.0 / D)
    for k in range(8):
        tmp = C["tmp_pool"].next()
        K.S.op("dve", lambda e, k=k, tmp=tmp: e.scalar_tensor_tensor(
            out=tmp.t[:, :], in0=y.t[:, k, sl], scalar=g.t[:, gcol0 + k:gcol0 + k + 1], in1=r.t[:, :],
            op0=ALU.mult, op1=ALU.mult), reads=[yb, r.b, g.b], writes=[tmp.b])
        K.S.op("dve", lambda e, k=k, tmp=tmp: e.scalar_tensor_tensor(
            out=x.t[:, k, sl], in0=tmp.t[:, :], scalar=float(alpha), in1=x.t[:, k, sl],
            op0=ALU.mult, op1=ALU.add), reads=[tmp.b, xb], writes=[xb])


def load_w(K, slot, src_ap, q="pool"):
    K.S.dma(q, lambda e: e.dma_start(out=slot.t[:], in_=src_ap), writes=[slot.b])


def ffn(K, C, h, hb, wg_ap, wu_ap, wd_ap, y, yb):
    a = C["a"]
    nsub = PASS // 512
    for f in range(NF):
        wg = C["wgu_pool"].next()
        load_w(K, wg, wg_ap[f].rearrange("p (k j) -> p k j", j=128))
        wu = C["wgu_pool"].next()
        load_w(K, wu, wu_ap[f].rearrange("p (k j) -> p k j", j=128))
        for s in range(nsub):
            sl = slice(s * 512, (s + 1) * 512)
            pg = K.psum()
            mm_group(K, pg, [(wg.t[:, k, :], h.t[:, k, sl], pg.t[:, :]) for k in range(8)], reads=[wg.b, hb])
            pu = K.psum()
            mm_group(K, pu, [(wu.t[:, k, :], h.t[:, k, sl], pu.t[:, :]) for k in range(8)], reads=[wu.b, hb])
            sg = C["tmp_pool"].next()
            K.S.op("act", lambda e, sg=sg, pg=pg: e.activation(out=sg.t[:, :], in_=pg.t[:, :], func=AF.Silu),
                   reads=[pg.b], writes=[sg.b])
            ab = a.b[f * nsub + s]
            K.S.op("dve", lambda e, sg=sg, pu=pu, f=f, sl=sl: e.tensor_tensor(
                out=a.t[:, f, sl], in0=sg.t[:, :], in1=pu.t[:, :], op=ALU.mult),
                reads=[sg.b, pu.b], writes=[ab])
    for d in range(8):
        wd = C["wd_pool"].next()
        load_w(K, wd, wd_ap[d].rearrange("p (c j) -> p c j", j=128))
        for s in range(nsub):
            sl = slice(s * 512, (s + 1) * 512)
            py = K.psum()
            mm_group(K, py, [(wd.t[:, c, :], a.t[:, c, sl], py.t[:, :]) for c in range(NF)],
                     reads=[wd.b] + [a.b[c * nsub + s] for c in range(NF)])
            K.S.op("act", lambda e, py=py, d=d, sl=sl: e.activation(out=y.t[:, d, sl], in_=py.t[:, :], func=AF.Copy),
                   reads=[py.b], writes=[yb])


class ATl:
    def __init__(self, t, bufs):
        self.t = t
        self.b = bufs


def common_consts(K):
    C = {}
    ones_bf = K.sb([128, 128], BF16, "ones_bf")
    K.S.op("dve", lambda e: e.memset(ones_bf.t[:, :], 1.0), writes=[ones_bf.b])
    C["ones_bf"] = ones_bf
    eps = K.sb([128, 1], F32, "eps")
    K.S.op("dve", lambda e: e.memset(eps.t[:, :], EPS), writes=[eps.b])
    C["eps_col"] = eps
    C["rstd_pool"] = K.pool(2, [128, 512], F32, "rstd")
    C["sq_pool"] = K.pool(1, [128, 8, 512], BF16, "sq")
    C["tmp_pool"] = K.pool(3, [128, 512], F32, "tmp")
    return C


def ffn_bufs(K, C):
    nsub = PASS // 512
    at = K.sb([128, NF, PASS], BF16, "a")
    C["a"] = ATl(at.t, [Buf("a%d" % i) for i in range(NF * nsub)])
    C["wgu_pool"] = K.pool(4, [128, 8, 128], BF16, "wgu")
    C["wd_pool"] = K.pool(2, [128, NF, 128], BF16, "wd")


OQ, OK_, OV, OCA, OCG, OGQ, OGK, OGV, OGR, OLR, OGT = 0, 512, 1024, 1536, 2048, 2560, 2816, 3072, 3584, 4096, 4112
FM_CHUNKS = ([("q", i, OQ + 128 * i) for i in range(4)] + [("k", i, OK_ + 128 * i) for i in range(4)] +
             [("a", i, OCA + 128 * i) for i in range(4)] + [("g", i, OCG + 128 * i) for i in range(4)] +
             [("gq", i, OGQ + 128 * i) for i in range(2)] + [("r", i, OGR + 128 * i) for i in range(4)] +
             [("lr", 0, OLR)])
NFM = len(FM_CHUNKS)
SB_QT, SB_KT, SB_V, SB_GQT, SB_GV = 0, 128 * TPC, 256 * TPC, 384 * TPC, 448 * TPC
SB_LEN = 576 * TPC
SF_UT, SF_GK, SF_LA = 0, 128 * TPC, 192 * TPC
SF_LEN = 256 * TPC


class XB:
    def __init__(self, U, nunits):
        self.U, self.nunits = U, nunits
        self.NCH = nunits // U
        self.CL = U * TPC

    def shape(self, nrank=4):
        return [self.NCH, nrank, self.CL]

    def fm_pieces(self, ap3, d, off, f0, nf):
        u0 = off // TPC + f0
        u1 = u0 + nf
        out = []
        u = u0
        while u < u1:
            c = u // self.U
            ue = min(u1, (c + 1) * self.U)
            lo = (u - c * self.U) * TPC
            out.append((u - u0, ue - u0, ap3[c, d, lo:lo + (ue - u) * TPC].rearrange("(f t) -> f t", t=TPC)))
            u = ue
        return out

    def tm_all(self, ap3, off, nfeat, tok0, ntok):
        e0 = off + tok0 * nfeat
        c = e0 // self.CL
        lo = e0 - c * self.CL
        assert lo + ntok * nfeat <= self.CL
        return ap3[c, :, lo:lo + ntok * nfeat].rearrange("d (t f) -> t d f", f=nfeat)

    def tm_src(self, ap3, j, off, nfeat, tok0, ntok):
        e0 = off + tok0 * nfeat
        c = e0 // self.CL
        lo = e0 - c * self.CL
        assert lo + ntok * nfeat <= self.CL
        return ap3[c, j, lo:lo + ntok * nfeat].rearrange("(t f) -> t f", f=nfeat)


X1B = XB(64, 576)
X1F = XB(32, 256)
X2B = XB(64, 128)
X2F = XB(32, 256)


def build_A():
    nc = bass.Bass("TRN2", target_bir_lowering=False)
    dt = nc.dram_tensor
    xT = dt("xT", [D, TPC], F32, kind="ExternalInput").ap()
    wg = dt("wg", [NF, 128, 8 * 128], F32, kind="ExternalInput").ap()
    wu = dt("wu", [NF, 128, 8 * 128], F32, kind="ExternalInput").ap()
    wd = dt("wd", [8, 128, NF * 128], F32, kind="ExternalInput").ap()
    wfm = dt("wfm", [NFM, 128, 8 * 128], F32, kind="ExternalInput").ap()
    wtm = dt("wtm", [128, 8 * 1280], F32, kind="ExternalInput").ap()
    walpha = dt("walpha", [16, 256], F32, kind="ExternalInput").ap()
    balpha = dt("balpha", [1, 256], F32, kind="ExternalInput").ap()
    gains = dt("gains", [128, 24], F32, kind="ExternalInput").ap()
    x1T = dt("x1T", [D, TPC], F32, kind="ExternalOutput").ap()
    send_bf = dt("send_bf", [4, SB_LEN], BF16, kind="ExternalOutput").ap()
    send_f = dt("send_f", [4, SF_LEN], F32, kind="ExternalOutput").ap()
    silur = dt("silur", [512, TPC], F32, kind="ExternalOutput").ap()
    with ExitStack() as st:
        K = Ctx(nc, st)
        K.init_psum(8)
        C = common_consts(K)
        ffn_bufs(K, C)
        bb = dict(b_x1=Buf("o_x1", True), b_sbf=Buf("o_sbf", True), b_sf=Buf("o_sf", True), b_sr=Buf("o_sr", True))
        K.outs += list(bb.values())
        emit_A(K, C, dict(xT=xT, wg=wg, wu=wu, wd=wd, wfm=wfm, wtm=wtm, walpha=walpha, balpha=balpha,
                          gains=gains, x1T=x1T, send_bf=send_bf, send_f=send_f, silur=silur, **bb))
        K.S.wait_all_on("sp", K.outs)
        K.S.emit()
    return nc


def emit_A(K, C, io):
    S = K.S
    nsub = PASS // 512
    g = K.sb([128, 24], F32, "gains")
    S.dma("sp", lambda e: e.dma_start(out=g.t[:, :], in_=io["gains"]), writes=[g.b])
    wtm = K.sb([128, 8, 1280], BF16, "wtm")
    load_w(K, wtm, io["wtm"].rearrange("p (k j) -> p k j", j=1280))
    wal = K.sb([16, 256], F32, "walpha")
    S.dma("sp", lambda e: e.dma_start(out=wal.t[:, :], in_=io["walpha"]), writes=[wal.b])
    bal = K.sb([1, 256], F32, "balpha")
    S.dma("sp", lambda e: e.dma_start(out=bal.t[:, :], in_=io["balpha"]), writes=[bal.b])
    ones_f = K.sb([1, 128], F32, "ones_f")
    S.op("dve", lambda e: e.memset(ones_f.t[:, :], 1.0), writes=[ones_f.b])

    x = K.sb([128, 8, PASS], F32, "x")
    h = K.sb([128, 8, PASS], BF16, "h")
    y = K.sb([128, 8, PASS], F32, "y")
    lrT = K.sb([16, PASS], F32, "lrT")
    wfm_pool = K.pool(2, [128, 8, 128], BF16, "wfm")
    st_bf = K.pool(4, [128, 512], BF16, "st_bf")
    st_f = K.pool(3, [128, 512], F32, "st_f")
    st_la = K.pool(2, [128, 256], F32, "st_la")
    xv = io["xT"].rearrange("(k p) t -> p k t", p=128)
    x1v = io["x1T"].rearrange("(k p) t -> p k t", p=128)
    sbf, sf = io["send_bf"], io["send_f"]
    o_x1, o_sbf, o_sf, o_sr = io["b_x1"], io["b_sbf"], io["b_sf"], io["b_sr"]
    b_xin = io.get("b_xin")

    def fm_out(xb, ap3, p, off, o, tsl, outbuf, pf0=0, nf=128):
        for (fa, fb, dst) in xb.fm_pieces(ap3, p, off, 0, nf):
            S.dma("sp", lambda e, dst=dst, fa=fa, fb=fb: e.dma_start(out=dst[:, tsl], in_=o.t[pf0 + fa:pf0 + fb, :]),
                  reads=[o.b], outs=[outbuf])

    for ps_ in range(TPC // PASS):
        t0 = ps_ * PASS
        S.dma("sp", lambda e, t0=t0: e.dma_start(out=x.t[:, :, :], in_=xv[:, :, t0:t0 + PASS]), writes=[x.b],
              reads=[b_xin] if b_xin is not None else [])
        for s in range(nsub):
            pre_norm(K, C, x, x.b, g, 0, h, h.b, s)
        ffn(K, C, h, h.b, io["wg"], io["wu"], io["wd"], y, y.b)
        for s in range(nsub):
            post_norm_residual(K, C, y, y.b, g, 16, 0.5, x, x.b, s)
        S.dma("sp", lambda e, t0=t0: e.dma_start(out=x1v[:, :, t0:t0 + PASS], in_=x.t[:, :, :]),
              reads=[x.b], outs=[o_x1])
        for s in range(nsub):
            pre_norm(K, C, x, x.b, g, 8, h, h.b, s)
        for ci, (kind, i, col) in enumerate(FM_CHUNKS):
            w = wfm_pool.next()
            load_w(K, w, io["wfm"][ci].rearrange("p (k j) -> p k j", j=128))
            for s in range(nsub):
                sl = slice(s * 512, (s + 1) * 512)
                tsl = slice(t0 + s * 512, t0 + (s + 1) * 512)
                pp = K.psum()
                mm_group(K, pp, [(w.t[:, k, :], h.t[:, k, sl], pp.t[:, :]) for k in range(8)], reads=[w.b, h.b])
                if kind in ("q", "k"):
                    o = st_bf.next()
                    sc = 0.125 if kind == "q" else 1.0
                    S.op("act", lambda e, o=o, pp=pp, sc=sc: e.activation(out=o.t[:, :], in_=pp.t[:, :], func=AF.Copy, scale=sc),
                         reads=[pp.b], writes=[o.b])
                    off = SB_QT if kind == "q" else SB_KT
                    fm_out(X1B, sbf, i, off, o, tsl, o_sbf)
                elif kind == "a":
                    S.op("act", lambda e, pp=pp, i=i, sl=sl: e.activation(out=y.t[:, i, sl], in_=pp.t[:, :], func=AF.Copy),
                         reads=[pp.b], writes=[y.b])
                elif kind == "g":
                    sg = C["tmp_pool"].next()
                    S.op("act", lambda e, sg=sg, pp=pp: e.activation(out=sg.t[:, :], in_=pp.t[:, :], func=AF.Sigmoid),
                         reads=[pp.b], writes=[sg.b])
                    o = st_f.next()
                    S.op("dve", lambda e, o=o, sg=sg, i=i, sl=sl: e.tensor_tensor(out=o.t[:, :], in0=y.t[:, i, sl], in1=sg.t[:, :], op=ALU.mult),
                         reads=[sg.b, y.b], writes=[o.b])
                    fm_out(X1F, sf, i, SF_UT, o, tsl, o_sf)
                elif kind == "gq":
                    o = st_bf.next()
                    S.op("act", lambda e, o=o, pp=pp: e.activation(out=o.t[:, :], in_=pp.t[:, :], func=AF.Copy, scale=0.125),
                         reads=[pp.b], writes=[o.b])
                    for half in range(2):
                        fm_out(X1B, sbf, 2 * i + half, SB_GQT, o, tsl, o_sbf, pf0=64 * half, nf=64)
                elif kind == "r":
                    o = st_f.next()
                    S.op("act", lambda e, o=o, pp=pp: e.activation(out=o.t[:, :], in_=pp.t[:, :], func=AF.Silu),
                         reads=[pp.b], writes=[o.b])
                    dst = io["silur"][128 * i:128 * i + 128, tsl]
                    S.dma("sp", lambda e, o=o, dst=dst: e.dma_start(out=dst, in_=o.t[:, :]), reads=[o.b], outs=[o_sr])
                elif kind == "lr":
                    S.op("act", lambda e, pp=pp, sl=sl: e.activation(out=lrT.t[:, sl], in_=pp.t[0:16, :], func=AF.Copy),
                         reads=[pp.b], writes=[lrT.b])
        for tb in range(PASS // 128):
            bsl = slice(tb * 128, (tb + 1) * 128)
            gsl = slice(t0 + tb * 128, t0 + (tb + 1) * 128)
            pp = K.psum()
            mm_group(K, pp, [(h.t[:, k, bsl], wtm.t[:, k, 0:512], pp.t[:, :]) for k in range(8)], reads=[wtm.b, h.b])
            o = st_bf.next()
            S.op("act", lambda e, o=o, pp=pp: e.activation(out=o.t[:, :], in_=pp.t[:, :], func=AF.Copy), reads=[pp.b], writes=[o.b])
            S.dma("sp", lambda e, o=o, gsl=gsl: e.dma_start(out=X1B.tm_all(sbf, SB_V, 128, gsl.start, 128), in_=o.t[:, :].rearrange("t (p f) -> t p f", f=128)),
                  reads=[o.b], outs=[o_sbf])
            pp = K.psum()
            mm_group(K, pp, [(h.t[:, k, bsl], wtm.t[:, k, 768:1280], pp.t[:, :]) for k in range(8)], reads=[wtm.b, h.b])
            o = st_bf.next()
            S.op("dve", lambda e, o=o, pp=pp: e.tensor_copy(out=o.t[:, :], in_=pp.t[:, :]), reads=[pp.b], writes=[o.b])
            S.dma("sp", lambda e, o=o, gsl=gsl: e.dma_start(out=X1B.tm_all(sbf, SB_GV, 128, gsl.start, 128), in_=o.t[:, :].rearrange("t (p f) -> t p f", f=128)),
                  reads=[o.b], outs=[o_sbf])
            pp = K.psum()
            mm_group(K, pp, [(h.t[:, k, bsl], wtm.t[:, k, 512:768], pp.t[:, 0:256]) for k in range(8)], reads=[wtm.b, h.b])
            o = st_la.next()
            S.op("act", lambda e, o=o, pp=pp: e.activation(out=o.t[:, :], in_=pp.t[:, 0:256], func=AF.Copy), reads=[pp.b], writes=[o.b])
            S.dma("sp", lambda e, o=o, gsl=gsl: e.dma_start(out=X1F.tm_all(sf, SF_GK, 64, gsl.start, 128), in_=o.t[:, :].rearrange("t (p f) -> t p f", f=64)),
                  reads=[o.b], outs=[o_sf])
            pp = K.psum()
            mm_group(K, pp, [(lrT.t[:, bsl], wal.t[:, :], pp.t[:, 0:256]), (ones_f.t[:, :], bal.t[:, :], pp.t[:, 0:256])],
                     reads=[lrT.b, wal.b, ones_f.b, bal.b])
            o = st_la.next()
            S.op("act", lambda e, o=o, pp=pp: e.activation(out=o.t[:, :], in_=pp.t[:, 0:256], func=AF.Exp, scale=-1.0), reads=[pp.b], writes=[o.b])
            S.op("act", lambda e, o=o: e.activation(out=o.t[:, :], in_=o.t[:, :], func=AF.Ln, bias=1.0), reads=[o.b], writes=[o.b])
            S.op("dve", lambda e, o=o: e.tensor_scalar(out=o.t[:, :], in0=o.t[:, :], scalar1=-1.0 / 16.0, scalar2=None, op0=ALU.mult),
                 reads=[o.b], writes=[o.b])
            S.dma("sp", lambda e, o=o, gsl=gsl: e.dma_start(out=X1F.tm_all(sf, SF_LA, 64, gsl.start, 128), in_=o.t[:, :].rearrange("t (p f) -> t p f", f=64)),
                  reads=[o.b], outs=[o_sf])


def _c(a):
    return np.ascontiguousarray(a)


def lay_fm_w(W, ncols_chunks):
    Kc = W.shape[0] // 128
    n = W.shape[1] // 128
    return _c(W.reshape(Kc, 128, n, 128).transpose(2, 1, 0, 3).reshape(n, 128, Kc * 128))


def lay_pk(v):
    return _c(v.reshape(-1, 128).T)


def prep_A_weights(inp, l):
    w_in = inp["w_in"][l]
    cols = []
    for (kind, i, col) in FM_CHUNKS:
        blk = np.zeros((D, 128), np.float32)
        n = 16 if kind == "lr" else 128
        blk[:, :n] = w_in[:, col:col + n]
        cols.append(blk)
    wfm = lay_fm_w(np.concatenate(cols, axis=1), NFM)
    tm = np.concatenate([w_in[:, OV:OV + 512], w_in[:, OGK:OGK + 256], w_in[:, OGV:OGV + 512]], axis=1)
    wtm = _c(tm.reshape(8, 128, 1280).transpose(1, 0, 2).reshape(128, 8 * 1280))
    gains = np.concatenate([lay_pk(inp["norm_pre"][l, 0]), lay_pk(inp["norm_pre"][l, 1]), lay_pk(inp["norm_post"][l, 0])], axis=1)
    return dict(wg=lay_fm_w(inp["ffn1_w_gate"][l], NF), wu=lay_fm_w(inp["ffn1_w_up"][l], NF),
                wd=lay_fm_w(inp["ffn1_w_down"][l], 8), wfm=wfm, wtm=wtm,
                walpha=_c(inp["gla_w_alpha"][l]), balpha=_c(inp["gla_b_alpha"][l][None, :]), gains=_c(gains))


RB_LEN = 128 * TPC
RF_CY, RF_GN = 0, 128 * TPC
RF_LEN = 256 * TPC


def build_C(dbg=0):
    nc = bass.Bass("TRN2", target_bir_lowering=False)
    dt = nc.dram_tensor
    io = dict(
        x1T=dt("x1T", [D, TPC], F32, kind="ExternalInput").ap(),
        recv_bf=dt("recv_bf", [4, RB_LEN], BF16, kind="ExternalInput").ap(),
        recv_f=dt("recv_f", [4, RF_LEN], F32, kind="ExternalInput").ap(),
        silur=dt("silur", [512, TPC], F32, kind="ExternalInput").ap(),
        wgt=dt("wgt", [24, 128, 8 * 128], F32, kind="ExternalInput").ap(),
        wbr=dt("wbr", [24, 128, 4 * 128], F32, kind="ExternalInput").ap(),
        wo=dt("wo", [8, 128, 8 * 128], F32, kind="ExternalInput").ap(),
        wg=dt("wg", [NF, 128, 8 * 128], F32, kind="ExternalInput").ap(),
        wu=dt("wu", [NF, 128, 8 * 128], F32, kind="ExternalInput").ap(),
        wd=dt("wd", [8, 128, NF * 128], F32, kind="ExternalInput").ap(),
        gains=dt("gains", [128, 40], F32, kind="ExternalInput").ap(),
        x3T=dt("x3T", [D, TPC], F32, kind="ExternalOutput").ap(),
    )
    with ExitStack() as st:
        K = Ctx(nc, st)
        K.init_psum(8)
        C = common_consts(K)
        ffn_bufs(K, C)
        r2bf_, r2f_ = io["recv_bf"], io["recv_f"]
        io["recv_bf"] = lambda p, a, b: r2bf_[p:p + 1, a:b]
        io["recv_f"] = lambda p, a, b: r2f_[p:p + 1, a:b]
        io["b_x3"] = Buf("o_x3", True)
        K.outs.append(io["b_x3"])
        emit_C(K, C, io, dbg)
        K.S.wait_all_on("sp", K.outs)
        K.S.emit()
    return nc


def emit_C(K, C, io, dbg=0):
    S = K.S
    nsub = PASS // 512
    T = TPC
    g = K.sb([128, 40], F32, "gainsC")
    S.dma("sp", lambda e: e.dma_start(out=g.t[:, :], in_=io["gains"]), writes=[g.b])
    x = K.sb([128, 8, PASS], F32, "x")
    h = K.sb([128, 8, PASS], BF16, "h")
    y = K.sb([128, 8, PASS], F32, "y")
    a = C["a"]
    sr_pool = K.pool(1, [128, 4, 512], F32, "sr")
    wgt_pool = K.pool(2, [128, 8, 128], BF16, "wgt")
    wbr_pool = K.pool(2, [128, 4, 128], BF16, "wbr")
    acc = [K.sb([128, 512], F32, "acc") for _ in range(nsub)]
    ln_mean = K.sb([128, 512], F32, "ln_mean")
    ln_msq = K.sb([128, 512], F32, "ln_msq")
    xv = io["x1T"].rearrange("(k p) t -> p k t", p=128)
    ov = io["x3T"].rearrange("(k p) t -> p k t", p=128)
    r2bf, r2f = io["recv_bf"], io["recv_f"]
    sr_v = io["silur"].rearrange("(c f) t -> f c t", f=128)
    o_x3 = io["b_x3"]
    rd2 = [io[k] for k in ("b_r2bf", "b_r2f") if io.get(k) is not None]
    rdx = [io[k] for k in ("b_x1",) if io.get(k) is not None]
    rds = [io[k] for k in ("b_sr",) if io.get(k) is not None]

    def ab(f0, f1, s):
        return [a.b[f * nsub + s] for f in range(f0, f1)]

    for ps_ in range(TPC // PASS):
        t0 = ps_ * PASS
        S.dma("sp", lambda e, t0=t0: e.dma_start(out=x.t[:, :, :], in_=xv[:, :, t0:t0 + PASS]), writes=[x.b], reads=rdx)
        for s in range(nsub):
            pre_norm(K, C, x, x.b, g, 0, h, h.b, s)
        for s in range(nsub):
            sl = slice(s * 512, (s + 1) * 512)
            tsl = slice(t0 + s * 512, t0 + (s + 1) * 512)
            for p in range(4):
                for (fa, fb, sap) in X2F.fm_pieces(r2f, p, RF_CY, 0, 128):
                    S.dma("sp", lambda e, sl=sl, tsl=tsl, p=p, fa=fa, fb=fb, sap=sap: e.dma_start(out=y.t[fa:fb, p, sl], in_=sap[:, tsl]),
                          writes=[y.b], reads=rd2)
                for (fa, fb, sap) in X2F.fm_pieces(r2f, p, RF_GN, 0, 128):
                    S.dma("sp", lambda e, sl=sl, tsl=tsl, p=p, fa=fa, fb=fb, sap=sap: e.dma_start(out=y.t[fa:fb, 4 + p, sl], in_=sap[:, tsl]),
                          writes=[y.b], reads=rd2)
                for (fa, fb, sap) in X2B.fm_pieces(r2bf, p, 0, 0, 128):
                    S.dma("sp", lambda e, sl=sl, tsl=tsl, p=p, fa=fa, fb=fb, sap=sap: e.dma_start(out=a.t[fa:fb, 8 + p, sl], in_=sap[:, tsl]),
                          writes=ab(8 + p, 9 + p, s), reads=rd2)
            sr = sr_pool.next()
            S.dma("sp", lambda e, sr=sr, tsl=tsl: e.dma_start(out=sr.t[:, :, :], in_=sr_v[:, :, tsl]), writes=[sr.b], reads=rds)
            sq = C["sq_pool"].next()
            S.op("act", lambda e, sq=sq, sl=sl: e.activation(out=sq.t[:, 0:4, :], in_=y.t[:, 0:4, sl], func=AF.Copy),
                 reads=[y.b], writes=[sq.b])
            S.op("act", lambda e, sq=sq, sl=sl: e.activation(out=sq.t[:, 4:8, :], in_=y.t[:, 0:4, sl], func=AF.Square),
                 reads=[y.b], writes=[sq.b])
            p1 = K.psum()
            mm_group(K, p1, [(C["ones_bf"].t[:, :], sq.t[:, c, :], p1.t[:, :]) for c in range(4)], reads=[C["ones_bf"].b, sq.b])
            p2 = K.psum()
            mm_group(K, p2, [(C["ones_bf"].t[:, :], sq.t[:, 4 + c, :], p2.t[:, :]) for c in range(4)], reads=[C["ones_bf"].b, sq.b])
            mean = ln_mean
            S.op("act", lambda e, mean=mean, p1=p1: e.activation(out=mean.t[:, :], in_=p1.t[:, :], func=AF.Copy, scale=1.0 / 512),
                 reads=[p1.b], writes=[mean.b])
            msq = ln_msq
            S.op("dve", lambda e, mean=mean, msq=msq: e.tensor_tensor(out=msq.t[:, :], in0=mean.t[:, :], in1=mean.t[:, :], op=ALU.mult),
                 reads=[mean.b], writes=[msq.b])
            S.op("dve", lambda e, msq=msq, p2=p2: e.scalar_tensor_tensor(out=msq.t[:, :], in0=p2.t[:, :], scalar=1.0 / 512, in1=msq.t[:, :],
                                                                       op0=ALU.mult, op1=ALU.subtract),
                 reads=[p2.b, msq.b], writes=[msq.b])
            rs = C["rstd_pool"].next()
            S.op("act", lambda e, rs=rs, msq=msq: e.activation(out=rs.t[:, :], in_=msq.t[:, :], func=AF.Sqrt, bias=C["eps_col"].t[:, 0:1], scale=1.0),
                 reads=[msq.b, C["eps_col"].b], writes=[rs.b])
            S.op("dve", lambda e, rs=rs: e.reciprocal(out=rs.t[:, :], in_=rs.t[:, :]), reads=[rs.b], writes=[rs.b])
            for c in range(4):
                t = C["tmp_pool"].next()
                S.op("dve", lambda e, t=t, c=c, sl=sl, mean=mean: e.tensor_tensor(out=t.t[:, :], in0=y.t[:, c, sl], in1=mean.t[:, :], op=ALU.subtract),
                     reads=[y.b, mean.b], writes=[t.b])
                S.op("dve", lambda e, t=t, rs=rs: e.tensor_tensor(out=t.t[:, :], in0=t.t[:, :], in1=rs.t[:, :], op=ALU.mult),
                     reads=[t.b, rs.b], writes=[t.b])
                S.op("act", lambda e, t=t, c=c, sl=sl: e.activation(out=a.t[:, c, sl], in_=t.t[:, :], func=AF.Silu,
                                                                      scale=g.t[:, 32 + c:33 + c], bias=g.t[:, 36 + c:37 + c]),
                     reads=[t.b, g.b], writes=ab(c, c + 1, s))
                S.op(POOL_ENG, lambda e, c=c, sl=sl, sr=sr: e.tensor_tensor(out=a.t[:, 4 + c, sl], in0=y.t[:, 4 + c, sl], in1=sr.t[:, c, :], op=ALU.mult),
                     reads=[y.b, sr.b], writes=ab(4 + c, 5 + c, s))
        base = {0: 8, 1: 0, 2: 4}
        for d in range(8):
            for gi in range(3):
                wb = wbr_pool.next()
                load_w(K, wb, io["wbr"][gi * 8 + d].rearrange("p (k j) -> p k j", j=128))
                wgt = wgt_pool.next()
                load_w(K, wgt, io["wgt"][gi * 8 + d].rearrange("p (k j) -> p k j", j=128))
                for s in range(nsub):
                    sl = slice(s * 512, (s + 1) * 512)
                    pb = K.psum()
                    mm_group(K, pb, [(wb.t[:, kc, :], a.t[:, base[gi] + kc, sl], pb.t[:, :]) for kc in range(4)],
                             reads=[wb.b] + ab(base[gi], base[gi] + 4, s))
                    pl = K.psum()
                    mm_group(K, pl, [(wgt.t[:, k, :], h.t[:, k, sl], pl.t[:, :]) for k in range(8)], reads=[wgt.b, h.b])
                    sig = C["tmp_pool"].next()
                    S.op("act", lambda e, sig=sig, pl=pl: e.activation(out=sig.t[:, :], in_=pl.t[:, :], func=AF.Sigmoid),
                         reads=[pl.b], writes=[sig.b])
                    if gi == 0:
                        S.op("dve", lambda e, sig=sig, pb=pb, s=s: e.tensor_tensor(out=acc[s].t[:, :], in0=sig.t[:, :], in1=pb.t[:, :], op=ALU.mult),
                             reads=[sig.b, pb.b], writes=[acc[s].b])
                    else:
                        S.op("dve", lambda e, sig=sig, pb=pb: e.tensor_tensor(out=sig.t[:, :], in0=sig.t[:, :], in1=pb.t[:, :], op=ALU.mult),
                             reads=[sig.b, pb.b], writes=[sig.b])
                        if gi == 1:
                            S.op(POOL_ENG, lambda e, sig=sig, s=s: e.tensor_tensor(out=acc[s].t[:, :], in0=acc[s].t[:, :], in1=sig.t[:, :], op=ALU.add),
                                 reads=[sig.b, acc[s].b], writes=[acc[s].b])
                        else:
                            S.op(POOL_ENG, lambda e, sig=sig, s=s, d=d, sl=sl: e.tensor_tensor(out=a.t[:, 12 + d, sl], in0=acc[s].t[:, :], in1=sig.t[:, :], op=ALU.add),
                                 reads=[sig.b, acc[s].b], writes=ab(12 + d, 13 + d, s))
        for do in range(8):
            wo = C["wgu_pool"].next()
            load_w(K, wo, io["wo"][do].rearrange("p (k j) -> p k j", j=128))
            for s in range(nsub):
                sl = slice(s * 512, (s + 1) * 512)
                pp = K.psum()
                mm_group(K, pp, [(wo.t[:, k, :], a.t[:, 12 + k, sl], pp.t[:, :]) for k in range(8)], reads=[wo.b] + ab(12, 20, s))
                S.op("act", lambda e, pp=pp, do=do, sl=sl: e.activation(out=y.t[:, do, sl], in_=pp.t[:, :], func=AF.Copy),
                     reads=[pp.b], writes=[y.b])
        if dbg == 2:
            S.dma("sp", lambda e, t0=t0: e.dma_start(out=ov[:, :, t0:t0 + PASS], in_=y.t[:, :, :]), reads=[y.b], outs=[o_x3])
            continue
        for s in range(nsub):
            post_norm_residual(K, C, y, y.b, g, 8, 1.0, x, x.b, s)
        if dbg == 1:
            S.dma("sp", lambda e, t0=t0: e.dma_start(out=ov[:, :, t0:t0 + PASS], in_=x.t[:, :, :]), reads=[x.b], outs=[o_x3])
            continue
        for s in range(nsub):
            pre_norm(K, C, x, x.b, g, 16, h, h.b, s)
        ffn(K, C, h, h.b, io["wg"], io["wu"], io["wd"], y, y.b)
        for s in range(nsub):
            post_norm_residual(K, C, y, y.b, g, 24, 0.5, x, x.b, s)
        S.dma("sp", lambda e, t0=t0: e.dma_start(out=ov[:, :, t0:t0 + PASS], in_=x.t[:, :, :]), reads=[x.b], outs=[o_x3])


def prep_C_weights(inp, l):
    w_in = inp["w_in"][l]
    wgt = lay_fm_w(w_in[:, OGT:OGT + 3072], 24)
    wb = inp["w_branch"][l]
    wbr = np.concatenate([lay_fm_w(wb[gi], 8) for gi in range(3)], axis=0)
    gains = np.concatenate([lay_pk(inp["norm_pre"][l, 1]), lay_pk(inp["norm_post"][l, 1]), lay_pk(inp["norm_pre"][l, 2]),
                            lay_pk(inp["norm_post"][l, 2]), lay_pk(inp["conv_ln_g"][l]), lay_pk(inp["conv_ln_b"][l])], axis=1)
    return dict(wgt=wgt, wbr=_c(wbr), wo=lay_fm_w(inp["w_out"][l], 8), wg=lay_fm_w(inp["ffn2_w_gate"][l], NF),
                wu=lay_fm_w(inp["ffn2_w_up"][l], NF), wd=lay_fm_w(inp["ffn2_w_down"][l], 8), gains=_c(gains))


NQT = SEQ // 512
NBLK = SEQ // 128
S2B_LEN = 128 * TPC
S2F_LEN = 256 * TPC


def build_B():
    nc = bass.Bass("TRN2", target_bir_lowering=False)
    dt = nc.dram_tensor
    io = dict(
        rbf=dt("rbf", [4, SB_LEN], BF16, kind="ExternalInput").ap(),
        rf=dt("rf", [4, SF_LEN], F32, kind="ExternalInput").ap(),
        bw=dt("bw", [128, 33], F32, kind="ExternalInput").ap(),
        s2bf=dt("s2bf", [4, S2B_LEN], BF16, kind="ExternalOutput").ap(),
        s2f=dt("s2f", [4, S2F_LEN], F32, kind="ExternalOutput").ap(),
    )
    with ExitStack() as st:
        K = Ctx(nc, st)
        K.init_psum(8)
        rbf_, rf_ = io["rbf"], io["rf"]
        io["rbf"] = lambda j, a, b: rbf_[j:j + 1, a:b]
        io["rf"] = lambda j, a, b: rf_[j:j + 1, a:b]
        io["b_s2bf"], io["b_s2f"] = Buf("o_s2bf", True), Buf("o_s2f", True)
        K.outs += [io["b_s2bf"], io["b_s2f"]]
        emit_B(K, io)
        K.S.wait_all_on("sp", K.outs)
        K.S.emit()
    return nc


def emit_B(K, io):
    S = K.S
    T = TPC
    rbf, rf = io["rbf"], io["rf"]
    o_bf, o_f = io["b_s2bf"], io["b_s2f"]
    rdb = [io[k] for k in ("b_rbf", "b_rf") if io.get(k) is not None]
    ones_bf = K.sb([128, 128], BF16, "ones_bf")
    S.op("dve", lambda e: e.memset(ones_bf.t[:, :], 1.0), writes=[ones_bf.b])
    negones = K.sb([128, 1], BF16, "negones")
    S.op("dve", lambda e: e.memset(negones.t[:, :], -1.0), writes=[negones.b])
    eps = K.sb([128, 1], F32, "eps")
    S.op("dve", lambda e: e.memset(eps.t[:, :], EPS), writes=[eps.b])
    ntri = K.sb([128, 128], BF16, "ntri")
    S.op("pool", lambda e: e.memset(ntri.t[:, :], -1.0), writes=[ntri.b])
    S.op("pool", lambda e: e.affine_select(out=ntri.t[:, :], in_=ntri.t[:, :], pattern=[[-1, 128]], compare_op=ALU.is_ge,
                                            fill=0.0, base=0, channel_multiplier=1), reads=[ntri.b], writes=[ntri.b])
    masks = []
    for i in range(4):
        m = K.sb([128, 512], BF16, "mask")
        S.op("pool", lambda e, m=m: e.memset(m.t[:, :], 1.0), writes=[m.b])
        S.op("pool", lambda e, m=m, i=i: e.affine_select(out=m.t[:, :], in_=m.t[:, :], pattern=[[1, 512]], compare_op=ALU.is_gt,
                                                           fill=0.0, base=-128 * i, channel_multiplier=-1), reads=[m.b], writes=[m.b])
        masks.append(m)
    tris = K.sb([128, 128], BF16, "tris")
    S.op("pool", lambda e: e.memset(tris.t[:, :], 1.0), writes=[tris.b])
    S.op("pool", lambda e: e.affine_select(out=tris.t[:, :], in_=tris.t[:, :], pattern=[[-1, 128]], compare_op=ALU.is_gt,
                                            fill=0.0, base=0, channel_multiplier=1), reads=[tris.b], writes=[tris.b])
    S.op("pool", lambda e: e.memset(tris.t[64:128, 0:64], 0.0), reads=[tris.b], writes=[tris.b])
    cind = K.sb([128, 2], BF16, "cind")
    S.op("pool", lambda e: e.memset(cind.t[:, :], 0.0), writes=[cind.b])
    S.op("pool", lambda e: e.memset(cind.t[0:64, 0:1], 1.0), reads=[cind.b], writes=[cind.b])
    S.op("pool", lambda e: e.memset(cind.t[64:128, 1:2], 1.0), reads=[cind.b], writes=[cind.b])
    bw = K.sb([128, 33], F32, "bw")
    S.dma("sp", lambda e: e.dma_start(out=bw.t[:, :], in_=io["bw"]), writes=[bw.b])
    sel = []
    nsel = []
    for hh in range(2):
        s_ = K.sb([128, 128], BF16, "sel")
        S.op("pool", lambda e, s_=s_: e.memset(s_.t[:, :], 0.0), writes=[s_.b])
        S.op("pool", lambda e, s_=s_, hh=hh: e.memset(s_.t[64 * hh:64 * hh + 1, :], 1.0), reads=[s_.b], writes=[s_.b])
        sel.append(s_)
        n_ = K.sb([128, 128], BF16, "nsel")
        S.op("pool", lambda e, n_=n_: e.memset(n_.t[:, :], 0.0), writes=[n_.b])
        S.op("pool", lambda e, n_=n_, hh=hh: e.memset(n_.t[:, 64 * hh:64 * hh + 64], -1.0), reads=[n_.b], writes=[n_.b])
        nsel.append(n_)
    qT = K.sb([128, SEQ], BF16, "qT")
    qT2 = K.sb([128, SEQ], BF16, "qT2")
    S.op("pool", lambda e: e.memset(qT.t[64:128, :], 0.0), writes=[qT.b])
    S.op("pool", lambda e: e.memset(qT2.t[0:64, :], 0.0), writes=[qT2.b])
    qz = [qT, qT2]
    kT = K.sb([128, SEQ], BF16, "kT")
    v = K.sb([128, NBLK, 128], BF16, "v")
    gqT = K.sb([64, SEQ], BF16, "gqT")
    gv = K.sb([128, NBLK, 128], BF16, "gv")
    gk = K.sb([128, NBLK, 64], F32, "gk")
    la = K.sb([128, NBLK, 64], F32, "la")
    u = K.sb([128, 32 + SEQ], F32, "u")
    S.op("pool", lambda e: e.memset(u.t[:, 0:32], 0.0), writes=[u.b])
    def ld(out_ap, in_ap, wbuf):
        S.dma("sp", lambda e: e.dma_start(out=out_ap, in_=in_ap), writes=[wbuf], reads=rdb)

    for j in range(4):
        ts = slice(j * T, (j + 1) * T)
        for (fa, fb, sap) in X1B.fm_pieces(rbf, j, SB_KT, 0, 128):
            ld(kT.t[fa:fb, ts], sap, kT.b)
        for (fa, fb, sap) in X1B.fm_pieces(rbf, j, SB_QT, 0, 128):
            tq = qT if fa < 64 else qT2
            assert (fa < 64) == (fb <= 64)
            S.dma("sp", lambda e, tq=tq, fa=fa, fb=fb, ts=ts, sap=sap: e.dma_start(out=tq.t[fa:fb, ts], in_=sap), writes=[], reads=rdb + [tq.b], outs=[tq.b])
        for (fa, fb, sap) in X1B.fm_pieces(rbf, j, SB_GQT, 0, 64):
            ld(gqT.t[fa:fb, ts], sap, gqT.b)
        for (fa, fb, sap) in X1F.fm_pieces(rf, j, SF_UT, 0, 128):
            ld(u.t[fa:fb, 32 + j * T:32 + (j + 1) * T], sap, u.b)
        for tok0 in (0, 1024):
            kb0 = j * 16 + tok0 // 128
            ld(v.t[:, kb0:kb0 + 8, :], X1B.tm_src(rbf, j, SB_V, 128, tok0, 1024).rearrange("(k t) f -> t k f", t=128), v.b)
            ld(gv.t[:, kb0:kb0 + 8, :], X1B.tm_src(rbf, j, SB_GV, 128, tok0, 1024).rearrange("(k t) f -> t k f", t=128), gv.b)
            ld(gk.t[:, kb0:kb0 + 8, :], X1F.tm_src(rf, j, SF_GK, 64, tok0, 1024).rearrange("(k t) f -> t k f", t=128), gk.b)
            ld(la.t[:, kb0:kb0 + 8, :], X1F.tm_src(rf, j, SF_LA, 64, tok0, 1024).rearrange("(k t) f -> t k f", t=128), la.b)
    ps_z = Rot(K.ps[0:3])
    ps_o = K.ps[3:5]
    ps_cb = K.ps[5:7]
    ps_g = K.ps[7]
    po_bufs = {(bk, hh): Buf("po%d_%d" % (bk, hh)) for bk in range(2) for hh in range(2)}
    negfull = K.sb([128, 128], BF16, "negfull")
    S.op("pool", lambda e: e.memset(negfull.t[:, :], -1.0), writes=[negfull.b])
    e_pool = K.pool(3, [128, 512], F32, "e")
    sp_pool = K.pool(5, [128, 512], BF16, "sp")
    w_pool = K.pool(3, [128, 512], BF16, "w")
    cb_pools = [K.pool(2, [128, 512], BF16, "carry") for _ in range(2)]
    for pl_ in cb_pools:
        for cbt in pl_.items:
            S.op("pool", lambda e, cbt=cbt: e.memset(cbt.t[:, :], 0.0), writes=[cbt.b])
    osb_pool = K.pool(2, [128, 512], BF16, "osb")

    def conv_gen():
        acc_pool = K.pool(2, [128, 512], F32, "cacc")
        for pc in range(SEQ // 512):
            acc = acc_pool.next()
            c0 = 2 + pc * 512
            S.op("dve", lambda e, acc=acc, c0=c0: e.tensor_scalar(out=acc.t[:, :], in0=u.t[:, c0:c0 + 512], scalar1=bw.t[:, 0:1], scalar2=bw.t[:, 31:32],
                                                                    op0=ALU.mult, op1=ALU.add), reads=[u.b, bw.b], writes=[acc.b])
            yield
            for k in range(1, 31):
                S.op("dve", lambda e, acc=acc, c0=c0, k=k: e.scalar_tensor_tensor(out=acc.t[:, :], in0=u.t[:, c0 + k:c0 + k + 512], scalar=bw.t[:, k:k + 1],
                                                                                  in1=acc.t[:, :], op0=ALU.mult, op1=ALU.add),
                     reads=[u.b, bw.b, acc.b], writes=[acc.b])
                yield
            cols = slice((pc % 4) * 512, (pc % 4 + 1) * 512)
            for (fa, fb, dst) in X2F.fm_pieces(io["s2f"], pc // 4, RF_CY, 0, 128):
                S.dma("sp", lambda e, acc=acc, dst=dst, fa=fa, fb=fb, cols=cols: e.dma_start(out=dst[:, cols], in_=acc.t[fa:fb, :]),
                      reads=[acc.b], outs=[o_f])
            yield

    def gla_gen():
        state = K.sb([64, 128], F32, "gstate")
        S.op("pool", lambda e: e.memset(state.t[:, :], 0.0), writes=[state.b])
        sbf_pool = K.pool(2, [64, 128], BF16, "gstbf")
        labf_pool = K.pool(2, [128, 64], BF16, "labf")
        ed_pool = K.pool(2, [128, 64], F32, "ged")
        kd_pool = K.pool(2, [128, 64], BF16, "gkd")
        lam_pool = K.pool(2, [64, 2], F32, "glam")
        osb_p = K.pool(2, [128, 128], F32, "gosb")
        sq_p = K.pool(2, [128, 128], BF16, "gsq")
        rs_p = K.pool(2, [128, 128], F32, "grs")
        gst_pool = K.pool(2, [128, 512], F32, "gnst")
        pg = ps_g
        r_dte = pg.t[:, 0:64]
        r_lam = pg.t[0:64, 64:66]
        r_st = pg.t[0:64, 128:256]
        r_o = pg.t[:, 256:384]
        r_ss = pg.t[:, 384:512]
        bg = pg.b
        gst = None
        for blk in range(NBLK):
            labf = labf_pool.next()
            S.op("pool", lambda e, labf=labf, blk=blk: e.tensor_copy(out=labf.t[:, :], in_=la.t[:, blk, :]), reads=[la.b], writes=[labf.b])
            yield
            S.op("pe", [lambda e, labf=labf: e.matmul(r_dte, tris.t[:, :], labf.t[:, :], start=True, stop=True)], reads=[tris.b, labf.b], writes=[bg])
            S.op("pe", [lambda e, labf=labf: e.matmul(r_lam, labf.t[:, :], cind.t[:, :], start=True, stop=True)], reads=[cind.b, labf.b], writes=[bg])
            yield
            ed = ed_pool.next()
            S.op("act", lambda e, ed=ed: e.activation(out=ed.t[:, :], in_=r_dte, func=AF.Exp), writes=[ed.b, bg])
            lam = lam_pool.next()
            S.op("act", lambda e, lam=lam: e.activation(out=lam.t[:, :], in_=r_lam, func=AF.Exp), writes=[lam.b, bg])
            yield
            kd = kd_pool.next()
            S.op("dve", lambda e, kd=kd, ed=ed, blk=blk: e.tensor_tensor(out=kd.t[:, :], in0=gk.t[:, blk, :], in1=ed.t[:, :], op=ALU.mult),
                 reads=[gk.b, ed.b], writes=[kd.b])
            yield
            for c in range(2):
                cs = slice(64 * c, 64 * c + 64)
                S.op("pe", [lambda e, kd=kd, cs=cs, blk=blk: e.matmul(r_st, kd.t[cs, :], gv.t[cs, blk, :], start=True, stop=True)],
                     reads=[kd.b, gv.b], writes=[bg])
                yield
                S.op("dve", lambda e, lam=lam, c=c: e.scalar_tensor_tensor(out=state.t[:, :], in0=state.t[:, :], scalar=lam.t[:, c:c + 1], in1=r_st,
                                                                          op0=ALU.mult, op1=ALU.add), reads=[state.b, lam.b], writes=[state.b, bg])
                sbf = sbf_pool.next()
                S.op("dve", lambda e, sbf=sbf: e.tensor_copy(out=sbf.t[:, :], in_=state.t[:, :]), reads=[state.b], writes=[sbf.b])
                yield
                tk = slice(blk * 128 + 64 * c, blk * 128 + 64 * c + 64)
                S.op("pe", [lambda e, sbf=sbf, tk=tk, c=c: e.matmul(pg.t[:, 256 + 64 * c:256 + 64 * c + 64], sbf.t[:, :], gqT.t[:, tk], start=True, stop=True)],
                     reads=[sbf.b, gqT.b], writes=[bg])
                yield
            osb = osb_p.next()
            S.op("dve", lambda e, osb=osb: e.tensor_copy(out=osb.t[:, :], in_=r_o), writes=[osb.b, bg])
            sq = sq_p.next()
            S.op("pool", lambda e, osb=osb, sq=sq: e.tensor_tensor(out=sq.t[:, :], in0=osb.t[:, :], in1=osb.t[:, :], op=ALU.mult), reads=[osb.b], writes=[sq.b])
            yield
            S.op("pe", [lambda e, sq=sq: e.matmul(r_ss, ones_bf.t[:, :], sq.t[:, :], start=True, stop=True)], reads=[ones_bf.b, sq.b], writes=[bg])
            yield
            rs = rs_p.next()
            S.op("act", lambda e, rs=rs: e.activation(out=rs.t[:, :], in_=r_ss, func=AF.Ln, bias=eps.t[:, 0:1], scale=1.0 / 128), reads=[eps.b], writes=[rs.b, bg])
            S.op("act", lambda e, rs=rs: e.activation(out=rs.t[:, :], in_=rs.t[:, :], func=AF.Exp, scale=-0.5), reads=[rs.b], writes=[rs.b])
            yield
            if blk % 4 == 0:
                gst = gst_pool.next()
            cs4 = slice((blk % 4) * 128, (blk % 4 + 1) * 128)
            S.op("dve", lambda e, gst=gst, cs4=cs4, osb=osb, rs=rs: e.scalar_tensor_tensor(out=gst.t[:, cs4], in0=osb.t[:, :], scalar=bw.t[:, 32:33], in1=rs.t[:, :],
                                                                                        op0=ALU.mult, op1=ALU.mult), reads=[osb.b, rs.b, bw.b], writes=[gst.b])
            if blk % 4 == 3:
                pc = blk // 4
                cols = slice((pc % 4) * 512, (pc % 4 + 1) * 512)
                for (fa, fb, dst) in X2F.fm_pieces(io["s2f"], pc // 4, RF_GN, 0, 128):
                    S.dma("sp", lambda e, gst=gst, dst=dst, fa=fa, fb=fb, cols=cols: e.dma_start(out=dst[:, cols], in_=gst.t[fa:fb, :]),
                          reads=[gst.b], outs=[o_f])
            yield

    import os as _os
    _mode = _os.environ.get("SIDE_MODE", "1,1")
    side = [conv_gen(), gla_gen()]
    periods = [int(t) for t in _mode.split(",")]
    tick = [0]

    def step_side(force=False):
        tick[0] += 1
        for gi, gnr in enumerate(list(side)):
            per = periods[min(gi, len(periods) - 1)]
            if per == 0 and not force:
                continue
            if force or tick[0] % max(per, 1) == 0:
                try:
                    next(gnr)
                except StopIteration:
                    side.remove(gnr)

    seq = []
    for qt in range(NQT):
        kbs = list(range(4 * qt + 3, -1, -1))
        for idx, kb in enumerate(kbs):
            seq.append((qt, kb, idx == 0, idx == len(kbs) - 1))
    tiles = []
    for t in seq:
        for hh in range(2):
            tiles.append((hh,) + t)
    stA = {}
    stB = {}
    cur_cb = [None, None]

    def stage_A(i, pre=None):
        hh, qt, kb, first, last = tiles[i]
        hs = slice(64 * hh, 64 * hh + 64)
        qs = slice(qt * 512, (qt + 1) * 512)
        ks = slice(kb * 128, (kb + 1) * 128)
        pz = ps_z.next()
        zfn = lambda e: e.matmul(pz.t[:, :], kT.t[:, ks], qz[hh].t[:, qs], start=True, stop=False)
        if pre is None:
            S.op("pe", [zfn], reads=[kT.b, qz[hh].b], writes=[pz.b])
        else:
            pfns, preads, pwrites, mid = pre
            S.op("pe", pfns + [zfn], reads=preads + [kT.b, qz[hh].b], writes=pwrites + [pz.b])
            mid()
        ee = e_pool.next()
        S.op("act", lambda e: e.activation(out=ee.t[:, :], in_=pz.t[:, :], func=AF.Exp), reads=[pz.b], writes=[ee.b])
        sp = sp_pool.next()
        S.op("act", lambda e: e.activation(out=sp.t[:, :], in_=ee.t[:, :], func=AF.Ln, bias=1.0), reads=[ee.b], writes=[sp.b])
        di = kb - 4 * qt
        if di >= 0:
            S.op("pool", lambda e: e.tensor_tensor(out=sp.t[:, :], in0=sp.t[:, :], in1=masks[di].t[:, :], op=ALU.mult), reads=[sp.b, masks[di].b], writes=[sp.b])
        stA[i] = (pz, sp)

    def stage_B1(i, nxt=None):
        hh, qt, kb, first, last = tiles[i]
        pr = slice(64 * hh, 64 * hh + 1)
        pz, sp = stA.pop(i)
        if first:
            fns = [lambda e: e.matmul(pz.t[:, :], ntri.t[:, :], sp.t[:, :], start=False, stop=True)]
            rds = [ntri.b, sp.b, pz.b]
        else:
            cbo = cur_cb[hh]
            fns = [lambda e: e.matmul(pz.t[:, :], ntri.t[:, :], sp.t[:, :], start=False, stop=False),
                   lambda e: e.matmul(pz.t[:, :], sel[hh].t[:, :], cbo.t[:, :], start=False, stop=True)]
            rds = [ntri.b, sp.b, pz.b, cbo.b, sel[hh].b]
        w = w_pool.next()
        di = kb - 4 * qt

        def mid():
            S.op("act", lambda e: e.activation(out=w.t[:, :], in_=pz.t[:, :], func=AF.Exp), reads=[pz.b], writes=[w.b])
            if di >= 0:
                S.op("pool", lambda e: e.tensor_tensor(out=w.t[:, :], in0=w.t[:, :], in1=masks[di].t[:, :], op=ALU.mult), reads=[w.b, masks[di].b], writes=[w.b])
        if nxt is None:
            S.op("pe", fns, reads=rds, writes=[pz.b])
            mid()
        else:
            stage_A(nxt, pre=(fns, rds, [pz.b], mid))
        stB[i] = (sp, w)

    def stage_B2(i):
        hh, qt, kb, first, last = tiles[i]
        hs = slice(64 * hh, 64 * hh + 64)
        pr = slice(64 * hh, 64 * hh + 1)
        sp, w = stB.pop(i)
        po = ps_o[hh]
        pob = po.b
        pcT = ps_cb[hh]
        pvfn = lambda e: e.matmul(po.t[:, :], v.t[:, kb, :], w.t[:, :], start=first, stop=last)
        if not last:
            S.op("pe", [pvfn,
                        lambda e: e.matmul(pcT.t[:, :], negfull.t[:, :], sp.t[:, :], start=first, stop=False, skip_group_check=True)],
                 reads=[v.b, w.b, negfull.b, sp.b], writes=[pob, pcT.b])
            cbn = cb_pools[hh].next()
            S.op("dve", lambda e: e.tensor_copy(out=cbn.t[pr, :], in_=pcT.t[pr, :]), reads=[pcT.b], writes=[cbn.b])
            cur_cb[hh] = cbn
        else:
            S.op("pe", [pvfn], reads=[v.b, w.b], writes=[pob])
            osb = osb_pool.next()
            S.op("dve", lambda e: e.tensor_copy(out=osb.t[hs, :], in_=po.t[hs, :]), reads=[pob], writes=[osb.b])
            cols = slice((qt % 4) * 512, (qt % 4 + 1) * 512)
            for (fa, fb, dst) in X2B.fm_pieces(io["s2bf"], qt // 4, 0, 64 * hh, 64):
                S.dma("sp", lambda e, dst=dst, fa=fa, fb=fb: e.dma_start(out=dst[:, cols], in_=osb.t[64 * hh + fa:64 * hh + fb, :]),
                      reads=[osb.b], outs=[o_bf])

    n = len(tiles)
    DEPTH_A = 2
    for i in range(min(DEPTH_A, n)):
        stage_A(i)
    for i in range(n):
        stage_B1(i, nxt=(i + DEPTH_A) if i + DEPTH_A < n else None)
        stage_B2(i)
        step_side()
    while side:
        step_side(force=True)


def prep_B_weights(inp, l, p):
    cw = inp["conv_w"][l][:, 128 * p:128 * p + 128].T
    cb = inp["conv_b"][l][128 * p:128 * p + 128][:, None]
    gg = inp["gla_norm_g"][l][128 * p:128 * p + 128][:, None]
    return _c(np.concatenate([cw, cb, gg], axis=1).astype(np.float32))


I32 = mybir.dt.int32
GROUPS = [[0, 1, 2, 3], [4, 5, 6, 7]]


def build_fused(L=DEPTH, phases="AXBYC"):
    nc = bass.Bass("TRN2", target_bir_lowering=False)
    dt = nc.dram_tensor

    def ein(name, shape, d=F32):
        return dt(name, list(shape), d, kind="ExternalInput").ap()

    xT = ein("xT", [D, TPC])
    cid = ein("cid", [1, 1], I32)
    bw = ein("bw", [L, 128, 33])
    wg1, wu1, wd1 = ein("wg1", [L, NF, 128, 1024]), ein("wu1", [L, NF, 128, 1024]), ein("wd1", [L, 8, 128, NF * 128])
    wfm, wtm = ein("wfm", [L, NFM, 128, 1024]), ein("wtm", [L, 128, 8 * 1280])
    walpha, balpha, gainsA = ein("walpha", [L, 16, 256]), ein("balpha", [L, 1, 256]), ein("gainsA", [L, 128, 24])
    wgt, wbr, wo = ein("wgt", [L, 24, 128, 1024]), ein("wbr", [L, 24, 128, 512]), ein("wo", [L, 8, 128, 1024])
    wg2, wu2, wd2 = ein("wg2", [L, NF, 128, 1024]), ein("wu2", [L, NF, 128, 1024]), ein("wd2", [L, 8, 128, NF * 128])
    gainsC = ein("gainsC", [L, 128, 40])
    outT = dt("outT", [D, TPC], F32, kind="ExternalOutput").ap()
    x1T = dt("x1T_i", [D, TPC], F32).ap()
    xbuf = dt("xbuf_i", [D, TPC], F32).ap()
    silur = dt("silur_i", [512, TPC], F32).ap()
    send_bf = dt("send_bf_i", X1B.shape(), BF16).ap()
    send_f = dt("send_f_i", X1F.shape(), F32).ap()
    ag_bf = dt("ag_bf_i", X1B.shape(16), BF16).ap()
    ag_f = dt("ag_f_i", X1F.shape(16), F32).ap()
    s2bf = dt("s2bf_i", X2B.shape(), BF16).ap()
    s2f = dt("s2f_i", X2F.shape(), F32).ap()
    ag2_bf = dt("ag2_bf_i", X2B.shape(16), BF16).ap()
    ag2_f = dt("ag2_f_i", X2F.shape(16), F32).ap()
    st_bf = dt("st_bf_i", X1B.shape(), BF16).ap()
    st_f = dt("st_f_i", X1F.shape(), F32).ap()
    st2_bf = dt("st2_bf_i", X2B.shape(), BF16).ap()
    st2_f = dt("st2_f_i", X2F.shape(), F32).ap()
    P = {n: Buf(n, True) for n in ("x1", "sr", "sbf", "sf", "agbf", "agf", "s2bf", "s2f", "ag2bf", "ag2f", "xbuf", "out",
                                   "stbf", "stf", "st2bf", "st2f")}
    with ExitStack() as st:
        K = Ctx(nc, st)
        S = K.S
        S.cid_ap = cid[0:1, 0:1]
        K.init_psum(8)

        def stat(ap):
            return ap

        def pick(ap16, stage, bsrc, bdst):
            v4 = ap16.rearrange("c (r d) n -> d c r n", d=4)
            S.dma("sp", lambda e: e.dma_start(out=stage, in_=v4[bass.ds(S.dyn["p"], 1)].rearrange("o c r n -> (o c) r n")),
                  reads=[bsrc], writes=[bdst])

        def gather(src, dst, bsrc, bdst):
            for c in range(src.shape[0]):
                S.dma("pool", lambda e, c=c: e.collective_compute("AllGather", ALU.bypass, replica_groups=GROUPS,
                                                                  ins=[src[c].opt()], outs=[dst[c].opt()]),
                      reads=[bsrc], outs=[bdst], sembuf=bdst, inc=1)

        for l in range(L):
            last = (l == L - 1)
            K.arena_reset()
            C = common_consts(K)
            ffn_bufs(K, C)
            if "A" in phases:
              emit_A(K, C, dict(xT=xT if l == 0 else xbuf, b_xin=None if l == 0 else P["xbuf"],
                              wg=wg1[l], wu=wu1[l], wd=wd1[l], wfm=wfm[l], wtm=wtm[l], walpha=walpha[l], balpha=balpha[l],
                              gains=gainsA[l], x1T=x1T, send_bf=send_bf, send_f=send_f, silur=silur,
                              b_x1=P["x1"], b_sbf=P["sbf"], b_sf=P["sf"], b_sr=P["sr"]))
            S.barrier()
            if "X" in phases:
                gather(send_bf, ag_bf, P["sbf"], P["agbf"])
                gather(send_f, ag_f, P["sf"], P["agf"])
                pick(ag_bf, st_bf, P["agbf"], P["stbf"])
                pick(ag_f, st_f, P["agf"], P["stf"])
            K.arena_reset()
            if "B" in phases:
              emit_B(K, dict(rbf=stat(st_bf), rf=stat(st_f), bw=bw[l], s2bf=s2bf, s2f=s2f,
                           b_rbf=P["stbf"], b_rf=P["stf"], b_s2bf=P["s2bf"], b_s2f=P["s2f"]))
            S.barrier()
            if "Y" in phases:
                gather(s2bf, ag2_bf, P["s2bf"], P["ag2bf"])
                gather(s2f, ag2_f, P["s2f"], P["ag2f"])
                pick(ag2_bf, st2_bf, P["ag2bf"], P["st2bf"])
                pick(ag2_f, st2_f, P["ag2f"], P["st2f"])
            K.arena_reset()
            C = common_consts(K)
            ffn_bufs(K, C)
            if "C" in phases:
              emit_C(K, C, dict(x1T=x1T, recv_bf=stat(st2_bf), recv_f=stat(st2_f), silur=silur,
                              wgt=wgt[l], wbr=wbr[l], wo=wo[l], wg=wg2[l], wu=wu2[l], wd=wd2[l], gains=gainsC[l],
                              x3T=outT if last else xbuf, b_x3=P["out"] if last else P["xbuf"],
                              b_x1=P["x1"], b_sr=P["sr"], b_r2bf=P["st2bf"], b_r2f=P["st2f"]))
            S.barrier()
        S.wait_all_on("sp", [P["out"]])
        S.emit()
    return nc, K


_FUSED = {}


def kernel(**inputs):
    inp = {k: np.asarray(v) for k, v in inputs.items()}
    x = inp["x"].astype(np.float32, copy=False)
    cores = list(range(NCORES))
    T = TPC
    L = DEPTH
    if "nc" not in _FUSED:
        _FUSED["nc"] = build_fused(L)[0]
    A = [prep_A_weights(inp, l) for l in range(L)]
    Cw = [prep_C_weights(inp, l) for l in range(L)]
    shared = dict(
        wg1=np.stack([a["wg"] for a in A]), wu1=np.stack([a["wu"] for a in A]), wd1=np.stack([a["wd"] for a in A]),
        wfm=np.stack([a["wfm"] for a in A]), wtm=np.stack([a["wtm"] for a in A]),
        walpha=np.stack([a["walpha"] for a in A]), balpha=np.stack([a["balpha"] for a in A]),
        gainsA=np.stack([a["gains"] for a in A]),
        wgt=np.stack([c["wgt"] for c in Cw]), wbr=np.stack([c["wbr"] for c in Cw]), wo=np.stack([c["wo"] for c in Cw]),
        wg2=np.stack([c["wg"] for c in Cw]), wu2=np.stack([c["wu"] for c in Cw]), wd2=np.stack([c["wd"] for c in Cw]),
        gainsC=np.stack([c["gains"] for c in Cw]))
    del A, Cw
    bws = [np.stack([prep_B_weights(inp, l, p) for l in range(L)]) for p in range(4)]
    in_maps = []
    for c in cores:
        m = dict(shared)
        m["xT"] = _c(x[c // 4, (c % 4) * T:(c % 4 + 1) * T, :].T)
        m["cid"] = np.array([[c % 4]], np.int32)
        m["bw"] = bws[c % 4]
        in_maps.append(m)
    res = run_bass_kernel_spmd(_FUSED["nc"], in_maps, core_ids=cores).results
    out = np.empty((BATCH, SEQ, D), np.float32)
    for c in cores:
        out[c // 4, (c % 4) * T:(c % 4 + 1) * T, :] = np.asarray(res[c]["outT"]).T
    return out
```

```python
import numpy as np
from contextlib import ExitStack
import ml_dtypes
import concourse.bass as bass
import concourse.mybir as mybir
from concourse.bass_utils import run_bass_kernel_spmd

F32 = mybir.dt.float32
BF16 = mybir.dt.bfloat16
AF = mybir.ActivationFunctionType
ALU = mybir.AluOpType

D = 1024
DFF = 2816
NF = DFF // 128
SEQ = 8192
BATCH = 2
DEPTH = 4
TPC = 2048
PASS = 1024
EPS = 1e-6
INW = 7184
NCORES = 8
POOL_ENG = "dve"

ENGS = ("pe", "act", "dve", "pool", "sp")
SEM_LIMIT = 28000


class Buf:
    __slots__ = ("name", "w", "r", "dsem", "persist")

    def __init__(self, name, persist=False):
        self.name = name
        self.w = None
        self.r = {}
        self.dsem = None
        self.persist = persist


class Sched:
    def __init__(self, nc, stack):
        self.nc = nc
        self.stack = stack
        self.sem = {}
        self.cnt = {}
        self.gen = {e: 0 for e in ENGS}
        self.seen = {e: {} for e in ENGS}
        self.q = {e: [] for e in ENGS}
        self.nsem = 0
        for e in ENGS:
            self._mksem((e, 0))
        self.n_inst = 0
        self.dma_free = []
        self.dma_bufs = []
        self.dyn = {}
        self.cid_ap = None

    def _mksem(self, key):
        self.nsem += 1
        h = self.stack.enter_context(self.nc.semaphore("s%d" % self.nsem))
        self.sem[key] = h
        self.cnt[key] = 0
        return key

    def ekey(self, e):
        return (e, self.gen[e])

    def _waits(self, e, deps):
        out = []
        seen = self.seen[e]
        for (k, v) in deps:
            if k[0] == e:
                continue
            if seen.get(k, 0) < v:
                seen[k] = v
                out.append((k, v))
        return out

    @staticmethod
    def _deps(reads, writes):
        deps = []
        for b in reads:
            if b.w is not None:
                deps.append(b.w)
        for b in writes:
            if b.w is not None:
                deps.append(b.w)
            deps.extend(b.r.items())
        return deps

    def op(self, e, fns, reads=(), writes=()):
        if not isinstance(fns, (list, tuple)):
            fns = [fns]
        waits = self._waits(e, self._deps(reads, writes))
        k = self.ekey(e)
        if self.cnt[k] >= SEM_LIMIT:
            self.gen[e] += 1
            k = self._mksem(self.ekey(e))
        self.cnt[k] += 1
        ev = (k, self.cnt[k])
        for b in reads:
            if b.r.get(k, 0) < ev[1]:
                b.r[k] = ev[1]
        for b in writes:
            b.w = ev
            b.r = {}
        self.q[e].append((waits, fns, k, 1))
        self.n_inst += len(fns)
        return ev

    def dma(self, qe, fn, reads=(), writes=(), sembuf=None, outs=(), inc=16):
        deps = self._deps(reads, writes)
        for b in outs:
            deps.extend(b.r.items())
        waits = self._waits(qe, deps)
        tgt = sembuf if sembuf is not None else (outs[0] if outs else (writes[0] if writes else reads[0]))
        if tgt.dsem is None:
            if self.dma_free and not tgt.persist:
                tgt.dsem = self.dma_free.pop()
            else:
                tgt.dsem = self._mksem(("dma", tgt.name, self.nsem))
            if not tgt.persist:
                self.dma_bufs.append(tgt)
        semkey = tgt.dsem
        self.cnt[semkey] += inc
        ev = (semkey, self.cnt[semkey])
        for b in reads:
            if b.r.get(semkey, 0) < ev[1]:
                b.r[semkey] = ev[1]
        for b in writes:
            b.w = ev
            b.r = {}
        for b in outs:
            b.w = ev
        self.q[qe].append((waits, [fn], semkey, inc))
        self.n_inst += 1
        return ev

    def barrier(self):
        deps = [(k, v) for k, v in self.cnt.items() if v > 0 and k[0] != "sp"]
        waits = self._waits("sp", deps)
        k = self.ekey("sp")
        self.cnt[k] += 1
        ev = (k, self.cnt[k])
        self.q["sp"].append((waits, [lambda e: e.nop()], k, 1))
        for e in ENGS:
            if e != "sp":
                self.q[e].append((self._waits(e, [ev]), [], None, 0))
        for b in self.dma_bufs:
            if b.dsem is not None and self.cnt[b.dsem] < SEM_LIMIT:
                self.dma_free.append(b.dsem)
            b.dsem = None
        self.dma_bufs = []

    def wait_all_on(self, e, bufs):
        deps = [b.w for b in bufs if b.w is not None]
        waits = self._waits(e, deps)
        self.q[e].append((waits, [], None, 0))

    def emit(self):
        nc = self.nc
        with nc.Block() as block:
            def mk(e):
                def body(engobj):
                    if e == "sp" and self.cid_ap is not None:
                        with engobj.register("rank_reg") as rr:
                            engobj.reg_load(rr, self.cid_ap)
                            self.dyn["p"] = engobj.snap(rr)
                            run(engobj)
                    else:
                        run(engobj)

                def run(engobj):
                    for (waits, fns, k, inc) in self.q[e]:
                        for (wk, wv) in waits:
                            engobj.wait_ge(self.sem[wk], wv)
                        n = len(fns)
                        for i, fn in enumerate(fns):
                            ins = fn(engobj)
                            if i == n - 1:
                                ins.then_inc(self.sem[k], inc)
                return body
            block.tensor(mk("pe"))
            block.scalar(mk("act"))
            block.vector(mk("dve"))
            block.gpsimd(mk("pool"))
            block.sync(mk("sp"))


class Tl:
    __slots__ = ("t", "b")

    def __init__(self, t, b):
        self.t = t
        self.b = b


class Ctx:
    def __init__(self, nc, stack):
        self.nc = nc
        self.st = stack
        self.S = Sched(nc, stack)
        self.uid = 0
        self.ps = []
        self.psi = 0
        self.outs = []

    def name(self, p):
        self.uid += 1
        return "%s_%d" % (p, self.uid)

    ARENA_LO = 16512
    ARENA_HI = 229344

    def sb(self, shape, dt, name="t"):
        n = self.name(name)
        esz = 2 if dt == BF16 else 4
        nbytes = esz
        for d_ in shape[1:]:
            nbytes *= int(d_)
        off = (getattr(self, "arena_ptr", self.ARENA_LO) + 31) // 32 * 32
        assert off + nbytes <= self.ARENA_HI, "SBUF arena overflow at %s: need %d have %d" % (n, nbytes, self.ARENA_HI - off)
        self.arena_ptr = off + nbytes
        t = self.nc.alloc_sbuf_tensor_at(n, list(shape), dt, offset=off)
        return Tl(t, Buf(n))

    def arena_reset(self):
        self.arena_ptr = self.ARENA_LO

    def pool(self, n, shape, dt, name="p"):
        return Rot([self.sb(shape, dt, name) for _ in range(n)])

    def init_psum(self, n=8):
        for i in range(n):
            nm = self.name("ps")
            t = self.st.enter_context(self.nc.psum_tensor(nm, [128, 512], F32))
            self.ps.append(Tl(t, Buf(nm)))

    def psum(self):
        p = self.ps[self.psi % len(self.ps)]
        self.psi += 1
        return p

    def dram_buf(self, name):
        return Buf(name)


class Rot:
    def __init__(self, items):
        self.items = items
        self.i = 0

    def next(self):
        it = self.items[self.i % len(self.items)]
        self.i += 1
        return it


def mm_group(K, ps, pairs, reads):
    n = len(pairs)

    def mk(i, l, r, o):
        return lambda e: e.matmul(o, l, r, start=(i == 0), stop=(i == n - 1))
    fns = [mk(i, l, r, o) for i, (l, r, o) in enumerate(pairs)]
    K.S.op("pe", fns, reads=reads, writes=[ps.b])


def rstd_from_sq(K, C, sq, sq_reads, nk, ncols, inv_n, extra=None):
    ps = K.psum()
    pairs = [(C["ones_bf"].t[:, :], sq.t[:, k, 0:ncols], ps.t[:, 0:ncols]) for k in range(nk)]
    mm_group(K, ps, pairs, reads=[C["ones_bf"].b, sq.b])
    r = C["rstd_pool"].next()
    K.S.op("act", lambda e: e.activation(out=r.t[:, 0:ncols], in_=ps.t[:, 0:ncols], func=AF.Sqrt,
                                         bias=C["eps_col"].t[:, 0:1], scale=inv_n),
           reads=[ps.b, C["eps_col"].b], writes=[r.b])
    K.S.op("dve", lambda e: e.reciprocal(out=r.t[:, 0:ncols], in_=r.t[:, 0:ncols]), reads=[r.b], writes=[r.b])
    return r


def pre_norm(K, C, x, xb, g, gcol0, h, hb, s):
    sl = slice(s * 512, (s + 1) * 512)
    sq = C["sq_pool"].next()
    K.S.op("act", lambda e: e.activation(out=sq.t[:, :, :], in_=x.t[:, :, sl], func=AF.Square),
           reads=[xb], writes=[sq.b])
    r = rstd_from_sq(K, C, sq, None, 8, 512, 1.0 / D)
    fns = [(lambda e, k=k: e.scalar_tensor_tensor(
        out=h.t[:, k, sl], in0=x.t[:, k, sl], scalar=g.t[:, gcol0 + k:gcol0 + k + 1], in1=r.t[:, :],
        op0=ALU.mult, op1=ALU.mult)) for k in range(8)]
    K.S.op("dve", fns, reads=[xb, r.b, g.b], writes=[hb])


def post_norm_residual(K, C, y, yb, g, gcol0, alpha, x, xb, s):
    sl = slice(s * 512, (s + 1) * 512)
    sq = C["sq_pool"].next()
    K.S.op("act", lambda e: e.activation(out=sq.t[:, :, :], in_=y.t[:, :, sl], func=AF.Square),
           reads=[yb], writes=[sq.b])
    r = rstd_from_sq(K, C, sq, None, 8, 512, 1.0 / D)
    for k in range(8):
        tmp = C["tmp_pool"].next()
        K.S.op("dve", lambda e, k=k, tmp=tmp: e.scalar_tensor_tensor(
            out=tmp.t[:, :], in0=y.t[:, k, sl], scalar=g.t[:, gcol0 + k:gcol0 + k + 1], in1=r.t[:, :],
            op0=ALU.mult, op1=ALU.mult), reads=[yb, r.b, g.b], writes=[tmp.b])
        K.S.op("dve", lambda e, k=k, tmp=tmp: e.scalar_tensor_tensor(
            out=x.t[:, k, sl], in0=tmp.t[:, :], scalar=float(alpha), in1=x.t[:, k, sl],
            op0=ALU.mult, op1=ALU.add), reads=[tmp.b, xb], writes=[xb])


def load_w(K, slot, src_ap, q="pool"):
    K.S.dma(q, lambda e: e.dma_start(out=slot.t[:], in_=src_ap), writes=[slot.b])


def ffn(K, C, h, hb, wg_ap, wu_ap, wd_ap, y, yb):
    a = C["a"]
    nsub = PASS // 512
    for f in range(NF):
        wg = C["wgu_pool"].next()
        load_w(K, wg, wg_ap[f].rearrange("p (k j) -> p k j", j=128))
        wu = C["wgu_pool"].next()
        load_w(K, wu, wu_ap[f].rearrange("p (k j) -> p k j", j=128))
        for s in range(nsub):
            sl = slice(s * 512, (s + 1) * 512)
            pg = K.psum()
            mm_group(K, pg, [(wg.t[:, k, :], h.t[:, k, sl], pg.t[:, :]) for k in range(8)], reads=[wg.b, hb])
            pu = K.psum()
            mm_group(K, pu, [(wu.t[:, k, :], h.t[:, k, sl], pu.t[:, :]) for k in range(8)], reads=[wu.b, hb])
            sg = C["tmp_pool"].next()
            K.S.op("act", lambda e, sg=sg, pg=pg: e.activation(out=sg.t[:, :], in_=pg.t[:, :], func=AF.Silu),
                   reads=[pg.b], writes=[sg.b])
            ab = a.b[f * nsub + s]
            K.S.op("dve", lambda e, sg=sg, pu=pu, f=f, sl=sl: e.tensor_tensor(
                out=a.t[:, f, sl], in0=sg.t[:, :], in1=pu.t[:, :], op=ALU.mult),
                reads=[sg.b, pu.b], writes=[ab])
    for d in range(8):
        wd = C["wd_pool"].next()
        load_w(K, wd, wd_ap[d].rearrange("p (c j) -> p c j", j=128))
        for s in range(nsub):
            sl = slice(s * 512, (s + 1) * 512)
            py = K.psum()
            mm_group(K, py, [(wd.t[:, c, :], a.t[:, c, sl], py.t[:, :]) for c in range(NF)],
                     reads=[wd.b] + [a.b[c * nsub + s] for c in range(NF)])
            K.S.op("act", lambda e, py=py, d=d, sl=sl: e.activation(out=y.t[:, d, sl], in_=py.t[:, :], func=AF.Copy),
                   reads=[py.b], writes=[yb])


class ATl:
    def __init__(self, t, bufs):
        self.t = t
        self.b = bufs


def common_consts(K):
    C = {}
    ones_bf = K.sb([128, 128], BF16, "ones_bf")
    K.S.op("dve", lambda e: e.memset(ones_bf.t[:, :], 1.0), writes=[ones_bf.b])
    C["ones_bf"] = ones_bf
    eps = K.sb([128, 1], F32, "eps")
    K.S.op("dve", lambda e: e.memset(eps.t[:, :], EPS), writes=[eps.b])
    C["eps_col"] = eps
    C["rstd_pool"] = K.pool(2, [128, 512], F32, "rstd")
    C["sq_pool"] = K.pool(1, [128, 8, 512], BF16, "sq")
    C["tmp_pool"] = K.pool(3, [128, 512], F32, "tmp")
    return C


def ffn_bufs(K, C):
    nsub = PASS // 512
    at = K.sb([128, NF, PASS], BF16, "a")
    C["a"] = ATl(at.t, [Buf("a%d" % i) for i in range(NF * nsub)])
    C["wgu_pool"] = K.pool(4, [128, 8, 128], BF16, "wgu")
    C["wd_pool"] = K.pool(2, [128, NF, 128], BF16, "wd")


OQ, OK_, OV, OCA, OCG, OGQ, OGK, OGV, OGR, OLR, OGT = 0, 512, 1024, 1536, 2048, 2560, 2816, 3072, 3584, 4096, 4112
FM_CHUNKS = ([("q", i, OQ + 128 * i) for i in range(4)] + [("k", i, OK_ + 128 * i) for i in range(4)] +
             [("a", i, OCA + 128 * i) for i in range(4)] + [("g", i, OCG + 128 * i) for i in range(4)] +
             [("gq", i, OGQ + 128 * i) for i in range(2)] + [("r", i, OGR + 128 * i) for i in range(4)] +
             [("lr", 0, OLR)])
NFM = len(FM_CHUNKS)
SB_QT, SB_KT, SB_V, SB_GQT, SB_GV = 0, 128 * TPC, 256 * TPC, 384 * TPC, 448 * TPC
SB_LEN = 576 * TPC
SF_UT, SF_GK, SF_LA = 0, 128 * TPC, 192 * TPC
SF_LEN = 256 * TPC


class XB:
    def __init__(self, U, nunits):
        self.U, self.nunits = U, nunits
        self.NCH = nunits // U
        self.CL = U * TPC

    def shape(self, nrank=4):
        return [self.NCH, nrank, self.CL]

    def fm_pieces(self, ap3, d, off, f0, nf):
        u0 = off // TPC + f0
        u1 = u0 + nf
        out = []
        u = u0
        while u < u1:
            c = u // self.U
            ue = min(u1, (c + 1) * self.U)
            lo = (u - c * self.U) * TPC
            out.append((u - u0, ue - u0, ap3[c, d, lo:lo + (ue - u) * TPC].rearrange("(f t) -> f t", t=TPC)))
            u = ue
        return out

    def tm_all(self, ap3, off, nfeat, tok0, ntok):
        e0 = off + tok0 * nfeat
        c = e0 // self.CL
        lo = e0 - c * self.CL
        assert lo + ntok * nfeat <= self.CL
        return ap3[c, :, lo:lo + ntok * nfeat].rearrange("d (t f) -> t d f", f=nfeat)

    def tm_src(self, ap3, j, off, nfeat, tok0, ntok):
        e0 = off + tok0 * nfeat
        c = e0 // self.CL
        lo = e0 - c * self.CL
        assert lo + ntok * nfeat <= self.CL
        return ap3[c, j, lo:lo + ntok * nfeat].rearrange("(t f) -> t f", f=nfeat)


X1B = XB(64, 576)
X1F = XB(32, 256)
X2B = XB(64, 128)
X2F = XB(32, 256)


def build_A():
    nc = bass.Bass("TRN2", target_bir_lowering=False)
    dt = nc.dram_tensor
    xT = dt("xT", [D, TPC], F32, kind="ExternalInput").ap()
    wg = dt("wg", [NF, 128, 8 * 128], F32, kind="ExternalInput").ap()
    wu = dt("wu", [NF, 128, 8 * 128], F32, kind="ExternalInput").ap()
    wd = dt("wd", [8, 128, NF * 128], F32, kind="ExternalInput").ap()
    wfm = dt("wfm", [NFM, 128, 8 * 128], F32, kind="ExternalInput").ap()
    wtm = dt("wtm", [128, 8 * 1280], F32, kind="ExternalInput").ap()
    walpha = dt("walpha", [16, 256], F32, kind="ExternalInput").ap()
    balpha = dt("balpha", [1, 256], F32, kind="ExternalInput").ap()
    gains = dt("gains", [128, 24], F32, kind="ExternalInput").ap()
    x1T = dt("x1T", [D, TPC], F32, kind="ExternalOutput").ap()
    send_bf = dt("send_bf", [4, SB_LEN], BF16, kind="ExternalOutput").ap()
    send_f = dt("send_f", [4, SF_LEN], F32, kind="ExternalOutput").ap()
    silur = dt("silur", [512, TPC], F32, kind="ExternalOutput").ap()
    with ExitStack() as st:
        K = Ctx(nc, st)
        K.init_psum(8)
        C = common_consts(K)
        ffn_bufs(K, C)
        bb = dict(b_x1=Buf("o_x1", True), b_sbf=Buf("o_sbf", True), b_sf=Buf("o_sf", True), b_sr=Buf("o_sr", True))
        K.outs += list(bb.values())
        emit_A(K, C, dict(xT=xT, wg=wg, wu=wu, wd=wd, wfm=wfm, wtm=wtm, walpha=walpha, balpha=balpha,
                          gains=gains, x1T=x1T, send_bf=send_bf, send_f=send_f, silur=silur, **bb))
        K.S.wait_all_on("sp", K.outs)
        K.S.emit()
    return nc


def emit_A(K, C, io):
    S = K.S
    nsub = PASS // 512
    g = K.sb([128, 24], F32, "gains")
    S.dma("sp", lambda e: e.dma_start(out=g.t[:, :], in_=io["gains"]), writes=[g.b])
    wtm = K.sb([128, 8, 1280], BF16, "wtm")
    load_w(K, wtm, io["wtm"].rearrange("p (k j) -> p k j", j=1280))
    wal = K.sb([16, 256], F32, "walpha")
    S.dma("sp", lambda e: e.dma_start(out=wal.t[:, :], in_=io["walpha"]), writes=[wal.b])
    bal = K.sb([1, 256], F32, "balpha")
    S.dma("sp", lambda e: e.dma_start(out=bal.t[:, :], in_=io["balpha"]), writes=[bal.b])
    ones_f = K.sb([1, 128], F32, "ones_f")
    S.op("dve", lambda e: e.memset(ones_f.t[:, :], 1.0), writes=[ones_f.b])

    x = K.sb([128, 8, PASS], F32, "x")
    h = K.sb([128, 8, PASS], BF16, "h")
    y = K.sb([128, 8, PASS], F32, "y")
    lrT = K.sb([16, PASS], F32, "lrT")
    wfm_pool = K.pool(2, [128, 8, 128], BF16, "wfm")
    st_bf = K.pool(4, [128, 512], BF16, "st_bf")
    st_f = K.pool(3, [128, 512], F32, "st_f")
    st_la = K.pool(2, [128, 256], F32, "st_la")
    xv = io["xT"].rearrange("(k p) t -> p k t", p=128)
    x1v = io["x1T"].rearrange("(k p) t -> p k t", p=128)
    sbf, sf = io["send_bf"], io["send_f"]
    o_x1, o_sbf, o_sf, o_sr = io["b_x1"], io["b_sbf"], io["b_sf"], io["b_sr"]
    b_xin = io.get("b_xin")

    def fm_out(xb, ap3, p, off, o, tsl, outbuf, pf0=0, nf=128):
        for (fa, fb, dst) in xb.fm_pieces(ap3, p, off, 0, nf):
            S.dma("sp", lambda e, dst=dst, fa=fa, fb=fb: e.dma_start(out=dst[:, tsl], in_=o.t[pf0 + fa:pf0 + fb, :]),
                  reads=[o.b], outs=[outbuf])

    for ps_ in range(TPC // PASS):
        t0 = ps_ * PASS
        S.dma("sp", lambda e, t0=t0: e.dma_start(out=x.t[:, :, :], in_=xv[:, :, t0:t0 + PASS]), writes=[x.b],
              reads=[b_xin] if b_xin is not None else [])
        for s in range(nsub):
            pre_norm(K, C, x, x.b, g, 0, h, h.b, s)
        ffn(K, C, h, h.b, io["wg"], io["wu"], io["wd"], y, y.b)
        for s in range(nsub):
            post_norm_residual(K, C, y, y.b, g, 16, 0.5, x, x.b, s)
        S.dma("sp", lambda e, t0=t0: e.dma_start(out=x1v[:, :, t0:t0 + PASS], in_=x.t[:, :, :]),
              reads=[x.b], outs=[o_x1])
        for s in range(nsub):
            pre_norm(K, C, x, x.b, g, 8, h, h.b, s)
        for ci, (kind, i, col) in enumerate(FM_CHUNKS):
            w = wfm_pool.next()
            load_w(K, w, io["wfm"][ci].rearrange("p (k j) -> p k j", j=128))
            for s in range(nsub):
                sl = slice(s * 512, (s + 1) * 512)
                tsl = slice(t0 + s * 512, t0 + (s + 1) * 512)
                pp = K.psum()
                mm_group(K, pp, [(w.t[:, k, :], h.t[:, k, sl], pp.t[:, :]) for k in range(8)], reads=[w.b, h.b])
                if kind in ("q", "k"):
                    o = st_bf.next()
                    sc = 0.125 if kind == "q" else 1.0
                    S.op("act", lambda e, o=o, pp=pp, sc=sc: e.activation(out=o.t[:, :], in_=pp.t[:, :], func=AF.Copy, scale=sc),
                         reads=[pp.b], writes=[o.b])
                    off = SB_QT if kind == "q" else SB_KT
                    fm_out(X1B, sbf, i, off, o, tsl, o_sbf)
                elif kind == "a":
                    S.op("act", lambda e, pp=pp, i=i, sl=sl: e.activation(out=y.t[:, i, sl], in_=pp.t[:, :], func=AF.Copy),
                         reads=[pp.b], writes=[y.b])
                elif kind == "g":
                    sg = C["tmp_pool"].next()
                    S.op("act", lambda e, sg=sg, pp=pp: e.activation(out=sg.t[:, :], in_=pp.t[:, :], func=AF.Sigmoid),
                         reads=[pp.b], writes=[sg.b])
                    o = st_f.next()
                    S.op("dve", lambda e, o=o, sg=sg, i=i, sl=sl: e.tensor_tensor(out=o.t[:, :], in0=y.t[:, i, sl], in1=sg.t[:, :], op=ALU.mult),
                         reads=[sg.b, y.b], writes=[o.b])
                    fm_out(X1F, sf, i, SF_UT, o, tsl, o_sf)
                elif kind == "gq":
                    o = st_bf.next()
                    S.op("act", lambda e, o=o, pp=pp: e.activation(out=o.t[:, :], in_=pp.t[:, :], func=AF.Copy, scale=0.125),
                         reads=[pp.b], writes=[o.b])
                    for half in range(2):
                        fm_out(X1B, sbf, 2 * i + half, SB_GQT, o, tsl, o_sbf, pf0=64 * half, nf=64)
                elif kind == "r":
                    o = st_f.next()
                    S.op("act", lambda e, o=o, pp=pp: e.activation(out=o.t[:, :], in_=pp.t[:, :], func=AF.Silu),
                         reads=[pp.b], writes=[o.b])
                    dst = io["silur"][128 * i:128 * i + 128, tsl]
                    S.dma("sp", lambda e, o=o, dst=dst: e.dma_start(out=dst, in_=o.t[:, :]), reads=[o.b], outs=[o_sr])
                elif kind == "lr":
                    S.op("act", lambda e, pp=pp, sl=sl: e.activation(out=lrT.t[:, sl], in_=pp.t[0:16, :], func=AF.Copy),
                         reads=[pp.b], writes=[lrT.b])
        for tb in range(PASS // 128):
            bsl = slice(tb * 128, (tb + 1) * 128)
            gsl = slice(t0 + tb * 128, t0 + (tb + 1) * 128)
            pp = K.psum()
            mm_group(K, pp, [(h.t[:, k, bsl], wtm.t[:, k, 0:512], pp.t[:, :]) for k in range(8)], reads=[wtm.b, h.b])
            o = st_bf.next()
            S.op("act", lambda e, o=o, pp=pp: e.activation(out=o.t[:, :], in_=pp.t[:, :], func=AF.Copy), reads=[pp.b], writes=[o.b])
            S.dma("sp", lambda e, o=o, gsl=gsl: e.dma_start(out=X1B.tm_all(sbf, SB_V, 128, gsl.start, 128), in_=o.t[:, :].rearrange("t (p f) -> t p f", f=128)),
                  reads=[o.b], outs=[o_sbf])
            pp = K.psum()
            mm_group(K, pp, [(h.t[:, k, bsl], wtm.t[:, k, 768:1280], pp.t[:, :]) for k in range(8)], reads=[wtm.b, h.b])
            o = st_bf.next()
            S.op("dve", lambda e, o=o, pp=pp: e.tensor_copy(out=o.t[:, :], in_=pp.t[:, :]), reads=[pp.b], writes=[o.b])
            S.dma("sp", lambda e, o=o, gsl=gsl: e.dma_start(out=X1B.tm_all(sbf, SB_GV, 128, gsl.start, 128), in_=o.t[:, :].rearrange("t (p f) -> t p f", f=128)),
                  reads=[o.b], outs=[o_sbf])
            pp = K.psum()
            mm_group(K, pp, [(h.t[:, k, bsl], wtm.t[:, k, 512:768], pp.t[:, 0:256]) for k in range(8)], reads=[wtm.b, h.b])
            o = st_la.next()
            S.op("act", lambda e, o=o, pp=pp: e.activation(out=o.t[:, :], in_=pp.t[:, 0:256], func=AF.Copy), reads=[pp.b], writes=[o.b])
            S.dma("sp", lambda e, o=o, gsl=gsl: e.dma_start(out=X1F.tm_all(sf, SF_GK, 64, gsl.start, 128), in_=o.t[:, :].rearrange("t (p f) -> t p f", f=64)),
                  reads=[o.b], outs=[o_sf])
            pp = K.psum()
            mm_group(K, pp, [(lrT.t[:, bsl], wal.t[:, :], pp.t[:, 0:256]), (ones_f.t[:, :], bal.t[:, :], pp.t[:, 0:256])],
                     reads=[lrT.b, wal.b, ones_f.b, bal.b])
            o = st_la.next()
            S.op("act", lambda e, o=o, pp=pp: e.activation(out=o.t[:, :], in_=pp.t[:, 0:256], func=AF.Exp, scale=-1.0), reads=[pp.b], writes=[o.b])
            S.op("act", lambda e, o=o: e.activation(out=o.t[:, :], in_=o.t[:, :], func=AF.Ln, bias=1.0), reads=[o.b], writes=[o.b])
            S.op("dve", lambda e, o=o: e.tensor_scalar(out=o.t[:, :], in0=o.t[:, :], scalar1=-1.0 / 16.0, scalar2=None, op0=ALU.mult),
                 reads=[o.b], writes=[o.b])
            S.dma("sp", lambda e, o=o, gsl=gsl: e.dma_start(out=X1F.tm_all(sf, SF_LA, 64, gsl.start, 128), in_=o.t[:, :].rearrange("t (p f) -> t p f", f=64)),
                  reads=[o.b], outs=[o_sf])


def _c(a):
    return np.ascontiguousarray(a)


def lay_fm_w(W, ncols_chunks):
    Kc = W.shape[0] // 128
    n = W.shape[1] // 128
    return _c(W.reshape(Kc, 128, n, 128).transpose(2, 1, 0, 3).reshape(n, 128, Kc * 128))


def lay_pk(v):
    return _c(v.reshape(-1, 128).T)


def prep_A_weights(inp, l):
    w_in = inp["w_in"][l]
    cols = []
    for (kind, i, col) in FM_CHUNKS:
        blk = np.zeros((D, 128), np.float32)
        n = 16 if kind == "lr" else 128
        blk[:, :n] = w_in[:, col:col + n]
        cols.append(blk)
    wfm = lay_fm_w(np.concatenate(cols, axis=1), NFM)
    tm = np.concatenate([w_in[:, OV:OV + 512], w_in[:, OGK:OGK + 256], w_in[:, OGV:OGV + 512]], axis=1)
    wtm = _c(tm.reshape(8, 128, 1280).transpose(1, 0, 2).reshape(128, 8 * 1280))
    gains = np.concatenate([lay_pk(inp["norm_pre"][l, 0]), lay_pk(inp["norm_pre"][l, 1]), lay_pk(inp["norm_post"][l, 0])], axis=1)
    return dict(wg=lay_fm_w(inp["ffn1_w_gate"][l], NF), wu=lay_fm_w(inp["ffn1_w_up"][l], NF),
                wd=lay_fm_w(inp["ffn1_w_down"][l], 8), wfm=wfm, wtm=wtm,
                walpha=_c(inp["gla_w_alpha"][l]), balpha=_c(inp["gla_b_alpha"][l][None, :]), gains=_c(gains))


RB_LEN = 128 * TPC
RF_CY, RF_GN = 0, 128 * TPC
RF_LEN = 256 * TPC


def build_C(dbg=0):
    nc = bass.Bass("TRN2", target_bir_lowering=False)
    dt = nc.dram_tensor
    io = dict(
        x1T=dt("x1T", [D, TPC], F32, kind="ExternalInput").ap(),
        recv_bf=dt("recv_bf", [4, RB_LEN], BF16, kind="ExternalInput").ap(),
        recv_f=dt("recv_f", [4, RF_LEN], F32, kind="ExternalInput").ap(),
        silur=dt("silur", [512, TPC], F32, kind="ExternalInput").ap(),
        wgt=dt("wgt", [24, 128, 8 * 128], F32, kind="ExternalInput").ap(),
        wbr=dt("wbr", [24, 128, 4 * 128], F32, kind="ExternalInput").ap(),
        wo=dt("wo", [8, 128, 8 * 128], F32, kind="ExternalInput").ap(),
        wg=dt("wg", [NF, 128, 8 * 128], F32, kind="ExternalInput").ap(),
        wu=dt("wu", [NF, 128, 8 * 128], F32, kind="ExternalInput").ap(),
        wd=dt("wd", [8, 128, NF * 128], F32, kind="ExternalInput").ap(),
        gains=dt("gains", [128, 40], F32, kind="ExternalInput").ap(),
        x3T=dt("x3T", [D, TPC], F32, kind="ExternalOutput").ap(),
    )
    with ExitStack() as st:
        K = Ctx(nc, st)
        K.init_psum(8)
        C = common_consts(K)
        ffn_bufs(K, C)
        r2bf_, r2f_ = io["recv_bf"], io["recv_f"]
        io["recv_bf"] = lambda p, a, b: r2bf_[p:p + 1, a:b]
        io["recv_f"] = lambda p, a, b: r2f_[p:p + 1, a:b]
        io["b_x3"] = Buf("o_x3", True)
        K.outs.append(io["b_x3"])
        emit_C(K, C, io, dbg)
        K.S.wait_all_on("sp", K.outs)
        K.S.emit()
    return nc


def emit_C(K, C, io, dbg=0):
    S = K.S
    nsub = PASS // 512
    T = TPC
    g = K.sb([128, 40], F32, "gainsC")
    S.dma("sp", lambda e: e.dma_start(out=g.t[:, :], in_=io["gains"]), writes=[g.b])
    x = K.sb([128, 8, PASS], F32, "x")
    h = K.sb([128, 8, PASS], BF16, "h")
    y = K.sb([128, 8, PASS], F32, "y")
    a = C["a"]
    sr_pool = K.pool(1, [128, 4, 512], F32, "sr")
    wgt_pool = K.pool(2, [128, 8, 128], BF16, "wgt")
    wbr_pool = K.pool(2, [128, 4, 128], BF16, "wbr")
    acc = [K.sb([128, 512], F32, "acc") for _ in range(nsub)]
    ln_mean = K.sb([128, 512], F32, "ln_mean")
    ln_msq = K.sb([128, 512], F32, "ln_msq")
    xv = io["x1T"].rearrange("(k p) t -> p k t", p=128)
    ov = io["x3T"].rearrange("(k p) t -> p k t", p=128)
    r2bf, r2f = io["recv_bf"], io["recv_f"]
    sr_v = io["silur"].rearrange("(c f) t -> f c t", f=128)
    o_x3 = io["b_x3"]
    rd2 = [io[k] for k in ("b_r2bf", "b_r2f") if io.get(k) is not None]
    rdx = [io[k] for k in ("b_x1",) if io.get(k) is not None]
    rds = [io[k] for k in ("b_sr",) if io.get(k) is not None]

    def ab(f0, f1, s):
        return [a.b[f * nsub + s] for f in range(f0, f1)]

    for ps_ in range(TPC // PASS):
        t0 = ps_ * PASS
        S.dma("sp", lambda e, t0=t0: e.dma_start(out=x.t[:, :, :], in_=xv[:, :, t0:t0 + PASS]), writes=[x.b], reads=rdx)
        for s in range(nsub):
            pre_norm(K, C, x, x.b, g, 0, h, h.b, s)
        for s in range(nsub):
            sl = slice(s * 512, (s + 1) * 512)
            tsl = slice(t0 + s * 512, t0 + (s + 1) * 512)
            for p in range(4):
                for (fa, fb, sap) in X2F.fm_pieces(r2f, p, RF_CY, 0, 128):
                    S.dma("sp", lambda e, sl=sl, tsl=tsl, p=p, fa=fa, fb=fb, sap=sap: e.dma_start(out=y.t[fa:fb, p, sl], in_=sap[:, tsl]),
                          writes=[y.b], reads=rd2)
                for (fa, fb, sap) in X2F.fm_pieces(r2f, p, RF_GN, 0, 128):
                    S.dma("sp", lambda e, sl=sl, tsl=tsl, p=p, fa=fa, fb=fb, sap=sap: e.dma_start(out=y.t[fa:fb, 4 + p, sl], in_=sap[:, tsl]),
                          writes=[y.b], reads=rd2)
                for (fa, fb, sap) in X2B.fm_pieces(r2bf, p, 0, 0, 128):
                    S.dma("sp", lambda e, sl=sl, tsl=tsl, p=p, fa=fa, fb=fb, sap=sap: e.dma_start(out=a.t[fa:fb, 8 + p, sl], in_=sap[:, tsl]),
                          writes=ab(8 + p, 9 + p, s), reads=rd2)
            sr = sr_pool.next()
            S.dma("sp", lambda e, sr=sr, tsl=tsl: e.dma_start(out=sr.t[:, :, :], in_=sr_v[:, :, tsl]), writes=[sr.b], reads=rds)
            sq = C["sq_pool"].next()
            S.op("act", lambda e, sq=sq, sl=sl: e.activation(out=sq.t[:, 0:4, :], in_=y.t[:, 0:4, sl], func=AF.Copy),
                 reads=[y.b], writes=[sq.b])
            S.op("act", lambda e, sq=sq, sl=sl: e.activation(out=sq.t[:, 4:8, :], in_=y.t[:, 0:4, sl], func=AF.Square),
                 reads=[y.b], writes=[sq.b])
            p1 = K.psum()
            mm_group(K, p1, [(C["ones_bf"].t[:, :], sq.t[:, c, :], p1.t[:, :]) for c in range(4)], reads=[C["ones_bf"].b, sq.b])
            p2 = K.psum()
            mm_group(K, p2, [(C["ones_bf"].t[:, :], sq.t[:, 4 + c, :], p2.t[:, :]) for c in range(4)], reads=[C["ones_bf"].b, sq.b])
            mean = ln_mean
            S.op("act", lambda e, mean=mean, p1=p1: e.activation(out=mean.t[:, :], in_=p1.t[:, :], func=AF.Copy, scale=1.0 / 512),
                 reads=[p1.b], writes=[mean.b])
            msq = ln_msq
            S.op("dve", lambda e, mean=mean, msq=msq: e.tensor_tensor(out=msq.t[:, :], in0=mean.t[:, :], in1=mean.t[:, :], op=ALU.mult),
                 reads=[mean.b], writes=[msq.b])
            S.op("dve", lambda e, msq=msq, p2=p2: e.scalar_tensor_tensor(out=msq.t[:, :], in0=p2.t[:, :], scalar=1.0 / 512, in1=msq.t[:, :],
                                                                       op0=ALU.mult, op1=ALU.subtract),
                 reads=[p2.b, msq.b], writes=[msq.b])
            rs = C["rstd_pool"].next()
            S.op("act", lambda e, rs=rs, msq=msq: e.activation(out=rs.t[:, :], in_=msq.t[:, :], func=AF.Sqrt, bias=C["eps_col"].t[:, 0:1], scale=1.0),
                 reads=[msq.b, C["eps_col"].b], writes=[rs.b])
            S.op("dve", lambda e, rs=rs: e.reciprocal(out=rs.t[:, :], in_=rs.t[:, :]), reads=[rs.b], writes=[rs.b])
            for c in range(4):
                t = C["tmp_pool"].next()
                S.op("dve", lambda e, t=t, c=c, sl=sl, mean=mean: e.tensor_tensor(out=t.t[:, :], in0=y.t[:, c, sl], in1=mean.t[:, :], op=ALU.subtract),
                     reads=[y.b, mean.b], writes=[t.b])
                S.op("dve", lambda e, t=t, rs=rs: e.tensor_tensor(out=t.t[:, :], in0=t.t[:, :], in1=rs.t[:, :], op=ALU.mult),
                     reads=[t.b, rs.b], writes=[t.b])
                S.op("act", lambda e, t=t, c=c, sl=sl: e.activation(out=a.t[:, c, sl], in_=t.t[:, :], func=AF.Silu,
                                                                      scale=g.t[:, 32 + c:33 + c], bias=g.t[:, 36 + c:37 + c]),
                     reads=[t.b, g.b], writes=ab(c, c + 1, s))
                S.op(POOL_ENG, lambda e, c=c, sl=sl, sr=sr: e.tensor_tensor(out=a.t[:, 4 + c, sl], in0=y.t[:, 4 + c, sl], in1=sr.t[:, c, :], op=ALU.mult),
                     reads=[y.b, sr.b], writes=ab(4 + c, 5 + c, s))
        base = {0: 8, 1: 0, 2: 4}
        for d in range(8):
            for gi in range(3):
                wb = wbr_pool.next()
                load_w(K, wb, io["wbr"][gi * 8 + d].rearrange("p (k j) -> p k j", j=128))
                wgt = wgt_pool.next()
                load_w(K, wgt, io["wgt"][gi * 8 + d].rearrange("p (k j) -> p k j", j=128))
                for s in range(nsub):
                    sl = slice(s * 512, (s + 1) * 512)
                    pb = K.psum()
                    mm_group(K, pb, [(wb.t[:, kc, :], a.t[:, base[gi] + kc, sl], pb.t[:, :]) for kc in range(4)],
                             reads=[wb.b] + ab(base[gi], base[gi] + 4, s))
                    pl = K.psum()
                    mm_group(K, pl, [(wgt.t[:, k, :], h.t[:, k, sl], pl.t[:, :]) for k in range(8)], reads=[wgt.b, h.b])
                    sig = C["tmp_pool"].next()
                    S.op("act", lambda e, sig=sig, pl=pl: e.activation(out=sig.t[:, :], in_=pl.t[:, :], func=AF.Sigmoid),
                         reads=[pl.b], writes=[sig.b])
                    if gi == 0:
                        S.op("dve", lambda e, sig=sig, pb=pb, s=s: e.tensor_tensor(out=acc[s].t[:, :], in0=sig.t[:, :], in1=pb.t[:, :], op=ALU.mult),
                             reads=[sig.b, pb.b], writes=[acc[s].b])
                    else:
                        S.op("dve", lambda e, sig=sig, pb=pb: e.tensor_tensor(out=sig.t[:, :], in0=sig.t[:, :], in1=pb.t[:, :], op=ALU.mult),
                             reads=[sig.b, pb.b], writes=[sig.b])
                        if gi == 1:
                            S.op(POOL_ENG, lambda e, sig=sig, s=s: e.tensor_tensor(out=acc[s].t[:, :], in0=acc[s].t[:, :], in1=sig.t[:, :], op=ALU.add),
                                 reads=[sig.b, acc[s].b], writes=[acc[s].b])
                        else:
                            S.op(POOL_ENG, lambda e, sig=sig, s=s, d=d, sl=sl: e.tensor_tensor(out=a.t[:, 12 + d, sl], in0=acc[s].t[:, :], in1=sig.t[:, :], op=ALU.add),
                                 reads=[sig.b, acc[s].b], writes=ab(12 + d, 13 + d, s))
        for do in range(8):
            wo = C["wgu_pool"].next()
            load_w(K, wo, io["wo"][do].rearrange("p (k j) -> p k j", j=128))
            for s in range(nsub):
                sl = slice(s * 512, (s + 1) * 512)
                pp = K.psum()
                mm_group(K, pp, [(wo.t[:, k, :], a.t[:, 12 + k, sl], pp.t[:, :]) for k in range(8)], reads=[wo.b] + ab(12, 20, s))
                S.op("act", lambda e, pp=pp, do=do, sl=sl: e.activation(out=y.t[:, do, sl], in_=pp.t[:, :], func=AF.Copy),
                     reads=[pp.b], writes=[y.b])
        if dbg == 2:
            S.dma("sp", lambda e, t0=t0: e.dma_start(out=ov[:, :, t0:t0 + PASS], in_=y.t[:, :, :]), reads=[y.b], outs=[o_x3])
            continue
        for s in range(nsub):
            post_norm_residual(K, C, y, y.b, g, 8, 1.0, x, x.b, s)
        if dbg == 1:
            S.dma("sp", lambda e, t0=t0: e.dma_start(out=ov[:, :, t0:t0 + PASS], in_=x.t[:, :, :]), reads=[x.b], outs=[o_x3])
            continue
        for s in range(nsub):
            pre_norm(K, C, x, x.b, g, 16, h, h.b, s)
        ffn(K, C, h, h.b, io["wg"], io["wu"], io["wd"], y, y.b)
        for s in range(nsub):
            post_norm_residual(K, C, y, y.b, g, 24, 0.5, x, x.b, s)
        S.dma("sp", lambda e, t0=t0: e.dma_start(out=ov[:, :, t0:t0 + PASS], in_=x.t[:, :, :]), reads=[x.b], outs=[o_x3])


def prep_C_weights(inp, l):
    w_in = inp["w_in"][l]
    wgt = lay_fm_w(w_in[:, OGT:OGT + 3072], 24)
    wb = inp["w_branch"][l]
    wbr = np.concatenate([lay_fm_w(wb[gi], 8) for gi in range(3)], axis=0)
    gains = np.concatenate([lay_pk(inp["norm_pre"][l, 1]), lay_pk(inp["norm_post"][l, 1]), lay_pk(inp["norm_pre"][l, 2]),
                            lay_pk(inp["norm_post"][l, 2]), lay_pk(inp["conv_ln_g"][l]), lay_pk(inp["conv_ln_b"][l])], axis=1)
    return dict(wgt=wgt, wbr=_c(wbr), wo=lay_fm_w(inp["w_out"][l], 8), wg=lay_fm_w(inp["ffn2_w_gate"][l], NF),
                wu=lay_fm_w(inp["ffn2_w_up"][l], NF), wd=lay_fm_w(inp["ffn2_w_down"][l], 8), gains=_c(gains))


NQT = SEQ // 512
NBLK = SEQ // 128
S2B_LEN = 128 * TPC
S2F_LEN = 256 * TPC


def build_B():
    nc = bass.Bass("TRN2", target_bir_lowering=False)
    dt = nc.dram_tensor
    io = dict(
        rbf=dt("rbf", [4, SB_LEN], BF16, kind="ExternalInput").ap(),
        rf=dt("rf", [4, SF_LEN], F32, kind="ExternalInput").ap(),
        bw=dt("bw", [128, 33], F32, kind="ExternalInput").ap(),
        s2bf=dt("s2bf", [4, S2B_LEN], BF16, kind="ExternalOutput").ap(),
        s2f=dt("s2f", [4, S2F_LEN], F32, kind="ExternalOutput").ap(),
    )
    with ExitStack() as st:
        K = Ctx(nc, st)
        K.init_psum(8)
        rbf_, rf_ = io["rbf"], io["rf"]
        io["rbf"] = lambda j, a, b: rbf_[j:j + 1, a:b]
        io["rf"] = lambda j, a, b: rf_[j:j + 1, a:b]
        io["b_s2bf"], io["b_s2f"] = Buf("o_s2bf", True), Buf("o_s2f", True)
        K.outs += [io["b_s2bf"], io["b_s2f"]]
        emit_B(K, io)
        K.S.wait_all_on("sp", K.outs)
        K.S.emit()
    return nc


def emit_B(K, io):
    S = K.S
    T = TPC
    rbf, rf = io["rbf"], io["rf"]
    o_bf, o_f = io["b_s2bf"], io["b_s2f"]
    rdb = [io[k] for k in ("b_rbf", "b_rf") if io.get(k) is not None]
    ones_bf = K.sb([128, 128], BF16, "ones_bf")
    S.op("dve", lambda e: e.memset(ones_bf.t[:, :], 1.0), writes=[ones_bf.b])
    negones = K.sb([128, 1], BF16, "negones")
    S.op("dve", lambda e: e.memset(negones.t[:, :], -1.0), writes=[negones.b])
    eps = K.sb([128, 1], F32, "eps")
    S.op("dve", lambda e: e.memset(eps.t[:, :], EPS), writes=[eps.b])
    tri2 = K.sb([128, 128], BF16, "tri2")
    S.op("pool", lambda e: e.memset(tri2.t[:, :], -1.0), writes=[tri2.b])
    S.op("pool", lambda e: e.affine_select(out=tri2.t[:, :], in_=tri2.t[:, :], pattern=[[-1, 128]], compare_op=ALU.is_ge,
                                            fill=0.0, base=0, channel_multiplier=1), reads=[tri2.b], writes=[tri2.b])
    S.op("pool", lambda e: e.memset(tri2.t[0:1, :], 1.0), reads=[tri2.b], writes=[tri2.b])
    tris = K.sb([128, 128], BF16, "tris")
    S.op("pool", lambda e: e.memset(tris.t[:, :], 1.0), writes=[tris.b])
    S.op("pool", lambda e: e.affine_select(out=tris.t[:, :], in_=tris.t[:, :], pattern=[[-1, 128]], compare_op=ALU.is_gt,
                                            fill=0.0, base=0, channel_multiplier=1), reads=[tris.b], writes=[tris.b])
    S.op("pool", lambda e: e.memset(tris.t[64:128, 0:64], 0.0), reads=[tris.b], writes=[tris.b])
    cind = K.sb([128, 2], BF16, "cind")
    S.op("pool", lambda e: e.memset(cind.t[:, :], 0.0), writes=[cind.b])
    S.op("pool", lambda e: e.memset(cind.t[0:64, 0:1], 1.0), reads=[cind.b], writes=[cind.b])
    S.op("pool", lambda e: e.memset(cind.t[64:128, 1:2], 1.0), reads=[cind.b], writes=[cind.b])
    bw = K.sb([128, 33], F32, "bw")
    S.dma("sp", lambda e: e.dma_start(out=bw.t[:, :], in_=io["bw"]), writes=[bw.b])
    qT = K.sb([128, SEQ], BF16, "qT")
    qT2 = K.sb([128, SEQ], BF16, "qT2")
    S.op("pool", lambda e: e.memset(qT.t[64:128, :], 0.0), writes=[qT.b])
    S.op("pool", lambda e: e.memset(qT2.t[0:64, :], 0.0), writes=[qT2.b])
    qz = [qT, qT2]
    NB2 = (SEQ + 126) // 127
    kT = K.sb([128, NB2 * 128], BF16, "kT2")
    v = K.sb([128, NB2, 128], BF16, "v2")
    S.op("pool", lambda e: e.memset(kT.t[:, :], 0.0), writes=[kT.b])
    S.op("pool", lambda e: e.memset(v.t[:, :, :], 0.0), writes=[v.b])
    kT3 = kT.t[:, :].rearrange("f (b s) -> f b s", s=128)

    def key_runs(ta, tb):
        runs = []
        t = ta
        while t < tb:
            b, s = t // 127, t % 127
            if s == 0 and tb - t >= 127:
                nb = (tb - t) // 127
                runs.append((b, nb, 0, 127, t - ta))
                t += nb * 127
            else:
                n = min(127 - s, tb - t)
                runs.append((b, 1, s, n, t - ta))
                t += n
        return runs
    gqT = K.sb([64, SEQ], BF16, "gqT")
    gv = K.sb([128, NBLK, 128], BF16, "gv")
    gk = K.sb([128, NBLK, 64], F32, "gk")
    la = K.sb([128, NBLK, 64], F32, "la")
    u = K.sb([128, 32 + SEQ], F32, "u")
    S.op("pool", lambda e: e.memset(u.t[:, 0:32], 0.0), writes=[u.b])
    def ld(out_ap, in_ap, wbuf):
        S.dma("sp", lambda e: e.dma_start(out=out_ap, in_=in_ap), writes=[wbuf], reads=rdb)

    for j in range(4):
        ts = slice(j * T, (j + 1) * T)
        for (fa, fb, sap) in X1B.fm_pieces(rbf, j, SB_KT, 0, 128):
            for (b0, nb, s0, ns, off) in key_runs(j * T, (j + 1) * T):
                if nb > 1 or ns == 127:
                    dst = kT3[fa:fb, b0:b0 + nb, 1:128]
                    srcv = sap[:, off:off + nb * 127].rearrange("f (b s) -> f b s", s=127)
                else:
                    dst = kT3[fa:fb, b0, 1 + s0:1 + s0 + ns]
                    srcv = sap[:, off:off + ns]
                S.dma("sp", lambda e, dst=dst, srcv=srcv: e.dma_start(out=dst, in_=srcv), reads=rdb + [kT.b], outs=[kT.b])
        for (fa, fb, sap) in X1B.fm_pieces(rbf, j, SB_QT, 0, 128):
            tq = qT if fa < 64 else qT2
            assert (fa < 64) == (fb <= 64)
            S.dma("sp", lambda e, tq=tq, fa=fa, fb=fb, ts=ts, sap=sap: e.dma_start(out=tq.t[fa:fb, ts], in_=sap), writes=[], reads=rdb + [tq.b], outs=[tq.b])
        for (fa, fb, sap) in X1B.fm_pieces(rbf, j, SB_GQT, 0, 64):
            ld(gqT.t[fa:fb, ts], sap, gqT.b)
        for (fa, fb, sap) in X1F.fm_pieces(rf, j, SF_UT, 0, 128):
            ld(u.t[fa:fb, 32 + j * T:32 + (j + 1) * T], sap, u.b)
        for tok0 in (0, 1024):
            kb0 = j * 16 + tok0 // 128
            vsrc = X1B.tm_src(rbf, j, SB_V, 128, tok0, 1024)
            for (b0, nb, s0, ns, off) in key_runs(j * T + tok0, j * T + tok0 + 1024):
                if nb > 1 or ns == 127:
                    dst = v.t[1:128, b0:b0 + nb, :]
                    srcv = vsrc[off:off + nb * 127, :].rearrange("(b s) f -> s b f", s=127)
                else:
                    dst = v.t[1 + s0:1 + s0 + ns, b0, :]
                    srcv = vsrc[off:off + ns, :]
                S.dma("sp", lambda e, dst=dst, srcv=srcv: e.dma_start(out=dst, in_=srcv), reads=rdb + [v.b], outs=[v.b])
            ld(gv.t[:, kb0:kb0 + 8, :], X1B.tm_src(rbf, j, SB_GV, 128, tok0, 1024).rearrange("(k t) f -> t k f", t=128), gv.b)
            ld(gk.t[:, kb0:kb0 + 8, :], X1F.tm_src(rf, j, SF_GK, 64, tok0, 1024).rearrange("(k t) f -> t k f", t=128), gk.b)
            ld(la.t[:, kb0:kb0 + 8, :], X1F.tm_src(rf, j, SF_LA, 64, tok0, 1024).rearrange("(k t) f -> t k f", t=128), la.b)
    ps_z = Rot(K.ps[0:3] + K.ps[5:6])
    ps_o = K.ps[3:5]
    ps_g = K.ps[7]
    e_pool = K.pool(3, [128, 512], F32, "e")
    sp_pool = K.pool(4, [128, 512], BF16, "sp")
    w_pool = K.pool(3, [128, 512], BF16, "w")
    osb_pool = K.pool(2, [128, 512], BF16, "osb")

    def conv_gen():
        acc_pool = K.pool(2, [128, 512], F32, "cacc")
        for pc in range(SEQ // 512):
            acc = acc_pool.next()
            c0 = 2 + pc * 512
            S.op("dve", lambda e, acc=acc, c0=c0: e.tensor_scalar(out=acc.t[:, :], in0=u.t[:, c0:c0 + 512], scalar1=bw.t[:, 0:1], scalar2=bw.t[:, 31:32],
                                                                    op0=ALU.mult, op1=ALU.add), reads=[u.b, bw.b], writes=[acc.b])
            yield
            for k in range(1, 31):
                S.op("dve", lambda e, acc=acc, c0=c0, k=k: e.scalar_tensor_tensor(out=acc.t[:, :], in0=u.t[:, c0 + k:c0 + k + 512], scalar=bw.t[:, k:k + 1],
                                                                                  in1=acc.t[:, :], op0=ALU.mult, op1=ALU.add),
                     reads=[u.b, bw.b, acc.b], writes=[acc.b])
                yield
            cols = slice((pc % 4) * 512, (pc % 4 + 1) * 512)
            for (fa, fb, dst) in X2F.fm_pieces(io["s2f"], pc // 4, RF_CY, 0, 128):
                S.dma("sp", lambda e, acc=acc, dst=dst, fa=fa, fb=fb, cols=cols: e.dma_start(out=dst[:, cols], in_=acc.t[fa:fb, :]),
                      reads=[acc.b], outs=[o_f])
            yield

    def gla_gen():
        state = K.sb([64, 128], F32, "gstate")
        S.op("pool", lambda e: e.memset(state.t[:, :], 0.0), writes=[state.b])
        sbf_pool = K.pool(2, [64, 128], BF16, "gstbf")
        labf_pool = K.pool(2, [128, 64], BF16, "labf")
        ed_pool = K.pool(2, [128, 64], F32, "ged")
        kd_pool = K.pool(2, [128, 64], BF16, "gkd")
        lam_pool = K.pool(2, [64, 2], F32, "glam")
        osb_p = K.pool(2, [128, 128], F32, "gosb")
        sq_p = K.pool(2, [128, 128], BF16, "gsq")
        rs_p = K.pool(2, [128, 128], F32, "grs")
        gst_pool = K.pool(2, [128, 512], F32, "gnst")
        pg = ps_g
        r_dte = pg.t[:, 0:64]
        r_lam = pg.t[0:64, 64:66]
        r_st = pg.t[0:64, 128:256]
        r_o = pg.t[:, 256:384]
        r_ss = pg.t[:, 384:512]
        bg = pg.b
        gst = None
        for blk in range(NBLK):
            labf = labf_pool.next()
            S.op("pool", lambda e, labf=labf, blk=blk: e.tensor_copy(out=labf.t[:, :], in_=la.t[:, blk, :]), reads=[la.b], writes=[labf.b])
            yield
            S.op("pe", [lambda e, labf=labf: e.matmul(r_dte, tris.t[:, :], labf.t[:, :], start=True, stop=True)], reads=[tris.b, labf.b], writes=[bg])
            S.op("pe", [lambda e, labf=labf: e.matmul(r_lam, labf.t[:, :], cind.t[:, :], start=True, stop=True)], reads=[cind.b, labf.b], writes=[bg])
            yield
            ed = ed_pool.next()
            S.op("act", lambda e, ed=ed: e.activation(out=ed.t[:, :], in_=r_dte, func=AF.Exp), writes=[ed.b, bg])
            lam = lam_pool.next()
            S.op("act", lambda e, lam=lam: e.activation(out=lam.t[:, :], in_=r_lam, func=AF.Exp), writes=[lam.b, bg])
            yield
            kd = kd_pool.next()
            S.op("dve", lambda e, kd=kd, ed=ed, blk=blk: e.tensor_tensor(out=kd.t[:, :], in0=gk.t[:, blk, :], in1=ed.t[:, :], op=ALU.mult),
                 reads=[gk.b, ed.b], writes=[kd.b])
            yield
            for c in range(2):
                cs = slice(64 * c, 64 * c + 64)
                S.op("pe", [lambda e, kd=kd, cs=cs, blk=blk: e.matmul(r_st, kd.t[cs, :], gv.t[cs, blk, :], start=True, stop=True)],
                     reads=[kd.b, gv.b], writes=[bg])
                yield
                S.op("dve", lambda e, lam=lam, c=c: e.scalar_tensor_tensor(out=state.t[:, :], in0=state.t[:, :], scalar=lam.t[:, c:c + 1], in1=r_st,
                                                                          op0=ALU.mult, op1=ALU.add), reads=[state.b, lam.b], writes=[state.b, bg])
                sbf = sbf_pool.next()
                S.op("dve", lambda e, sbf=sbf: e.tensor_copy(out=sbf.t[:, :], in_=state.t[:, :]), reads=[state.b], writes=[sbf.b])
                yield
                tk = slice(blk * 128 + 64 * c, blk * 128 + 64 * c + 64)
                S.op("pe", [lambda e, sbf=sbf, tk=tk, c=c: e.matmul(pg.t[:, 256 + 64 * c:256 + 64 * c + 64], sbf.t[:, :], gqT.t[:, tk], start=True, stop=True)],
                     reads=[sbf.b, gqT.b], writes=[bg])
                yield
            osb = osb_p.next()
            S.op("dve", lambda e, osb=osb: e.tensor_copy(out=osb.t[:, :], in_=r_o), writes=[osb.b, bg])
            sq = sq_p.next()
            S.op("pool", lambda e, osb=osb, sq=sq: e.tensor_tensor(out=sq.t[:, :], in0=osb.t[:, :], in1=osb.t[:, :], op=ALU.mult), reads=[osb.b], writes=[sq.b])
            yield
            S.op("pe", [lambda e, sq=sq: e.matmul(r_ss, ones_bf.t[:, :], sq.t[:, :], start=True, stop=True)], reads=[ones_bf.b, sq.b], writes=[bg])
            yield
            rs = rs_p.next()
            S.op("act", lambda e, rs=rs: e.activation(out=rs.t[:, :], in_=r_ss, func=AF.Ln, bias=eps.t[:, 0:1], scale=1.0 / 128), reads=[eps.b], writes=[rs.b, bg])
            S.op("act", lambda e, rs=rs: e.activation(out=rs.t[:, :], in_=rs.t[:, :], func=AF.Exp, scale=-0.5), reads=[rs.b], writes=[rs.b])
            yield
            if blk % 4 == 0:
                gst = gst_pool.next()
            cs4 = slice((blk % 4) * 128, (blk % 4 + 1) * 128)
            S.op("dve", lambda e, gst=gst, cs4=cs4, osb=osb, rs=rs: e.scalar_tensor_tensor(out=gst.t[:, cs4], in0=osb.t[:, :], scalar=bw.t[:, 32:33], in1=rs.t[:, :],
                                                                                        op0=ALU.mult, op1=ALU.mult), reads=[osb.b, rs.b, bw.b], writes=[gst.b])
            if blk % 4 == 3:
                pc = blk // 4
                cols = slice((pc % 4) * 512, (pc % 4 + 1) * 512)
                for (fa, fb, dst) in X2F.fm_pieces(io["s2f"], pc // 4, RF_GN, 0, 128):
                    S.dma("sp", lambda e, gst=gst, dst=dst, fa=fa, fb=fb, cols=cols: e.dma_start(out=dst[:, cols], in_=gst.t[fa:fb, :]),
                          reads=[gst.b], outs=[o_f])
            yield

    import os as _os
    _mode = _os.environ.get("SIDE_MODE", "1,1")
    side = [conv_gen(), gla_gen()]
    periods = [int(t) for t in _mode.split(",")]
    tick = [0]

    def step_side(force=False):
        tick[0] += 1
        for gi, gnr in enumerate(list(side)):
            per = periods[min(gi, len(periods) - 1)]
            if per == 0 and not force:
                continue
            if force or tick[0] % max(per, 1) == 0:
                try:
                    next(gnr)
                except StopIteration:
                    side.remove(gnr)

    seq = []
    for qt in range(NQT):
        q0 = qt * 512
        bmax = (q0 + 510) // 127
        bl = list(range(bmax, -1, -1))
        for idx, b in enumerate(bl):
            seq.append((qt, b, idx == 0, idx == len(bl) - 1))
    tiles = []
    for t in seq:
        for hh in range(2):
            tiles.append((hh,) + t)
    w_pool4 = K.pool(4, [128, 512], BF16, "w4")
    st = {}

    def mask_op(buf, qt, b):
        base = 512 * qt - 127 * b + 1
        S.op("pool", lambda e: e.affine_select(out=buf.t[:, :], in_=buf.t[:, :], pattern=[[1, 512]], compare_op=ALU.is_gt,
                                                fill=0.0, base=base, channel_multiplier=-1), reads=[buf.b], writes=[buf.b])

    def is_diag(qt, b):
        return 127 * b + 127 > 512 * qt

    def zfn_of(i):
        hh, qt, b, first, last = tiles[i]
        pz = ps_z.next()
        st[i] = {"pz": pz}
        qs = slice(qt * 512, (qt + 1) * 512)
        ks = slice(b * 128, (b + 1) * 128)
        return (lambda e: e.matmul(pz.t[:, :], kT.t[:, ks], qz[hh].t[:, qs], start=True, stop=False)), [kT.b, qz[hh].b], pz

    def act_exp(i):
        pz = st[i]["pz"]
        ee = e_pool.next()
        st[i]["ee"] = ee
        S.op("act", lambda e: e.activation(out=ee.t[:, :], in_=pz.t[:, :], func=AF.Exp), reads=[pz.b], writes=[ee.b])

    def act_ln(i):
        hh, qt, b, first, last = tiles[i]
        ee = st[i]["ee"]
        sp = sp_pool.next()
        st[i]["sp"] = sp
        S.op("act", lambda e: e.activation(out=sp.t[:, :], in_=ee.t[:, :], func=AF.Ln, bias=1.0), reads=[ee.b], writes=[sp.b])
        if is_diag(qt, b):
            mask_op(sp, qt, b)
        if first:
            S.op("dve", lambda e: e.memset(sp.t[0:1, :], 0.0), reads=[sp.b], writes=[sp.b])
        else:
            pzp = st[i - 2]["pz"]
            S.op("dve", lambda e: e.tensor_copy(out=sp.t[0:1, :], in_=pzp.t[0:1, :]), reads=[sp.b], writes=[sp.b, pzp.b])

    def pv(i):
        hh, qt, b, first, last = tiles[i]
        hs = slice(64 * hh, 64 * hh + 64)
        w = st[i]["w"]
        po = ps_o[hh]
        S.op("pe", [lambda e: e.matmul(po.t[:, :], v.t[:, b, :], w.t[:, :], start=first, stop=last)], reads=[v.b, w.b], writes=[po.b])
        if last:
            osb = osb_pool.next()
            S.op("dve", lambda e: e.tensor_copy(out=osb.t[hs, :], in_=po.t[hs, :]), reads=[po.b], writes=[osb.b])
            cols = slice((qt % 4) * 512, (qt % 4 + 1) * 512)
            for (fa, fb, dst) in X2B.fm_pieces(io["s2bf"], qt // 4, 0, 64 * hh, 64):
                S.dma("sp", lambda e, dst=dst, fa=fa, fb=fb: e.dma_start(out=dst[:, cols], in_=osb.t[64 * hh + fa:64 * hh + fb, :]),
                      reads=[osb.b], outs=[o_bf])
        st.pop(i - 3, None)

    n = len(tiles)
    for i in range(min(2, n)):
        zf, zr, pz = zfn_of(i)
        S.op("pe", [zf], reads=zr, writes=[pz.b])
        act_exp(i)
        act_ln(i)
    for j in range(n):
        hh, qt, b, first, last = tiles[j]
        pz, sp = st[j]["pz"], st[j]["sp"]
        fns = [lambda e, pz=pz, sp=sp: e.matmul(pz.t[:, :], tri2.t[:, :], sp.t[:, :], start=False, stop=True)]
        rds, wrs = [tri2.b, sp.b, pz.b], [pz.b]
        if j + 2 < n:
            zf, zr, pz2 = zfn_of(j + 2)
            fns.append(zf)
            rds += zr
            wrs.append(pz2.b)
        S.op("pe", fns, reads=rds, writes=wrs)
        if j + 2 < n:
            act_exp(j + 2)
        w = w_pool4.next()
        st[j]["w"] = w
        S.op("act", lambda e, w=w, pz=pz: e.activation(out=w.t[:, :], in_=pz.t[:, :], func=AF.Exp), reads=[pz.b], writes=[w.b])
        if is_diag(qt, b):
            mask_op(w, qt, b)
        if j + 2 < n:
            act_ln(j + 2)
        if j >= 1:
            pv(j - 1)
        step_side()
    pv(n - 1)
    while side:
        step_side(force=True)


def prep_B_weights(inp, l, p):
    cw = inp["conv_w"][l][:, 128 * p:128 * p + 128].T
    cb = inp["conv_b"][l][128 * p:128 * p + 128][:, None]
    gg = inp["gla_norm_g"][l][128 * p:128 * p + 128][:, None]
    return _c(np.concatenate([cw, cb, gg], axis=1).astype(np.float32))


I32 = mybir.dt.int32
GROUPS = [[0, 1, 2, 3], [4, 5, 6, 7]]


def build_fused(L=DEPTH, phases="AXBYC"):
    nc = bass.Bass("TRN2", target_bir_lowering=False)
    dt = nc.dram_tensor

    def ein(name, shape, d=F32):
        return dt(name, list(shape), d, kind="ExternalInput").ap()

    xT = ein("xT", [D, TPC])
    cid = ein("cid", [1, 1], I32)
    bw = ein("bw", [L, 128, 33])
    wg1, wu1, wd1 = ein("wg1", [L, NF, 128, 1024]), ein("wu1", [L, NF, 128, 1024]), ein("wd1", [L, 8, 128, NF * 128])
    wfm, wtm = ein("wfm", [L, NFM, 128, 1024]), ein("wtm", [L, 128, 8 * 1280])
    walpha, balpha, gainsA = ein("walpha", [L, 16, 256]), ein("balpha", [L, 1, 256]), ein("gainsA", [L, 128, 24])
    wgt, wbr, wo = ein("wgt", [L, 24, 128, 1024]), ein("wbr", [L, 24, 128, 512]), ein("wo", [L, 8, 128, 1024])
    wg2, wu2, wd2 = ein("wg2", [L, NF, 128, 1024]), ein("wu2", [L, NF, 128, 1024]), ein("wd2", [L, 8, 128, NF * 128])
    gainsC = ein("gainsC", [L, 128, 40])
    outT = dt("outT", [D, TPC], F32, kind="ExternalOutput").ap()
    x1T = dt("x1T_i", [D, TPC], F32).ap()
    xbuf = dt("xbuf_i", [D, TPC], F32).ap()
    silur = dt("silur_i", [512, TPC], F32).ap()
    send_bf = dt("send_bf_i", X1B.shape(), BF16).ap()
    send_f = dt("send_f_i", X1F.shape(), F32).ap()
    ag_bf = dt("ag_bf_i", X1B.shape(16), BF16).ap()
    ag_f = dt("ag_f_i", X1F.shape(16), F32).ap()
    s2bf = dt("s2bf_i", X2B.shape(), BF16).ap()
    s2f = dt("s2f_i", X2F.shape(), F32).ap()
    ag2_bf = dt("ag2_bf_i", X2B.shape(16), BF16).ap()
    ag2_f = dt("ag2_f_i", X2F.shape(16), F32).ap()
    st_bf = dt("st_bf_i", X1B.shape(), BF16).ap()
    st_f = dt("st_f_i", X1F.shape(), F32).ap()
    st2_bf = dt("st2_bf_i", X2B.shape(), BF16).ap()
    st2_f = dt("st2_f_i", X2F.shape(), F32).ap()
    P = {n: Buf(n, True) for n in ("x1", "sr", "sbf", "sf", "agbf", "agf", "s2bf", "s2f", "ag2bf", "ag2f", "xbuf", "out",
                                   "stbf", "stf", "st2bf", "st2f")}
    with ExitStack() as st:
        K = Ctx(nc, st)
        S = K.S
        S.cid_ap = cid[0:1, 0:1]
        K.init_psum(8)

        def stat(ap):
            return ap

        def pick(ap16, stage, bsrc, bdst):
            v4 = ap16.rearrange("c (r d) n -> d c r n", d=4)
            S.dma("sp", lambda e: e.dma_start(out=stage, in_=v4[bass.ds(S.dyn["p"], 1)].rearrange("o c r n -> (o c) r n")),
                  reads=[bsrc], writes=[bdst])

        def gather(src, dst, bsrc, bdst):
            for c in range(src.shape[0]):
                S.dma("pool", lambda e, c=c: e.collective_compute("AllGather", ALU.bypass, replica_groups=GROUPS,
                                                                  ins=[src[c].opt()], outs=[dst[c].opt()]),
                      reads=[bsrc], outs=[bdst], sembuf=bdst, inc=1)

        for l in range(L):
            last = (l == L - 1)
            K.arena_reset()
            C = common_consts(K)
            ffn_bufs(K, C)
            if "A" in phases:
              emit_A(K, C, dict(xT=xT if l == 0 else xbuf, b_xin=None if l == 0 else P["xbuf"],
                              wg=wg1[l], wu=wu1[l], wd=wd1[l], wfm=wfm[l], wtm=wtm[l], walpha=walpha[l], balpha=balpha[l],
                              gains=gainsA[l], x1T=x1T, send_bf=send_bf, send_f=send_f, silur=silur,
                              b_x1=P["x1"], b_sbf=P["sbf"], b_sf=P["sf"], b_sr=P["sr"]))
            S.barrier()
            if "X" in phases:
                gather(send_bf, ag_bf, P["sbf"], P["agbf"])
                gather(send_f, ag_f, P["sf"], P["agf"])
                pick(ag_bf, st_bf, P["agbf"], P["stbf"])
                pick(ag_f, st_f, P["agf"], P["stf"])
            K.arena_reset()
            if "B" in phases:
              emit_B(K, dict(rbf=stat(st_bf), rf=stat(st_f), bw=bw[l], s2bf=s2bf, s2f=s2f,
                           b_rbf=P["stbf"], b_rf=P["stf"], b_s2bf=P["s2bf"], b_s2f=P["s2f"]))
            S.barrier()
            if "Y" in phases:
                gather(s2bf, ag2_bf, P["s2bf"], P["ag2bf"])
                gather(s2f, ag2_f, P["s2f"], P["ag2f"])
                pick(ag2_bf, st2_bf, P["ag2bf"], P["st2bf"])
                pick(ag2_f, st2_f, P["ag2f"], P["st2f"])
            K.arena_reset()
            C = common_consts(K)
            ffn_bufs(K, C)
            if "C" in phases:
              emit_C(K, C, dict(x1T=x1T, recv_bf=stat(st2_bf), recv_f=stat(st2_f), silur=silur,
                              wgt=wgt[l], wbr=wbr[l], wo=wo[l], wg=wg2[l], wu=wu2[l], wd=wd2[l], gains=gainsC[l],
                              x3T=outT if last else xbuf, b_x3=P["out"] if last else P["xbuf"],
                              b_x1=P["x1"], b_sr=P["sr"], b_r2bf=P["st2bf"], b_r2f=P["st2f"]))
            S.barrier()
        S.wait_all_on("sp", [P["out"]])
        S.emit()
    return nc, K


_FUSED = {}


def kernel(**inputs):
    inp = {k: np.asarray(v) for k, v in inputs.items()}
    x = inp["x"].astype(np.float32, copy=False)
    cores = list(range(NCORES))
    T = TPC
    L = DEPTH
    if "nc" not in _FUSED:
        _FUSED["nc"] = build_fused(L)[0]
    A = [prep_A_weights(inp, l) for l in range(L)]
    Cw = [prep_C_weights(inp, l) for l in range(L)]
    shared = dict(
        wg1=np.stack([a["wg"] for a in A]), wu1=np.stack([a["wu"] for a in A]), wd1=np.stack([a["wd"] for a in A]),
        wfm=np.stack([a["wfm"] for a in A]), wtm=np.stack([a["wtm"] for a in A]),
        walpha=np.stack([a["walpha"] for a in A]), balpha=np.stack([a["balpha"] for a in A]),
        gainsA=np.stack([a["gains"] for a in A]),
        wgt=np.stack([c["wgt"] for c in Cw]), wbr=np.stack([c["wbr"] for c in Cw]), wo=np.stack([c["wo"] for c in Cw]),
        wg2=np.stack([c["wg"] for c in Cw]), wu2=np.stack([c["wu"] for c in Cw]), wd2=np.stack([c["wd"] for c in Cw]),
        gainsC=np.stack([c["gains"] for c in Cw]))
    del A, Cw
    bws = [np.stack([prep_B_weights(inp, l, p) for l in range(L)]) for p in range(4)]
    in_maps = []
    for c in cores:
        m = dict(shared)
        m["xT"] = _c(x[c // 4, (c % 4) * T:(c % 4 + 1) * T, :].T)
        m["cid"] = np.array([[c % 4]], np.int32)
        m["bw"] = bws[c % 4]
        in_maps.append(m)
    res = run_bass_kernel_spmd(_FUSED["nc"], in_maps, core_ids=cores).results
    out = np.empty((BATCH, SEQ, D), np.float32)
    for c in cores:
        out[c // 4, (c % 4) * T:(c % 4 + 1) * T, :] = np.asarray(res[c]["outT"]).T
    return out
```

```python
import numpy as np
from contextlib import ExitStack
import ml_dtypes
import concourse.bass as bass
import concourse.mybir as mybir
from concourse.bass_utils import run_bass_kernel_spmd

F32 = mybir.dt.float32
BF16 = mybir.dt.bfloat16
AF = mybir.ActivationFunctionType
ALU = mybir.AluOpType

D = 1024
DFF = 2816
NF = DFF // 128
SEQ = 8192
BATCH = 2
DEPTH = 4
TPC = 2048
PASS = 1024
EPS = 1e-6
INW = 7184
NCORES = 8
POOL_ENG = "dve"

ENGS = ("pe", "act", "dve", "pool", "sp")
SEM_LIMIT = 28000


class Buf:
    __slots__ = ("name", "w", "r", "dsem", "persist")

    def __init__(self, name, persist=False):
        self.name = name
        self.w = None
        self.r = {}
        self.dsem = None
        self.persist = persist


class Sched:
    def __init__(self, nc, stack):
        self.nc = nc
        self.stack = stack
        self.sem = {}
        self.cnt = {}
        self.gen = {e: 0 for e in ENGS}
        self.seen = {e: {} for e in ENGS}
        self.q = {e: [] for e in ENGS}
        self.nsem = 0
        for e in ENGS:
            self._mksem((e, 0))
        self.n_inst = 0
        self.dma_free = []
        self.dma_bufs = []
        self.dyn = {}
        self.cid_ap = None

    def _mksem(self, key):
        self.nsem += 1
        h = self.stack.enter_context(self.nc.semaphore("s%d" % self.nsem))
        self.sem[key] = h
        self.cnt[key] = 0
        return key

    def ekey(self, e):
        return (e, self.gen[e])

    def _waits(self, e, deps):
        out = []
        seen = self.seen[e]
        for (k, v) in deps:
            if k[0] == e:
                continue
            if seen.get(k, 0) < v:
                seen[k] = v
                out.append((k, v))
        return out

    @staticmethod
    def _deps(reads, writes):
        deps = []
        for b in reads:
            if b.w is not None:
                deps.append(b.w)
        for b in writes:
            if b.w is not None:
                deps.append(b.w)
            deps.extend(b.r.items())
        return deps

    def op(self, e, fns, reads=(), writes=()):
        if not isinstance(fns, (list, tuple)):
            fns = [fns]
        waits = self._waits(e, self._deps(reads, writes))
        k = self.ekey(e)
        if self.cnt[k] >= SEM_LIMIT:
            self.gen[e] += 1
            k = self._mksem(self.ekey(e))
        self.cnt[k] += 1
        ev = (k, self.cnt[k])
        for b in reads:
            if b.r.get(k, 0) < ev[1]:
                b.r[k] = ev[1]
        for b in writes:
            b.w = ev
            b.r = {}
        self.q[e].append((waits, fns, k, 1))
        self.n_inst += len(fns)
        return ev

    def dma(self, qe, fn, reads=(), writes=(), sembuf=None, outs=(), inc=16):
        deps = self._deps(reads, writes)
        for b in outs:
            deps.extend(b.r.items())
        waits = self._waits(qe, deps)
        tgt = sembuf if sembuf is not None else (outs[0] if outs else (writes[0] if writes else reads[0]))
        if tgt.dsem is None:
            if self.dma_free and not tgt.persist:
                tgt.dsem = self.dma_free.pop()
            else:
                tgt.dsem = self._mksem(("dma", tgt.name, self.nsem))
            if not tgt.persist:
                self.dma_bufs.append(tgt)
        semkey = tgt.dsem
        self.cnt[semkey] += inc
        ev = (semkey, self.cnt[semkey])
        for b in reads:
            if b.r.get(semkey, 0) < ev[1]:
                b.r[semkey] = ev[1]
        for b in writes:
            b.w = ev
            b.r = {}
        for b in outs:
            b.w = ev
        self.q[qe].append((waits, [fn], semkey, inc))
        self.n_inst += 1
        return ev

    def barrier(self):
        deps = [(k, v) for k, v in self.cnt.items() if v > 0 and k[0] != "sp"]
        waits = self._waits("sp", deps)
        k = self.ekey("sp")
        self.cnt[k] += 1
        ev = (k, self.cnt[k])
        self.q["sp"].append((waits, [lambda e: e.nop()], k, 1))
        for e in ENGS:
            if e != "sp":
                self.q[e].append((self._waits(e, [ev]), [], None, 0))
        for b in self.dma_bufs:
            if b.dsem is not None and self.cnt[b.dsem] < SEM_LIMIT:
                self.dma_free.append(b.dsem)
            b.dsem = None
        self.dma_bufs = []

    def wait_all_on(self, e, bufs):
        deps = [b.w for b in bufs if b.w is not None]
        waits = self._waits(e, deps)
        self.q[e].append((waits, [], None, 0))

    def emit(self):
        nc = self.nc
        with nc.Block() as block:
            def mk(e):
                def body(engobj):
                    if e == "sp" and self.cid_ap is not None:
                        with engobj.register("rank_reg") as rr:
                            engobj.reg_load(rr, self.cid_ap)
                            self.dyn["p"] = engobj.snap(rr)
                            run(engobj)
                    else:
                        run(engobj)

                def run(engobj):
                    for (waits, fns, k, inc) in self.q[e]:
                        for (wk, wv) in waits:
                            engobj.wait_ge(self.sem[wk], wv)
                        n = len(fns)
                        for i, fn in enumerate(fns):
                            ins = fn(engobj)
                            if i == n - 1:
                                ins.then_inc(self.sem[k], inc)
                return body
            block.tensor(mk("pe"))
            block.scalar(mk("act"))
            block.vector(mk("dve"))
            block.gpsimd(mk("pool"))
            block.sync(mk("sp"))


class Tl:
    __slots__ = ("t", "b")

    def __init__(self, t, b):
        self.t = t
        self.b = b


class Ctx:
    def __init__(self, nc, stack):
        self.nc = nc
        self.st = stack
        self.S = Sched(nc, stack)
        self.uid = 0
        self.ps = []
        self.psi = 0
        self.outs = []

    def name(self, p):
        self.uid += 1
        return "%s_%d" % (p, self.uid)

    ARENA_LO = 16512
    ARENA_HI = 229344

    def sb(self, shape, dt, name="t"):
        n = self.name(name)
        esz = 2 if dt == BF16 else 4
        nbytes = esz
        for d_ in shape[1:]:
            nbytes *= int(d_)
        off = (getattr(self, "arena_ptr", self.ARENA_LO) + 31) // 32 * 32
        assert off + nbytes <= self.ARENA_HI, "SBUF arena overflow at %s: need %d have %d" % (n, nbytes, self.ARENA_HI - off)
        self.arena_ptr = off + nbytes
        t = self.nc.alloc_sbuf_tensor_at(n, list(shape), dt, offset=off)
        return Tl(t, Buf(n))

    def arena_reset(self):
        self.arena_ptr = self.ARENA_LO

    def pool(self, n, shape, dt, name="p"):
        return Rot([self.sb(shape, dt, name) for _ in range(n)])

    def init_psum(self, n=8):
        for i in range(n):
            nm = self.name("ps")
            t = self.st.enter_context(self.nc.psum_tensor(nm, [128, 512], F32))
            self.ps.append(Tl(t, Buf(nm)))

    def psum(self):
        p = self.ps[self.psi % len(self.ps)]
        self.psi += 1
        return p

    def dram_buf(self, name):
        return Buf(name)


class Rot:
    def __init__(self, items):
        self.items = items
        self.i = 0

    def next(self):
        it = self.items[self.i % len(self.items)]
        self.i += 1
        return it


def mm_group(K, ps, pairs, reads):
    n = len(pairs)

    def mk(i, l, r, o):
        return lambda e: e.matmul(o, l, r, start=(i == 0), stop=(i == n - 1))
    fns = [mk(i, l, r, o) for i, (l, r, o) in enumerate(pairs)]
    K.S.op("pe", fns, reads=reads, writes=[ps.b])


def rstd_from_sq(K, C, sq, sq_reads, nk, ncols, inv_n, extra=None):
    ps = K.psum()
    pairs = [(C["ones_bf"].t[:, :], sq.t[:, k, 0:ncols], ps.t[:, 0:ncols]) for k in range(nk)]
    mm_group(K, ps, pairs, reads=[C["ones_bf"].b, sq.b])
    r = C["rstd_pool"].next()
    K.S.op("act", lambda e: e.activation(out=r.t[:, 0:ncols], in_=ps.t[:, 0:ncols], func=AF.Sqrt,
                                         bias=C["eps_col"].t[:, 0:1], scale=inv_n),
           reads=[ps.b, C["eps_col"].b], writes=[r.b])
    K.S.op("dve", lambda e: e.reciprocal(out=r.t[:, 0:ncols], in_=r.t[:, 0:ncols]), reads=[r.b], writes=[r.b])
    return r


def pre_norm(K, C, x, xb, g, gcol0, h, hb, s):
    sl = slice(s * 512, (s + 1) * 512)
    sq = C["sq_pool"].next()
    K.S.op("act", lambda e: e.activation(out=sq.t[:, :, :], in_=x.t[:, :, sl], func=AF.Square),
           reads=[xb], writes=[sq.b])
    r = rstd_from_sq(K, C, sq, None, 8, 512, 1.0 / D)
    fns = [(lambda e, k=k: e.scalar_tensor_tensor(
        out=h.t[:, k, sl], in0=x.t[:, k, sl], scalar=g.t[:, gcol0 + k:gcol0 + k + 1], in1=r.t[:, :],
        op0=ALU.mult, op1=ALU.mult)) for k in range(8)]
    K.S.op("dve", fns, reads=[xb, r.b, g.b], writes=[hb])


def post_norm_residual(K, C, y, yb, g, gcol0, alpha, x, xb, s):
    sl = slice(s * 512, (s + 1) * 512)
    sq = C["sq_pool"].next()
    K.S.op("act", lambda e: e.activation(out=sq.t[:, :, :], in_=y.t[:, :, sl], func=AF.Square),
           reads=[yb], writes=[sq.b])
    r = rstd_from_sq(K, C, sq, None, 8, 512, 1.0 / D)
    for k in range(8):
        tmp = C["tmp_pool"].next()
        K.S.op("dve", lambda e, k=k, tmp=tmp: e.scalar_tensor_tensor(
            out=tmp.t[:, :], in0=y.t[:, k, sl], scalar=g.t[:, gcol0 + k:gcol0 + k + 1], in1=r.t[:, :],
            op0=ALU.mult, op1=ALU.mult), reads=[yb, r.b, g.b], writes=[tmp.b])
        K.S.op("dve", lambda e, k=k, tmp=tmp: e.scalar_tensor_tensor(
            out=x.t[:, k, sl], in0=tmp.t[:, :], scalar=float(alpha), in1=x.t[:, k, sl],
            op0=ALU.mult, op1=ALU.add), reads=[tmp.b, xb], writes=[xb])


def load_w(K, slot, src_ap, q="pool"):
    K.S.dma(q, lambda e: e.dma_start(out=slot.t[:], in_=src_ap), writes=[slot.b])


def ffn(K, C, h, hb, wg_ap, wu_ap, wd_ap, y, yb):
    a = C["a"]
    nsub = PASS // 512
    for f in range(NF):
        wg = C["wgu_pool"].next()
        load_w(K, wg, wg_ap[f].rearrange("p (k j) -> p k j", j=128))
        wu = C["wgu_pool"].next()
        load_w(K, wu, wu_ap[f].rearrange("p (k j) -> p k j", j=128))
        for s in range(nsub):
            sl = slice(s * 512, (s + 1) * 512)
            pg = K.psum()
            mm_group(K, pg, [(wg.t[:, k, :], h.t[:, k, sl], pg.t[:, :]) for k in range(8)], reads=[wg.b, hb])
            pu = K.psum()
            mm_group(K, pu, [(wu.t[:, k, :], h.t[:, k, sl], pu.t[:, :]) for k in range(8)], reads=[wu.b, hb])
            sg = C["tmp_pool"].next()
            K.S.op("act", lambda e, sg=sg, pg=pg: e.activation(out=sg.t[:, :], in_=pg.t[:, :], func=AF.Silu),
                   reads=[pg.b], writes=[sg.b])
            ab = a.b[f * nsub + s]
            K.S.op("dve", lambda e, sg=sg, pu=pu, f=f, sl=sl: e.tensor_tensor(
                out=a.t[:, f, sl], in0=sg.t[:, :], in1=pu.t[:, :], op=ALU.mult),
                reads=[sg.b, pu.b], writes=[ab])
    for d in range(8):
        wd = C["wd_pool"].next()
        load_w(K, wd, wd_ap[d].rearrange("p (c j) -> p c j", j=128))
        for s in range(nsub):
            sl = slice(s * 512, (s + 1) * 512)
            py = K.psum()
            mm_group(K, py, [(wd.t[:, c, :], a.t[:, c, sl], py.t[:, :]) for c in range(NF)],
                     reads=[wd.b] + [a.b[c * nsub + s] for c in range(NF)])
            K.S.op("act", lambda e, py=py, d=d, sl=sl: e.activation(out=y.t[:, d, sl], in_=py.t[:, :], func=AF.Copy),
                   reads=[py.b], writes=[yb])


class ATl:
    def __init__(self, t, bufs):
        self.t = t
        self.b = bufs


def common_consts(K):
    C = {}
    ones_bf = K.sb([128, 128], BF16, "ones_bf")
    K.S.op("dve", lambda e: e.memset(ones_bf.t[:, :], 1.0), writes=[ones_bf.b])
    C["ones_bf"] = ones_bf
    eps = K.sb([128, 1], F32, "eps")
    K.S.op("dve", lambda e: e.memset(eps.t[:, :], EPS), writes=[eps.b])
    C["eps_col"] = eps
    C["rstd_pool"] = K.pool(2, [128, 512], F32, "rstd")
    C["sq_pool"] = K.pool(1, [128, 8, 512], BF16, "sq")
    C["tmp_pool"] = K.pool(3, [128, 512], F32, "tmp")
    return C


def ffn_bufs(K, C):
    nsub = PASS // 512
    at = K.sb([128, NF, PASS], BF16, "a")
    C["a"] = ATl(at.t, [Buf("a%d" % i) for i in range(NF * nsub)])
    C["wgu_pool"] = K.pool(4, [128, 8, 128], BF16, "wgu")
    C["wd_pool"] = K.pool(2, [128, NF, 128], BF16, "wd")


OQ, OK_, OV, OCA, OCG, OGQ, OGK, OGV, OGR, OLR, OGT = 0, 512, 1024, 1536, 2048, 2560, 2816, 3072, 3584, 4096, 4112
FM_CHUNKS = ([("q", i, OQ + 128 * i) for i in range(4)] + [("k", i, OK_ + 128 * i) for i in range(4)] +
             [("a", i, OCA + 128 * i) for i in range(4)] + [("g", i, OCG + 128 * i) for i in range(4)] +
             [("gq", i, OGQ + 128 * i) for i in range(2)] + [("r", i, OGR + 128 * i) for i in range(4)] +
             [("lr", 0, OLR)])
NFM = len(FM_CHUNKS)
SB_QT, SB_KT, SB_V, SB_GQT, SB_GV = 0, 128 * TPC, 256 * TPC, 384 * TPC, 448 * TPC
SB_LEN = 576 * TPC
SF_UT, SF_GK, SF_LA = 0, 128 * TPC, 192 * TPC
SF_LEN = 256 * TPC


class XB:
    def __init__(self, U, nunits):
        self.U, self.nunits = U, nunits
        self.NCH = nunits // U
        self.CL = U * TPC

    def shape(self, nrank=4):
        return [self.NCH, nrank, self.CL]

    def fm_pieces(self, ap3, d, off, f0, nf):
        u0 = off // TPC + f0
        u1 = u0 + nf
        out = []
        u = u0
        while u < u1:
            c = u // self.U
            ue = min(u1, (c + 1) * self.U)
            lo = (u - c * self.U) * TPC
            out.append((u - u0, ue - u0, ap3[c, d, lo:lo + (ue - u) * TPC].rearrange("(f t) -> f t", t=TPC)))
            u = ue
        return out

    def tm_all(self, ap3, off, nfeat, tok0, ntok):
        e0 = off + tok0 * nfeat
        c = e0 // self.CL
        lo = e0 - c * self.CL
        assert lo + ntok * nfeat <= self.CL
        return ap3[c, :, lo:lo + ntok * nfeat].rearrange("d (t f) -> t d f", f=nfeat)

    def tm_src(self, ap3, j, off, nfeat, tok0, ntok):
        e0 = off + tok0 * nfeat
        c = e0 // self.CL
        lo = e0 - c * self.CL
        assert lo + ntok * nfeat <= self.CL
        return ap3[c, j, lo:lo + ntok * nfeat].rearrange("(t f) -> t f", f=nfeat)


X1B = XB(64, 576)
X1F = XB(32, 256)
X2B = XB(64, 128)
X2F = XB(32, 256)


def build_A():
    nc = bass.Bass("TRN2", target_bir_lowering=False)
    dt = nc.dram_tensor
    xT = dt("xT", [D, TPC], F32, kind="ExternalInput").ap()
    wg = dt("wg", [NF, 128, 8 * 128], F32, kind="ExternalInput").ap()
    wu = dt("wu", [NF, 128, 8 * 128], F32, kind="ExternalInput").ap()
    wd = dt("wd", [8, 128, NF * 128], F32, kind="ExternalInput").ap()
    wfm = dt("wfm", [NFM, 128, 8 * 128], F32, kind="ExternalInput").ap()
    wtm = dt("wtm", [128, 8 * 1280], F32, kind="ExternalInput").ap()
    walpha = dt("walpha", [16, 256], F32, kind="ExternalInput").ap()
    balpha = dt("balpha", [1, 256], F32, kind="ExternalInput").ap()
    gains = dt("gains", [128, 24], F32, kind="ExternalInput").ap()
    x1T = dt("x1T", [D, TPC], F32, kind="ExternalOutput").ap()
    send_bf = dt("send_bf", [4, SB_LEN], BF16, kind="ExternalOutput").ap()
    send_f = dt("send_f", [4, SF_LEN], F32, kind="ExternalOutput").ap()
    silur = dt("silur", [512, TPC], F32, kind="ExternalOutput").ap()
    with ExitStack() as st:
        K = Ctx(nc, st)
        K.init_psum(8)
        C = common_consts(K)
        ffn_bufs(K, C)
        bb = dict(b_x1=Buf("o_x1", True), b_sbf=Buf("o_sbf", True), b_sf=Buf("o_sf", True), b_sr=Buf("o_sr", True))
        K.outs += list(bb.values())
        emit_A(K, C, dict(xT=xT, wg=wg, wu=wu, wd=wd, wfm=wfm, wtm=wtm, walpha=walpha, balpha=balpha,
                          gains=gains, x1T=x1T, send_bf=send_bf, send_f=send_f, silur=silur, **bb))
        K.S.wait_all_on("sp", K.outs)
        K.S.emit()
    return nc


def emit_A(K, C, io):
    S = K.S
    nsub = PASS // 512
    g = K.sb([128, 24], F32, "gains")
    S.dma("sp", lambda e: e.dma_start(out=g.t[:, :], in_=io["gains"]), writes=[g.b])
    wtm = K.sb([128, 8, 1280], BF16, "wtm")
    load_w(K, wtm, io["wtm"].rearrange("p (k j) -> p k j", j=1280))
    wal = K.sb([16, 256], F32, "walpha")
    S.dma("sp", lambda e: e.dma_start(out=wal.t[:, :], in_=io["walpha"]), writes=[wal.b])
    bal = K.sb([1, 256], F32, "balpha")
    S.dma("sp", lambda e: e.dma_start(out=bal.t[:, :], in_=io["balpha"]), writes=[bal.b])
    ones_f = K.sb([1, 128], F32, "ones_f")
    S.op("dve", lambda e: e.memset(ones_f.t[:, :], 1.0), writes=[ones_f.b])

    x = K.sb([128, 8, PASS], F32, "x")
    h = K.sb([128, 8, PASS], BF16, "h")
    y = K.sb([128, 8, PASS], F32, "y")
    lrT = K.sb([16, PASS], F32, "lrT")
    wfm_pool = K.pool(2, [128, 8, 128], BF16, "wfm")
    st_bf = K.pool(4, [128, 512], BF16, "st_bf")
    st_f = K.pool(3, [128, 512], F32, "st_f")
    st_la = K.pool(2, [128, 256], F32, "st_la")
    xv = io["xT"].rearrange("(k p) t -> p k t", p=128)
    x1v = io["x1T"].rearrange("(k p) t -> p k t", p=128)
    sbf, sf = io["send_bf"], io["send_f"]
    o_x1, o_sbf, o_sf, o_sr = io["b_x1"], io["b_sbf"], io["b_sf"], io["b_sr"]
    b_xin = io.get("b_xin")

    def fm_out(xb, ap3, p, off, o, tsl, outbuf, pf0=0, nf=128):
        for (fa, fb, dst) in xb.fm_pieces(ap3, p, off, 0, nf):
            S.dma("sp", lambda e, dst=dst, fa=fa, fb=fb: e.dma_start(out=dst[:, tsl], in_=o.t[pf0 + fa:pf0 + fb, :]),
                  reads=[o.b], outs=[outbuf])

    for ps_ in range(TPC // PASS):
        t0 = ps_ * PASS
        S.dma("sp", lambda e, t0=t0: e.dma_start(out=x.t[:, :, :], in_=xv[:, :, t0:t0 + PASS]), writes=[x.b],
              reads=[b_xin] if b_xin is not None else [])
        for s in range(nsub):
            pre_norm(K, C, x, x.b, g, 0, h, h.b, s)
        ffn(K, C, h, h.b, io["wg"], io["wu"], io["wd"], y, y.b)
        for s in range(nsub):
            post_norm_residual(K, C, y, y.b, g, 16, 0.5, x, x.b, s)
        S.dma("sp", lambda e, t0=t0: e.dma_start(out=x1v[:, :, t0:t0 + PASS], in_=x.t[:, :, :]),
              reads=[x.b], outs=[o_x1])
        for s in range(nsub):
            pre_norm(K, C, x, x.b, g, 8, h, h.b, s)
        for ci, (kind, i, col) in enumerate(FM_CHUNKS):
            w = wfm_pool.next()
            load_w(K, w, io["wfm"][ci].rearrange("p (k j) -> p k j", j=128))
            for s in range(nsub):
                sl = slice(s * 512, (s + 1) * 512)
                tsl = slice(t0 + s * 512, t0 + (s + 1) * 512)
                pp = K.psum()
                mm_group(K, pp, [(w.t[:, k, :], h.t[:, k, sl], pp.t[:, :]) for k in range(8)], reads=[w.b, h.b])
                if kind in ("q", "k"):
                    o = st_bf.next()
                    sc = 0.125 if kind == "q" else 1.0
                    S.op("act", lambda e, o=o, pp=pp, sc=sc: e.activation(out=o.t[:, :], in_=pp.t[:, :], func=AF.Copy, scale=sc),
                         reads=[pp.b], writes=[o.b])
                    off = SB_QT if kind == "q" else SB_KT
                    fm_out(X1B, sbf, i, off, o, tsl, o_sbf)
                elif kind == "a":
                    S.op("act", lambda e, pp=pp, i=i, sl=sl: e.activation(out=y.t[:, i, sl], in_=pp.t[:, :], func=AF.Copy),
                         reads=[pp.b], writes=[y.b])
                elif kind == "g":
                    sg = C["tmp_pool"].next()
                    S.op("act", lambda e, sg=sg, pp=pp: e.activation(out=sg.t[:, :], in_=pp.t[:, :], func=AF.Sigmoid),
                         reads=[pp.b], writes=[sg.b])
                    o = st_f.next()
                    S.op("dve", lambda e, o=o, sg=sg, i=i, sl=sl: e.tensor_tensor(out=o.t[:, :], in0=y.t[:, i, sl], in1=sg.t[:, :], op=ALU.mult),
                         reads=[sg.b, y.b], writes=[o.b])
                    fm_out(X1F, sf, i, SF_UT, o, tsl, o_sf)
                elif kind == "gq":
                    o = st_bf.next()
                    S.op("act", lambda e, o=o, pp=pp: e.activation(out=o.t[:, :], in_=pp.t[:, :], func=AF.Copy, scale=0.125),
                         reads=[pp.b], writes=[o.b])
                    for half in range(2):
                        fm_out(X1B, sbf, 2 * i + half, SB_GQT, o, tsl, o_sbf, pf0=64 * half, nf=64)
                elif kind == "r":
                    o = st_f.next()
                    S.op("act", lambda e, o=o, pp=pp: e.activation(out=o.t[:, :], in_=pp.t[:, :], func=AF.Silu),
                         reads=[pp.b], writes=[o.b])
                    dst = io["silur"][128 * i:128 * i + 128, tsl]
                    S.dma("sp", lambda e, o=o, dst=dst: e.dma_start(out=dst, in_=o.t[:, :]), reads=[o.b], outs=[o_sr])
                elif kind == "lr":
                    S.op("act", lambda e, pp=pp, sl=sl: e.activation(out=lrT.t[:, sl], in_=pp.t[0:16, :], func=AF.Copy),
                         reads=[pp.b], writes=[lrT.b])
        for tb in range(PASS // 128):
            bsl = slice(tb * 128, (tb + 1) * 128)
            gsl = slice(t0 + tb * 128, t0 + (tb + 1) * 128)
            pp = K.psum()
            mm_group(K, pp, [(h.t[:, k, bsl], wtm.t[:, k, 0:512], pp.t[:, :]) for k in range(8)], reads=[wtm.b, h.b])
            o = st_bf.next()
            S.op("act", lambda e, o=o, pp=pp: e.activation(out=o.t[:, :], in_=pp.t[:, :], func=AF.Copy), reads=[pp.b], writes=[o.b])
            S.dma("sp", lambda e, o=o, gsl=gsl: e.dma_start(out=X1B.tm_all(sbf, SB_V, 128, gsl.start, 128), in_=o.t[:, :].rearrange("t (p f) -> t p f", f=128)),
                  reads=[o.b], outs=[o_sbf])
            pp = K.psum()
            mm_group(K, pp, [(h.t[:, k, bsl], wtm.t[:, k, 768:1280], pp.t[:, :]) for k in range(8)], reads=[wtm.b, h.b])
            o = st_bf.next()
            S.op("dve", lambda e, o=o, pp=pp: e.tensor_copy(out=o.t[:, :], in_=pp.t[:, :]), reads=[pp.b], writes=[o.b])
            S.dma("sp", lambda e, o=o, gsl=gsl: e.dma_start(out=X1B.tm_all(sbf, SB_GV, 128, gsl.start, 128), in_=o.t[:, :].rearrange("t (p f) -> t p f", f=128)),
                  reads=[o.b], outs=[o_sbf])
            pp = K.psum()
            mm_group(K, pp, [(h.t[:, k, bsl], wtm.t[:, k, 512:768], pp.t[:, 0:256]) for k in range(8)], reads=[wtm.b, h.b])
            o = st_la.next()
            S.op("act", lambda e, o=o, pp=pp: e.activation(out=o.t[:, :], in_=pp.t[:, 0:256], func=AF.Copy), reads=[pp.b], writes=[o.b])
            S.dma("sp", lambda e, o=o, gsl=gsl: e.dma_start(out=X1F.tm_all(sf, SF_GK, 64, gsl.start, 128), in_=o.t[:, :].rearrange("t (p f) -> t p f", f=64)),
                  reads=[o.b], outs=[o_sf])
            pp = K.psum()
            mm_group(K, pp, [(lrT.t[:, bsl], wal.t[:, :], pp.t[:, 0:256]), (ones_f.t[:, :], bal.t[:, :], pp.t[:, 0:256])],
                     reads=[lrT.b, wal.b, ones_f.b, bal.b])
            o = st_la.next()
            S.op("act", lambda e, o=o, pp=pp: e.activation(out=o.t[:, :], in_=pp.t[:, 0:256], func=AF.Exp, scale=-1.0), reads=[pp.b], writes=[o.b])
            S.op("act", lambda e, o=o: e.activation(out=o.t[:, :], in_=o.t[:, :], func=AF.Ln, bias=1.0), reads=[o.b], writes=[o.b])
            S.op("dve", lambda e, o=o: e.tensor_scalar(out=o.t[:, :], in0=o.t[:, :], scalar1=-1.0 / 16.0, scalar2=None, op0=ALU.mult),
                 reads=[o.b], writes=[o.b])
            S.dma("sp", lambda e, o=o, gsl=gsl: e.dma_start(out=X1F.tm_all(sf, SF_LA, 64, gsl.start, 128), in_=o.t[:, :].rearrange("t (p f) -> t p f", f=64)),
                  reads=[o.b], outs=[o_sf])


def _c(a):
    return np.ascontiguousarray(a)


def lay_fm_w(W, ncols_chunks):
    Kc = W.shape[0] // 128
    n = W.shape[1] // 128
    return _c(W.reshape(Kc, 128, n, 128).transpose(2, 1, 0, 3).reshape(n, 128, Kc * 128))


def lay_pk(v):
    return _c(v.reshape(-1, 128).T)


def prep_A_weights(inp, l):
    w_in = inp["w_in"][l]
    cols = []
    for (kind, i, col) in FM_CHUNKS:
        blk = np.zeros((D, 128), np.float32)
        n = 16 if kind == "lr" else 128
        blk[:, :n] = w_in[:, col:col + n]
        cols.append(blk)
    wfm = lay_fm_w(np.concatenate(cols, axis=1), NFM)
    tm = np.concatenate([w_in[:, OV:OV + 512], w_in[:, OGK:OGK + 256], w_in[:, OGV:OGV + 512]], axis=1)
    wtm = _c(tm.reshape(8, 128, 1280).transpose(1, 0, 2).reshape(128, 8 * 1280))
    gains = np.concatenate([lay_pk(inp["norm_pre"][l, 0]), lay_pk(inp["norm_pre"][l, 1]), lay_pk(inp["norm_post"][l, 0])], axis=1)
    return dict(wg=lay_fm_w(inp["ffn1_w_gate"][l], NF), wu=lay_fm_w(inp["ffn1_w_up"][l], NF),
                wd=lay_fm_w(inp["ffn1_w_down"][l], 8), wfm=wfm, wtm=wtm,
                walpha=_c(inp["gla_w_alpha"][l]), balpha=_c(inp["gla_b_alpha"][l][None, :]), gains=_c(gains))


RB_LEN = 128 * TPC
RF_CY, RF_GN = 0, 128 * TPC
RF_LEN = 256 * TPC


def build_C(dbg=0):
    nc = bass.Bass("TRN2", target_bir_lowering=False)
    dt = nc.dram_tensor
    io = dict(
        x1T=dt("x1T", [D, TPC], F32, kind="ExternalInput").ap(),
        recv_bf=dt("recv_bf", [4, RB_LEN], BF16, kind="ExternalInput").ap(),
        recv_f=dt("recv_f", [4, RF_LEN], F32, kind="ExternalInput").ap(),
        silur=dt("silur", [512, TPC], F32, kind="ExternalInput").ap(),
        wgt=dt("wgt", [24, 128, 8 * 128], F32, kind="ExternalInput").ap(),
        wbr=dt("wbr", [24, 128, 4 * 128], F32, kind="ExternalInput").ap(),
        wo=dt("wo", [8, 128, 8 * 128], F32, kind="ExternalInput").ap(),
        wg=dt("wg", [NF, 128, 8 * 128], F32, kind="ExternalInput").ap(),
        wu=dt("wu", [NF, 128, 8 * 128], F32, kind="ExternalInput").ap(),
        wd=dt("wd", [8, 128, NF * 128], F32, kind="ExternalInput").ap(),
        gains=dt("gains", [128, 40], F32, kind="ExternalInput").ap(),
        x3T=dt("x3T", [D, TPC], F32, kind="ExternalOutput").ap(),
    )
    with ExitStack() as st:
        K = Ctx(nc, st)
        K.init_psum(8)
        C = common_consts(K)
        ffn_bufs(K, C)
        r2bf_, r2f_ = io["recv_bf"], io["recv_f"]
        io["recv_bf"] = lambda p, a, b: r2bf_[p:p + 1, a:b]
        io["recv_f"] = lambda p, a, b: r2f_[p:p + 1, a:b]
        io["b_x3"] = Buf("o_x3", True)
        K.outs.append(io["b_x3"])
        emit_C(K, C, io, dbg)
        K.S.wait_all_on("sp", K.outs)
        K.S.emit()
    return nc


def emit_C(K, C, io, dbg=0):
    S = K.S
    nsub = PASS // 512
    T = TPC
    g = K.sb([128, 40], F32, "gainsC")
    S.dma("sp", lambda e: e.dma_start(out=g.t[:, :], in_=io["gains"]), writes=[g.b])
    x = K.sb([128, 8, PASS], F32, "x")
    h = K.sb([128, 8, PASS], BF16, "h")
    y = K.sb([128, 8, PASS], F32, "y")
    a = C["a"]
    sr_pool = K.pool(1, [128, 4, 512], F32, "sr")
    wgt_pool = K.pool(2, [128, 8, 128], BF16, "wgt")
    wbr_pool = K.pool(2, [128, 4, 128], BF16, "wbr")
    acc = [K.sb([128, 512], F32, "acc") for _ in range(nsub)]
    ln_mean = K.sb([128, 512], F32, "ln_mean")
    ln_msq = K.sb([128, 512], F32, "ln_msq")
    xv = io["x1T"].rearrange("(k p) t -> p k t", p=128)
    ov = io["x3T"].rearrange("(k p) t -> p k t", p=128)
    r2bf, r2f = io["recv_bf"], io["recv_f"]
    sr_v = io["silur"].rearrange("(c f) t -> f c t", f=128)
    o_x3 = io["b_x3"]
    rd2 = [io[k] for k in ("b_r2bf", "b_r2f") if io.get(k) is not None]
    rdx = [io[k] for k in ("b_x1",) if io.get(k) is not None]
    rds = [io[k] for k in ("b_sr",) if io.get(k) is not None]

    def ab(f0, f1, s):
        return [a.b[f * nsub + s] for f in range(f0, f1)]

    for ps_ in range(TPC // PASS):
        t0 = ps_ * PASS
        S.dma("sp", lambda e, t0=t0: e.dma_start(out=x.t[:, :, :], in_=xv[:, :, t0:t0 + PASS]), writes=[x.b], reads=rdx)
        for s in range(nsub):
            pre_norm(K, C, x, x.b, g, 0, h, h.b, s)
        for s in range(nsub):
            sl = slice(s * 512, (s + 1) * 512)
            tsl = slice(t0 + s * 512, t0 + (s + 1) * 512)
            for p in range(4):
                for (fa, fb, sap) in X2F.fm_pieces(r2f, p, RF_CY, 0, 128):
                    S.dma("sp", lambda e, sl=sl, tsl=tsl, p=p, fa=fa, fb=fb, sap=sap: e.dma_start(out=y.t[fa:fb, p, sl], in_=sap[:, tsl]),
                          writes=[y.b], reads=rd2)
                for (fa, fb, sap) in X2F.fm_pieces(r2f, p, RF_GN, 0, 128):
                    S.dma("sp", lambda e, sl=sl, tsl=tsl, p=p, fa=fa, fb=fb, sap=sap: e.dma_start(out=y.t[fa:fb, 4 + p, sl], in_=sap[:, tsl]),
                          writes=[y.b], reads=rd2)
                for (fa, fb, sap) in X2B.fm_pieces(r2bf, p, 0, 0, 128):
                    S.dma("sp", lambda e, sl=sl, tsl=tsl, p=p, fa=fa, fb=fb, sap=sap: e.dma_start(out=a.t[fa:fb, 8 + p, sl], in_=sap[:, tsl]),
                          writes=ab(8 + p, 9 + p, s), reads=rd2)
            sr = sr_pool.next()
            S.dma("sp", lambda e, sr=sr, tsl=tsl: e.dma_start(out=sr.t[:, :, :], in_=sr_v[:, :, tsl]), writes=[sr.b], reads=rds)
            sq = C["sq_pool"].next()
            S.op("act", lambda e, sq=sq, sl=sl: e.activation(out=sq.t[:, 0:4, :], in_=y.t[:, 0:4, sl], func=AF.Copy),
                 reads=[y.b], writes=[sq.b])
            S.op("act", lambda e, sq=sq, sl=sl: e.activation(out=sq.t[:, 4:8, :], in_=y.t[:, 0:4, sl], func=AF.Square),
                 reads=[y.b], writes=[sq.b])
            p1 = K.psum()
            mm_group(K, p1, [(C["ones_bf"].t[:, :], sq.t[:, c, :], p1.t[:, :]) for c in range(4)], reads=[C["ones_bf"].b, sq.b])
            p2 = K.psum()
            mm_group(K, p2, [(C["ones_bf"].t[:, :], sq.t[:, 4 + c, :], p2.t[:, :]) for c in range(4)], reads=[C["ones_bf"].b, sq.b])
            mean = ln_mean
            S.op("act", lambda e, mean=mean, p1=p1: e.activation(out=mean.t[:, :], in_=p1.t[:, :], func=AF.Copy, scale=1.0 / 512),
                 reads=[p1.b], writes=[mean.b])
            msq = ln_msq
            S.op("dve", lambda e, mean=mean, msq=msq: e.tensor_tensor(out=msq.t[:, :], in0=mean.t[:, :], in1=mean.t[:, :], op=ALU.mult),
                 reads=[mean.b], writes=[msq.b])
            S.op("dve", lambda e, msq=msq, p2=p2: e.scalar_tensor_tensor(out=msq.t[:, :], in0=p2.t[:, :], scalar=1.0 / 512, in1=msq.t[:, :],
                                                                       op0=ALU.mult, op1=ALU.subtract),
                 reads=[p2.b, msq.b], writes=[msq.b])
            rs = C["rstd_pool"].next()
            S.op("act", lambda e, rs=rs, msq=msq: e.activation(out=rs.t[:, :], in_=msq.t[:, :], func=AF.Sqrt, bias=C["eps_col"].t[:, 0:1], scale=1.0),
                 reads=[msq.b, C["eps_col"].b], writes=[rs.b])
            S.op("dve", lambda e, rs=rs: e.reciprocal(out=rs.t[:, :], in_=rs.t[:, :]), reads=[rs.b], writes=[rs.b])
            for c in range(4):
                t = C["tmp_pool"].next()
                S.op("dve", lambda e, t=t, c=c, sl=sl, mean=mean: e.tensor_tensor(out=t.t[:, :], in0=y.t[:, c, sl], in1=mean.t[:, :], op=ALU.subtract),
                     reads=[y.b, mean.b], writes=[t.b])
                S.op("dve", lambda e, t=t, rs=rs: e.tensor_tensor(out=t.t[:, :], in0=t.t[:, :], in1=rs.t[:, :], op=ALU.mult),
                     reads=[t.b, rs.b], writes=[t.b])
                S.op("act", lambda e, t=t, c=c, sl=sl: e.activation(out=a.t[:, c, sl], in_=t.t[:, :], func=AF.Silu,
                                                                      scale=g.t[:, 32 + c:33 + c], bias=g.t[:, 36 + c:37 + c]),
                     reads=[t.b, g.b], writes=ab(c, c + 1, s))
                S.op(POOL_ENG, lambda e, c=c, sl=sl, sr=sr: e.tensor_tensor(out=a.t[:, 4 + c, sl], in0=y.t[:, 4 + c, sl], in1=sr.t[:, c, :], op=ALU.mult),
                     reads=[y.b, sr.b], writes=ab(4 + c, 5 + c, s))
        base = {0: 8, 1: 0, 2: 4}
        for d in range(8):
            for gi in range(3):
                wb = wbr_pool.next()
                load_w(K, wb, io["wbr"][gi * 8 + d].rearrange("p (k j) -> p k j", j=128))
                wgt = wgt_pool.next()
                load_w(K, wgt, io["wgt"][gi * 8 + d].rearrange("p (k j) -> p k j", j=128))
                for s in range(nsub):
                    sl = slice(s * 512, (s + 1) * 512)
                    pb = K.psum()
                    mm_group(K, pb, [(wb.t[:, kc, :], a.t[:, base[gi] + kc, sl], pb.t[:, :]) for kc in range(4)],
                             reads=[wb.b] + ab(base[gi], base[gi] + 4, s))
                    pl = K.psum()
                    mm_group(K, pl, [(wgt.t[:, k, :], h.t[:, k, sl], pl.t[:, :]) for k in range(8)], reads=[wgt.b, h.b])
                    sig = C["tmp_pool"].next()
                    S.op("act", lambda e, sig=sig, pl=pl: e.activation(out=sig.t[:, :], in_=pl.t[:, :], func=AF.Sigmoid),
                         reads=[pl.b], writes=[sig.b])
                    if gi == 0:
                        S.op("dve", lambda e, sig=sig, pb=pb, s=s: e.tensor_tensor(out=acc[s].t[:, :], in0=sig.t[:, :], in1=pb.t[:, :], op=ALU.mult),
                             reads=[sig.b, pb.b], writes=[acc[s].b])
                    else:
                        S.op("dve", lambda e, sig=sig, pb=pb: e.tensor_tensor(out=sig.t[:, :], in0=sig.t[:, :], in1=pb.t[:, :], op=ALU.mult),
                             reads=[sig.b, pb.b], writes=[sig.b])
                        if gi == 1:
                            S.op(POOL_ENG, lambda e, sig=sig, s=s: e.tensor_tensor(out=acc[s].t[:, :], in0=acc[s].t[:, :], in1=sig.t[:, :], op=ALU.add),
                                 reads=[sig.b, acc[s].b], writes=[acc[s].b])
                        else:
                            S.op(POOL_ENG, lambda e, sig=sig, s=s, d=d, sl=sl: e.tensor_tensor(out=a.t[:, 12 + d, sl], in0=acc[s].t[:, :], in1=sig.t[:, :], op=ALU.add),
                                 reads=[sig.b, acc[s].b], writes=ab(12 + d, 13 + d, s))
        for do in range(8):
            wo = C["wgu_pool"].next()
            load_w(K, wo, io["wo"][do].rearrange("p (k j) -> p k j", j=128))
            for s in range(nsub):
                sl = slice(s * 512, (s + 1) * 512)
                pp = K.psum()
                mm_group(K, pp, [(wo.t[:, k, :], a.t[:, 12 + k, sl], pp.t[:, :]) for k in range(8)], reads=[wo.b] + ab(12, 20, s))
                S.op("act", lambda e, pp=pp, do=do, sl=sl: e.activation(out=y.t[:, do, sl], in_=pp.t[:, :], func=AF.Copy),
                     reads=[pp.b], writes=[y.b])
        if dbg == 2:
            S.dma("sp", lambda e, t0=t0: e.dma_start(out=ov[:, :, t0:t0 + PASS], in_=y.t[:, :, :]), reads=[y.b], outs=[o_x3])
            continue
        for s in range(nsub):
            post_norm_residual(K, C, y, y.b, g, 8, 1.0, x, x.b, s)
        if dbg == 1:
            S.dma("sp", lambda e, t0=t0: e.dma_start(out=ov[:, :, t0:t0 + PASS], in_=x.t[:, :, :]), reads=[x.b], outs=[o_x3])
            continue
        for s in range(nsub):
            pre_norm(K, C, x, x.b, g, 16, h, h.b, s)
        ffn(K, C, h, h.b, io["wg"], io["wu"], io["wd"], y, y.b)
        for s in range(nsub):
            post_norm_residual(K, C, y, y.b, g, 24, 0.5, x, x.b, s)
        S.dma("sp", lambda e, t0=t0: e.dma_start(out=ov[:, :, t0:t0 + PASS], in_=x.t[:, :, :]), reads=[x.b], outs=[o_x3])


def prep_C_weights(inp, l):
    w_in = inp["w_in"][l]
    wgt = lay_fm_w(w_in[:, OGT:OGT + 3072], 24)
    wb = inp["w_branch"][l]
    wbr = np.concatenate([lay_fm_w(wb[gi], 8) for gi in range(3)], axis=0)
    gains = np.concatenate([lay_pk(inp["norm_pre"][l, 1]), lay_pk(inp["norm_post"][l, 1]), lay_pk(inp["norm_pre"][l, 2]),
                            lay_pk(inp["norm_post"][l, 2]), lay_pk(inp["conv_ln_g"][l]), lay_pk(inp["conv_ln_b"][l])], axis=1)
    return dict(wgt=wgt, wbr=_c(wbr), wo=lay_fm_w(inp["w_out"][l], 8), wg=lay_fm_w(inp["ffn2_w_gate"][l], NF),
                wu=lay_fm_w(inp["ffn2_w_up"][l], NF), wd=lay_fm_w(inp["ffn2_w_down"][l], 8), gains=_c(gains))


NQT = SEQ // 512
NBLK = SEQ // 128
S2B_LEN = 128 * TPC
S2F_LEN = 256 * TPC


def build_B():
    nc = bass.Bass("TRN2", target_bir_lowering=False)
    dt = nc.dram_tensor
    io = dict(
        rbf=dt("rbf", [4, SB_LEN], BF16, kind="ExternalInput").ap(),
        rf=dt("rf", [4, SF_LEN], F32, kind="ExternalInput").ap(),
        bw=dt("bw", [128, 33], F32, kind="ExternalInput").ap(),
        s2bf=dt("s2bf", [4, S2B_LEN], BF16, kind="ExternalOutput").ap(),
        s2f=dt("s2f", [4, S2F_LEN], F32, kind="ExternalOutput").ap(),
    )
    with ExitStack() as st:
        K = Ctx(nc, st)
        K.init_psum(8)
        rbf_, rf_ = io["rbf"], io["rf"]
        io["rbf"] = lambda j, a, b: rbf_[j:j + 1, a:b]
        io["rf"] = lambda j, a, b: rf_[j:j + 1, a:b]
        io["b_s2bf"], io["b_s2f"] = Buf("o_s2bf", True), Buf("o_s2f", True)
        K.outs += [io["b_s2bf"], io["b_s2f"]]
        emit_B(K, io)
        K.S.wait_all_on("sp", K.outs)
        K.S.emit()
    return nc


def emit_B(K, io):
    S = K.S
    T = TPC
    rbf, rf = io["rbf"], io["rf"]
    o_bf, o_f = io["b_s2bf"], io["b_s2f"]
    rdb = [io[k] for k in ("b_rbf", "b_rf") if io.get(k) is not None]
    ones_bf = K.sb([128, 128], BF16, "ones_bf")
    S.op("dve", lambda e: e.memset(ones_bf.t[:, :], 1.0), writes=[ones_bf.b])
    negones = K.sb([128, 1], BF16, "negones")
    S.op("dve", lambda e: e.memset(negones.t[:, :], -1.0), writes=[negones.b])
    eps = K.sb([128, 1], F32, "eps")
    S.op("dve", lambda e: e.memset(eps.t[:, :], EPS), writes=[eps.b])
    tri2 = K.sb([128, 128], BF16, "tri2")
    S.op("pool", lambda e: e.memset(tri2.t[:, :], -1.0), writes=[tri2.b])
    S.op("pool", lambda e: e.affine_select(out=tri2.t[:, :], in_=tri2.t[:, :], pattern=[[-1, 128]], compare_op=ALU.is_ge,
                                            fill=0.0, base=0, channel_multiplier=1), reads=[tri2.b], writes=[tri2.b])
    S.op("pool", lambda e: e.memset(tri2.t[0:1, :], 1.0), reads=[tri2.b], writes=[tri2.b])
    tris = K.sb([128, 128], BF16, "tris")
    S.op("pool", lambda e: e.memset(tris.t[:, :], 1.0), writes=[tris.b])
    S.op("pool", lambda e: e.affine_select(out=tris.t[:, :], in_=tris.t[:, :], pattern=[[-1, 128]], compare_op=ALU.is_gt,
                                            fill=0.0, base=0, channel_multiplier=1), reads=[tris.b], writes=[tris.b])
    S.op("pool", lambda e: e.memset(tris.t[64:128, 0:64], 0.0), reads=[tris.b], writes=[tris.b])
    cind = K.sb([128, 2], BF16, "cind")
    S.op("pool", lambda e: e.memset(cind.t[:, :], 0.0), writes=[cind.b])
    S.op("pool", lambda e: e.memset(cind.t[0:64, 0:1], 1.0), reads=[cind.b], writes=[cind.b])
    S.op("pool", lambda e: e.memset(cind.t[64:128, 1:2], 1.0), reads=[cind.b], writes=[cind.b])
    bw = K.sb([128, 33], F32, "bw")
    S.dma("sp", lambda e: e.dma_start(out=bw.t[:, :], in_=io["bw"]), writes=[bw.b])
    qT = K.sb([128, SEQ], BF16, "qT")
    qT2 = K.sb([128, SEQ], BF16, "qT2")
    S.op("pool", lambda e: e.memset(qT.t[64:128, :], 0.0), writes=[qT.b])
    S.op("pool", lambda e: e.memset(qT2.t[0:64, :], 0.0), writes=[qT2.b])
    qz = [qT, qT2]
    KB = [Buf("kT_j%d" % j) for j in range(4)]
    VB = [Buf("v_j%d" % j) for j in range(4)]
    QB = [[Buf("q%d_j%d" % (h, j)) for j in range(4)] for h in range(2)]
    GB = [Buf("gla_j%d" % j) for j in range(4)]
    UB = [Buf("u_j%d" % j) for j in range(4)]

    def ksrc(b):
        return sorted({min(127 * b, SEQ - 1) // T, min(127 * b + 126, SEQ - 1) // T})
    NB2 = (SEQ + 126) // 127
    kT = K.sb([128, NB2 * 128], BF16, "kT2")
    v = K.sb([128, NB2, 128], BF16, "v2")
    S.op("pool", lambda e: e.memset(kT.t[:, :], 0.0), writes=[kT.b])
    S.op("pool", lambda e: e.memset(v.t[:, :, :], 0.0), writes=[v.b])
    kT3 = kT.t[:, :].rearrange("f (b s) -> f b s", s=128)

    def key_runs(ta, tb):
        runs = []
        t = ta
        while t < tb:
            b, s = t // 127, t % 127
            if s == 0 and tb - t >= 127:
                nb = (tb - t) // 127
                runs.append((b, nb, 0, 127, t - ta))
                t += nb * 127
            else:
                n = min(127 - s, tb - t)
                runs.append((b, 1, s, n, t - ta))
                t += n
        return runs
    gqT = K.sb([64, SEQ], BF16, "gqT")
    gv = K.sb([128, NBLK, 128], BF16, "gv")
    gk = K.sb([128, NBLK, 64], F32, "gk")
    la = K.sb([128, NBLK, 64], F32, "la")
    u = K.sb([128, 32 + SEQ], F32, "u")
    S.op("pool", lambda e: e.memset(u.t[:, 0:32], 0.0), writes=[u.b])
    def ld(out_ap, in_ap, jbuf, after=()):
        S.dma("sp", lambda e: e.dma_start(out=out_ap, in_=in_ap), reads=rdb + list(after), outs=[jbuf])

    for j in range(4):
        ts = slice(j * T, (j + 1) * T)
        for (fa, fb, sap) in X1B.fm_pieces(rbf, j, SB_KT, 0, 128):
            for (b0, nb, s0, ns, off) in key_runs(j * T, (j + 1) * T):
                if nb > 1 or ns == 127:
                    dst = kT3[fa:fb, b0:b0 + nb, 1:128]
                    srcv = sap[:, off:off + nb * 127].rearrange("f (b s) -> f b s", s=127)
                else:
                    dst = kT3[fa:fb, b0, 1 + s0:1 + s0 + ns]
                    srcv = sap[:, off:off + ns]
                ld(dst, srcv, KB[j], after=[kT.b])
        for (fa, fb, sap) in X1B.fm_pieces(rbf, j, SB_QT, 0, 128):
            tq = qT if fa < 64 else qT2
            assert (fa < 64) == (fb <= 64)
            ld(tq.t[fa:fb, ts], sap, QB[0 if fa < 64 else 1][j], after=[tq.b])
        for (fa, fb, sap) in X1B.fm_pieces(rbf, j, SB_GQT, 0, 64):
            ld(gqT.t[fa:fb, ts], sap, GB[j])
        for (fa, fb, sap) in X1F.fm_pieces(rf, j, SF_UT, 0, 128):
            ld(u.t[fa:fb, 32 + j * T:32 + (j + 1) * T], sap, UB[j], after=[u.b])
        for tok0 in (0, 1024):
            kb0 = j * 16 + tok0 // 128
            vsrc = X1B.tm_src(rbf, j, SB_V, 128, tok0, 1024)
            for (b0, nb, s0, ns, off) in key_runs(j * T + tok0, j * T + tok0 + 1024):
                if nb > 1 or ns == 127:
                    dst = v.t[1:128, b0:b0 + nb, :]
                    srcv = vsrc[off:off + nb * 127, :].rearrange("(b s) f -> s b f", s=127)
                else:
                    dst = v.t[1 + s0:1 + s0 + ns, b0, :]
                    srcv = vsrc[off:off + ns, :]
                ld(dst, srcv, VB[j], after=[v.b])
            ld(gv.t[:, kb0:kb0 + 8, :], X1B.tm_src(rbf, j, SB_GV, 128, tok0, 1024).rearrange("(k t) f -> t k f", t=128), GB[j])
            ld(gk.t[:, kb0:kb0 + 8, :], X1F.tm_src(rf, j, SF_GK, 64, tok0, 1024).rearrange("(k t) f -> t k f", t=128), GB[j])
            ld(la.t[:, kb0:kb0 + 8, :], X1F.tm_src(rf, j, SF_LA, 64, tok0, 1024).rearrange("(k t) f -> t k f", t=128), GB[j])
    ps_z = Rot(K.ps[0:3] + K.ps[5:6])
    ps_o = K.ps[3:5]
    ps_g = K.ps[7]
    e_pool = K.pool(3, [128, 512], F32, "e")
    sp_pool = K.pool(4, [128, 512], BF16, "sp")
    w_pool = K.pool(3, [128, 512], BF16, "w")
    osb_pool = K.pool(2, [128, 512], BF16, "osb")

    def conv_gen():
        acc_pool = K.pool(2, [128, 512], F32, "cacc")

        def ub_of(pc):
            js = {(pc * 512) // T, max(pc * 512 - 30, 0) // T, (pc * 512 + 511) // T}
            return [u.b] + [UB[j] for j in sorted(js)]
        for pc in range(SEQ // 512):
            acc = acc_pool.next()
            c0 = 2 + pc * 512
            S.op("dve", lambda e, acc=acc, c0=c0: e.tensor_scalar(out=acc.t[:, :], in0=u.t[:, c0:c0 + 512], scalar1=bw.t[:, 0:1], scalar2=bw.t[:, 31:32],
                                                                    op0=ALU.mult, op1=ALU.add), reads=ub_of(pc) + [bw.b], writes=[acc.b])
            yield
            for k in range(1, 31):
                S.op("dve", lambda e, acc=acc, c0=c0, k=k: e.scalar_tensor_tensor(out=acc.t[:, :], in0=u.t[:, c0 + k:c0 + k + 512], scalar=bw.t[:, k:k + 1],
                                                                                  in1=acc.t[:, :], op0=ALU.mult, op1=ALU.add),
                     reads=ub_of(pc) + [bw.b, acc.b], writes=[acc.b])
                yield
            cols = slice((pc % 4) * 512, (pc % 4 + 1) * 512)
            for (fa, fb, dst) in X2F.fm_pieces(io["s2f"], pc // 4, RF_CY, 0, 128):
                S.dma("sp", lambda e, acc=acc, dst=dst, fa=fa, fb=fb, cols=cols: e.dma_start(out=dst[:, cols], in_=acc.t[fa:fb, :]),
                      reads=[acc.b], outs=[o_f])
            yield

    def gla_gen():
        state = K.sb([64, 128], F32, "gstate")
        S.op("pool", lambda e: e.memset(state.t[:, :], 0.0), writes=[state.b])
        sbf_pool = K.pool(2, [64, 128], BF16, "gstbf")
        labf_pool = K.pool(2, [128, 64], BF16, "labf")
        ed_pool = K.pool(2, [128, 64], F32, "ged")
        kd_pool = K.pool(2, [128, 64], BF16, "gkd")
        lam_pool = K.pool(2, [64, 2], F32, "glam")
        osb_p = K.pool(2, [128, 128], F32, "gosb")
        sq_p = K.pool(2, [128, 128], BF16, "gsq")
        rs_p = K.pool(2, [128, 128], F32, "grs")
        gst_pool = K.pool(2, [128, 512], F32, "gnst")
        pg = ps_g
        r_dte = pg.t[:, 0:64]
        r_lam = pg.t[0:64, 64:66]
        r_st = pg.t[0:64, 128:256]
        r_o = pg.t[:, 256:384]
        r_ss = pg.t[:, 384:512]
        bg = pg.b
        gst = None
        for blk in range(NBLK):
            labf = labf_pool.next()
            S.op("pool", lambda e, labf=labf, blk=blk: e.tensor_copy(out=labf.t[:, :], in_=la.t[:, blk, :]), reads=[GB[blk // 16]], writes=[labf.b])
            yield
            S.op("pe", [lambda e, labf=labf: e.matmul(r_dte, tris.t[:, :], labf.t[:, :], start=True, stop=True)], reads=[tris.b, labf.b], writes=[bg])
            S.op("pe", [lambda e, labf=labf: e.matmul(r_lam, labf.t[:, :], cind.t[:, :], start=True, stop=True)], reads=[cind.b, labf.b], writes=[bg])
            yield
            ed = ed_pool.next()
            S.op("act", lambda e, ed=ed: e.activation(out=ed.t[:, :], in_=r_dte, func=AF.Exp), writes=[ed.b, bg])
            lam = lam_pool.next()
            S.op("act", lambda e, lam=lam: e.activation(out=lam.t[:, :], in_=r_lam, func=AF.Exp), writes=[lam.b, bg])
            yield
            kd = kd_pool.next()
            S.op("dve", lambda e, kd=kd, ed=ed, blk=blk: e.tensor_tensor(out=kd.t[:, :], in0=gk.t[:, blk, :], in1=ed.t[:, :], op=ALU.mult),
                 reads=[GB[blk // 16], ed.b], writes=[kd.b])
            yield
            for c in range(2):
                cs = slice(64 * c, 64 * c + 64)
                S.op("pe", [lambda e, kd=kd, cs=cs, blk=blk: e.matmul(r_st, kd.t[cs, :], gv.t[cs, blk, :], start=True, stop=True)],
                     reads=[kd.b, GB[blk // 16]], writes=[bg])
                yield
                S.op("dve", lambda e, lam=lam, c=c: e.scalar_tensor_tensor(out=state.t[:, :], in0=state.t[:, :], scalar=lam.t[:, c:c + 1], in1=r_st,
                                                                          op0=ALU.mult, op1=ALU.add), reads=[state.b, lam.b], writes=[state.b, bg])
                sbf = sbf_pool.next()
                S.op("dve", lambda e, sbf=sbf: e.tensor_copy(out=sbf.t[:, :], in_=state.t[:, :]), reads=[state.b], writes=[sbf.b])
                yield
                tk = slice(blk * 128 + 64 * c, blk * 128 + 64 * c + 64)
                S.op("pe", [lambda e, sbf=sbf, tk=tk, c=c: e.matmul(pg.t[:, 256 + 64 * c:256 + 64 * c + 64], sbf.t[:, :], gqT.t[:, tk], start=True, stop=True)],
                     reads=[sbf.b, GB[blk // 16]], writes=[bg])
                yield
            osb = osb_p.next()
            S.op("dve", lambda e, osb=osb: e.tensor_copy(out=osb.t[:, :], in_=r_o), writes=[osb.b, bg])
            sq = sq_p.next()
            S.op("pool", lambda e, osb=osb, sq=sq: e.tensor_tensor(out=sq.t[:, :], in0=osb.t[:, :], in1=osb.t[:, :], op=ALU.mult), reads=[osb.b], writes=[sq.b])
            yield
            S.op("pe", [lambda e, sq=sq: e.matmul(r_ss, ones_bf.t[:, :], sq.t[:, :], start=True, stop=True)], reads=[ones_bf.b, sq.b], writes=[bg])
            yield
            rs = rs_p.next()
            S.op("act", lambda e, rs=rs: e.activation(out=rs.t[:, :], in_=r_ss, func=AF.Ln, bias=eps.t[:, 0:1], scale=1.0 / 128), reads=[eps.b], writes=[rs.b, bg])
            S.op("act", lambda e, rs=rs: e.activation(out=rs.t[:, :], in_=rs.t[:, :], func=AF.Exp, scale=-0.5), reads=[rs.b], writes=[rs.b])
            yield
            if blk % 4 == 0:
                gst = gst_pool.next()
            cs4 = slice((blk % 4) * 128, (blk % 4 + 1) * 128)
            S.op("dve", lambda e, gst=gst, cs4=cs4, osb=osb, rs=rs: e.scalar_tensor_tensor(out=gst.t[:, cs4], in0=osb.t[:, :], scalar=bw.t[:, 32:33], in1=rs.t[:, :],
                                                                                        op0=ALU.mult, op1=ALU.mult), reads=[osb.b, rs.b, bw.b], writes=[gst.b])
            if blk % 4 == 3:
                pc = blk // 4
                cols = slice((pc % 4) * 512, (pc % 4 + 1) * 512)
                for (fa, fb, dst) in X2F.fm_pieces(io["s2f"], pc // 4, RF_GN, 0, 128):
                    S.dma("sp", lambda e, gst=gst, dst=dst, fa=fa, fb=fb, cols=cols: e.dma_start(out=dst[:, cols], in_=gst.t[fa:fb, :]),
                          reads=[gst.b], outs=[o_f])
            yield

    import os as _os
    _mode = _os.environ.get("SIDE_MODE", "1,2")
    side = [conv_gen(), gla_gen()]
    periods = [int(t) for t in _mode.split(",")]
    tick = [0]

    def step_side(force=False):
        tick[0] += 1
        for gi, gnr in enumerate(list(side)):
            per = periods[min(gi, len(periods) - 1)]
            if per == 0 and not force:
                continue
            if force or tick[0] % max(per, 1) == 0:
                try:
                    next(gnr)
                except StopIteration:
                    side.remove(gnr)

    seq = []
    for qt in range(NQT):
        q0 = qt * 512
        bmax = (q0 + 510) // 127
        bl = list(range(bmax, -1, -1))
        for idx, b in enumerate(bl):
            seq.append((qt, b, idx == 0, idx == len(bl) - 1))
    tiles = []
    for t in seq:
        for hh in range(2):
            tiles.append((hh,) + t)
    w_pool4 = K.pool(4, [128, 512], BF16, "w4")
    st = {}

    def mask_op(buf, qt, b):
        base = 512 * qt - 127 * b + 1
        S.op("pool", lambda e: e.affine_select(out=buf.t[:, :], in_=buf.t[:, :], pattern=[[1, 512]], compare_op=ALU.is_gt,
                                                fill=0.0, base=base, channel_multiplier=-1), reads=[buf.b], writes=[buf.b])

    def is_diag(qt, b):
        return 127 * b + 127 > 512 * qt

    def zfn_of(i):
        hh, qt, b, first, last = tiles[i]
        pz = ps_z.next()
        st[i] = {"pz": pz}
        qs = slice(qt * 512, (qt + 1) * 512)
        ks = slice(b * 128, (b + 1) * 128)
        return ((lambda e: e.matmul(pz.t[:, :], kT.t[:, ks], qz[hh].t[:, qs], start=True, stop=False)),
                [kT.b, qz[hh].b, QB[hh][qt // 4]] + [KB[j] for j in ksrc(b)], pz)

    def act_exp(i):
        pz = st[i]["pz"]
        ee = e_pool.next()
        st[i]["ee"] = ee
        S.op("act", lambda e: e.activation(out=ee.t[:, :], in_=pz.t[:, :], func=AF.Exp), reads=[pz.b], writes=[ee.b])

    def act_ln(i):
        hh, qt, b, first, last = tiles[i]
        ee = st[i]["ee"]
        sp = sp_pool.next()
        st[i]["sp"] = sp
        S.op("act", lambda e: e.activation(out=sp.t[:, :], in_=ee.t[:, :], func=AF.Ln, bias=1.0), reads=[ee.b], writes=[sp.b])
        if is_diag(qt, b):
            mask_op(sp, qt, b)
        if first:
            S.op("dve", lambda e: e.memset(sp.t[0:1, :], 0.0), reads=[sp.b], writes=[sp.b])
        else:
            pzp = st[i - 2]["pz"]
            S.op("dve", lambda e: e.tensor_copy(out=sp.t[0:1, :], in_=pzp.t[0:1, :]), reads=[sp.b], writes=[sp.b, pzp.b])

    def pv(i):
        hh, qt, b, first, last = tiles[i]
        hs = slice(64 * hh, 64 * hh + 64)
        w = st[i]["w"]
        po = ps_o[hh]
        S.op("pe", [lambda e: e.matmul(po.t[:, :], v.t[:, b, :], w.t[:, :], start=first, stop=last)], reads=[v.b, w.b] + [VB[j] for j in ksrc(b)], writes=[po.b])
        if last:
            osb = osb_pool.next()
            S.op("dve", lambda e: e.tensor_copy(out=osb.t[hs, :], in_=po.t[hs, :]), reads=[po.b], writes=[osb.b])
            cols = slice((qt % 4) * 512, (qt % 4 + 1) * 512)
            for (fa, fb, dst) in X2B.fm_pieces(io["s2bf"], qt // 4, 0, 64 * hh, 64):
                S.dma("sp", lambda e, dst=dst, fa=fa, fb=fb: e.dma_start(out=dst[:, cols], in_=osb.t[64 * hh + fa:64 * hh + fb, :]),
                      reads=[osb.b], outs=[o_bf])
        st.pop(i - 3, None)

    n = len(tiles)
    for i in range(min(2, n)):
        zf, zr, pz = zfn_of(i)
        S.op("pe", [zf], reads=zr, writes=[pz.b])
        act_exp(i)
        act_ln(i)
    for j in range(n):
        hh, qt, b, first, last = tiles[j]
        pz, sp = st[j]["pz"], st[j]["sp"]
        fns = [lambda e, pz=pz, sp=sp: e.matmul(pz.t[:, :], tri2.t[:, :], sp.t[:, :], start=False, stop=True)]
        rds, wrs = [tri2.b, sp.b, pz.b], [pz.b]
        if j + 2 < n:
            zf, zr, pz2 = zfn_of(j + 2)
            fns.append(zf)
            rds += zr
            wrs.append(pz2.b)
        S.op("pe", fns, reads=rds, writes=wrs)
        if j + 2 < n:
            act_exp(j + 2)
        w = w_pool4.next()
        st[j]["w"] = w
        S.op("act", lambda e, w=w, pz=pz: e.activation(out=w.t[:, :], in_=pz.t[:, :], func=AF.Exp), reads=[pz.b], writes=[w.b])
        if is_diag(qt, b):
            mask_op(w, qt, b)
        if j + 2 < n:
            act_ln(j + 2)
        if j >= 1:
            pv(j - 1)
        step_side()
    pv(n - 1)
    while side:
        step_side(force=True)


def prep_B_weights(inp, l, p):
    cw = inp["conv_w"][l][:, 128 * p:128 * p + 128].T
    cb = inp["conv_b"][l][128 * p:128 * p + 128][:, None]
    gg = inp["gla_norm_g"][l][128 * p:128 * p + 128][:, None]
    return _c(np.concatenate([cw, cb, gg], axis=1).astype(np.float32))


I32 = mybir.dt.int32
GROUPS = [[0, 1, 2, 3], [4, 5, 6, 7]]


def build_fused(L=DEPTH, phases="AXBYC"):
    nc = bass.Bass("TRN2", target_bir_lowering=False)
    dt = nc.dram_tensor

    def ein(name, shape, d=F32):
        return dt(name, list(shape), d, kind="ExternalInput").ap()

    xT = ein("xT", [D, TPC])
    cid = ein("cid", [1, 1], I32)
    bw = ein("bw", [L, 128, 33])
    wg1, wu1, wd1 = ein("wg1", [L, NF, 128, 1024]), ein("wu1", [L, NF, 128, 1024]), ein("wd1", [L, 8, 128, NF * 128])
    wfm, wtm = ein("wfm", [L, NFM, 128, 1024]), ein("wtm", [L, 128, 8 * 1280])
    walpha, balpha, gainsA = ein("walpha", [L, 16, 256]), ein("balpha", [L, 1, 256]), ein("gainsA", [L, 128, 24])
    wgt, wbr, wo = ein("wgt", [L, 24, 128, 1024]), ein("wbr", [L, 24, 128, 512]), ein("wo", [L, 8, 128, 1024])
    wg2, wu2, wd2 = ein("wg2", [L, NF, 128, 1024]), ein("wu2", [L, NF, 128, 1024]), ein("wd2", [L, 8, 128, NF * 128])
    gainsC = ein("gainsC", [L, 128, 40])
    outT = dt("outT", [D, TPC], F32, kind="ExternalOutput").ap()
    x1T = dt("x1T_i", [D, TPC], F32).ap()
    xbuf = dt("xbuf_i", [D, TPC], F32).ap()
    silur = dt("silur_i", [512, TPC], F32).ap()
    send_bf = dt("send_bf_i", X1B.shape(), BF16).ap()
    send_f = dt("send_f_i", X1F.shape(), F32).ap()
    ag_bf = dt("ag_bf_i", X1B.shape(16), BF16).ap()
    ag_f = dt("ag_f_i", X1F.shape(16), F32).ap()
    s2bf = dt("s2bf_i", X2B.shape(), BF16).ap()
    s2f = dt("s2f_i", X2F.shape(), F32).ap()
    ag2_bf = dt("ag2_bf_i", X2B.shape(16), BF16).ap()
    ag2_f = dt("ag2_f_i", X2F.shape(16), F32).ap()
    st_bf = dt("st_bf_i", X1B.shape(), BF16).ap()
    st_f = dt("st_f_i", X1F.shape(), F32).ap()
    st2_bf = dt("st2_bf_i", X2B.shape(), BF16).ap()
    st2_f = dt("st2_f_i", X2F.shape(), F32).ap()
    P = {n: Buf(n, True) for n in ("x1", "sr", "sbf", "sf", "agbf", "agf", "s2bf", "s2f", "ag2bf", "ag2f", "xbuf", "out",
                                   "stbf", "stf", "st2bf", "st2f")}
    with ExitStack() as st:
        K = Ctx(nc, st)
        S = K.S
        S.cid_ap = cid[0:1, 0:1]
        K.init_psum(8)

        def stat(ap):
            return ap

        def pick(ap16, stage, bsrc, bdst):
            v4 = ap16.rearrange("c (r d) n -> d c r n", d=4)
            S.dma("sp", lambda e: e.dma_start(out=stage, in_=v4[bass.ds(S.dyn["p"], 1)].rearrange("o c r n -> (o c) r n")),
                  reads=[bsrc], writes=[bdst])

        def gather(src, dst, bsrc, bdst):
            for c in range(src.shape[0]):
                S.dma("pool", lambda e, c=c: e.collective_compute("AllGather", ALU.bypass, replica_groups=GROUPS,
                                                                  ins=[src[c].opt()], outs=[dst[c].opt()]),
                      reads=[bsrc], outs=[bdst], sembuf=bdst, inc=1)

        for l in range(L):
            last = (l == L - 1)
            K.arena_reset()
            C = common_consts(K)
            ffn_bufs(K, C)
            if "A" in phases:
              emit_A(K, C, dict(xT=xT if l == 0 else xbuf, b_xin=None if l == 0 else P["xbuf"],
                              wg=wg1[l], wu=wu1[l], wd=wd1[l], wfm=wfm[l], wtm=wtm[l], walpha=walpha[l], balpha=balpha[l],
                              gains=gainsA[l], x1T=x1T, send_bf=send_bf, send_f=send_f, silur=silur,
                              b_x1=P["x1"], b_sbf=P["sbf"], b_sf=P["sf"], b_sr=P["sr"]))
            S.barrier()
            if "X" in phases:
                gather(send_bf, ag_bf, P["sbf"], P["agbf"])
                gather(send_f, ag_f, P["sf"], P["agf"])
                pick(ag_bf, st_bf, P["agbf"], P["stbf"])
                pick(ag_f, st_f, P["agf"], P["stf"])
            K.arena_reset()
            if "B" in phases:
              emit_B(K, dict(rbf=stat(st_bf), rf=stat(st_f), bw=bw[l], s2bf=s2bf, s2f=s2f,
                           b_rbf=P["stbf"], b_rf=P["stf"], b_s2bf=P["s2bf"], b_s2f=P["s2f"]))
            S.barrier()
            if "Y" in phases:
                gather(s2bf, ag2_bf, P["s2bf"], P["ag2bf"])
                gather(s2f, ag2_f, P["s2f"], P["ag2f"])
                pick(ag2_bf, st2_bf, P["ag2bf"], P["st2bf"])
                pick(ag2_f, st2_f, P["ag2f"], P["st2f"])
            K.arena_reset()
            C = common_consts(K)
            ffn_bufs(K, C)
            if "C" in phases:
              emit_C(K, C, dict(x1T=x1T, recv_bf=stat(st2_bf), recv_f=stat(st2_f), silur=silur,
                              wgt=wgt[l], wbr=wbr[l], wo=wo[l], wg=wg2[l], wu=wu2[l], wd=wd2[l], gains=gainsC[l],
                              x3T=outT if last else xbuf, b_x3=P["out"] if last else P["xbuf"],
                              b_x1=P["x1"], b_sr=P["sr"], b_r2bf=P["st2bf"], b_r2f=P["st2f"]))
            S.barrier()
        S.wait_all_on("sp", [P["out"]])
        S.emit()
    return nc, K


_FUSED = {}


def kernel(**inputs):
    inp = {k: np.asarray(v) for k, v in inputs.items()}
    x = inp["x"].astype(np.float32, copy=False)
    cores = list(range(NCORES))
    T = TPC
    L = DEPTH
    if "nc" not in _FUSED:
        _FUSED["nc"] = build_fused(L)[0]
    A = [prep_A_weights(inp, l) for l in range(L)]
    Cw = [prep_C_weights(inp, l) for l in range(L)]
    shared = dict(
        wg1=np.stack([a["wg"] for a in A]), wu1=np.stack([a["wu"] for a in A]), wd1=np.stack([a["wd"] for a in A]),
        wfm=np.stack([a["wfm"] for a in A]), wtm=np.stack([a["wtm"] for a in A]),
        walpha=np.stack([a["walpha"] for a in A]), balpha=np.stack([a["balpha"] for a in A]),
        gainsA=np.stack([a["gains"] for a in A]),
        wgt=np.stack([c["wgt"] for c in Cw]), wbr=np.stack([c["wbr"] for c in Cw]), wo=np.stack([c["wo"] for c in Cw]),
        wg2=np.stack([c["wg"] for c in Cw]), wu2=np.stack([c["wu"] for c in Cw]), wd2=np.stack([c["wd"] for c in Cw]),
        gainsC=np.stack([c["gains"] for c in Cw]))
    del A, Cw
    bws = [np.stack([prep_B_weights(inp, l, p) for l in range(L)]) for p in range(4)]
    in_maps = []
    for c in cores:
        m = dict(shared)
        m["xT"] = _c(x[c // 4, (c % 4) * T:(c % 4 + 1) * T, :].T)
        m["cid"] = np.array([[c % 4]], np.int32)
        m["bw"] = bws[c % 4]
        in_maps.append(m)
    res = run_bass_kernel_spmd(_FUSED["nc"], in_maps, core_ids=cores).results
    out = np.empty((BATCH, SEQ, D), np.float32)
    for c in cores:
        out[c // 4, (c % 4) * T:(c % 4 + 1) * T, :] = np.asarray(res[c]["outT"]).T
    return out
```

```python
import numpy as np
from contextlib import ExitStack
import ml_dtypes
import concourse.bass as bass
import concourse.mybir as mybir
from concourse.bass_utils import run_bass_kernel_spmd

F32 = mybir.dt.float32
BF16 = mybir.dt.bfloat16
AF = mybir.ActivationFunctionType
ALU = mybir.AluOpType

D = 1024
DFF = 2816
NF = DFF // 128
SEQ = 8192
BATCH = 2
DEPTH = 4
TPC = 2048
PASS = 1024
EPS = 1e-6
INW = 7184
NCORES = 8
POOL_ENG = "dve"

ENGS = ("pe", "act", "dve", "pool", "sp")
SEM_LIMIT = 28000


class Buf:
    __slots__ = ("name", "w", "r", "dsem", "persist")

    def __init__(self, name, persist=False):
        self.name = name
        self.w = None
        self.r = {}
        self.dsem = None
        self.persist = persist


class Sched:
    def __init__(self, nc, stack):
        self.nc = nc
        self.stack = stack
        self.sem = {}
        self.cnt = {}
        self.gen = {e: 0 for e in ENGS}
        self.seen = {e: {} for e in ENGS}
        self.q = {e: [] for e in ENGS}
        self.nsem = 0
        for e in ENGS:
            self._mksem((e, 0))
        self.n_inst = 0
        self.dma_free = []
        self.dma_bufs = []
        self.dyn = {}
        self.cid_ap = None

    def _mksem(self, key):
        self.nsem += 1
        h = self.stack.enter_context(self.nc.semaphore("s%d" % self.nsem))
        self.sem[key] = h
        self.cnt[key] = 0
        return key

    def ekey(self, e):
        return (e, self.gen[e])

    def _waits(self, e, deps):
        out = []
        seen = self.seen[e]
        for (k, v) in deps:
            if k[0] == e:
                continue
            if seen.get(k, 0) < v:
                seen[k] = v
                out.append((k, v))
        return out

    @staticmethod
    def _deps(reads, writes):
        deps = []
        for b in reads:
            if b.w is not None:
                deps.append(b.w)
        for b in writes:
            if b.w is not None:
                deps.append(b.w)
            deps.extend(b.r.items())
        return deps

    def op(self, e, fns, reads=(), writes=()):
        if not isinstance(fns, (list, tuple)):
            fns = [fns]
        waits = self._waits(e, self._deps(reads, writes))
        k = self.ekey(e)
        if self.cnt[k] >= SEM_LIMIT:
            self.gen[e] += 1
            k = self._mksem(self.ekey(e))
        self.cnt[k] += 1
        ev = (k, self.cnt[k])
        for b in reads:
            if b.r.get(k, 0) < ev[1]:
                b.r[k] = ev[1]
        for b in writes:
            b.w = ev
            b.r = {}
        self.q[e].append((waits, fns, k, 1))
        self.n_inst += len(fns)
        return ev

    def dma(self, qe, fn, reads=(), writes=(), sembuf=None, outs=(), inc=16):
        deps = self._deps(reads, writes)
        for b in outs:
            deps.extend(b.r.items())
        waits = self._waits(qe, deps)
        tgt = sembuf if sembuf is not None else (outs[0] if outs else (writes[0] if writes else reads[0]))
        if tgt.dsem is None:
            if self.dma_free and not tgt.persist:
                tgt.dsem = self.dma_free.pop()
            else:
                tgt.dsem = self._mksem(("dma", tgt.name, self.nsem))
            if not tgt.persist:
                self.dma_bufs.append(tgt)
        semkey = tgt.dsem
        self.cnt[semkey] += inc
        ev = (semkey, self.cnt[semkey])
        for b in reads:
            if b.r.get(semkey, 0) < ev[1]:
                b.r[semkey] = ev[1]
        for b in writes:
            b.w = ev
            b.r = {}
        for b in outs:
            b.w = ev
        self.q[qe].append((waits, [fn], semkey, inc))
        self.n_inst += 1
        return ev

    def barrier(self):
        deps = [(k, v) for k, v in self.cnt.items() if v > 0 and k[0] != "sp"]
        waits = self._waits("sp", deps)
        k = self.ekey("sp")
        self.cnt[k] += 1
        ev = (k, self.cnt[k])
        self.q["sp"].append((waits, [lambda e: e.nop()], k, 1))
        for e in ENGS:
            if e != "sp":
                self.q[e].append((self._waits(e, [ev]), [], None, 0))
        for b in self.dma_bufs:
            if b.dsem is not None and self.cnt[b.dsem] < SEM_LIMIT:
                self.dma_free.append(b.dsem)
            b.dsem = None
        self.dma_bufs = []

    def wait_all_on(self, e, bufs):
        deps = [b.w for b in bufs if b.w is not None]
        waits = self._waits(e, deps)
        self.q[e].append((waits, [], None, 0))

    def emit(self):
        nc = self.nc
        with nc.Block() as block:
            def mk(e):
                def body(engobj):
                    if e == "sp" and self.cid_ap is not None:
                        with engobj.register("rank_reg") as rr:
                            engobj.reg_load(rr, self.cid_ap)
                            self.dyn["p"] = engobj.snap(rr)
                            run(engobj)
                    else:
                        run(engobj)

                def run(engobj):
                    for (waits, fns, k, inc) in self.q[e]:
                        for (wk, wv) in waits:
                            engobj.wait_ge(self.sem[wk], wv)
                        n = len(fns)
                        for i, fn in enumerate(fns):
                            ins = fn(engobj)
                            if i == n - 1:
                                ins.then_inc(self.sem[k], inc)
                return body
            block.tensor(mk("pe"))
            block.scalar(mk("act"))
            block.vector(mk("dve"))
            block.gpsimd(mk("pool"))
            block.sync(mk("sp"))


class Tl:
    __slots__ = ("t", "b")

    def __init__(self, t, b):
        self.t = t
        self.b = b


class Ctx:
    def __init__(self, nc, stack):
        self.nc = nc
        self.st = stack
        self.S = Sched(nc, stack)
        self.uid = 0
        self.ps = []
        self.psi = 0
        self.outs = []

    def name(self, p):
        self.uid += 1
        return "%s_%d" % (p, self.uid)

    ARENA_LO = 16512
    ARENA_HI = 229344

    def sb(self, shape, dt, name="t"):
        n = self.name(name)
        esz = 2 if dt == BF16 else 4
        nbytes = esz
        for d_ in shape[1:]:
            nbytes *= int(d_)
        off = (getattr(self, "arena_ptr", self.ARENA_LO) + 31) // 32 * 32
        assert off + nbytes <= self.ARENA_HI, "SBUF arena overflow at %s: need %d have %d" % (n, nbytes, self.ARENA_HI - off)
        self.arena_ptr = off + nbytes
        t = self.nc.alloc_sbuf_tensor_at(n, list(shape), dt, offset=off)
        return Tl(t, Buf(n))

    def arena_reset(self):
        self.arena_ptr = self.ARENA_LO

    def pool(self, n, shape, dt, name="p"):
        return Rot([self.sb(shape, dt, name) for _ in range(n)])

    def init_psum(self, n=8):
        for i in range(n):
            nm = self.name("ps")
            t = self.st.enter_context(self.nc.psum_tensor(nm, [128, 512], F32))
            self.ps.append(Tl(t, Buf(nm)))

    def psum(self):
        p = self.ps[self.psi % len(self.ps)]
        self.psi += 1
        return p

    def dram_buf(self, name):
        return Buf(name)


class Rot:
    def __init__(self, items):
        self.items = items
        self.i = 0

    def next(self):
        it = self.items[self.i % len(self.items)]
        self.i += 1
        return it


def mm_group(K, ps, pairs, reads):
    n = len(pairs)

    def mk(i, l, r, o):
        return lambda e: e.matmul(o, l, r, start=(i == 0), stop=(i == n - 1))
    fns = [mk(i, l, r, o) for i, (l, r, o) in enumerate(pairs)]
    K.S.op("pe", fns, reads=reads, writes=[ps.b])


def rstd_from_sq(K, C, sq, sq_reads, nk, ncols, inv_n, extra=None):
    ps = K.psum()
    pairs = [(C["ones_bf"].t[:, :], sq.t[:, k, 0:ncols], ps.t[:, 0:ncols]) for k in range(nk)]
    mm_group(K, ps, pairs, reads=[C["ones_bf"].b, sq.b])
    r = C["rstd_pool"].next()
    K.S.op("act", lambda e: e.activation(out=r.t[:, 0:ncols], in_=ps.t[:, 0:ncols], func=AF.Sqrt,
                                         bias=C["eps_col"].t[:, 0:1], scale=inv_n),
           reads=[ps.b, C["eps_col"].b], writes=[r.b])
    K.S.op("dve", lambda e: e.reciprocal(out=r.t[:, 0:ncols], in_=r.t[:, 0:ncols]), reads=[r.b], writes=[r.b])
    return r


def pre_norm(K, C, x, xb, g, gcol0, h, hb, s):
    sl = slice(s * 512, (s + 1) * 512)
    sq = C["sq_pool"].next()
    K.S.op("act", lambda e: e.activation(out=sq.t[:, :, :], in_=x.t[:, :, sl], func=AF.Square),
           reads=[xb], writes=[sq.b])
    r = rstd_from_sq(K, C, sq, None, 8, 512, 1.0 / D)
    fns = [(lambda e, k=k: e.scalar_tensor_tensor(
        out=h.t[:, k, sl], in0=x.t[:, k, sl], scalar=g.t[:, gcol0 + k:gcol0 + k + 1], in1=r.t[:, :],
        op0=ALU.mult, op1=ALU.mult)) for k in range(8)]
    K.S.op("dve", fns, reads=[xb, r.b, g.b], writes=[hb])


def post_norm_residual(K, C, y, yb, g, gcol0, alpha, x, xb, s):
    sl = slice(s * 512, (s + 1) * 512)
    sq = C["sq_pool"].next()
    K.S.op("act", lambda e: e.activation(out=sq.t[:, :, :], in_=y.t[:, :, sl], func=AF.Square),
           reads=[yb], writes=[sq.b])
    r = rstd_from_sq(K, C, sq, None, 8, 512, 1.0 / D)
    for k in range(8):
        tmp = C["tmp_pool"].next()
        K.S.op("dve", lambda e, k=k, tmp=tmp: e.scalar_tensor_tensor(
            out=tmp.t[:, :], in0=y.t[:, k, sl], scalar=g.t[:, gcol0 + k:gcol0 + k + 1], in1=r.t[:, :],
            op0=ALU.mult, op1=ALU.mult), reads=[yb, r.b, g.b], writes=[tmp.b])
        K.S.op("dve", lambda e, k=k, tmp=tmp: e.scalar_tensor_tensor(
            out=x.t[:, k, sl], in0=tmp.t[:, :], scalar=float(alpha), in1=x.t[:, k, sl],
            op0=ALU.mult, op1=ALU.add), reads=[tmp.b, xb], writes=[xb])


def load_w(K, slot, src_ap, q="pool"):
    K.S.dma(q, lambda e: e.dma_start(out=slot.t[:], in_=src_ap), writes=[slot.b])


def ffn(K, C, h, hb, wg_ap, wu_ap, wd_ap, y, yb):
    a = C["a"]
    nsub = PASS // 512
    for f in range(NF):
        wg = C["wgu_pool"].next()
        load_w(K, wg, wg_ap[f].rearrange("p (k j) -> p k j", j=128))
        wu = C["wgu_pool"].next()
        load_w(K, wu, wu_ap[f].rearrange("p (k j) -> p k j", j=128))
        for s in range(nsub):
            sl = slice(s * 512, (s + 1) * 512)
            pg = K.psum()
            mm_group(K, pg, [(wg.t[:, k, :], h.t[:, k, sl], pg.t[:, :]) for k in range(8)], reads=[wg.b, hb])
            pu = K.psum()
            mm_group(K, pu, [(wu.t[:, k, :], h.t[:, k, sl], pu.t[:, :]) for k in range(8)], reads=[wu.b, hb])
            sg = C["tmp_pool"].next()
            K.S.op("act", lambda e, sg=sg, pg=pg: e.activation(out=sg.t[:, :], in_=pg.t[:, :], func=AF.Silu),
                   reads=[pg.b], writes=[sg.b])
            ab = a.b[f * nsub + s]
            K.S.op("dve", lambda e, sg=sg, pu=pu, f=f, sl=sl: e.tensor_tensor(
                out=a.t[:, f, sl], in0=sg.t[:, :], in1=pu.t[:, :], op=ALU.mult),
                reads=[sg.b, pu.b], writes=[ab])
    for d in range(8):
        wd = C["wd_pool"].next()
        load_w(K, wd, wd_ap[d].rearrange("p (c j) -> p c j", j=128))
        for s in range(nsub):
            sl = slice(s * 512, (s + 1) * 512)
            py = K.psum()
            mm_group(K, py, [(wd.t[:, c, :], a.t[:, c, sl], py.t[:, :]) for c in range(NF)],
                     reads=[wd.b] + [a.b[c * nsub + s] for c in range(NF)])
            K.S.op("act", lambda e, py=py, d=d, sl=sl: e.activation(out=y.t[:, d, sl], in_=py.t[:, :], func=AF.Copy),
                   reads=[py.b], writes=[yb])


class ATl:
    def __init__(self, t, bufs):
        self.t = t
        self.b = bufs


def common_consts(K):
    C = {}
    ones_bf = K.sb([128, 128], BF16, "ones_bf")
    K.S.op("dve", lambda e: e.memset(ones_bf.t[:, :], 1.0), writes=[ones_bf.b])
    C["ones_bf"] = ones_bf
    eps = K.sb([128, 1], F32, "eps")
    K.S.op("dve", lambda e: e.memset(eps.t[:, :], EPS), writes=[eps.b])
    C["eps_col"] = eps
    C["rstd_pool"] = K.pool(2, [128, 512], F32, "rstd")
    C["sq_pool"] = K.pool(1, [128, 8, 512], BF16, "sq")
    C["tmp_pool"] = K.pool(3, [128, 512], F32, "tmp")
    return C


def ffn_bufs(K, C):
    nsub = PASS // 512
    at = K.sb([128, NF, PASS], BF16, "a")
    C["a"] = ATl(at.t, [Buf("a%d" % i) for i in range(NF * nsub)])
    C["wgu_pool"] = K.pool(4, [128, 8, 128], BF16, "wgu")
    C["wd_pool"] = K.pool(2, [128, NF, 128], BF16, "wd")


OQ, OK_, OV, OCA, OCG, OGQ, OGK, OGV, OGR, OLR, OGT = 0, 512, 1024, 1536, 2048, 2560, 2816, 3072, 3584, 4096, 4112
FM_CHUNKS = ([("q", i, OQ + 128 * i) for i in range(4)] + [("k", i, OK_ + 128 * i) for i in range(4)] +
             [("a", i, OCA + 128 * i) for i in range(4)] + [("g", i, OCG + 128 * i) for i in range(4)] +
             [("gq", i, OGQ + 128 * i) for i in range(2)] + [("r", i, OGR + 128 * i) for i in range(4)] +
             [("lr", 0, OLR)])
NFM = len(FM_CHUNKS)
SB_QT, SB_KT, SB_V, SB_GQT, SB_GV = 0, 128 * TPC, 256 * TPC, 384 * TPC, 448 * TPC
SB_LEN = 576 * TPC
SF_UT, SF_GK, SF_LA = 0, 128 * TPC, 192 * TPC
SF_LEN = 256 * TPC


class XB:
    def __init__(self, U, nunits):
        self.U, self.nunits = U, nunits
        self.NCH = nunits // U
        self.CL = U * TPC

    def shape(self, nrank=4):
        return [self.NCH, nrank, self.CL]

    def fm_pieces(self, ap3, d, off, f0, nf):
        u0 = off // TPC + f0
        u1 = u0 + nf
        out = []
        u = u0
        while u < u1:
            c = u // self.U
            ue = min(u1, (c + 1) * self.U)
            lo = (u - c * self.U) * TPC
            out.append((u - u0, ue - u0, ap3[c, d, lo:lo + (ue - u) * TPC].rearrange("(f t) -> f t", t=TPC)))
            u = ue
        return out

    def tm_all(self, ap3, off, nfeat, tok0, ntok):
        e0 = off + tok0 * nfeat
        c = e0 // self.CL
        lo = e0 - c * self.CL
        assert lo + ntok * nfeat <= self.CL
        return ap3[c, :, lo:lo + ntok * nfeat].rearrange("d (t f) -> t d f", f=nfeat)

    def tm_src(self, ap3, j, off, nfeat, tok0, ntok):
        e0 = off + tok0 * nfeat
        c = e0 // self.CL
        lo = e0 - c * self.CL
        assert lo + ntok * nfeat <= self.CL
        return ap3[c, j, lo:lo + ntok * nfeat].rearrange("(t f) -> t f", f=nfeat)


X1B = XB(64, 576)
X1F = XB(32, 256)
X2B = XB(64, 128)
X2F = XB(32, 256)


def build_A():
    nc = bass.Bass("TRN2", target_bir_lowering=False)
    dt = nc.dram_tensor
    xT = dt("xT", [D, TPC], F32, kind="ExternalInput").ap()
    wg = dt("wg", [NF, 128, 8 * 128], F32, kind="ExternalInput").ap()
    wu = dt("wu", [NF, 128, 8 * 128], F32, kind="ExternalInput").ap()
    wd = dt("wd", [8, 128, NF * 128], F32, kind="ExternalInput").ap()
    wfm = dt("wfm", [NFM, 128, 8 * 128], F32, kind="ExternalInput").ap()
    wtm = dt("wtm", [128, 8 * 1280], F32, kind="ExternalInput").ap()
    walpha = dt("walpha", [16, 256], F32, kind="ExternalInput").ap()
    balpha = dt("balpha", [1, 256], F32, kind="ExternalInput").ap()
    gains = dt("gains", [128, 24], F32, kind="ExternalInput").ap()
    x1T = dt("x1T", [D, TPC], F32, kind="ExternalOutput").ap()
    send_bf = dt("send_bf", [4, SB_LEN], BF16, kind="ExternalOutput").ap()
    send_f = dt("send_f", [4, SF_LEN], F32, kind="ExternalOutput").ap()
    silur = dt("silur", [512, TPC], F32, kind="ExternalOutput").ap()
    with ExitStack() as st:
        K = Ctx(nc, st)
        K.init_psum(8)
        C = common_consts(K)
        ffn_bufs(K, C)
        bb = dict(b_x1=Buf("o_x1", True), b_sbf=Buf("o_sbf", True), b_sf=Buf("o_sf", True), b_sr=Buf("o_sr", True))
        K.outs += list(bb.values())
        emit_A(K, C, dict(xT=xT, wg=wg, wu=wu, wd=wd, wfm=wfm, wtm=wtm, walpha=walpha, balpha=balpha,
                          gains=gains, x1T=x1T, send_bf=send_bf, send_f=send_f, silur=silur, **bb))
        K.S.wait_all_on("sp", K.outs)
        K.S.emit()
    return nc


def emit_A(K, C, io):
    S = K.S
    nsub = PASS // 512
    g = K.sb([128, 24], F32, "gains")
    S.dma("sp", lambda e: e.dma_start(out=g.t[:, :], in_=io["gains"]), writes=[g.b])
    wtm = K.sb([128, 8, 1280], BF16, "wtm")
    load_w(K, wtm, io["wtm"].rearrange("p (k j) -> p k j", j=1280))
    wal = K.sb([16, 256], F32, "walpha")
    S.dma("sp", lambda e: e.dma_start(out=wal.t[:, :], in_=io["walpha"]), writes=[wal.b])
    bal = K.sb([1, 256], F32, "balpha")
    S.dma("sp", lambda e: e.dma_start(out=bal.t[:, :], in_=io["balpha"]), writes=[bal.b])
    ones_f = K.sb([1, 128], F32, "ones_f")
    S.op("dve", lambda e: e.memset(ones_f.t[:, :], 1.0), writes=[ones_f.b])

    x = K.sb([128, 8, PASS], F32, "x")
    h = K.sb([128, 8, PASS], BF16, "h")
    y = K.sb([128, 8, PASS], F32, "y")
    lrT = K.sb([16, PASS], F32, "lrT")
    wfm_pool = K.pool(2, [128, 8, 128], BF16, "wfm")
    st_bf = K.pool(4, [128, 512], BF16, "st_bf")
    st_f = K.pool(3, [128, 512], F32, "st_f")
    st_la = K.pool(2, [128, 256], F32, "st_la")
    xv = io["xT"].rearrange("(k p) t -> p k t", p=128)
    x1v = io["x1T"].rearrange("(k p) t -> p k t", p=128)
    sbf, sf = io["send_bf"], io["send_f"]
    o_x1, o_sbf, o_sf, o_sr = io["b_x1"], io["b_sbf"], io["b_sf"], io["b_sr"]
    b_xin = io.get("b_xin")

    def fm_out(xb, ap3, p, off, o, tsl, outbuf, pf0=0, nf=128):
        for (fa, fb, dst) in xb.fm_pieces(ap3, p, off, 0, nf):
            S.dma("sp", lambda e, dst=dst, fa=fa, fb=fb: e.dma_start(out=dst[:, tsl], in_=o.t[pf0 + fa:pf0 + fb, :]),
                  reads=[o.b], outs=[outbuf])

    for ps_ in range(TPC // PASS):
        t0 = ps_ * PASS
        S.dma("sp", lambda e, t0=t0: e.dma_start(out=x.t[:, :, :], in_=xv[:, :, t0:t0 + PASS]), writes=[x.b],
              reads=[b_xin] if b_xin is not None else [])
        for s in range(nsub):
            pre_norm(K, C, x, x.b, g, 0, h, h.b, s)
        ffn(K, C, h, h.b, io["wg"], io["wu"], io["wd"], y, y.b)
        for s in range(nsub):
            post_norm_residual(K, C, y, y.b, g, 16, 0.5, x, x.b, s)
        S.dma("sp", lambda e, t0=t0: e.dma_start(out=x1v[:, :, t0:t0 + PASS], in_=x.t[:, :, :]),
              reads=[x.b], outs=[o_x1])
        for s in range(nsub):
            pre_norm(K, C, x, x.b, g, 8, h, h.b, s)
        for ci, (kind, i, col) in enumerate(FM_CHUNKS):
            w = wfm_pool.next()
            load_w(K, w, io["wfm"][ci].rearrange("p (k j) -> p k j", j=128))
            for s in range(nsub):
                sl = slice(s * 512, (s + 1) * 512)
                tsl = slice(t0 + s * 512, t0 + (s + 1) * 512)
                pp = K.psum()
                mm_group(K, pp, [(w.t[:, k, :], h.t[:, k, sl], pp.t[:, :]) for k in range(8)], reads=[w.b, h.b])
                if kind in ("q", "k"):
                    o = st_bf.next()
                    sc = 0.125 if kind == "q" else 1.0
                    S.op("act", lambda e, o=o, pp=pp, sc=sc: e.activation(out=o.t[:, :], in_=pp.t[:, :], func=AF.Copy, scale=sc),
                         reads=[pp.b], writes=[o.b])
                    off = SB_QT if kind == "q" else SB_KT
                    fm_out(X1B, sbf, i, off, o, tsl, o_sbf)
                elif kind == "a":
                    S.op("act", lambda e, pp=pp, i=i, sl=sl: e.activation(out=y.t[:, i, sl], in_=pp.t[:, :], func=AF.Copy),
                         reads=[pp.b], writes=[y.b])
                elif kind == "g":
                    sg = C["tmp_pool"].next()
                    S.op("act", lambda e, sg=sg, pp=pp: e.activation(out=sg.t[:, :], in_=pp.t[:, :], func=AF.Sigmoid),
                         reads=[pp.b], writes=[sg.b])
                    o = st_f.next()
                    S.op("dve", lambda e, o=o, sg=sg, i=i, sl=sl: e.tensor_tensor(out=o.t[:, :], in0=y.t[:, i, sl], in1=sg.t[:, :], op=ALU.mult),
                         reads=[sg.b, y.b], writes=[o.b])
                    fm_out(X1F, sf, i, SF_UT, o, tsl, o_sf)
                elif kind == "gq":
                    o = st_bf.next()
                    S.op("act", lambda e, o=o, pp=pp: e.activation(out=o.t[:, :], in_=pp.t[:, :], func=AF.Copy, scale=0.125),
                         reads=[pp.b], writes=[o.b])
                    for half in range(2):
                        fm_out(X1B, sbf, 2 * i + half, SB_GQT, o, tsl, o_sbf, pf0=64 * half, nf=64)
                elif kind == "r":
                    o = st_f.next()
                    S.op("act", lambda e, o=o, pp=pp: e.activation(out=o.t[:, :], in_=pp.t[:, :], func=AF.Silu),
                         reads=[pp.b], writes=[o.b])
                    dst = io["silur"][128 * i:128 * i + 128, tsl]
                    S.dma("sp", lambda e, o=o, dst=dst: e.dma_start(out=dst, in_=o.t[:, :]), reads=[o.b], outs=[o_sr])
                elif kind == "lr":
                    S.op("act", lambda e, pp=pp, sl=sl: e.activation(out=lrT.t[:, sl], in_=pp.t[0:16, :], func=AF.Copy),
                         reads=[pp.b], writes=[lrT.b])
        for tb in range(PASS // 128):
            bsl = slice(tb * 128, (tb + 1) * 128)
            gsl = slice(t0 + tb * 128, t0 + (tb + 1) * 128)
            pp = K.psum()
            mm_group(K, pp, [(h.t[:, k, bsl], wtm.t[:, k, 0:512], pp.t[:, :]) for k in range(8)], reads=[wtm.b, h.b])
            o = st_bf.next()
            S.op("act", lambda e, o=o, pp=pp: e.activation(out=o.t[:, :], in_=pp.t[:, :], func=AF.Copy), reads=[pp.b], writes=[o.b])
            S.dma("sp", lambda e, o=o, gsl=gsl: e.dma_start(out=X1B.tm_all(sbf, SB_V, 128, gsl.start, 128), in_=o.t[:, :].rearrange("t (p f) -> t p f", f=128)),
                  reads=[o.b], outs=[o_sbf])
            pp = K.psum()
            mm_group(K, pp, [(h.t[:, k, bsl], wtm.t[:, k, 768:1280], pp.t[:, :]) for k in range(8)], reads=[wtm.b, h.b])
            o = st_bf.next()
            S.op("dve", lambda e, o=o, pp=pp: e.tensor_copy(out=o.t[:, :], in_=pp.t[:, :]), reads=[pp.b], writes=[o.b])
            S.dma("sp", lambda e, o=o, gsl=gsl: e.dma_start(out=X1B.tm_all(sbf, SB_GV, 128, gsl.start, 128), in_=o.t[:, :].rearrange("t (p f) -> t p f", f=128)),
                  reads=[o.b], outs=[o_sbf])
            pp = K.psum()
            mm_group(K, pp, [(h.t[:, k, bsl], wtm.t[:, k, 512:768], pp.t[:, 0:256]) for k in range(8)], reads=[wtm.b, h.b])
            o = st_la.next()
            S.op("act", lambda e, o=o, pp=pp: e.activation(out=o.t[:, :], in_=pp.t[:, 0:256], func=AF.Copy), reads=[pp.b], writes=[o.b])
            S.dma("sp", lambda e, o=o, gsl=gsl: e.dma_start(out=X1F.tm_all(sf, SF_GK, 64, gsl.start, 128), in_=o.t[:, :].rearrange("t (p f) -> t p f", f=64)),
                  reads=[o.b], outs=[o_sf])
            pp = K.psum()
            mm_group(K, pp, [(lrT.t[:, bsl], wal.t[:, :], pp.t[:, 0:256]), (ones_f.t[:, :], bal.t[:, :], pp.t[:, 0:256])],
                     reads=[lrT.b, wal.b, ones_f.b, bal.b])
            o = st_la.next()
            S.op("act", lambda e, o=o, pp=pp: e.activation(out=o.t[:, :], in_=pp.t[:, 0:256], func=AF.Exp, scale=-1.0), reads=[pp.b], writes=[o.b])
            S.op("act", lambda e, o=o: e.activation(out=o.t[:, :], in_=o.t[:, :], func=AF.Ln, bias=1.0), reads=[o.b], writes=[o.b])
            S.op("dve", lambda e, o=o: e.tensor_scalar(out=o.t[:, :], in0=o.t[:, :], scalar1=-1.0 / 16.0, scalar2=None, op0=ALU.mult),
                 reads=[o.b], writes=[o.b])
            S.dma("sp", lambda e, o=o, gsl=gsl: e.dma_start(out=X1F.tm_all(sf, SF_LA, 64, gsl.start, 128), in_=o.t[:, :].rearrange("t (p f) -> t p f", f=64)),
                  reads=[o.b], outs=[o_sf])


def _c(a):
    return np.ascontiguousarray(a)


def lay_fm_w(W, ncols_chunks):
    Kc = W.shape[0] // 128
    n = W.shape[1] // 128
    return _c(W.reshape(Kc, 128, n, 128).transpose(2, 1, 0, 3).reshape(n, 128, Kc * 128))


def lay_pk(v):
    return _c(v.reshape(-1, 128).T)


def prep_A_weights(inp, l):
    w_in = inp["w_in"][l]
    cols = []
    for (kind, i, col) in FM_CHUNKS:
        blk = np.zeros((D, 128), np.float32)
        n = 16 if kind == "lr" else 128
        blk[:, :n] = w_in[:, col:col + n]
        cols.append(blk)
    wfm = lay_fm_w(np.concatenate(cols, axis=1), NFM)
    tm = np.concatenate([w_in[:, OV:OV + 512], w_in[:, OGK:OGK + 256], w_in[:, OGV:OGV + 512]], axis=1)
    wtm = _c(tm.reshape(8, 128, 1280).transpose(1, 0, 2).reshape(128, 8 * 1280))
    gains = np.concatenate([lay_pk(inp["norm_pre"][l, 0]), lay_pk(inp["norm_pre"][l, 1]), lay_pk(inp["norm_post"][l, 0])], axis=1)
    return dict(wg=lay_fm_w(inp["ffn1_w_gate"][l], NF), wu=lay_fm_w(inp["ffn1_w_up"][l], NF),
                wd=lay_fm_w(inp["ffn1_w_down"][l], 8), wfm=wfm, wtm=wtm,
                walpha=_c(inp["gla_w_alpha"][l]), balpha=_c(inp["gla_b_alpha"][l][None, :]), gains=_c(gains))


RB_LEN = 128 * TPC
RF_CY, RF_GN = 0, 128 * TPC
RF_LEN = 256 * TPC


def build_C(dbg=0):
    nc = bass.Bass("TRN2", target_bir_lowering=False)
    dt = nc.dram_tensor
    io = dict(
        x1T=dt("x1T", [D, TPC], F32, kind="ExternalInput").ap(),
        recv_bf=dt("recv_bf", [4, RB_LEN], BF16, kind="ExternalInput").ap(),
        recv_f=dt("recv_f", [4, RF_LEN], F32, kind="ExternalInput").ap(),
        silur=dt("silur", [512, TPC], F32, kind="ExternalInput").ap(),
        wgt=dt("wgt", [24, 128, 8 * 128], F32, kind="ExternalInput").ap(),
        wbr=dt("wbr", [24, 128, 4 * 128], F32, kind="ExternalInput").ap(),
        wo=dt("wo", [8, 128, 8 * 128], F32, kind="ExternalInput").ap(),
        wg=dt("wg", [NF, 128, 8 * 128], F32, kind="ExternalInput").ap(),
        wu=dt("wu", [NF, 128, 8 * 128], F32, kind="ExternalInput").ap(),
        wd=dt("wd", [8, 128, NF * 128], F32, kind="ExternalInput").ap(),
        gains=dt("gains", [128, 40], F32, kind="ExternalInput").ap(),
        x3T=dt("x3T", [D, TPC], F32, kind="ExternalOutput").ap(),
    )
    with ExitStack() as st:
        K = Ctx(nc, st)
        K.init_psum(8)
        C = common_consts(K)
        ffn_bufs(K, C)
        r2bf_, r2f_ = io["recv_bf"], io["recv_f"]
        io["recv_bf"] = lambda p, a, b: r2bf_[p:p + 1, a:b]
        io["recv_f"] = lambda p, a, b: r2f_[p:p + 1, a:b]
        io["b_x3"] = Buf("o_x3", True)
        K.outs.append(io["b_x3"])
        emit_C(K, C, io, dbg)
        K.S.wait_all_on("sp", K.outs)
        K.S.emit()
    return nc


def emit_C(K, C, io, dbg=0):
    S = K.S
    nsub = PASS // 512
    T = TPC
    g = K.sb([128, 40], F32, "gainsC")
    S.dma("sp", lambda e: e.dma_start(out=g.t[:, :], in_=io["gains"]), writes=[g.b])
    x = K.sb([128, 8, PASS], F32, "x")
    h = K.sb([128, 8, PASS], BF16, "h")
    y = K.sb([128, 8, PASS], F32, "y")
    a = C["a"]
    sr_pool = K.pool(1, [128, 4, 512], F32, "sr")
    wgt_pool = K.pool(2, [128, 8, 128], BF16, "wgt")
    wbr_pool = K.pool(2, [128, 4, 128], BF16, "wbr")
    acc = [K.sb([128, 512], F32, "acc") for _ in range(nsub)]
    ln_mean = K.sb([128, 512], F32, "ln_mean")
    ln_msq = K.sb([128, 512], F32, "ln_msq")
    xv = io["x1T"].rearrange("(k p) t -> p k t", p=128)
    ov = io["x3T"].rearrange("(k p) t -> p k t", p=128)
    r2bf, r2f = io["recv_bf"], io["recv_f"]
    sr_v = io["silur"].rearrange("(c f) t -> f c t", f=128)
    o_x3 = io["b_x3"]
    rd2 = [io[k] for k in ("b_r2bf", "b_r2f") if io.get(k) is not None]
    rdx = [io[k] for k in ("b_x1",) if io.get(k) is not None]
    rds = [io[k] for k in ("b_sr",) if io.get(k) is not None]

    def ab(f0, f1, s):
        return [a.b[f * nsub + s] for f in range(f0, f1)]

    for ps_ in range(TPC // PASS):
        t0 = ps_ * PASS
        S.dma("sp", lambda e, t0=t0: e.dma_start(out=x.t[:, :, :], in_=xv[:, :, t0:t0 + PASS]), writes=[x.b], reads=rdx)
        for s in range(nsub):
            pre_norm(K, C, x, x.b, g, 0, h, h.b, s)
        for s in range(nsub):
            sl = slice(s * 512, (s + 1) * 512)
            tsl = slice(t0 + s * 512, t0 + (s + 1) * 512)
            for p in range(4):
                for (fa, fb, sap) in X2F.fm_pieces(r2f, p, RF_CY, 0, 128):
                    S.dma("sp", lambda e, sl=sl, tsl=tsl, p=p, fa=fa, fb=fb, sap=sap: e.dma_start(out=y.t[fa:fb, p, sl], in_=sap[:, tsl]),
                          writes=[y.b], reads=rd2)
                for (fa, fb, sap) in X2F.fm_pieces(r2f, p, RF_GN, 0, 128):
                    S.dma("sp", lambda e, sl=sl, tsl=tsl, p=p, fa=fa, fb=fb, sap=sap: e.dma_start(out=y.t[fa:fb, 4 + p, sl], in_=sap[:, tsl]),
                          writes=[y.b], reads=rd2)
                for (fa, fb, sap) in X2B.fm_pieces(r2bf, p, 0, 0, 128):
                    S.dma("sp", lambda e, sl=sl, tsl=tsl, p=p, fa=fa, fb=fb, sap=sap: e.dma_start(out=a.t[fa:fb, 8 + p, sl], in_=sap[:, tsl]),
                          writes=ab(8 + p, 9 + p, s), reads=rd2)
            sr = sr_pool.next()
            S.dma("sp", lambda e, sr=sr, tsl=tsl: e.dma_start(out=sr.t[:, :, :], in_=sr_v[:, :, tsl]), writes=[sr.b], reads=rds)
            sq = C["sq_pool"].next()
            S.op("act", lambda e, sq=sq, sl=sl: e.activation(out=sq.t[:, 0:4, :], in_=y.t[:, 0:4, sl], func=AF.Copy),
                 reads=[y.b], writes=[sq.b])
            S.op("act", lambda e, sq=sq, sl=sl: e.activation(out=sq.t[:, 4:8, :], in_=y.t[:, 0:4, sl], func=AF.Square),
                 reads=[y.b], writes=[sq.b])
            p1 = K.psum()
            mm_group(K, p1, [(C["ones_bf"].t[:, :], sq.t[:, c, :], p1.t[:, :]) for c in range(4)], reads=[C["ones_bf"].b, sq.b])
            p2 = K.psum()
            mm_group(K, p2, [(C["ones_bf"].t[:, :], sq.t[:, 4 + c, :], p2.t[:, :]) for c in range(4)], reads=[C["ones_bf"].b, sq.b])
            mean = ln_mean
            S.op("act", lambda e, mean=mean, p1=p1: e.activation(out=mean.t[:, :], in_=p1.t[:, :], func=AF.Copy, scale=1.0 / 512),
                 reads=[p1.b], writes=[mean.b])
            msq = ln_msq
            S.op("dve", lambda e, mean=mean, msq=msq: e.tensor_tensor(out=msq.t[:, :], in0=mean.t[:, :], in1=mean.t[:, :], op=ALU.mult),
                 reads=[mean.b], writes=[msq.b])
            S.op("dve", lambda e, msq=msq, p2=p2: e.scalar_tensor_tensor(out=msq.t[:, :], in0=p2.t[:, :], scalar=1.0 / 512, in1=msq.t[:, :],
                                                                       op0=ALU.mult, op1=ALU.subtract),
                 reads=[p2.b, msq.b], writes=[msq.b])
            rs = C["rstd_pool"].next()
            S.op("act", lambda e, rs=rs, msq=msq: e.activation(out=rs.t[:, :], in_=msq.t[:, :], func=AF.Sqrt, bias=C["eps_col"].t[:, 0:1], scale=1.0),
                 reads=[msq.b, C["eps_col"].b], writes=[rs.b])
            S.op("dve", lambda e, rs=rs: e.reciprocal(out=rs.t[:, :], in_=rs.t[:, :]), reads=[rs.b], writes=[rs.b])
            for c in range(4):
                t = C["tmp_pool"].next()
                S.op("dve", lambda e, t=t, c=c, sl=sl, mean=mean: e.tensor_tensor(out=t.t[:, :], in0=y.t[:, c, sl], in1=mean.t[:, :], op=ALU.subtract),
                     reads=[y.b, mean.b], writes=[t.b])
                S.op("dve", lambda e, t=t, rs=rs: e.tensor_tensor(out=t.t[:, :], in0=t.t[:, :], in1=rs.t[:, :], op=ALU.mult),
                     reads=[t.b, rs.b], writes=[t.b])
                S.op("act", lambda e, t=t, c=c, sl=sl: e.activation(out=a.t[:, c, sl], in_=t.t[:, :], func=AF.Silu,
                                                                      scale=g.t[:, 32 + c:33 + c], bias=g.t[:, 36 + c:37 + c]),
                     reads=[t.b, g.b], writes=ab(c, c + 1, s))
                S.op(POOL_ENG, lambda e, c=c, sl=sl, sr=sr: e.tensor_tensor(out=a.t[:, 4 + c, sl], in0=y.t[:, 4 + c, sl], in1=sr.t[:, c, :], op=ALU.mult),
                     reads=[y.b, sr.b], writes=ab(4 + c, 5 + c, s))
        base = {0: 8, 1: 0, 2: 4}
        for d in range(8):
            for gi in range(3):
                wb = wbr_pool.next()
                load_w(K, wb, io["wbr"][gi * 8 + d].rearrange("p (k j) -> p k j", j=128))
                wgt = wgt_pool.next()
                load_w(K, wgt, io["wgt"][gi * 8 + d].rearrange("p (k j) -> p k j", j=128))
                for s in range(nsub):
                    sl = slice(s * 512, (s + 1) * 512)
                    pb = K.psum()
                    mm_group(K, pb, [(wb.t[:, kc, :], a.t[:, base[gi] + kc, sl], pb.t[:, :]) for kc in range(4)],
                             reads=[wb.b] + ab(base[gi], base[gi] + 4, s))
                    pl = K.psum()
                    mm_group(K, pl, [(wgt.t[:, k, :], h.t[:, k, sl], pl.t[:, :]) for k in range(8)], reads=[wgt.b, h.b])
                    sig = C["tmp_pool"].next()
                    S.op("act", lambda e, sig=sig, pl=pl: e.activation(out=sig.t[:, :], in_=pl.t[:, :], func=AF.Sigmoid),
                         reads=[pl.b], writes=[sig.b])
                    if gi == 0:
                        S.op("dve", lambda e, sig=sig, pb=pb, s=s: e.tensor_tensor(out=acc[s].t[:, :], in0=sig.t[:, :], in1=pb.t[:, :], op=ALU.mult),
                             reads=[sig.b, pb.b], writes=[acc[s].b])
                    else:
                        S.op("dve", lambda e, sig=sig, pb=pb: e.tensor_tensor(out=sig.t[:, :], in0=sig.t[:, :], in1=pb.t[:, :], op=ALU.mult),
                             reads=[sig.b, pb.b], writes=[sig.b])
                        if gi == 1:
                            S.op(POOL_ENG, lambda e, sig=sig, s=s: e.tensor_tensor(out=acc[s].t[:, :], in0=acc[s].t[:, :], in1=sig.t[:, :], op=ALU.add),
                                 reads=[sig.b, acc[s].b], writes=[acc[s].b])
                        else:
                            S.op(POOL_ENG, lambda e, sig=sig, s=s, d=d, sl=sl: e.tensor_tensor(out=a.t[:, 12 + d, sl], in0=acc[s].t[:, :], in1=sig.t[:, :], op=ALU.add),
                                 reads=[sig.b, acc[s].b], writes=ab(12 + d, 13 + d, s))
        for do in range(8):
            wo = C["wgu_pool"].next()
            load_w(K, wo, io["wo"][do].rearrange("p (k j) -> p k j", j=128))
            for s in range(nsub):
                sl = slice(s * 512, (s + 1) * 512)
                pp = K.psum()
                mm_group(K, pp, [(wo.t[:, k, :], a.t[:, 12 + k, sl], pp.t[:, :]) for k in range(8)], reads=[wo.b] + ab(12, 20, s))
                S.op("act", lambda e, pp=pp, do=do, sl=sl: e.activation(out=y.t[:, do, sl], in_=pp.t[:, :], func=AF.Copy),
                     reads=[pp.b], writes=[y.b])
        if dbg == 2:
            S.dma("sp", lambda e, t0=t0: e.dma_start(out=ov[:, :, t0:t0 + PASS], in_=y.t[:, :, :]), reads=[y.b], outs=[o_x3])
            continue
        for s in range(nsub):
            post_norm_residual(K, C, y, y.b, g, 8, 1.0, x, x.b, s)
        if dbg == 1:
            S.dma("sp", lambda e, t0=t0: e.dma_start(out=ov[:, :, t0:t0 + PASS], in_=x.t[:, :, :]), reads=[x.b], outs=[o_x3])
            continue
        for s in range(nsub):
            pre_norm(K, C, x, x.b, g, 16, h, h.b, s)
        ffn(K, C, h, h.b, io["wg"], io["wu"], io["wd"], y, y.b)
        for s in range(nsub):
            post_norm_residual(K, C, y, y.b, g, 24, 0.5, x, x.b, s)
        S.dma("sp", lambda e, t0=t0: e.dma_start(out=ov[:, :, t0:t0 + PASS], in_=x.t[:, :, :]), reads=[x.b], outs=[o_x3])


def prep_C_weights(inp, l):
    w_in = inp["w_in"][l]
    wgt = lay_fm_w(w_in[:, OGT:OGT + 3072], 24)
    wb = inp["w_branch"][l]
    wbr = np.concatenate([lay_fm_w(wb[gi], 8) for gi in range(3)], axis=0)
    gains = np.concatenate([lay_pk(inp["norm_pre"][l, 1]), lay_pk(inp["norm_post"][l, 1]), lay_pk(inp["norm_pre"][l, 2]),
                            lay_pk(inp["norm_post"][l, 2]), lay_pk(inp["conv_ln_g"][l]), lay_pk(inp["conv_ln_b"][l])], axis=1)
    return dict(wgt=wgt, wbr=_c(wbr), wo=lay_fm_w(inp["w_out"][l], 8), wg=lay_fm_w(inp["ffn2_w_gate"][l], NF),
                wu=lay_fm_w(inp["ffn2_w_up"][l], NF), wd=lay_fm_w(inp["ffn2_w_down"][l], 8), gains=_c(gains))


NQT = SEQ // 512
NBLK = SEQ // 128
S2B_LEN = 128 * TPC
S2F_LEN = 256 * TPC


def build_B():
    nc = bass.Bass("TRN2", target_bir_lowering=False)
    dt = nc.dram_tensor
    io = dict(
        rbf=dt("rbf", [4, SB_LEN], BF16, kind="ExternalInput").ap(),
        rf=dt("rf", [4, SF_LEN], F32, kind="ExternalInput").ap(),
        bw=dt("bw", [128, 33], F32, kind="ExternalInput").ap(),
        s2bf=dt("s2bf", [4, S2B_LEN], BF16, kind="ExternalOutput").ap(),
        s2f=dt("s2f", [4, S2F_LEN], F32, kind="ExternalOutput").ap(),
    )
    with ExitStack() as st:
        K = Ctx(nc, st)
        K.init_psum(8)
        rbf_, rf_ = io["rbf"], io["rf"]
        io["rbf"] = lambda j, a, b: rbf_[j:j + 1, a:b]
        io["rf"] = lambda j, a, b: rf_[j:j + 1, a:b]
        io["b_s2bf"], io["b_s2f"] = Buf("o_s2bf", True), Buf("o_s2f", True)
        K.outs += [io["b_s2bf"], io["b_s2f"]]
        emit_B(K, io)
        K.S.wait_all_on("sp", K.outs)
        K.S.emit()
    return nc


def emit_B(K, io):
    S = K.S
    T = TPC
    rbf, rf = io["rbf"], io["rf"]
    o_bf, o_f = io["b_s2bf"], io["b_s2f"]
    STQ = "pool" if io.get("late_picks") is not None else "sp"
    rd_qkv = [io[k] for k in ("b_rbf",) if io.get(k) is not None]
    rd_g = [io[k] for k in ("b_rbf2",) if io.get(k) is not None]
    rd_f = [io[k] for k in ("b_rf",) if io.get(k) is not None]
    ones_bf = K.sb([128, 128], BF16, "ones_bf")
    S.op("dve", lambda e: e.memset(ones_bf.t[:, :], 1.0), writes=[ones_bf.b])
    negones = K.sb([128, 1], BF16, "negones")
    S.op("dve", lambda e: e.memset(negones.t[:, :], -1.0), writes=[negones.b])
    eps = K.sb([128, 1], F32, "eps")
    S.op("dve", lambda e: e.memset(eps.t[:, :], EPS), writes=[eps.b])
    tri2 = K.sb([128, 128], BF16, "tri2")
    S.op("pool", lambda e: e.memset(tri2.t[:, :], -1.0), writes=[tri2.b])
    S.op("pool", lambda e: e.affine_select(out=tri2.t[:, :], in_=tri2.t[:, :], pattern=[[-1, 128]], compare_op=ALU.is_ge,
                                            fill=0.0, base=0, channel_multiplier=1), reads=[tri2.b], writes=[tri2.b])
    S.op("pool", lambda e: e.memset(tri2.t[0:1, :], 1.0), reads=[tri2.b], writes=[tri2.b])
    tris = K.sb([128, 128], BF16, "tris")
    S.op("pool", lambda e: e.memset(tris.t[:, :], 1.0), writes=[tris.b])
    S.op("pool", lambda e: e.affine_select(out=tris.t[:, :], in_=tris.t[:, :], pattern=[[-1, 128]], compare_op=ALU.is_gt,
                                            fill=0.0, base=0, channel_multiplier=1), reads=[tris.b], writes=[tris.b])
    S.op("pool", lambda e: e.memset(tris.t[64:128, 0:64], 0.0), reads=[tris.b], writes=[tris.b])
    cind = K.sb([128, 2], BF16, "cind")
    S.op("pool", lambda e: e.memset(cind.t[:, :], 0.0), writes=[cind.b])
    S.op("pool", lambda e: e.memset(cind.t[0:64, 0:1], 1.0), reads=[cind.b], writes=[cind.b])
    S.op("pool", lambda e: e.memset(cind.t[64:128, 1:2], 1.0), reads=[cind.b], writes=[cind.b])
    bw = K.sb([128, 33], F32, "bw")
    S.dma("sp", lambda e: e.dma_start(out=bw.t[:, :], in_=io["bw"]), writes=[bw.b])
    qT = K.sb([128, SEQ], BF16, "qT")
    qT2 = K.sb([128, SEQ], BF16, "qT2")
    S.op("pool", lambda e: e.memset(qT.t[64:128, :], 0.0), writes=[qT.b])
    S.op("pool", lambda e: e.memset(qT2.t[0:64, :], 0.0), writes=[qT2.b])
    qz = [qT, qT2]
    KB = [Buf("kT_j%d" % j) for j in range(4)]
    VB = [Buf("v_j%d" % j) for j in range(4)]
    QB = [[Buf("q%d_j%d" % (h, j)) for j in range(4)] for h in range(2)]
    GB = [Buf("gla_j%d" % j) for j in range(4)]
    UB = [Buf("u_j%d" % j) for j in range(4)]

    def ksrc(b):
        return sorted({min(127 * b, SEQ - 1) // T, min(127 * b + 126, SEQ - 1) // T})
    NB2 = (SEQ + 126) // 127
    kT = K.sb([128, NB2 * 128], BF16, "kT2")
    v = K.sb([128, NB2, 128], BF16, "v2")
    S.op("pool", lambda e: e.memset(kT.t[:, :], 0.0), writes=[kT.b])
    S.op("pool", lambda e: e.memset(v.t[:, :, :], 0.0), writes=[v.b])
    kT3 = kT.t[:, :].rearrange("f (b s) -> f b s", s=128)

    def key_runs(ta, tb):
        runs = []
        t = ta
        while t < tb:
            b, s = t // 127, t % 127
            if s == 0 and tb - t >= 127:
                nb = (tb - t) // 127
                runs.append((b, nb, 0, 127, t - ta))
                t += nb * 127
            else:
                n = min(127 - s, tb - t)
                runs.append((b, 1, s, n, t - ta))
                t += n
        return runs
    gqT = K.sb([64, SEQ], BF16, "gqT")
    gv = K.sb([128, NBLK, 128], BF16, "gv")
    gk = K.sb([128, NBLK, 64], F32, "gk")
    la = K.sb([128, NBLK, 64], F32, "la")
    u = K.sb([128, 32 + SEQ], F32, "u")
    S.op("pool", lambda e: e.memset(u.t[:, 0:32], 0.0), writes=[u.b])
    def ld(out_ap, in_ap, jbuf, after=(), rdb=None):
        S.dma("sp", lambda e: e.dma_start(out=out_ap, in_=in_ap), reads=list(rdb) + list(after), outs=[jbuf])

    for phase_, j in [(0, 0), (0, 1), (0, 2), (0, 3), (1, 0), (1, 1), (1, 2), (1, 3)]:
        ts = slice(j * T, (j + 1) * T)
        if phase_ == 1 and j == 0 and io.get("late_picks") is not None:
            io["late_picks"]()
        for (fa, fb, sap) in (X1B.fm_pieces(rbf, j, SB_KT, 0, 128) if phase_ == 0 else []):
            for (b0, nb, s0, ns, off) in key_runs(j * T, (j + 1) * T):
                if nb > 1 or ns == 127:
                    dst = kT3[fa:fb, b0:b0 + nb, 1:128]
                    srcv = sap[:, off:off + nb * 127].rearrange("f (b s) -> f b s", s=127)
                else:
                    dst = kT3[fa:fb, b0, 1 + s0:1 + s0 + ns]
                    srcv = sap[:, off:off + ns]
                ld(dst, srcv, KB[j], after=[kT.b], rdb=rd_qkv)
        for (fa, fb, sap) in (X1B.fm_pieces(rbf, j, SB_QT, 0, 128) if phase_ == 0 else []):
            tq = qT if fa < 64 else qT2
            assert (fa < 64) == (fb <= 64)
            ld(tq.t[fa:fb, ts], sap, QB[0 if fa < 64 else 1][j], after=[tq.b], rdb=rd_qkv)
        for (fa, fb, sap) in (X1B.fm_pieces(rbf, j, SB_GQT, 0, 64) if phase_ == 1 else []):
            ld(gqT.t[fa:fb, ts], sap, GB[j], rdb=rd_g)
        for (fa, fb, sap) in (X1F.fm_pieces(rf, j, SF_UT, 0, 128) if phase_ == 1 else []):
            ld(u.t[fa:fb, 32 + j * T:32 + (j + 1) * T], sap, UB[j], after=[u.b], rdb=rd_f)
        for tok0 in (0, 1024):
            kb0 = j * 16 + tok0 // 128
            vsrc = X1B.tm_src(rbf, j, SB_V, 128, tok0, 1024)
            for (b0, nb, s0, ns, off) in (key_runs(j * T + tok0, j * T + tok0 + 1024) if phase_ == 0 else []):
                if nb > 1 or ns == 127:
                    dst = v.t[1:128, b0:b0 + nb, :]
                    srcv = vsrc[off:off + nb * 127, :].rearrange("(b s) f -> s b f", s=127)
                else:
                    dst = v.t[1 + s0:1 + s0 + ns, b0, :]
                    srcv = vsrc[off:off + ns, :]
                ld(dst, srcv, VB[j], after=[v.b], rdb=rd_qkv)
            if phase_ == 0:
                continue
            ld(gv.t[:, kb0:kb0 + 8, :], X1B.tm_src(rbf, j, SB_GV, 128, tok0, 1024).rearrange("(k t) f -> t k f", t=128), GB[j], rdb=rd_g)
            ld(gk.t[:, kb0:kb0 + 8, :], X1F.tm_src(rf, j, SF_GK, 64, tok0, 1024).rearrange("(k t) f -> t k f", t=128), GB[j], rdb=rd_f)
            ld(la.t[:, kb0:kb0 + 8, :], X1F.tm_src(rf, j, SF_LA, 64, tok0, 1024).rearrange("(k t) f -> t k f", t=128), GB[j], rdb=rd_f)
    ps_z = Rot(K.ps[0:3] + K.ps[5:6])
    ps_o = K.ps[3:5]
    ps_g = K.ps[7]
    e_pool = K.pool(3, [128, 512], F32, "e")
    sp_pool = K.pool(4, [128, 512], BF16, "sp")
    w_pool = K.pool(3, [128, 512], BF16, "w")
    osb_pool = K.pool(2, [128, 512], BF16, "osb")

    def conv_gen():
        acc_pool = K.pool(2, [128, 512], F32, "cacc")

        def ub_of(pc):
            js = {(pc * 512) // T, max(pc * 512 - 30, 0) // T, (pc * 512 + 511) // T}
            return [u.b] + [UB[j] for j in sorted(js)]
        for pc in range(SEQ // 512):
            acc = acc_pool.next()
            c0 = 2 + pc * 512
            S.op("dve", lambda e, acc=acc, c0=c0: e.tensor_scalar(out=acc.t[:, :], in0=u.t[:, c0:c0 + 512], scalar1=bw.t[:, 0:1], scalar2=bw.t[:, 31:32],
                                                                    op0=ALU.mult, op1=ALU.add), reads=ub_of(pc) + [bw.b], writes=[acc.b])
            yield
            for k in range(1, 31):
                S.op("dve", lambda e, acc=acc, c0=c0, k=k: e.scalar_tensor_tensor(out=acc.t[:, :], in0=u.t[:, c0 + k:c0 + k + 512], scalar=bw.t[:, k:k + 1],
                                                                                  in1=acc.t[:, :], op0=ALU.mult, op1=ALU.add),
                     reads=ub_of(pc) + [bw.b, acc.b], writes=[acc.b])
                yield
            cols = slice((pc % 4) * 512, (pc % 4 + 1) * 512)
            for (fa, fb, dst) in X2F.fm_pieces(io["s2f"], pc // 4, RF_CY, 0, 128):
                S.dma(STQ, lambda e, acc=acc, dst=dst, fa=fa, fb=fb, cols=cols: e.dma_start(out=dst[:, cols], in_=acc.t[fa:fb, :]),
                      reads=[acc.b], outs=[o_f])
            yield

    def gla_gen():
        state = K.sb([64, 128], F32, "gstate")
        S.op("pool", lambda e: e.memset(state.t[:, :], 0.0), writes=[state.b])
        sbf_pool = K.pool(2, [64, 128], BF16, "gstbf")
        labf_pool = K.pool(2, [128, 64], BF16, "labf")
        ed_pool = K.pool(2, [128, 64], F32, "ged")
        kd_pool = K.pool(2, [128, 64], BF16, "gkd")
        lam_pool = K.pool(2, [64, 2], F32, "glam")
        osb_p = K.pool(2, [128, 128], F32, "gosb")
        sq_p = K.pool(2, [128, 128], BF16, "gsq")
        rs_p = K.pool(2, [128, 128], F32, "grs")
        gst_pool = K.pool(2, [128, 512], F32, "gnst")
        pg = ps_g
        r_dte = pg.t[:, 0:64]
        r_lam = pg.t[0:64, 64:66]
        r_st = pg.t[0:64, 128:256]
        r_o = pg.t[:, 256:384]
        r_ss = pg.t[:, 384:512]
        bg = pg.b
        gst = None
        for blk in range(NBLK):
            labf = labf_pool.next()
            S.op("pool", lambda e, labf=labf, blk=blk: e.tensor_copy(out=labf.t[:, :], in_=la.t[:, blk, :]), reads=[GB[blk // 16]], writes=[labf.b])
            yield
            S.op("pe", [lambda e, labf=labf: e.matmul(r_dte, tris.t[:, :], labf.t[:, :], start=True, stop=True)], reads=[tris.b, labf.b], writes=[bg])
            S.op("pe", [lambda e, labf=labf: e.matmul(r_lam, labf.t[:, :], cind.t[:, :], start=True, stop=True)], reads=[cind.b, labf.b], writes=[bg])
            yield
            ed = ed_pool.next()
            S.op("act", lambda e, ed=ed: e.activation(out=ed.t[:, :], in_=r_dte, func=AF.Exp), writes=[ed.b, bg])
            lam = lam_pool.next()
            S.op("act", lambda e, lam=lam: e.activation(out=lam.t[:, :], in_=r_lam, func=AF.Exp), writes=[lam.b, bg])
            yield
            kd = kd_pool.next()
            S.op("dve", lambda e, kd=kd, ed=ed, blk=blk: e.tensor_tensor(out=kd.t[:, :], in0=gk.t[:, blk, :], in1=ed.t[:, :], op=ALU.mult),
                 reads=[GB[blk // 16], ed.b], writes=[kd.b])
            yield
            for c in range(2):
                cs = slice(64 * c, 64 * c + 64)
                S.op("pe", [lambda e, kd=kd, cs=cs, blk=blk: e.matmul(r_st, kd.t[cs, :], gv.t[cs, blk, :], start=True, stop=True)],
                     reads=[kd.b, GB[blk // 16]], writes=[bg])
                yield
                S.op("dve", lambda e, lam=lam, c=c: e.scalar_tensor_tensor(out=state.t[:, :], in0=state.t[:, :], scalar=lam.t[:, c:c + 1], in1=r_st,
                                                                          op0=ALU.mult, op1=ALU.add), reads=[state.b, lam.b], writes=[state.b, bg])
                sbf = sbf_pool.next()
                S.op("dve", lambda e, sbf=sbf: e.tensor_copy(out=sbf.t[:, :], in_=state.t[:, :]), reads=[state.b], writes=[sbf.b])
                yield
                tk = slice(blk * 128 + 64 * c, blk * 128 + 64 * c + 64)
                S.op("pe", [lambda e, sbf=sbf, tk=tk, c=c: e.matmul(pg.t[:, 256 + 64 * c:256 + 64 * c + 64], sbf.t[:, :], gqT.t[:, tk], start=True, stop=True)],
                     reads=[sbf.b, GB[blk // 16]], writes=[bg])
                yield
            osb = osb_p.next()
            S.op("dve", lambda e, osb=osb: e.tensor_copy(out=osb.t[:, :], in_=r_o), writes=[osb.b, bg])
            sq = sq_p.next()
            S.op("pool", lambda e, osb=osb, sq=sq: e.tensor_tensor(out=sq.t[:, :], in0=osb.t[:, :], in1=osb.t[:, :], op=ALU.mult), reads=[osb.b], writes=[sq.b])
            yield
            S.op("pe", [lambda e, sq=sq: e.matmul(r_ss, ones_bf.t[:, :], sq.t[:, :], start=True, stop=True)], reads=[ones_bf.b, sq.b], writes=[bg])
            yield
            rs = rs_p.next()
            S.op("act", lambda e, rs=rs: e.activation(out=rs.t[:, :], in_=r_ss, func=AF.Ln, bias=eps.t[:, 0:1], scale=1.0 / 128), reads=[eps.b], writes=[rs.b, bg])
            S.op("act", lambda e, rs=rs: e.activation(out=rs.t[:, :], in_=rs.t[:, :], func=AF.Exp, scale=-0.5), reads=[rs.b], writes=[rs.b])
            yield
            if blk % 4 == 0:
                gst = gst_pool.next()
            cs4 = slice((blk % 4) * 128, (blk % 4 + 1) * 128)
            S.op("dve", lambda e, gst=gst, cs4=cs4, osb=osb, rs=rs: e.scalar_tensor_tensor(out=gst.t[:, cs4], in0=osb.t[:, :], scalar=bw.t[:, 32:33], in1=rs.t[:, :],
                                                                                        op0=ALU.mult, op1=ALU.mult), reads=[osb.b, rs.b, bw.b], writes=[gst.b])
            if blk % 4 == 3:
                pc = blk // 4
                cols = slice((pc % 4) * 512, (pc % 4 + 1) * 512)
                for (fa, fb, dst) in X2F.fm_pieces(io["s2f"], pc // 4, RF_GN, 0, 128):
                    S.dma(STQ, lambda e, gst=gst, dst=dst, fa=fa, fb=fb, cols=cols: e.dma_start(out=dst[:, cols], in_=gst.t[fa:fb, :]),
                          reads=[gst.b], outs=[o_f])
            yield

    import os as _os
    _mode = _os.environ.get("SIDE_MODE", "1,2")
    side = [conv_gen(), gla_gen()]
    periods = [int(t) for t in _mode.split(",")]
    SIDE_DELAY = int(_os.environ.get("SIDE_DELAY", "250")) if io.get("b_rbf2") is not None else 0
    tick = [0]

    def step_side(force=False):
        tick[0] += 1
        for gi, gnr in enumerate(list(side)):
            per = periods[min(gi, len(periods) - 1)]
            if per == 0 and not force:
                continue
            if force or tick[0] % max(per, 1) == 0:
                try:
                    next(gnr)
                except StopIteration:
                    side.remove(gnr)

    seq = []
    for qt in range(NQT):
        q0 = qt * 512
        bmax = (q0 + 510) // 127
        bl = list(range(bmax, -1, -1))
        for idx, b in enumerate(bl):
            seq.append((qt, b, idx == 0, idx == len(bl) - 1))
    tiles = []
    for t in seq:
        for hh in range(2):
            tiles.append((hh,) + t)
    w_pool4 = K.pool(4, [128, 512], BF16, "w4")
    st = {}

    def mask_op(buf, qt, b):
        base = 512 * qt - 127 * b + 1
        S.op("pool", lambda e: e.affine_select(out=buf.t[:, :], in_=buf.t[:, :], pattern=[[1, 512]], compare_op=ALU.is_gt,
                                                fill=0.0, base=base, channel_multiplier=-1), reads=[buf.b], writes=[buf.b])

    def is_diag(qt, b):
        return 127 * b + 127 > 512 * qt

    def zfn_of(i):
        hh, qt, b, first, last = tiles[i]
        pz = ps_z.next()
        st[i] = {"pz": pz}
        qs = slice(qt * 512, (qt + 1) * 512)
        ks = slice(b * 128, (b + 1) * 128)
        return ((lambda e: e.matmul(pz.t[:, :], kT.t[:, ks], qz[hh].t[:, qs], start=True, stop=False)),
                [kT.b, qz[hh].b, QB[hh][qt // 4]] + [KB[j] for j in ksrc(b)], pz)

    def act_exp(i):
        pz = st[i]["pz"]
        ee = e_pool.next()
        st[i]["ee"] = ee
        S.op("act", lambda e: e.activation(out=ee.t[:, :], in_=pz.t[:, :], func=AF.Exp), reads=[pz.b], writes=[ee.b])

    def act_ln(i):
        hh, qt, b, first, last = tiles[i]
        ee = st[i]["ee"]
        sp = sp_pool.next()
        st[i]["sp"] = sp
        S.op("act", lambda e: e.activation(out=sp.t[:, :], in_=ee.t[:, :], func=AF.Ln, bias=1.0), reads=[ee.b], writes=[sp.b])
        if is_diag(qt, b):
            mask_op(sp, qt, b)
        if first:
            S.op("dve", lambda e: e.memset(sp.t[0:1, :], 0.0), reads=[sp.b], writes=[sp.b])
        else:
            pzp = st[i - 2]["pz"]
            S.op("dve", lambda e: e.tensor_copy(out=sp.t[0:1, :], in_=pzp.t[0:1, :]), reads=[sp.b], writes=[sp.b, pzp.b])

    def pv(i):
        hh, qt, b, first, last = tiles[i]
        hs = slice(64 * hh, 64 * hh + 64)
        w = st[i]["w"]
        po = ps_o[hh]
        S.op("pe", [lambda e: e.matmul(po.t[:, :], v.t[:, b, :], w.t[:, :], start=first, stop=last)], reads=[v.b, w.b] + [VB[j] for j in ksrc(b)], writes=[po.b])
        if last:
            osb = osb_pool.next()
            S.op("dve", lambda e: e.tensor_copy(out=osb.t[hs, :], in_=po.t[hs, :]), reads=[po.b], writes=[osb.b])
            cols = slice((qt % 4) * 512, (qt % 4 + 1) * 512)
            for (fa, fb, dst) in X2B.fm_pieces(io["s2bf"], qt // 4, 0, 64 * hh, 64):
                S.dma(STQ, lambda e, dst=dst, fa=fa, fb=fb: e.dma_start(out=dst[:, cols], in_=osb.t[64 * hh + fa:64 * hh + fb, :]),
                      reads=[osb.b], outs=[o_bf])
        st.pop(i - 3, None)

    n = len(tiles)
    for i in range(min(2, n)):
        zf, zr, pz = zfn_of(i)
        S.op("pe", [zf], reads=zr, writes=[pz.b])
        act_exp(i)
        act_ln(i)
    for j in range(n):
        hh, qt, b, first, last = tiles[j]
        pz, sp = st[j]["pz"], st[j]["sp"]
        fns = [lambda e, pz=pz, sp=sp: e.matmul(pz.t[:, :], tri2.t[:, :], sp.t[:, :], start=False, stop=True)]
        rds, wrs = [tri2.b, sp.b, pz.b], [pz.b]
        if j + 2 < n:
            zf, zr, pz2 = zfn_of(j + 2)
            fns.append(zf)
            rds += zr
            wrs.append(pz2.b)
        S.op("pe", fns, reads=rds, writes=wrs)
        if j + 2 < n:
            act_exp(j + 2)
        w = w_pool4.next()
        st[j]["w"] = w
        S.op("act", lambda e, w=w, pz=pz: e.activation(out=w.t[:, :], in_=pz.t[:, :], func=AF.Exp), reads=[pz.b], writes=[w.b])
        if is_diag(qt, b):
            mask_op(w, qt, b)
        if j + 2 < n:
            act_ln(j + 2)
        if j >= 1:
            pv(j - 1)
        if j >= SIDE_DELAY:
            step_side()
    pv(n - 1)
    while side:
        step_side(force=True)


def prep_B_weights(inp, l, p):
    cw = inp["conv_w"][l][:, 128 * p:128 * p + 128].T
    cb = inp["conv_b"][l][128 * p:128 * p + 128][:, None]
    gg = inp["gla_norm_g"][l][128 * p:128 * p + 128][:, None]
    return _c(np.concatenate([cw, cb, gg], axis=1).astype(np.float32))


I32 = mybir.dt.int32
GROUPS = [[0, 1, 2, 3], [4, 5, 6, 7]]


def build_fused(L=DEPTH, phases="AXBYC"):
    nc = bass.Bass("TRN2", target_bir_lowering=False)
    dt = nc.dram_tensor

    def ein(name, shape, d=F32):
        return dt(name, list(shape), d, kind="ExternalInput").ap()

    xT = ein("xT", [D, TPC])
    cid = ein("cid", [1, 1], I32)
    bw = ein("bw", [L, 128, 33])
    wg1, wu1, wd1 = ein("wg1", [L, NF, 128, 1024]), ein("wu1", [L, NF, 128, 1024]), ein("wd1", [L, 8, 128, NF * 128])
    wfm, wtm = ein("wfm", [L, NFM, 128, 1024]), ein("wtm", [L, 128, 8 * 1280])
    walpha, balpha, gainsA = ein("walpha", [L, 16, 256]), ein("balpha", [L, 1, 256]), ein("gainsA", [L, 128, 24])
    wgt, wbr, wo = ein("wgt", [L, 24, 128, 1024]), ein("wbr", [L, 24, 128, 512]), ein("wo", [L, 8, 128, 1024])
    wg2, wu2, wd2 = ein("wg2", [L, NF, 128, 1024]), ein("wu2", [L, NF, 128, 1024]), ein("wd2", [L, 8, 128, NF * 128])
    gainsC = ein("gainsC", [L, 128, 40])
    outT = dt("outT", [D, TPC], F32, kind="ExternalOutput").ap()
    x1T = dt("x1T_i", [D, TPC], F32).ap()
    xbuf = dt("xbuf_i", [D, TPC], F32).ap()
    silur = dt("silur_i", [512, TPC], F32).ap()
    send_bf = dt("send_bf_i", X1B.shape(), BF16).ap()
    send_f = dt("send_f_i", X1F.shape(), F32).ap()
    ag_bf = dt("ag_bf_i", X1B.shape(16), BF16).ap()
    ag_f = dt("ag_f_i", X1F.shape(16), F32).ap()
    s2bf = dt("s2bf_i", X2B.shape(), BF16).ap()
    s2f = dt("s2f_i", X2F.shape(), F32).ap()
    ag2_bf = dt("ag2_bf_i", X2B.shape(16), BF16).ap()
    ag2_f = dt("ag2_f_i", X2F.shape(16), F32).ap()
    st_bf = dt("st_bf_i", X1B.shape(), BF16).ap()
    st_f = dt("st_f_i", X1F.shape(), F32).ap()
    st2_bf = dt("st2_bf_i", X2B.shape(), BF16).ap()
    st2_f = dt("st2_f_i", X2F.shape(), F32).ap()
    P = {n: Buf(n, True) for n in ("x1", "sr", "sbf", "sf", "agbf", "agf", "s2bf", "s2f", "ag2bf", "ag2f", "xbuf", "out",
                                   "stbf", "stf", "st2bf", "st2f", "agbf2", "stbf2")}
    with ExitStack() as st:
        K = Ctx(nc, st)
        S = K.S
        S.cid_ap = cid[0:1, 0:1]
        K.init_psum(8)

        def stat(ap):
            return ap

        def pick(ap16, stage, bsrc, bdst, c0=0, c1=None):
            c1 = ap16.shape[0] if c1 is None else c1
            v4 = ap16[c0:c1].rearrange("c (r d) n -> d c r n", d=4)
            S.dma("sp", lambda e: e.dma_start(out=stage[c0:c1], in_=v4[bass.ds(S.dyn["p"], 1)].rearrange("o c r n -> (o c) r n")),
                  reads=[bsrc], writes=[bdst])

        def gather(src, dst, bsrc, bdst, c0=0, c1=None):
            for c in range(c0, src.shape[0] if c1 is None else c1):
                S.dma("pool", lambda e, c=c: e.collective_compute("AllGather", ALU.bypass, replica_groups=GROUPS,
                                                                  ins=[src[c].opt()], outs=[dst[c].opt()]),
                      reads=[bsrc], outs=[bdst], sembuf=bdst, inc=1)

        for l in range(L):
            last = (l == L - 1)
            K.arena_reset()
            C = common_consts(K)
            ffn_bufs(K, C)
            if "A" in phases:
              emit_A(K, C, dict(xT=xT if l == 0 else xbuf, b_xin=None if l == 0 else P["xbuf"],
                              wg=wg1[l], wu=wu1[l], wd=wd1[l], wfm=wfm[l], wtm=wtm[l], walpha=walpha[l], balpha=balpha[l],
                              gains=gainsA[l], x1T=x1T, send_bf=send_bf, send_f=send_f, silur=silur,
                              b_x1=P["x1"], b_sbf=P["sbf"], b_sf=P["sf"], b_sr=P["sr"]))
            S.barrier()
            if "X" in phases:
                gather(send_bf, ag_bf, P["sbf"], P["agbf"], 0, 6)
                pick(ag_bf, st_bf, P["agbf"], P["stbf"], 0, 6)
                gather(send_bf, ag_bf, P["sbf"], P["agbf2"], 6, None)
                gather(send_f, ag_f, P["sf"], P["agf"])

                def late_picks():
                    pick(ag_bf, st_bf, P["agbf2"], P["stbf2"], 6, None)
                    pick(ag_f, st_f, P["agf"], P["stf"])
            K.arena_reset()
            if "B" in phases:
              emit_B(K, dict(rbf=stat(st_bf), rf=stat(st_f), bw=bw[l], s2bf=s2bf, s2f=s2f,
                           b_rbf=P["stbf"], b_rbf2=P["stbf2"], b_rf=P["stf"], b_s2bf=P["s2bf"], b_s2f=P["s2f"],
                           late_picks=late_picks if "X" in phases else None))
            S.barrier()
            if "Y" in phases:
                gather(s2bf, ag2_bf, P["s2bf"], P["ag2bf"])
                gather(s2f, ag2_f, P["s2f"], P["ag2f"])
                pick(ag2_bf, st2_bf, P["ag2bf"], P["st2bf"])
                pick(ag2_f, st2_f, P["ag2f"], P["st2f"])
            K.arena_reset()
            C = common_consts(K)
            ffn_bufs(K, C)
            if "C" in phases:
              emit_C(K, C, dict(x1T=x1T, recv_bf=stat(st2_bf), recv_f=stat(st2_f), silur=silur,
                              wgt=wgt[l], wbr=wbr[l], wo=wo[l], wg=wg2[l], wu=wu2[l], wd=wd2[l], gains=gainsC[l],
                              x3T=outT if last else xbuf, b_x3=P["out"] if last else P["xbuf"],
                              b_x1=P["x1"], b_sr=P["sr"], b_r2bf=P["st2bf"], b_r2f=P["st2f"]))
            S.barrier()
        S.wait_all_on("sp", [P["out"]])
        S.emit()
    return nc, K


_FUSED = {}


def kernel(**inputs):
    inp = {k: np.asarray(v) for k, v in inputs.items()}
    x = inp["x"].astype(np.float32, copy=False)
    cores = list(range(NCORES))
    T = TPC
    L = DEPTH
    if "nc" not in _FUSED:
        _FUSED["nc"] = build_fused(L)[0]
    A = [prep_A_weights(inp, l) for l in range(L)]
    Cw = [prep_C_weights(inp, l) for l in range(L)]
    shared = dict(
        wg1=np.stack([a["wg"] for a in A]), wu1=np.stack([a["wu"] for a in A]), wd1=np.stack([a["wd"] for a in A]),
        wfm=np.stack([a["wfm"] for a in A]), wtm=np.stack([a["wtm"] for a in A]),
        walpha=np.stack([a["walpha"] for a in A]), balpha=np.stack([a["balpha"] for a in A]),
        gainsA=np.stack([a["gains"] for a in A]),
        wgt=np.stack([c["wgt"] for c in Cw]), wbr=np.stack([c["wbr"] for c in Cw]), wo=np.stack([c["wo"] for c in Cw]),
        wg2=np.stack([c["wg"] for c in Cw]), wu2=np.stack([c["wu"] for c in Cw]), wd2=np.stack([c["wd"] for c in Cw]),
        gainsC=np.stack([c["gains"] for c in Cw]))
    del A, Cw
    bws = [np.stack([prep_B_weights(inp, l, p) for l in range(L)]) for p in range(4)]
    in_maps = []
    for c in cores:
        m = dict(shared)
        m["xT"] = _c(x[c // 4, (c % 4) * T:(c % 4 + 1) * T, :].T)
        m["cid"] = np.array([[c % 4]], np.int32)
        m["bw"] = bws[c % 4]
        in_maps.append(m)
    res = run_bass_kernel_spmd(_FUSED["nc"], in_maps, core_ids=cores).results
    out = np.empty((BATCH, SEQ, D), np.float32)
    for c in cores:
        out[c // 4, (c % 4) * T:(c % 4 + 1) * T, :] = np.asarray(res[c]["outT"]).T
    return out
```

```python
import numpy as np
from contextlib import ExitStack
import ml_dtypes
import concourse.bass as bass
import concourse.mybir as mybir
from concourse.bass_utils import run_bass_kernel_spmd

F32 = mybir.dt.float32
BF16 = mybir.dt.bfloat16
AF = mybir.ActivationFunctionType
ALU = mybir.AluOpType

D = 1024
DFF = 2816
NF = DFF // 128
SEQ = 8192
BATCH = 2
DEPTH = 4
TPC = 2048
PASS = 1024
EPS = 1e-6
INW = 7184
NCORES = 8
POOL_ENG = "dve"

ENGS = ("pe", "act", "dve", "pool", "sp")
SEM_LIMIT = 28000


class Buf:
    __slots__ = ("name", "w", "r", "dsem", "persist")

    def __init__(self, name, persist=False):
        self.name = name
        self.w = None
        self.r = {}
        self.dsem = None
        self.persist = persist


class Sched:
    def __init__(self, nc, stack):
        self.nc = nc
        self.stack = stack
        self.sem = {}
        self.cnt = {}
        self.gen = {e: 0 for e in ENGS}
        self.seen = {e: {} for e in ENGS}
        self.q = {e: [] for e in ENGS}
        self.nsem = 0
        for e in ENGS:
            self._mksem((e, 0))
        self.n_inst = 0
        self.dma_free = []
        self.dma_bufs = []
        self.dyn = {}
        self.cid_ap = None

    def _mksem(self, key):
        self.nsem += 1
        h = self.stack.enter_context(self.nc.semaphore("s%d" % self.nsem))
        self.sem[key] = h
        self.cnt[key] = 0
        return key

    def ekey(self, e):
        return (e, self.gen[e])

    def _waits(self, e, deps):
        out = []
        seen = self.seen[e]
        for (k, v) in deps:
            if k[0] == e:
                continue
            if seen.get(k, 0) < v:
                seen[k] = v
                out.append((k, v))
        return out

    @staticmethod
    def _deps(reads, writes):
        deps = []
        for b in reads:
            if b.w is not None:
                deps.append(b.w)
        for b in writes:
            if b.w is not None:
                deps.append(b.w)
            deps.extend(b.r.items())
        return deps

    def op(self, e, fns, reads=(), writes=()):
        if not isinstance(fns, (list, tuple)):
            fns = [fns]
        waits = self._waits(e, self._deps(reads, writes))
        k = self.ekey(e)
        if self.cnt[k] >= SEM_LIMIT:
            self.gen[e] += 1
            k = self._mksem(self.ekey(e))
        self.cnt[k] += 1
        ev = (k, self.cnt[k])
        for b in reads:
            if b.r.get(k, 0) < ev[1]:
                b.r[k] = ev[1]
        for b in writes:
            b.w = ev
            b.r = {}
        self.q[e].append((waits, fns, k, 1))
        self.n_inst += len(fns)
        return ev

    def dma(self, qe, fn, reads=(), writes=(), sembuf=None, outs=(), inc=16):
        deps = self._deps(reads, writes)
        for b in outs:
            deps.extend(b.r.items())
        waits = self._waits(qe, deps)
        tgt = sembuf if sembuf is not None else (outs[0] if outs else (writes[0] if writes else reads[0]))
        if tgt.dsem is None:
            if self.dma_free and not tgt.persist:
                tgt.dsem = self.dma_free.pop()
            else:
                tgt.dsem = self._mksem(("dma", tgt.name, self.nsem))
            if not tgt.persist:
                self.dma_bufs.append(tgt)
        semkey = tgt.dsem
        self.cnt[semkey] += inc
        ev = (semkey, self.cnt[semkey])
        for b in reads:
            if b.r.get(semkey, 0) < ev[1]:
                b.r[semkey] = ev[1]
        for b in writes:
            b.w = ev
            b.r = {}
        for b in outs:
            b.w = ev
        self.q[qe].append((waits, [fn], semkey, inc))
        self.n_inst += 1
        return ev

    def barrier(self):
        deps = [(k, v) for k, v in self.cnt.items() if v > 0 and k[0] != "sp"]
        waits = self._waits("sp", deps)
        k = self.ekey("sp")
        self.cnt[k] += 1
        ev = (k, self.cnt[k])
        self.q["sp"].append((waits, [lambda e: e.nop()], k, 1))
        for e in ENGS:
            if e != "sp":
                self.q[e].append((self._waits(e, [ev]), [], None, 0))
        for b in self.dma_bufs:
            if b.dsem is not None and self.cnt[b.dsem] < SEM_LIMIT:
                self.dma_free.append(b.dsem)
            b.dsem = None
        self.dma_bufs = []

    def wait_all_on(self, e, bufs):
        deps = [b.w for b in bufs if b.w is not None]
        waits = self._waits(e, deps)
        self.q[e].append((waits, [], None, 0))

    def emit(self):
        nc = self.nc
        with nc.Block() as block:
            def mk(e):
                def body(engobj):
                    if e == "sp" and self.cid_ap is not None:
                        with engobj.register("rank_reg") as rr:
                            engobj.reg_load(rr, self.cid_ap)
                            self.dyn["p"] = engobj.snap(rr)
                            run(engobj)
                    else:
                        run(engobj)

                def run(engobj):
                    for (waits, fns, k, inc) in self.q[e]:
                        for (wk, wv) in waits:
                            engobj.wait_ge(self.sem[wk], wv)
                        n = len(fns)
                        for i, fn in enumerate(fns):
                            ins = fn(engobj)
                            if i == n - 1:
                                ins.then_inc(self.sem[k], inc)
                return body
            block.tensor(mk("pe"))
            block.scalar(mk("act"))
            block.vector(mk("dve"))
            block.gpsimd(mk("pool"))
            block.sync(mk("sp"))


class Tl:
    __slots__ = ("t", "b")

    def __init__(self, t, b):
        self.t = t
        self.b = b


class Ctx:
    def __init__(self, nc, stack):
        self.nc = nc
        self.st = stack
        self.S = Sched(nc, stack)
        self.uid = 0
        self.ps = []
        self.psi = 0
        self.outs = []

    def name(self, p):
        self.uid += 1
        return "%s_%d" % (p, self.uid)

    ARENA_LO = 16512
    ARENA_HI = 229344

    def sb(self, shape, dt, name="t"):
        n = self.name(name)
        esz = 2 if dt == BF16 else 4
        nbytes = esz
        for d_ in shape[1:]:
            nbytes *= int(d_)
        off = (getattr(self, "arena_ptr", self.ARENA_LO) + 31) // 32 * 32
        assert off + nbytes <= self.ARENA_HI, "SBUF arena overflow at %s: need %d have %d" % (n, nbytes, self.ARENA_HI - off)
        self.arena_ptr = off + nbytes
        t = self.nc.alloc_sbuf_tensor_at(n, list(shape), dt, offset=off)
        return Tl(t, Buf(n))

    def arena_reset(self):
        self.arena_ptr = self.ARENA_LO

    def pool(self, n, shape, dt, name="p"):
        return Rot([self.sb(shape, dt, name) for _ in range(n)])

    def init_psum(self, n=8):
        for i in range(n):
            nm = self.name("ps")
            t = self.st.enter_context(self.nc.psum_tensor(nm, [128, 512], F32))
            self.ps.append(Tl(t, Buf(nm)))

    def psum(self):
        p = self.ps[self.psi % len(self.ps)]
        self.psi += 1
        return p

    def dram_buf(self, name):
        return Buf(name)


class Rot:
    def __init__(self, items):
        self.items = items
        self.i = 0

    def next(self):
        it = self.items[self.i % len(self.items)]
        self.i += 1
        return it


def mm_group(K, ps, pairs, reads):
    n = len(pairs)

    def mk(i, l, r, o):
        return lambda e: e.matmul(o, l, r, start=(i == 0), stop=(i == n - 1))
    fns = [mk(i, l, r, o) for i, (l, r, o) in enumerate(pairs)]
    K.S.op("pe", fns, reads=reads, writes=[ps.b])


def rstd_from_sq(K, C, sq, sq_reads, nk, ncols, inv_n, extra=None):
    ps = K.psum()
    pairs = [(C["ones_bf"].t[:, :], sq.t[:, k, 0:ncols], ps.t[:, 0:ncols]) for k in range(nk)]
    mm_group(K, ps, pairs, reads=[C["ones_bf"].b, sq.b])
    r = C["rstd_pool"].next()
    K.S.op("act", lambda e: e.activation(out=r.t[:, 0:ncols], in_=ps.t[:, 0:ncols], func=AF.Sqrt,
                                         bias=C["eps_col"].t[:, 0:1], scale=inv_n),
           reads=[ps.b, C["eps_col"].b], writes=[r.b])
    K.S.op("dve", lambda e: e.reciprocal(out=r.t[:, 0:ncols], in_=r.t[:, 0:ncols]), reads=[r.b], writes=[r.b])
    return r


def pre_norm(K, C, x, xb, g, gcol0, h, hb, s):
    sl = slice(s * 512, (s + 1) * 512)
    sq = C["sq_pool"].next()
    K.S.op("act", lambda e: e.activation(out=sq.t[:, :, :], in_=x.t[:, :, sl], func=AF.Square),
           reads=[xb], writes=[sq.b])
    r = rstd_from_sq(K, C, sq, None, 8, 512, 1.0 / D)
    fns = [(lambda e, k=k: e.scalar_tensor_tensor(
        out=h.t[:, k, sl], in0=x.t[:, k, sl], scalar=g.t[:, gcol0 + k:gcol0 + k + 1], in1=r.t[:, :],
        op0=ALU.mult, op1=ALU.mult)) for k in range(8)]
    K.S.op("dve", fns, reads=[xb, r.b, g.b], writes=[hb])


def post_norm_residual(K, C, y, yb, g, gcol0, alpha, x, xb, s):
    sl = slice(s * 512, (s + 1) * 512)
    sq = C["sq_pool"].next()
    K.S.op("act", lambda e: e.activation(out=sq.t[:, :, :], in_=y.t[:, :, sl], func=AF.Square),
           reads=[yb], writes=[sq.b])
    r = rstd_from_sq(K, C, sq, None, 8, 512, 1.0 / D)
    for k in range(8):
        tmp = C["tmp_pool"].next()
        K.S.op("dve", lambda e, k=k, tmp=tmp: e.scalar_tensor_tensor(
            out=tmp.t[:, :], in0=y.t[:, k, sl], scalar=g.t[:, gcol0 + k:gcol0 + k + 1], in1=r.t[:, :],
            op0=ALU.mult, op1=ALU.mult), reads=[yb, r.b, g.b], writes=[tmp.b])
        K.S.op("dve", lambda e, k=k, tmp=tmp: e.scalar_tensor_tensor(
            out=x.t[:, k, sl], in0=tmp.t[:, :], scalar=float(alpha), in1=x.t[:, k, sl],
            op0=ALU.mult, op1=ALU.add), reads=[tmp.b, xb], writes=[xb])


def load_w(K, slot, src_ap, q="pool"):
    K.S.dma(q, lambda e: e.dma_start(out=slot.t[:], in_=src_ap), writes=[slot.b])


def ffn(K, C, h, hb, wg_ap, wu_ap, wd_ap, y, yb):
    a = C["a"]
    nsub = PASS // 512
    for f in range(NF):
        wg = C["wgu_pool"].next()
        load_w(K, wg, wg_ap[f].rearrange("p (k j) -> p k j", j=128))
        wu = C["wgu_pool"].next()
        load_w(K, wu, wu_ap[f].rearrange("p (k j) -> p k j", j=128))
        for s in range(nsub):
            sl = slice(s * 512, (s + 1) * 512)
            pg = K.psum()
            mm_group(K, pg, [(wg.t[:, k, :], h.t[:, k, sl], pg.t[:, :]) for k in range(8)], reads=[wg.b, hb])
            pu = K.psum()
            mm_group(K, pu, [(wu.t[:, k, :], h.t[:, k, sl], pu.t[:, :]) for k in range(8)], reads=[wu.b, hb])
            sg = C["tmp_pool"].next()
            K.S.op("act", lambda e, sg=sg, pg=pg: e.activation(out=sg.t[:, :], in_=pg.t[:, :], func=AF.Silu),
                   reads=[pg.b], writes=[sg.b])
            ab = a.b[f * nsub + s]
            K.S.op("dve", lambda e, sg=sg, pu=pu, f=f, sl=sl: e.tensor_tensor(
                out=a.t[:, f, sl], in0=sg.t[:, :], in1=pu.t[:, :], op=ALU.mult),
                reads=[sg.b, pu.b], writes=[ab])
    for d in range(8):
        wd = C["wd_pool"].next()
        load_w(K, wd, wd_ap[d].rearrange("p (c j) -> p c j", j=128))
        for s in range(nsub):
            sl = slice(s * 512, (s + 1) * 512)
            py = K.psum()
            mm_group(K, py, [(wd.t[:, c, :], a.t[:, c, sl], py.t[:, :]) for c in range(NF)],
                     reads=[wd.b] + [a.b[c * nsub + s] for c in range(NF)])
            K.S.op("act", lambda e, py=py, d=d, sl=sl: e.activation(out=y.t[:, d, sl], in_=py.t[:, :], func=AF.Copy),
                   reads=[py.b], writes=[yb])


class ATl:
    def __init__(self, t, bufs):
        self.t = t
        self.b = bufs


def common_consts(K):
    C = {}
    ones_bf = K.sb([128, 128], BF16, "ones_bf")
    K.S.op("dve", lambda e: e.memset(ones_bf.t[:, :], 1.0), writes=[ones_bf.b])
    C["ones_bf"] = ones_bf
    eps = K.sb([128, 1], F32, "eps")
    K.S.op("dve", lambda e: e.memset(eps.t[:, :], EPS), writes=[eps.b])
    C["eps_col"] = eps
    C["rstd_pool"] = K.pool(2, [128, 512], F32, "rstd")
    C["sq_pool"] = K.pool(1, [128, 8, 512], BF16, "sq")
    C["tmp_pool"] = K.pool(3, [128, 512], F32, "tmp")
    return C


def ffn_bufs(K, C):
    nsub = PASS // 512
    at = K.sb([128, NF, PASS], BF16, "a")
    C["a"] = ATl(at.t, [Buf("a%d" % i) for i in range(NF * nsub)])
    C["wgu_pool"] = K.pool(4, [128, 8, 128], BF16, "wgu")
    C["wd_pool"] = K.pool(2, [128, NF, 128], BF16, "wd")


OQ, OK_, OV, OCA, OCG, OGQ, OGK, OGV, OGR, OLR, OGT = 0, 512, 1024, 1536, 2048, 2560, 2816, 3072, 3584, 4096, 4112
FM_CHUNKS = ([("q", i, OQ + 128 * i) for i in range(4)] + [("k", i, OK_ + 128 * i) for i in range(4)] +
             [("a", i, OCA + 128 * i) for i in range(4)] + [("g", i, OCG + 128 * i) for i in range(4)] +
             [("gq", i, OGQ + 128 * i) for i in range(2)] + [("r", i, OGR + 128 * i) for i in range(4)] +
             [("lr", 0, OLR)])
NFM = len(FM_CHUNKS)
SB_QT, SB_KT, SB_V, SB_GQT, SB_GV = 0, 128 * TPC, 256 * TPC, 384 * TPC, 448 * TPC
SB_LEN = 576 * TPC
SF_UT, SF_GK, SF_LA = 0, 128 * TPC, 192 * TPC
SF_LEN = 256 * TPC


class XB:
    def __init__(self, U, nunits):
        self.U, self.nunits = U, nunits
        self.NCH = nunits // U
        self.CL = U * TPC

    def shape(self, nrank=4):
        return [self.NCH, nrank, self.CL]

    def fm_pieces(self, ap3, d, off, f0, nf):
        u0 = off // TPC + f0
        u1 = u0 + nf
        out = []
        u = u0
        while u < u1:
            c = u // self.U
            ue = min(u1, (c + 1) * self.U)
            lo = (u - c * self.U) * TPC
            out.append((u - u0, ue - u0, ap3[c, d, lo:lo + (ue - u) * TPC].rearrange("(f t) -> f t", t=TPC)))
            u = ue
        return out

    def tm_all(self, ap3, off, nfeat, tok0, ntok):
        e0 = off + tok0 * nfeat
        c = e0 // self.CL
        lo = e0 - c * self.CL
        assert lo + ntok * nfeat <= self.CL
        return ap3[c, :, lo:lo + ntok * nfeat].rearrange("d (t f) -> t d f", f=nfeat)

    def tm_src(self, ap3, j, off, nfeat, tok0, ntok):
        e0 = off + tok0 * nfeat
        c = e0 // self.CL
        lo = e0 - c * self.CL
        assert lo + ntok * nfeat <= self.CL
        return ap3[c, j, lo:lo + ntok * nfeat].rearrange("(t f) -> t f", f=nfeat)


X1B = XB(64, 576)
X1F = XB(32, 256)
X2B = XB(64, 128)
X2F = XB(32, 256)


def build_A():
    nc = bass.Bass("TRN2", target_bir_lowering=False)
    dt = nc.dram_tensor
    xT = dt("xT", [D, TPC], F32, kind="ExternalInput").ap()
    wg = dt("wg", [NF, 128, 8 * 128], F32, kind="ExternalInput").ap()
    wu = dt("wu", [NF, 128, 8 * 128], F32, kind="ExternalInput").ap()
    wd = dt("wd", [8, 128, NF * 128], F32, kind="ExternalInput").ap()
    wfm = dt("wfm", [NFM, 128, 8 * 128], F32, kind="ExternalInput").ap()
    wtm = dt("wtm", [128, 8 * 1280], F32, kind="ExternalInput").ap()
    walpha = dt("walpha", [16, 256], F32, kind="ExternalInput").ap()
    balpha = dt("balpha", [1, 256], F32, kind="ExternalInput").ap()
    gains = dt("gains", [128, 24], F32, kind="ExternalInput").ap()
    x1T = dt("x1T", [D, TPC], F32, kind="ExternalOutput").ap()
    send_bf = dt("send_bf", [4, SB_LEN], BF16, kind="ExternalOutput").ap()
    send_f = dt("send_f", [4, SF_LEN], F32, kind="ExternalOutput").ap()
    silur = dt("silur", [512, TPC], F32, kind="ExternalOutput").ap()
    with ExitStack() as st:
        K = Ctx(nc, st)
        K.init_psum(8)
        C = common_consts(K)
        ffn_bufs(K, C)
        bb = dict(b_x1=Buf("o_x1", True), b_sbf=Buf("o_sbf", True), b_sf=Buf("o_sf", True), b_sr=Buf("o_sr", True))
        K.outs += list(bb.values())
        emit_A(K, C, dict(xT=xT, wg=wg, wu=wu, wd=wd, wfm=wfm, wtm=wtm, walpha=walpha, balpha=balpha,
                          gains=gains, x1T=x1T, send_bf=send_bf, send_f=send_f, silur=silur, **bb))
        K.S.wait_all_on("sp", K.outs)
        K.S.emit()
    return nc


def emit_A(K, C, io):
    S = K.S
    nsub = PASS // 512
    g = K.sb([128, 24], F32, "gains")
    S.dma("sp", lambda e: e.dma_start(out=g.t[:, :], in_=io["gains"]), writes=[g.b])
    wtm = K.sb([128, 8, 1280], BF16, "wtm")
    load_w(K, wtm, io["wtm"].rearrange("p (k j) -> p k j", j=1280))
    wal = K.sb([16, 256], F32, "walpha")
    S.dma("sp", lambda e: e.dma_start(out=wal.t[:, :], in_=io["walpha"]), writes=[wal.b])
    bal = K.sb([1, 256], F32, "balpha")
    S.dma("sp", lambda e: e.dma_start(out=bal.t[:, :], in_=io["balpha"]), writes=[bal.b])
    ones_f = K.sb([1, 128], F32, "ones_f")
    S.op("dve", lambda e: e.memset(ones_f.t[:, :], 1.0), writes=[ones_f.b])

    x = K.sb([128, 8, PASS], F32, "x")
    h = K.sb([128, 8, PASS], BF16, "h")
    y = K.sb([128, 8, PASS], F32, "y")
    lrT = K.sb([16, PASS], F32, "lrT")
    wfm_pool = K.pool(2, [128, 8, 128], BF16, "wfm")
    st_bf = K.pool(4, [128, 512], BF16, "st_bf")
    st_f = K.pool(3, [128, 512], F32, "st_f")
    st_la = K.pool(2, [128, 256], F32, "st_la")
    xv = io["xT"].rearrange("(k p) t -> p k t", p=128)
    x1v = io["x1T"].rearrange("(k p) t -> p k t", p=128)
    sbf, sf = io["send_bf"], io["send_f"]
    o_x1, o_sbf, o_sf, o_sr = io["b_x1"], io["b_sbf"], io["b_sf"], io["b_sr"]
    b_xin = io.get("b_xin")

    def fm_out(xb, ap3, p, off, o, tsl, outbuf, pf0=0, nf=128):
        for (fa, fb, dst) in xb.fm_pieces(ap3, p, off, 0, nf):
            S.dma("sp", lambda e, dst=dst, fa=fa, fb=fb: e.dma_start(out=dst[:, tsl], in_=o.t[pf0 + fa:pf0 + fb, :]),
                  reads=[o.b], outs=[outbuf])

    for ps_ in range(TPC // PASS):
        t0 = ps_ * PASS
        S.dma("sp", lambda e, t0=t0: e.dma_start(out=x.t[:, :, :], in_=xv[:, :, t0:t0 + PASS]), writes=[x.b],
              reads=[b_xin] if b_xin is not None else [])
        for s in range(nsub):
            pre_norm(K, C, x, x.b, g, 0, h, h.b, s)
        ffn(K, C, h, h.b, io["wg"], io["wu"], io["wd"], y, y.b)
        for s in range(nsub):
            post_norm_residual(K, C, y, y.b, g, 16, 0.5, x, x.b, s)
        S.dma("sp", lambda e, t0=t0: e.dma_start(out=x1v[:, :, t0:t0 + PASS], in_=x.t[:, :, :]),
              reads=[x.b], outs=[o_x1])
        for s in range(nsub):
            pre_norm(K, C, x, x.b, g, 8, h, h.b, s)
        for ci, (kind, i, col) in enumerate(FM_CHUNKS):
            w = wfm_pool.next()
            load_w(K, w, io["wfm"][ci].rearrange("p (k j) -> p k j", j=128))
            for s in range(nsub):
                sl = slice(s * 512, (s + 1) * 512)
                tsl = slice(t0 + s * 512, t0 + (s + 1) * 512)
                pp = K.psum()
                mm_group(K, pp, [(w.t[:, k, :], h.t[:, k, sl], pp.t[:, :]) for k in range(8)], reads=[w.b, h.b])
                if kind in ("q", "k"):
                    o = st_bf.next()
                    sc = 0.125 if kind == "q" else 1.0
                    S.op("act", lambda e, o=o, pp=pp, sc=sc: e.activation(out=o.t[:, :], in_=pp.t[:, :], func=AF.Copy, scale=sc),
                         reads=[pp.b], writes=[o.b])
                    off = SB_QT if kind == "q" else SB_KT
                    fm_out(X1B, sbf, i, off, o, tsl, o_sbf)
                elif kind == "a":
                    S.op("act", lambda e, pp=pp, i=i, sl=sl: e.activation(out=y.t[:, i, sl], in_=pp.t[:, :], func=AF.Copy),
                         reads=[pp.b], writes=[y.b])
                elif kind == "g":
                    sg = C["tmp_pool"].next()
                    S.op("act", lambda e, sg=sg, pp=pp: e.activation(out=sg.t[:, :], in_=pp.t[:, :], func=AF.Sigmoid),
                         reads=[pp.b], writes=[sg.b])
                    o = st_f.next()
                    S.op("dve", lambda e, o=o, sg=sg, i=i, sl=sl: e.tensor_tensor(out=o.t[:, :], in0=y.t[:, i, sl], in1=sg.t[:, :], op=ALU.mult),
                         reads=[sg.b, y.b], writes=[o.b])
                    fm_out(X1F, sf, i, SF_UT, o, tsl, o_sf)
                elif kind == "gq":
                    o = st_bf.next()
                    S.op("act", lambda e, o=o, pp=pp: e.activation(out=o.t[:, :], in_=pp.t[:, :], func=AF.Copy, scale=0.125),
                         reads=[pp.b], writes=[o.b])
                    for half in range(2):
                        fm_out(X1B, sbf, 2 * i + half, SB_GQT, o, tsl, o_sbf, pf0=64 * half, nf=64)
                elif kind == "r":
                    o = st_f.next()
                    S.op("act", lambda e, o=o, pp=pp: e.activation(out=o.t[:, :], in_=pp.t[:, :], func=AF.Silu),
                         reads=[pp.b], writes=[o.b])
                    dst = io["silur"][128 * i:128 * i + 128, tsl]
                    S.dma("sp", lambda e, o=o, dst=dst: e.dma_start(out=dst, in_=o.t[:, :]), reads=[o.b], outs=[o_sr])
                elif kind == "lr":
                    S.op("act", lambda e, pp=pp, sl=sl: e.activation(out=lrT.t[:, sl], in_=pp.t[0:16, :], func=AF.Copy),
                         reads=[pp.b], writes=[lrT.b])
        for tb in range(PASS // 128):
            bsl = slice(tb * 128, (tb + 1) * 128)
            gsl = slice(t0 + tb * 128, t0 + (tb + 1) * 128)
            pp = K.psum()
            mm_group(K, pp, [(h.t[:, k, bsl], wtm.t[:, k, 0:512], pp.t[:, :]) for k in range(8)], reads=[wtm.b, h.b])
            o = st_bf.next()
            S.op("act", lambda e, o=o, pp=pp: e.activation(out=o.t[:, :], in_=pp.t[:, :], func=AF.Copy), reads=[pp.b], writes=[o.b])
            S.dma("sp", lambda e, o=o, gsl=gsl: e.dma_start(out=X1B.tm_all(sbf, SB_V, 128, gsl.start, 128), in_=o.t[:, :].rearrange("t (p f) -> t p f", f=128)),
                  reads=[o.b], outs=[o_sbf])
            pp = K.psum()
            mm_group(K, pp, [(h.t[:, k, bsl], wtm.t[:, k, 768:1280], pp.t[:, :]) for k in range(8)], reads=[wtm.b, h.b])
            o = st_bf.next()
            S.op("dve", lambda e, o=o, pp=pp: e.tensor_copy(out=o.t[:, :], in_=pp.t[:, :]), reads=[pp.b], writes=[o.b])
            S.dma("sp", lambda e, o=o, gsl=gsl: e.dma_start(out=X1B.tm_all(sbf, SB_GV, 128, gsl.start, 128), in_=o.t[:, :].rearrange("t (p f) -> t p f", f=128)),
                  reads=[o.b], outs=[o_sbf])
            pp = K.psum()
            mm_group(K, pp, [(h.t[:, k, bsl], wtm.t[:, k, 512:768], pp.t[:, 0:256]) for k in range(8)], reads=[wtm.b, h.b])
            o = st_la.next()
            S.op("act", lambda e, o=o, pp=pp: e.activation(out=o.t[:, :], in_=pp.t[:, 0:256], func=AF.Copy), reads=[pp.b], writes=[o.b])
            S.dma("sp", lambda e, o=o, gsl=gsl: e.dma_start(out=X1F.tm_all(sf, SF_GK, 64, gsl.start, 128), in_=o.t[:, :].rearrange("t (p f) -> t p f", f=64)),
                  reads=[o.b], outs=[o_sf])
            pp = K.psum()
            mm_group(K, pp, [(lrT.t[:, bsl], wal.t[:, :], pp.t[:, 0:256]), (ones_f.t[:, :], bal.t[:, :], pp.t[:, 0:256])],
                     reads=[lrT.b, wal.b, ones_f.b, bal.b])
            o = st_la.next()
            S.op("act", lambda e, o=o, pp=pp: e.activation(out=o.t[:, :], in_=pp.t[:, 0:256], func=AF.Exp, scale=-1.0), reads=[pp.b], writes=[o.b])
            S.op("act", lambda e, o=o: e.activation(out=o.t[:, :], in_=o.t[:, :], func=AF.Ln, bias=1.0), reads=[o.b], writes=[o.b])
            S.op("dve", lambda e, o=o: e.tensor_scalar(out=o.t[:, :], in0=o.t[:, :], scalar1=-1.0 / 16.0, scalar2=None, op0=ALU.mult),
                 reads=[o.b], writes=[o.b])
            S.dma("sp", lambda e, o=o, gsl=gsl: e.dma_start(out=X1F.tm_all(sf, SF_LA, 64, gsl.start, 128), in_=o.t[:, :].rearrange("t (p f) -> t p f", f=64)),
                  reads=[o.b], outs=[o_sf])


def _c(a):
    return np.ascontiguousarray(a)


def lay_fm_w(W, ncols_chunks):
    Kc = W.shape[0] // 128
    n = W.shape[1] // 128
    return _c(W.reshape(Kc, 128, n, 128).transpose(2, 1, 0, 3).reshape(n, 128, Kc * 128))


def lay_pk(v):
    return _c(v.reshape(-1, 128).T)


def prep_A_weights(inp, l):
    w_in = inp["w_in"][l]
    cols = []
    for (kind, i, col) in FM_CHUNKS:
        blk = np.zeros((D, 128), np.float32)
        n = 16 if kind == "lr" else 128
        blk[:, :n] = w_in[:, col:col + n]
        cols.append(blk)
    wfm = lay_fm_w(np.concatenate(cols, axis=1), NFM)
    tm = np.concatenate([w_in[:, OV:OV + 512], w_in[:, OGK:OGK + 256], w_in[:, OGV:OGV + 512]], axis=1)
    wtm = _c(tm.reshape(8, 128, 1280).transpose(1, 0, 2).reshape(128, 8 * 1280))
    gains = np.concatenate([lay_pk(inp["norm_pre"][l, 0]), lay_pk(inp["norm_pre"][l, 1]), lay_pk(inp["norm_post"][l, 0])], axis=1)
    return dict(wg=lay_fm_w(inp["ffn1_w_gate"][l], NF), wu=lay_fm_w(inp["ffn1_w_up"][l], NF),
                wd=lay_fm_w(inp["ffn1_w_down"][l], 8), wfm=wfm, wtm=wtm,
                walpha=_c(inp["gla_w_alpha"][l]), balpha=_c(inp["gla_b_alpha"][l][None, :]), gains=_c(gains))


RB_LEN = 128 * TPC
RF_CY, RF_GN = 0, 128 * TPC
RF_LEN = 256 * TPC


def build_C(dbg=0):
    nc = bass.Bass("TRN2", target_bir_lowering=False)
    dt = nc.dram_tensor
    io = dict(
        x1T=dt("x1T", [D, TPC], F32, kind="ExternalInput").ap(),
        recv_bf=dt("recv_bf", [4, RB_LEN], BF16, kind="ExternalInput").ap(),
        recv_f=dt("recv_f", [4, RF_LEN], F32, kind="ExternalInput").ap(),
        silur=dt("silur", [512, TPC], F32, kind="ExternalInput").ap(),
        wgt=dt("wgt", [24, 128, 8 * 128], F32, kind="ExternalInput").ap(),
        wbr=dt("wbr", [24, 128, 4 * 128], F32, kind="ExternalInput").ap(),
        wo=dt("wo", [8, 128, 8 * 128], F32, kind="ExternalInput").ap(),
        wg=dt("wg", [NF, 128, 8 * 128], F32, kind="ExternalInput").ap(),
        wu=dt("wu", [NF, 128, 8 * 128], F32, kind="ExternalInput").ap(),
        wd=dt("wd", [8, 128, NF * 128], F32, kind="ExternalInput").ap(),
        gains=dt("gains", [128, 40], F32, kind="ExternalInput").ap(),
        x3T=dt("x3T", [D, TPC], F32, kind="ExternalOutput").ap(),
    )
    with ExitStack() as st:
        K = Ctx(nc, st)
        K.init_psum(8)
        C = common_consts(K)
        ffn_bufs(K, C)
        r2bf_, r2f_ = io["recv_bf"], io["recv_f"]
        io["recv_bf"] = lambda p, a, b: r2bf_[p:p + 1, a:b]
        io["recv_f"] = lambda p, a, b: r2f_[p:p + 1, a:b]
        io["b_x3"] = Buf("o_x3", True)
        K.outs.append(io["b_x3"])
        emit_C(K, C, io, dbg)
        K.S.wait_all_on("sp", K.outs)
        K.S.emit()
    return nc


def emit_C(K, C, io, dbg=0):
    S = K.S
    nsub = PASS // 512
    T = TPC
    g = K.sb([128, 40], F32, "gainsC")
    S.dma("sp", lambda e: e.dma_start(out=g.t[:, :], in_=io["gains"]), writes=[g.b])
    x = K.sb([128, 8, PASS], F32, "x")
    h = K.sb([128, 8, PASS], BF16, "h")
    y = K.sb([128, 8, PASS], F32, "y")
    a = C["a"]
    sr_pool = K.pool(1, [128, 4, 512], F32, "sr")
    wgt_pool = K.pool(2, [128, 8, 128], BF16, "wgt")
    wbr_pool = K.pool(2, [128, 4, 128], BF16, "wbr")
    acc = [K.sb([128, 512], F32, "acc") for _ in range(nsub)]
    ln_mean = K.sb([128, 512], F32, "ln_mean")
    ln_msq = K.sb([128, 512], F32, "ln_msq")
    xv = io["x1T"].rearrange("(k p) t -> p k t", p=128)
    ov = io["x3T"].rearrange("(k p) t -> p k t", p=128)
    r2bf, r2f = io["recv_bf"], io["recv_f"]
    sr_v = io["silur"].rearrange("(c f) t -> f c t", f=128)
    o_x3 = io["b_x3"]
    rd2 = [io[k] for k in ("b_r2bf", "b_r2f") if io.get(k) is not None]
    rdx = [io[k] for k in ("b_x1",) if io.get(k) is not None]
    rds = [io[k] for k in ("b_sr",) if io.get(k) is not None]

    def ab(f0, f1, s):
        return [a.b[f * nsub + s] for f in range(f0, f1)]

    for ps_ in range(TPC // PASS):
        t0 = ps_ * PASS
        S.dma("sp", lambda e, t0=t0: e.dma_start(out=x.t[:, :, :], in_=xv[:, :, t0:t0 + PASS]), writes=[x.b], reads=rdx)
        for s in range(nsub):
            pre_norm(K, C, x, x.b, g, 0, h, h.b, s)
        for s in range(nsub):
            sl = slice(s * 512, (s + 1) * 512)
            tsl = slice(t0 + s * 512, t0 + (s + 1) * 512)
            for p in range(4):
                S.dma("sp", lambda e, sl=sl, tsl=tsl, p=p: e.dma_start(out=y.t[:, p, sl], in_=r2f[0, p].rearrange("(f t) -> f t", t=T)[:, tsl]),
                      writes=[y.b], reads=rd2)
                S.dma("sp", lambda e, sl=sl, tsl=tsl, p=p: e.dma_start(out=y.t[:, 4 + p, sl], in_=r2f[1, p].rearrange("(f t) -> f t", t=T)[:, tsl]),
                      writes=[y.b], reads=rd2)
                S.dma("sp", lambda e, sl=sl, tsl=tsl, p=p: e.dma_start(out=a.t[:, 8 + p, sl], in_=r2bf[p].rearrange("(f t) -> f t", t=T)[:, tsl]),
                      writes=ab(8 + p, 9 + p, s), reads=rd2)
            sr = sr_pool.next()
            S.dma("sp", lambda e, sr=sr, tsl=tsl: e.dma_start(out=sr.t[:, :, :], in_=sr_v[:, :, tsl]), writes=[sr.b], reads=rds)
            sq = C["sq_pool"].next()
            S.op("act", lambda e, sq=sq, sl=sl: e.activation(out=sq.t[:, 0:4, :], in_=y.t[:, 0:4, sl], func=AF.Copy),
                 reads=[y.b], writes=[sq.b])
            S.op("act", lambda e, sq=sq, sl=sl: e.activation(out=sq.t[:, 4:8, :], in_=y.t[:, 0:4, sl], func=AF.Square),
                 reads=[y.b], writes=[sq.b])
            p1 = K.psum()
            mm_group(K, p1, [(C["ones_bf"].t[:, :], sq.t[:, c, :], p1.t[:, :]) for c in range(4)], reads=[C["ones_bf"].b, sq.b])
            p2 = K.psum()
            mm_group(K, p2, [(C["ones_bf"].t[:, :], sq.t[:, 4 + c, :], p2.t[:, :]) for c in range(4)], reads=[C["ones_bf"].b, sq.b])
            mean = ln_mean
            S.op("act", lambda e, mean=mean, p1=p1: e.activation(out=mean.t[:, :], in_=p1.t[:, :], func=AF.Copy, scale=1.0 / 512),
                 reads=[p1.b], writes=[mean.b])
            msq = ln_msq
            S.op("dve", lambda e, mean=mean, msq=msq: e.tensor_tensor(out=msq.t[:, :], in0=mean.t[:, :], in1=mean.t[:, :], op=ALU.mult),
                 reads=[mean.b], writes=[msq.b])
            S.op("dve", lambda e, msq=msq, p2=p2: e.scalar_tensor_tensor(out=msq.t[:, :], in0=p2.t[:, :], scalar=1.0 / 512, in1=msq.t[:, :],
                                                                       op0=ALU.mult, op1=ALU.subtract),
                 reads=[p2.b, msq.b], writes=[msq.b])
            rs = C["rstd_pool"].next()
            S.op("act", lambda e, rs=rs, msq=msq: e.activation(out=rs.t[:, :], in_=msq.t[:, :], func=AF.Sqrt, bias=C["eps_col"].t[:, 0:1], scale=1.0),
                 reads=[msq.b, C["eps_col"].b], writes=[rs.b])
            S.op("dve", lambda e, rs=rs: e.reciprocal(out=rs.t[:, :], in_=rs.t[:, :]), reads=[rs.b], writes=[rs.b])
            for c in range(4):
                t = C["tmp_pool"].next()
                S.op("dve", lambda e, t=t, c=c, sl=sl, mean=mean: e.tensor_tensor(out=t.t[:, :], in0=y.t[:, c, sl], in1=mean.t[:, :], op=ALU.subtract),
                     reads=[y.b, mean.b], writes=[t.b])
                S.op("dve", lambda e, t=t, rs=rs: e.tensor_tensor(out=t.t[:, :], in0=t.t[:, :], in1=rs.t[:, :], op=ALU.mult),
                     reads=[t.b, rs.b], writes=[t.b])
                S.op("act", lambda e, t=t, c=c, sl=sl: e.activation(out=a.t[:, c, sl], in_=t.t[:, :], func=AF.Silu,
                                                                      scale=g.t[:, 32 + c:33 + c], bias=g.t[:, 36 + c:37 + c]),
                     reads=[t.b, g.b], writes=ab(c, c + 1, s))
                S.op(POOL_ENG, lambda e, c=c, sl=sl, sr=sr: e.tensor_tensor(out=a.t[:, 4 + c, sl], in0=y.t[:, 4 + c, sl], in1=sr.t[:, c, :], op=ALU.mult),
                     reads=[y.b, sr.b], writes=ab(4 + c, 5 + c, s))
        base = {0: 8, 1: 0, 2: 4}
        for d in range(8):
            for gi in range(3):
                wb = wbr_pool.next()
                load_w(K, wb, io["wbr"][gi * 8 + d].rearrange("p (k j) -> p k j", j=128))
                wgt = wgt_pool.next()
                load_w(K, wgt, io["wgt"][gi * 8 + d].rearrange("p (k j) -> p k j", j=128))
                for s in range(nsub):
                    sl = slice(s * 512, (s + 1) * 512)
                    pb = K.psum()
                    mm_group(K, pb, [(wb.t[:, kc, :], a.t[:, base[gi] + kc, sl], pb.t[:, :]) for kc in range(4)],
                             reads=[wb.b] + ab(base[gi], base[gi] + 4, s))
                    pl = K.psum()
                    mm_group(K, pl, [(wgt.t[:, k, :], h.t[:, k, sl], pl.t[:, :]) for k in range(8)], reads=[wgt.b, h.b])
                    sig = C["tmp_pool"].next()
                    S.op("act", lambda e, sig=sig, pl=pl: e.activation(out=sig.t[:, :], in_=pl.t[:, :], func=AF.Sigmoid),
                         reads=[pl.b], writes=[sig.b])
                    if gi == 0:
                        S.op("dve", lambda e, sig=sig, pb=pb, s=s: e.tensor_tensor(out=acc[s].t[:, :], in0=sig.t[:, :], in1=pb.t[:, :], op=ALU.mult),
                             reads=[sig.b, pb.b], writes=[acc[s].b])
                    else:
                        S.op("dve", lambda e, sig=sig, pb=pb: e.tensor_tensor(out=sig.t[:, :], in0=sig.t[:, :], in1=pb.t[:, :], op=ALU.mult),
                             reads=[sig.b, pb.b], writes=[sig.b])
                        if gi == 1:
                            S.op(POOL_ENG, lambda e, sig=sig, s=s: e.tensor_tensor(out=acc[s].t[:, :], in0=acc[s].t[:, :], in1=sig.t[:, :], op=ALU.add),
                                 reads=[sig.b, acc[s].b], writes=[acc[s].b])
                        else:
                            S.op(POOL_ENG, lambda e, sig=sig, s=s, d=d, sl=sl: e.tensor_tensor(out=a.t[:, 12 + d, sl], in0=acc[s].t[:, :], in1=sig.t[:, :], op=ALU.add),
                                 reads=[sig.b, acc[s].b], writes=ab(12 + d, 13 + d, s))
        for do in range(8):
            wo = C["wgu_pool"].next()
            load_w(K, wo, io["wo"][do].rearrange("p (k j) -> p k j", j=128))
            for s in range(nsub):
                sl = slice(s * 512, (s + 1) * 512)
                pp = K.psum()
                mm_group(K, pp, [(wo.t[:, k, :], a.t[:, 12 + k, sl], pp.t[:, :]) for k in range(8)], reads=[wo.b] + ab(12, 20, s))
                S.op("act", lambda e, pp=pp, do=do, sl=sl: e.activation(out=y.t[:, do, sl], in_=pp.t[:, :], func=AF.Copy),
                     reads=[pp.b], writes=[y.b])
        if dbg == 2:
            S.dma("sp", lambda e, t0=t0: e.dma_start(out=ov[:, :, t0:t0 + PASS], in_=y.t[:, :, :]), reads=[y.b], outs=[o_x3])
            continue
        for s in range(nsub):
            post_norm_residual(K, C, y, y.b, g, 8, 1.0, x, x.b, s)
        if dbg == 1:
            S.dma("sp", lambda e, t0=t0: e.dma_start(out=ov[:, :, t0:t0 + PASS], in_=x.t[:, :, :]), reads=[x.b], outs=[o_x3])
            continue
        for s in range(nsub):
            pre_norm(K, C, x, x.b, g, 16, h, h.b, s)
        ffn(K, C, h, h.b, io["wg"], io["wu"], io["wd"], y, y.b)
        for s in range(nsub):
            post_norm_residual(K, C, y, y.b, g, 24, 0.5, x, x.b, s)
        S.dma("sp", lambda e, t0=t0: e.dma_start(out=ov[:, :, t0:t0 + PASS], in_=x.t[:, :, :]), reads=[x.b], outs=[o_x3])


def prep_C_weights(inp, l):
    w_in = inp["w_in"][l]
    wgt = lay_fm_w(w_in[:, OGT:OGT + 3072], 24)
    wb = inp["w_branch"][l]
    wbr = np.concatenate([lay_fm_w(wb[gi], 8) for gi in range(3)], axis=0)
    gains = np.concatenate([lay_pk(inp["norm_pre"][l, 1]), lay_pk(inp["norm_post"][l, 1]), lay_pk(inp["norm_pre"][l, 2]),
                            lay_pk(inp["norm_post"][l, 2]), lay_pk(inp["conv_ln_g"][l]), lay_pk(inp["conv_ln_b"][l])], axis=1)
    return dict(wgt=wgt, wbr=_c(wbr), wo=lay_fm_w(inp["w_out"][l], 8), wg=lay_fm_w(inp["ffn2_w_gate"][l], NF),
                wu=lay_fm_w(inp["ffn2_w_up"][l], NF), wd=lay_fm_w(inp["ffn2_w_down"][l], 8), gains=_c(gains))


NQT = SEQ // 512
NBLK = SEQ // 128
S2B_LEN = 128 * TPC
S2F_LEN = 256 * TPC


def build_B():
    nc = bass.Bass("TRN2", target_bir_lowering=False)
    dt = nc.dram_tensor
    io = dict(
        rbf=dt("rbf", [4, SB_LEN], BF16, kind="ExternalInput").ap(),
        rf=dt("rf", [4, SF_LEN], F32, kind="ExternalInput").ap(),
        bw=dt("bw", [128, 33], F32, kind="ExternalInput").ap(),
        s2bf=dt("s2bf", [4, S2B_LEN], BF16, kind="ExternalOutput").ap(),
        s2f=dt("s2f", [4, S2F_LEN], F32, kind="ExternalOutput").ap(),
    )
    with ExitStack() as st:
        K = Ctx(nc, st)
        K.init_psum(8)
        rbf_, rf_ = io["rbf"], io["rf"]
        io["rbf"] = lambda j, a, b: rbf_[j:j + 1, a:b]
        io["rf"] = lambda j, a, b: rf_[j:j + 1, a:b]
        io["b_s2bf"], io["b_s2f"] = Buf("o_s2bf", True), Buf("o_s2f", True)
        K.outs += [io["b_s2bf"], io["b_s2f"]]
        emit_B(K, io)
        K.S.wait_all_on("sp", K.outs)
        K.S.emit()
    return nc


def emit_B(K, io):
    S = K.S
    T = TPC
    rbf, rf = io["rbf"], io["rf"]
    o_bfs, o_cys, o_gns = io["b_s2bf"], io["b_s2cy"], io["b_s2gn"]
    coll2 = io.get("coll2")
    N2 = 128 * T
    STQ = "pool" if io.get("late_picks") is not None else "sp"
    rd_qkv = [io[k] for k in ("b_rbf",) if io.get(k) is not None]
    rd_g = [io[k] for k in ("b_rbf2",) if io.get(k) is not None]
    rd_f = [io[k] for k in ("b_rf",) if io.get(k) is not None]
    ones_bf = K.sb([128, 128], BF16, "ones_bf")
    S.op("dve", lambda e: e.memset(ones_bf.t[:, :], 1.0), writes=[ones_bf.b])
    negones = K.sb([128, 1], BF16, "negones")
    S.op("dve", lambda e: e.memset(negones.t[:, :], -1.0), writes=[negones.b])
    eps = K.sb([128, 1], F32, "eps")
    S.op("dve", lambda e: e.memset(eps.t[:, :], EPS), writes=[eps.b])
    tri2 = K.sb([128, 128], BF16, "tri2")
    S.op("pool", lambda e: e.memset(tri2.t[:, :], -1.0), writes=[tri2.b])
    S.op("pool", lambda e: e.affine_select(out=tri2.t[:, :], in_=tri2.t[:, :], pattern=[[-1, 128]], compare_op=ALU.is_ge,
                                            fill=0.0, base=0, channel_multiplier=1), reads=[tri2.b], writes=[tri2.b])
    S.op("pool", lambda e: e.memset(tri2.t[0:1, :], 1.0), reads=[tri2.b], writes=[tri2.b])
    tris = K.sb([128, 128], BF16, "tris")
    S.op("pool", lambda e: e.memset(tris.t[:, :], 1.0), writes=[tris.b])
    S.op("pool", lambda e: e.affine_select(out=tris.t[:, :], in_=tris.t[:, :], pattern=[[-1, 128]], compare_op=ALU.is_gt,
                                            fill=0.0, base=0, channel_multiplier=1), reads=[tris.b], writes=[tris.b])
    S.op("pool", lambda e: e.memset(tris.t[64:128, 0:64], 0.0), reads=[tris.b], writes=[tris.b])
    cind = K.sb([128, 2], BF16, "cind")
    S.op("pool", lambda e: e.memset(cind.t[:, :], 0.0), writes=[cind.b])
    S.op("pool", lambda e: e.memset(cind.t[0:64, 0:1], 1.0), reads=[cind.b], writes=[cind.b])
    S.op("pool", lambda e: e.memset(cind.t[64:128, 1:2], 1.0), reads=[cind.b], writes=[cind.b])
    bw = K.sb([128, 33], F32, "bw")
    S.dma("sp", lambda e: e.dma_start(out=bw.t[:, :], in_=io["bw"]), writes=[bw.b])
    qT = K.sb([128, SEQ], BF16, "qT")
    qT2 = K.sb([128, SEQ], BF16, "qT2")
    S.op("pool", lambda e: e.memset(qT.t[64:128, :], 0.0), writes=[qT.b])
    S.op("pool", lambda e: e.memset(qT2.t[0:64, :], 0.0), writes=[qT2.b])
    qz = [qT, qT2]
    KB = [Buf("kT_j%d" % j) for j in range(4)]
    VB = [Buf("v_j%d" % j) for j in range(4)]
    QB = [[Buf("q%d_j%d" % (h, j)) for j in range(4)] for h in range(2)]
    GB = [Buf("gla_j%d" % j) for j in range(4)]
    UB = [Buf("u_j%d" % j) for j in range(4)]

    def ksrc(b):
        return sorted({min(127 * b, SEQ - 1) // T, min(127 * b + 126, SEQ - 1) // T})
    NB2 = (SEQ + 126) // 127
    kT = K.sb([128, NB2 * 128], BF16, "kT2")
    v = K.sb([128, NB2, 128], BF16, "v2")
    S.op("pool", lambda e: e.memset(kT.t[:, :], 0.0), writes=[kT.b])
    S.op("pool", lambda e: e.memset(v.t[:, :, :], 0.0), writes=[v.b])
    kT3 = kT.t[:, :].rearrange("f (b s) -> f b s", s=128)

    def key_runs(ta, tb):
        runs = []
        t = ta
        while t < tb:
            b, s = t // 127, t % 127
            if s == 0 and tb - t >= 127:
                nb = (tb - t) // 127
                runs.append((b, nb, 0, 127, t - ta))
                t += nb * 127
            else:
                n = min(127 - s, tb - t)
                runs.append((b, 1, s, n, t - ta))
                t += n
        return runs
    gqT = K.sb([64, SEQ], BF16, "gqT")
    gv = K.sb([128, NBLK, 128], BF16, "gv")
    gk = K.sb([128, NBLK, 64], F32, "gk")
    la = K.sb([128, NBLK, 64], F32, "la")
    u = K.sb([128, 32 + SEQ], F32, "u")
    S.op("pool", lambda e: e.memset(u.t[:, 0:32], 0.0), writes=[u.b])
    def ld(out_ap, in_ap, jbuf, after=(), rdb=None):
        S.dma("sp", lambda e: e.dma_start(out=out_ap, in_=in_ap), reads=list(rdb) + list(after), outs=[jbuf])

    for phase_, j in [(0, 0), (0, 1), (0, 2), (0, 3), (1, 0), (1, 1), (1, 2), (1, 3)]:
        ts = slice(j * T, (j + 1) * T)
        if phase_ == 1 and j == 0 and io.get("late_picks") is not None:
            io["late_picks"]()
        for (fa, fb, sap) in (X1B.fm_pieces(rbf, j, SB_KT, 0, 128) if phase_ == 0 else []):
            for (b0, nb, s0, ns, off) in key_runs(j * T, (j + 1) * T):
                if nb > 1 or ns == 127:
                    dst = kT3[fa:fb, b0:b0 + nb, 1:128]
                    srcv = sap[:, off:off + nb * 127].rearrange("f (b s) -> f b s", s=127)
                else:
                    dst = kT3[fa:fb, b0, 1 + s0:1 + s0 + ns]
                    srcv = sap[:, off:off + ns]
                ld(dst, srcv, KB[j], after=[kT.b], rdb=rd_qkv)
        for (fa, fb, sap) in (X1B.fm_pieces(rbf, j, SB_QT, 0, 128) if phase_ == 0 else []):
            tq = qT if fa < 64 else qT2
            assert (fa < 64) == (fb <= 64)
            ld(tq.t[fa:fb, ts], sap, QB[0 if fa < 64 else 1][j], after=[tq.b], rdb=rd_qkv)
        for (fa, fb, sap) in (X1B.fm_pieces(rbf, j, SB_GQT, 0, 64) if phase_ == 1 else []):
            ld(gqT.t[fa:fb, ts], sap, GB[j], rdb=rd_g)
        for (fa, fb, sap) in (X1F.fm_pieces(rf, j, SF_UT, 0, 128) if phase_ == 1 else []):
            ld(u.t[fa:fb, 32 + j * T:32 + (j + 1) * T], sap, UB[j], after=[u.b], rdb=rd_f)
        for tok0 in (0, 1024):
            kb0 = j * 16 + tok0 // 128
            vsrc = X1B.tm_src(rbf, j, SB_V, 128, tok0, 1024)
            for (b0, nb, s0, ns, off) in (key_runs(j * T + tok0, j * T + tok0 + 1024) if phase_ == 0 else []):
                if nb > 1 or ns == 127:
                    dst = v.t[1:128, b0:b0 + nb, :]
                    srcv = vsrc[off:off + nb * 127, :].rearrange("(b s) f -> s b f", s=127)
                else:
                    dst = v.t[1 + s0:1 + s0 + ns, b0, :]
                    srcv = vsrc[off:off + ns, :]
                ld(dst, srcv, VB[j], after=[v.b], rdb=rd_qkv)
            if phase_ == 0:
                continue
            ld(gv.t[:, kb0:kb0 + 8, :], X1B.tm_src(rbf, j, SB_GV, 128, tok0, 1024).rearrange("(k t) f -> t k f", t=128), GB[j], rdb=rd_g)
            ld(gk.t[:, kb0:kb0 + 8, :], X1F.tm_src(rf, j, SF_GK, 64, tok0, 1024).rearrange("(k t) f -> t k f", t=128), GB[j], rdb=rd_f)
            ld(la.t[:, kb0:kb0 + 8, :], X1F.tm_src(rf, j, SF_LA, 64, tok0, 1024).rearrange("(k t) f -> t k f", t=128), GB[j], rdb=rd_f)
    ps_z = Rot(K.ps[0:3] + K.ps[5:6])
    ps_o = K.ps[3:5]
    ps_g = K.ps[7]
    e_pool = K.pool(3, [128, 512], F32, "e")
    sp_pool = K.pool(4, [128, 512], BF16, "sp")
    w_pool = K.pool(3, [128, 512], BF16, "w")
    osb_pool = K.pool(2, [128, 512], BF16, "osb")

    def conv_gen():
        acc_pool = K.pool(2, [128, 512], F32, "cacc")

        def ub_of(pc):
            js = {(pc * 512) // T, max(pc * 512 - 30, 0) // T, (pc * 512 + 511) // T}
            return [u.b] + [UB[j] for j in sorted(js)]
        for pc in range(SEQ // 512):
            acc = acc_pool.next()
            c0 = 2 + pc * 512
            S.op("dve", lambda e, acc=acc, c0=c0: e.tensor_scalar(out=acc.t[:, :], in0=u.t[:, c0:c0 + 512], scalar1=bw.t[:, 0:1], scalar2=bw.t[:, 31:32],
                                                                    op0=ALU.mult, op1=ALU.add), reads=ub_of(pc) + [bw.b], writes=[acc.b])
            yield
            for k in range(1, 31):
                S.op("dve", lambda e, acc=acc, c0=c0, k=k: e.scalar_tensor_tensor(out=acc.t[:, :], in0=u.t[:, c0 + k:c0 + k + 512], scalar=bw.t[:, k:k + 1],
                                                                                  in1=acc.t[:, :], op0=ALU.mult, op1=ALU.add),
                     reads=ub_of(pc) + [bw.b, acc.b], writes=[acc.b])
                yield
            cols = slice((pc % 4) * 512, (pc % 4 + 1) * 512)
            dst = io["s2f"][pc // 4, 0].rearrange("(f t) -> f t", t=T)[:, cols]
            S.dma(STQ, lambda e, acc=acc, dst=dst: e.dma_start(out=dst, in_=acc.t[:, :]), reads=[acc.b], outs=[o_cys[pc // 4]])
            if pc % 4 == 3 and coll2 is not None:
                coll2("cy", pc // 4)
            yield

    def gla_gen():
        state = K.sb([64, 128], F32, "gstate")
        S.op("pool", lambda e: e.memset(state.t[:, :], 0.0), writes=[state.b])
        sbf_pool = K.pool(2, [64, 128], BF16, "gstbf")
        labf_pool = K.pool(2, [128, 64], BF16, "labf")
        ed_pool = K.pool(2, [128, 64], F32, "ged")
        kd_pool = K.pool(2, [128, 64], BF16, "gkd")
        lam_pool = K.pool(2, [64, 2], F32, "glam")
        osb_p = K.pool(2, [128, 128], F32, "gosb")
        sq_p = K.pool(2, [128, 128], BF16, "gsq")
        rs_p = K.pool(2, [128, 128], F32, "grs")
        gst_pool = K.pool(2, [128, 512], F32, "gnst")
        pg = ps_g
        r_dte = pg.t[:, 0:64]
        r_lam = pg.t[0:64, 64:66]
        r_st = pg.t[0:64, 128:256]
        r_o = pg.t[:, 256:384]
        r_ss = pg.t[:, 384:512]
        bg = pg.b
        gst = None
        for blk in range(NBLK):
            labf = labf_pool.next()
            S.op("pool", lambda e, labf=labf, blk=blk: e.tensor_copy(out=labf.t[:, :], in_=la.t[:, blk, :]), reads=[GB[blk // 16]], writes=[labf.b])
            yield
            S.op("pe", [lambda e, labf=labf: e.matmul(r_dte, tris.t[:, :], labf.t[:, :], start=True, stop=True)], reads=[tris.b, labf.b], writes=[bg])
            S.op("pe", [lambda e, labf=labf: e.matmul(r_lam, labf.t[:, :], cind.t[:, :], start=True, stop=True)], reads=[cind.b, labf.b], writes=[bg])
            yield
            ed = ed_pool.next()
            S.op("act", lambda e, ed=ed: e.activation(out=ed.t[:, :], in_=r_dte, func=AF.Exp), writes=[ed.b, bg])
            lam = lam_pool.next()
            S.op("act", lambda e, lam=lam: e.activation(out=lam.t[:, :], in_=r_lam, func=AF.Exp), writes=[lam.b, bg])
            yield
            kd = kd_pool.next()
            S.op("dve", lambda e, kd=kd, ed=ed, blk=blk: e.tensor_tensor(out=kd.t[:, :], in0=gk.t[:, blk, :], in1=ed.t[:, :], op=ALU.mult),
                 reads=[GB[blk // 16], ed.b], writes=[kd.b])
            yield
            for c in range(2):
                cs = slice(64 * c, 64 * c + 64)
                S.op("pe", [lambda e, kd=kd, cs=cs, blk=blk: e.matmul(r_st, kd.t[cs, :], gv.t[cs, blk, :], start=True, stop=True)],
                     reads=[kd.b, GB[blk // 16]], writes=[bg])
                yield
                S.op("dve", lambda e, lam=lam, c=c: e.scalar_tensor_tensor(out=state.t[:, :], in0=state.t[:, :], scalar=lam.t[:, c:c + 1], in1=r_st,
                                                                          op0=ALU.mult, op1=ALU.add), reads=[state.b, lam.b], writes=[state.b, bg])
                sbf = sbf_pool.next()
                S.op("dve", lambda e, sbf=sbf: e.tensor_copy(out=sbf.t[:, :], in_=state.t[:, :]), reads=[state.b], writes=[sbf.b])
                yield
                tk = slice(blk * 128 + 64 * c, blk * 128 + 64 * c + 64)
                S.op("pe", [lambda e, sbf=sbf, tk=tk, c=c: e.matmul(pg.t[:, 256 + 64 * c:256 + 64 * c + 64], sbf.t[:, :], gqT.t[:, tk], start=True, stop=True)],
                     reads=[sbf.b, GB[blk // 16]], writes=[bg])
                yield
            osb = osb_p.next()
            S.op("dve", lambda e, osb=osb: e.tensor_copy(out=osb.t[:, :], in_=r_o), writes=[osb.b, bg])
            sq = sq_p.next()
            S.op("pool", lambda e, osb=osb, sq=sq: e.tensor_tensor(out=sq.t[:, :], in0=osb.t[:, :], in1=osb.t[:, :], op=ALU.mult), reads=[osb.b], writes=[sq.b])
            yield
            S.op("pe", [lambda e, sq=sq: e.matmul(r_ss, ones_bf.t[:, :], sq.t[:, :], start=True, stop=True)], reads=[ones_bf.b, sq.b], writes=[bg])
            yield
            rs = rs_p.next()
            S.op("act", lambda e, rs=rs: e.activation(out=rs.t[:, :], in_=r_ss, func=AF.Ln, bias=eps.t[:, 0:1], scale=1.0 / 128), reads=[eps.b], writes=[rs.b, bg])
            S.op("act", lambda e, rs=rs: e.activation(out=rs.t[:, :], in_=rs.t[:, :], func=AF.Exp, scale=-0.5), reads=[rs.b], writes=[rs.b])
            yield
            if blk % 4 == 0:
                gst = gst_pool.next()
            cs4 = slice((blk % 4) * 128, (blk % 4 + 1) * 128)
            S.op("dve", lambda e, gst=gst, cs4=cs4, osb=osb, rs=rs: e.scalar_tensor_tensor(out=gst.t[:, cs4], in0=osb.t[:, :], scalar=bw.t[:, 32:33], in1=rs.t[:, :],
                                                                                        op0=ALU.mult, op1=ALU.mult), reads=[osb.b, rs.b, bw.b], writes=[gst.b])
            if blk % 4 == 3:
                pc = blk // 4
                cols = slice((pc % 4) * 512, (pc % 4 + 1) * 512)
                dst = io["s2f"][pc // 4, 1].rearrange("(f t) -> f t", t=T)[:, cols]
                S.dma(STQ, lambda e, gst=gst, dst=dst: e.dma_start(out=dst, in_=gst.t[:, :]), reads=[gst.b], outs=[o_gns[pc // 4]])
                if pc % 4 == 3 and coll2 is not None:
                    coll2("gn", pc // 4)
            yield

    import os as _os
    _mode = _os.environ.get("SIDE_MODE", "1,2")
    side = [conv_gen(), gla_gen()]
    periods = [int(t) for t in _mode.split(",")]
    SIDE_DELAY = int(_os.environ.get("SIDE_DELAY", "250")) if io.get("b_rbf2") is not None else 0
    tick = [0]

    def step_side(force=False):
        tick[0] += 1
        for gi, gnr in enumerate(list(side)):
            per = periods[min(gi, len(periods) - 1)]
            if per == 0 and not force:
                continue
            if force or tick[0] % max(per, 1) == 0:
                try:
                    next(gnr)
                except StopIteration:
                    side.remove(gnr)

    seq = []
    for qt in range(NQT):
        q0 = qt * 512
        bmax = (q0 + 510) // 127
        bl = list(range(bmax, -1, -1))
        for idx, b in enumerate(bl):
            seq.append((qt, b, idx == 0, idx == len(bl) - 1))
    tiles = []
    for t in seq:
        for hh in range(2):
            tiles.append((hh,) + t)
    w_pool4 = K.pool(4, [128, 512], BF16, "w4")
    st = {}

    def mask_op(buf, qt, b):
        base = 512 * qt - 127 * b + 1
        S.op("pool", lambda e: e.affine_select(out=buf.t[:, :], in_=buf.t[:, :], pattern=[[1, 512]], compare_op=ALU.is_gt,
                                                fill=0.0, base=base, channel_multiplier=-1), reads=[buf.b], writes=[buf.b])

    def is_diag(qt, b):
        return 127 * b + 127 > 512 * qt

    def zfn_of(i):
        hh, qt, b, first, last = tiles[i]
        pz = ps_z.next()
        st[i] = {"pz": pz}
        qs = slice(qt * 512, (qt + 1) * 512)
        ks = slice(b * 128, (b + 1) * 128)
        return ((lambda e: e.matmul(pz.t[:, :], kT.t[:, ks], qz[hh].t[:, qs], start=True, stop=False)),
                [kT.b, qz[hh].b, QB[hh][qt // 4]] + [KB[j] for j in ksrc(b)], pz)

    def act_exp(i):
        pz = st[i]["pz"]
        ee = e_pool.next()
        st[i]["ee"] = ee
        S.op("act", lambda e: e.activation(out=ee.t[:, :], in_=pz.t[:, :], func=AF.Exp), reads=[pz.b], writes=[ee.b])

    def act_ln(i):
        hh, qt, b, first, last = tiles[i]
        ee = st[i]["ee"]
        sp = sp_pool.next()
        st[i]["sp"] = sp
        S.op("act", lambda e: e.activation(out=sp.t[:, :], in_=ee.t[:, :], func=AF.Ln, bias=1.0), reads=[ee.b], writes=[sp.b])
        if is_diag(qt, b):
            mask_op(sp, qt, b)
        if first:
            S.op("dve", lambda e: e.memset(sp.t[0:1, :], 0.0), reads=[sp.b], writes=[sp.b])
        else:
            pzp = st[i - 2]["pz"]
            S.op("dve", lambda e: e.tensor_copy(out=sp.t[0:1, :], in_=pzp.t[0:1, :]), reads=[sp.b], writes=[sp.b, pzp.b])

    def pv(i):
        hh, qt, b, first, last = tiles[i]
        hs = slice(64 * hh, 64 * hh + 64)
        w = st[i]["w"]
        po = ps_o[hh]
        S.op("pe", [lambda e: e.matmul(po.t[:, :], v.t[:, b, :], w.t[:, :], start=first, stop=last)], reads=[v.b, w.b] + [VB[j] for j in ksrc(b)], writes=[po.b])
        if last:
            osb = osb_pool.next()
            S.op("dve", lambda e: e.tensor_copy(out=osb.t[hs, :], in_=po.t[hs, :]), reads=[po.b], writes=[osb.b])
            cols = slice((qt % 4) * 512, (qt % 4 + 1) * 512)
            dst = io["s2bf"][qt // 4].rearrange("(f t) -> f t", t=T)[hs, cols]
            S.dma(STQ, lambda e, dst=dst: e.dma_start(out=dst, in_=osb.t[hs, :]), reads=[osb.b], outs=[o_bfs[qt // 4]])
            if qt % 4 == 3 and hh == 1 and coll2 is not None:
                coll2("bf", qt // 4)
        st.pop(i - 3, None)

    n = len(tiles)
    for i in range(min(2, n)):
        zf, zr, pz = zfn_of(i)
        S.op("pe", [zf], reads=zr, writes=[pz.b])
        act_exp(i)
        act_ln(i)
    for j in range(n):
        hh, qt, b, first, last = tiles[j]
        pz, sp = st[j]["pz"], st[j]["sp"]
        fns = [lambda e, pz=pz, sp=sp: e.matmul(pz.t[:, :], tri2.t[:, :], sp.t[:, :], start=False, stop=True)]
        rds, wrs = [tri2.b, sp.b, pz.b], [pz.b]
        if j + 2 < n:
            zf, zr, pz2 = zfn_of(j + 2)
            fns.append(zf)
            rds += zr
            wrs.append(pz2.b)
        S.op("pe", fns, reads=rds, writes=wrs)
        if j + 2 < n:
            act_exp(j + 2)
        w = w_pool4.next()
        st[j]["w"] = w
        S.op("act", lambda e, w=w, pz=pz: e.activation(out=w.t[:, :], in_=pz.t[:, :], func=AF.Exp), reads=[pz.b], writes=[w.b])
        if is_diag(qt, b):
            mask_op(w, qt, b)
        if j + 2 < n:
            act_ln(j + 2)
        if j >= 1:
            pv(j - 1)
        if j >= SIDE_DELAY:
            step_side()
    pv(n - 1)
    while side:
        step_side(force=True)


def prep_B_weights(inp, l, p):
    cw = inp["conv_w"][l][:, 128 * p:128 * p + 128].T
    cb = inp["conv_b"][l][128 * p:128 * p + 128][:, None]
    gg = inp["gla_norm_g"][l][128 * p:128 * p + 128][:, None]
    return _c(np.concatenate([cw, cb, gg], axis=1).astype(np.float32))


I32 = mybir.dt.int32
GROUPS = [[0, 1, 2, 3], [4, 5, 6, 7]]


def build_fused(L=DEPTH, phases="AXBYC"):
    nc = bass.Bass("TRN2", target_bir_lowering=False)
    dt = nc.dram_tensor

    def ein(name, shape, d=F32):
        return dt(name, list(shape), d, kind="ExternalInput").ap()

    xT = ein("xT", [D, TPC])
    cid = ein("cid", [1, 1], I32)
    bw = ein("bw", [L, 128, 33])
    wg1, wu1, wd1 = ein("wg1", [L, NF, 128, 1024]), ein("wu1", [L, NF, 128, 1024]), ein("wd1", [L, 8, 128, NF * 128])
    wfm, wtm = ein("wfm", [L, NFM, 128, 1024]), ein("wtm", [L, 128, 8 * 1280])
    walpha, balpha, gainsA = ein("walpha", [L, 16, 256]), ein("balpha", [L, 1, 256]), ein("gainsA", [L, 128, 24])
    wgt, wbr, wo = ein("wgt", [L, 24, 128, 1024]), ein("wbr", [L, 24, 128, 512]), ein("wo", [L, 8, 128, 1024])
    wg2, wu2, wd2 = ein("wg2", [L, NF, 128, 1024]), ein("wu2", [L, NF, 128, 1024]), ein("wd2", [L, 8, 128, NF * 128])
    gainsC = ein("gainsC", [L, 128, 40])
    outT = dt("outT", [D, TPC], F32, kind="ExternalOutput").ap()
    x1T = dt("x1T_i", [D, TPC], F32).ap()
    xbuf = dt("xbuf_i", [D, TPC], F32).ap()
    silur = dt("silur_i", [512, TPC], F32).ap()
    send_bf = dt("send_bf_i", X1B.shape(), BF16).ap()
    send_f = dt("send_f_i", X1F.shape(), F32).ap()
    ag_bf = dt("ag_bf_i", X1B.shape(16), BF16).ap()
    ag_f = dt("ag_f_i", X1F.shape(16), F32).ap()
    N2 = 128 * TPC
    s2bf = dt("s2bf_i", [4, N2], BF16).ap()
    s2f = dt("s2f_i", [4, 2, N2], F32).ap()
    ag2_bf = dt("ag2_bf_i", [4, 4, N2], BF16).ap()
    ag2_f = dt("ag2_f_i", [4, 2, 4, N2], F32).ap()
    st_bf = dt("st_bf_i", X1B.shape(), BF16).ap()
    st_f = dt("st_f_i", X1F.shape(), F32).ap()
    st2_bf = dt("st2_bf_i", [4, N2], BF16).ap()
    st2_f = dt("st2_f_i", [2, 4, N2], F32).ap()
    P = {n: Buf(n, True) for n in ("x1", "sr", "sbf", "sf", "agbf", "agf", "s2bf", "s2f", "ag2bf", "ag2f", "xbuf", "out",
                                   "stbf", "stf", "st2bf", "st2f", "agbf2", "stbf2")}
    PB2 = {k: [Buf("%s%d" % (k, j), True) for j in range(4)] for k in ("s2bf", "s2cy", "s2gn")}
    with ExitStack() as st:
        K = Ctx(nc, st)
        S = K.S
        S.cid_ap = cid[0:1, 0:1]
        K.init_psum(8)

        def stat(ap):
            return ap

        def pick(ap16, stage, bsrc, bdst, c0=0, c1=None):
            c1 = ap16.shape[0] if c1 is None else c1
            v4 = ap16[c0:c1].rearrange("c (r d) n -> d c r n", d=4)
            S.dma("sp", lambda e: e.dma_start(out=stage[c0:c1], in_=v4[bass.ds(S.dyn["p"], 1)].rearrange("o c r n -> (o c) r n")),
                  reads=[bsrc], writes=[bdst])

        def gather(src, dst, bsrc, bdst, c0=0, c1=None):
            for c in range(c0, src.shape[0] if c1 is None else c1):
                S.dma("pool", lambda e, c=c: e.collective_compute("AllGather", ALU.bypass, replica_groups=GROUPS,
                                                                  ins=[src[c].opt()], outs=[dst[c].opt()]),
                      reads=[bsrc], outs=[bdst], sembuf=bdst, inc=1)

        for l in range(L):
            last = (l == L - 1)
            K.arena_reset()
            C = common_consts(K)
            ffn_bufs(K, C)
            if "A" in phases:
              emit_A(K, C, dict(xT=xT if l == 0 else xbuf, b_xin=None if l == 0 else P["xbuf"],
                              wg=wg1[l], wu=wu1[l], wd=wd1[l], wfm=wfm[l], wtm=wtm[l], walpha=walpha[l], balpha=balpha[l],
                              gains=gainsA[l], x1T=x1T, send_bf=send_bf, send_f=send_f, silur=silur,
                              b_x1=P["x1"], b_sbf=P["sbf"], b_sf=P["sf"], b_sr=P["sr"]))
            S.barrier()
            if "X" in phases:
                gather(send_bf, ag_bf, P["sbf"], P["agbf"], 0, 6)
                pick(ag_bf, st_bf, P["agbf"], P["stbf"], 0, 6)
                gather(send_bf, ag_bf, P["sbf"], P["agbf2"], 6, None)
                gather(send_f, ag_f, P["sf"], P["agf"])

                def late_picks():
                    pick(ag_bf, st_bf, P["agbf2"], P["stbf2"], 6, None)
                    pick(ag_f, st_f, P["agf"], P["stf"])
            K.arena_reset()

            def coll2(kind, j):
                if kind == "bf":
                    src_, dst_, bs_, bd_ = s2bf[j], ag2_bf[j], PB2["s2bf"][j], P["ag2bf"]
                else:
                    a_ = 0 if kind == "cy" else 1
                    src_, dst_, bs_, bd_ = s2f[j, a_], ag2_f[j, a_], PB2["s2cy" if a_ == 0 else "s2gn"][j], P["ag2f"]
                S.dma("pool", lambda e: e.collective_compute("AllGather", ALU.bypass, replica_groups=GROUPS,
                                                             ins=[src_.opt()], outs=[dst_.opt()]),
                      reads=[bs_], outs=[bd_], sembuf=bd_, inc=1)
            if "B" in phases:
              emit_B(K, dict(rbf=stat(st_bf), rf=stat(st_f), bw=bw[l], s2bf=s2bf, s2f=s2f,
                           b_rbf=P["stbf"], b_rbf2=P["stbf2"], b_rf=P["stf"],
                           b_s2bf=PB2["s2bf"], b_s2cy=PB2["s2cy"], b_s2gn=PB2["s2gn"], coll2=coll2 if "Y" in phases else None,
                           late_picks=late_picks if "X" in phases else None))
            S.barrier()
            if "Y" in phases:
                S.dma("sp", lambda e: e.dma_start(out=st2_bf, in_=ag2_bf[bass.ds(S.dyn["p"], 1)].rearrange("o r n -> (o r) n")),
                      reads=[P["ag2bf"]], writes=[P["st2bf"]])
                S.dma("sp", lambda e: e.dma_start(out=st2_f, in_=ag2_f[bass.ds(S.dyn["p"], 1)].rearrange("o a r n -> (o a) r n")),
                      reads=[P["ag2f"]], writes=[P["st2f"]])
            K.arena_reset()
            C = common_consts(K)
            ffn_bufs(K, C)
            if "C" in phases:
              emit_C(K, C, dict(x1T=x1T, recv_bf=stat(st2_bf), recv_f=stat(st2_f), silur=silur,
                              wgt=wgt[l], wbr=wbr[l], wo=wo[l], wg=wg2[l], wu=wu2[l], wd=wd2[l], gains=gainsC[l],
                              x3T=outT if last else xbuf, b_x3=P["out"] if last else P["xbuf"],
                              b_x1=P["x1"], b_sr=P["sr"], b_r2bf=P["st2bf"], b_r2f=P["st2f"]))
            S.barrier()
        S.wait_all_on("sp", [P["out"]])
        S.emit()
    return nc, K


_FUSED = {}


def kernel(**inputs):
    inp = {k: np.asarray(v) for k, v in inputs.items()}
    x = inp["x"].astype(np.float32, copy=False)
    cores = list(range(NCORES))
    T = TPC
    L = DEPTH
    if "nc" not in _FUSED:
        _FUSED["nc"] = build_fused(L)[0]
    A = [prep_A_weights(inp, l) for l in range(L)]
    Cw = [prep_C_weights(inp, l) for l in range(L)]
    shared = dict(
        wg1=np.stack([a["wg"] for a in A]), wu1=np.stack([a["wu"] for a in A]), wd1=np.stack([a["wd"] for a in A]),
        wfm=np.stack([a["wfm"] for a in A]), wtm=np.stack([a["wtm"] for a in A]),
        walpha=np.stack([a["walpha"] for a in A]), balpha=np.stack([a["balpha"] for a in A]),
        gainsA=np.stack([a["gains"] for a in A]),
        wgt=np.stack([c["wgt"] for c in Cw]), wbr=np.stack([c["wbr"] for c in Cw]), wo=np.stack([c["wo"] for c in Cw]),
        wg2=np.stack([c["wg"] for c in Cw]), wu2=np.stack([c["wu"] for c in Cw]), wd2=np.stack([c["wd"] for c in Cw]),
        gainsC=np.stack([c["gains"] for c in Cw]))
    del A, Cw
    bws = [np.stack([prep_B_weights(inp, l, p) for l in range(L)]) for p in range(4)]
    in_maps = []
    for c in cores:
        m = dict(shared)
        m["xT"] = _c(x[c // 4, (c % 4) * T:(c % 4 + 1) * T, :].T)
        m["cid"] = np.array([[c % 4]], np.int32)
        m["bw"] = bws[c % 4]
        in_maps.append(m)
    res = run_bass_kernel_spmd(_FUSED["nc"], in_maps, core_ids=cores).results
    out = np.empty((BATCH, SEQ, D), np.float32)
    for c in cores:
        out[c // 4, (c % 4) * T:(c % 4 + 1) * T, :] = np.asarray(res[c]["outT"]).T
    return out
```

```python
import numpy as np
from contextlib import ExitStack
import ml_dtypes
import concourse.bass as bass
import concourse.mybir as mybir
from concourse.bass_utils import run_bass_kernel_spmd

F32 = mybir.dt.float32
BF16 = mybir.dt.bfloat16
AF = mybir.ActivationFunctionType
ALU = mybir.AluOpType

D = 1024
DFF = 2816
NF = DFF // 128
SEQ = 8192
BATCH = 2
DEPTH = 4
TPC = 2048
PASS = 1024
EPS = 1e-6
INW = 7184
NCORES = 8
POOL_ENG = "dve"

ENGS = ("pe", "act", "dve", "pool", "sp")
SEM_LIMIT = 28000


class Buf:
    __slots__ = ("name", "w", "r", "dsem", "persist")

    def __init__(self, name, persist=False):
        self.name = name
        self.w = None
        self.r = {}
        self.dsem = None
        self.persist = persist


class Sched:
    def __init__(self, nc, stack):
        self.nc = nc
        self.stack = stack
        self.sem = {}
        self.cnt = {}
        self.gen = {e: 0 for e in ENGS}
        self.seen = {e: {} for e in ENGS}
        self.q = {e: [] for e in ENGS}
        self.nsem = 0
        for e in ENGS:
            self._mksem((e, 0))
        self.n_inst = 0
        self.dma_free = []
        self.dma_bufs = []
        self.dyn = {}
        self.cid_ap = None

    def _mksem(self, key):
        self.nsem += 1
        h = self.stack.enter_context(self.nc.semaphore("s%d" % self.nsem))
        self.sem[key] = h
        self.cnt[key] = 0
        return key

    def ekey(self, e):
        return (e, self.gen[e])

    def _waits(self, e, deps):
        out = []
        seen = self.seen[e]
        for (k, v) in deps:
            if k[0] == e:
                continue
            if seen.get(k, 0) < v:
                seen[k] = v
                out.append((k, v))
        return out

    @staticmethod
    def _deps(reads, writes):
        deps = []
        for b in reads:
            if b.w is not None:
                deps.append(b.w)
        for b in writes:
            if b.w is not None:
                deps.append(b.w)
            deps.extend(b.r.items())
        return deps

    def op(self, e, fns, reads=(), writes=()):
        if not isinstance(fns, (list, tuple)):
            fns = [fns]
        waits = self._waits(e, self._deps(reads, writes))
        k = self.ekey(e)
        if self.cnt[k] >= SEM_LIMIT:
            self.gen[e] += 1
            k = self._mksem(self.ekey(e))
        self.cnt[k] += 1
        ev = (k, self.cnt[k])
        for b in reads:
            if b.r.get(k, 0) < ev[1]:
                b.r[k] = ev[1]
        for b in writes:
            b.w = ev
            b.r = {}
        self.q[e].append((waits, fns, k, 1))
        self.n_inst += len(fns)
        return ev

    def dma(self, qe, fn, reads=(), writes=(), sembuf=None, outs=(), inc=16):
        deps = self._deps(reads, writes)
        for b in outs:
            deps.extend(b.r.items())
        waits = self._waits(qe, deps)
        tgt = sembuf if sembuf is not None else (outs[0] if outs else (writes[0] if writes else reads[0]))
        if tgt.dsem is None:
            if self.dma_free and not tgt.persist:
                tgt.dsem = self.dma_free.pop()
            else:
                tgt.dsem = self._mksem(("dma", tgt.name, self.nsem))
            if not tgt.persist:
                self.dma_bufs.append(tgt)
        semkey = tgt.dsem
        self.cnt[semkey] += inc
        ev = (semkey, self.cnt[semkey])
        for b in reads:
            if b.r.get(semkey, 0) < ev[1]:
                b.r[semkey] = ev[1]
        for b in writes:
            b.w = ev
            b.r = {}
        for b in outs:
            b.w = ev
        self.q[qe].append((waits, [fn], semkey, inc))
        self.n_inst += 1
        return ev

    def barrier(self):
        deps = [(k, v) for k, v in self.cnt.items() if v > 0 and k[0] != "sp"]
        waits = self._waits("sp", deps)
        k = self.ekey("sp")
        self.cnt[k] += 1
        ev = (k, self.cnt[k])
        self.q["sp"].append((waits, [lambda e: e.nop()], k, 1))
        for e in ENGS:
            if e != "sp":
                self.q[e].append((self._waits(e, [ev]), [], None, 0))
        for b in self.dma_bufs:
            if b.dsem is not None and self.cnt[b.dsem] < SEM_LIMIT:
                self.dma_free.append(b.dsem)
            b.dsem = None
        self.dma_bufs = []

    def wait_all_on(self, e, bufs):
        deps = [b.w for b in bufs if b.w is not None]
        waits = self._waits(e, deps)
        self.q[e].append((waits, [], None, 0))

    def emit(self):
        nc = self.nc
        with nc.Block() as block:
            def mk(e):
                def body(engobj):
                    if e == "sp" and self.cid_ap is not None:
                        with engobj.register("rank_reg") as rr:
                            engobj.reg_load(rr, self.cid_ap)
                            self.dyn["p"] = engobj.snap(rr)
                            run(engobj)
                    else:
                        run(engobj)

                def run(engobj):
                    for (waits, fns, k, inc) in self.q[e]:
                        for (wk, wv) in waits:
                            engobj.wait_ge(self.sem[wk], wv)
                        n = len(fns)
                        for i, fn in enumerate(fns):
                            ins = fn(engobj)
                            if i == n - 1:
                                ins.then_inc(self.sem[k], inc)
                return body
            block.tensor(mk("pe"))
            block.scalar(mk("act"))
            block.vector(mk("dve"))
            block.gpsimd(mk("pool"))
            block.sync(mk("sp"))


class Tl:
    __slots__ = ("t", "b")

    def __init__(self, t, b):
        self.t = t
        self.b = b


class Ctx:
    def __init__(self, nc, stack):
        self.nc = nc
        self.st = stack
        self.S = Sched(nc, stack)
        self.uid = 0
        self.ps = []
        self.psi = 0
        self.outs = []

    def name(self, p):
        self.uid += 1
        return "%s_%d" % (p, self.uid)

    ARENA_LO = 16512
    ARENA_HI = 229344

    def sb(self, shape, dt, name="t"):
        n = self.name(name)
        esz = 2 if dt == BF16 else 4
        nbytes = esz
        for d_ in shape[1:]:
            nbytes *= int(d_)
        off = (getattr(self, "arena_ptr", self.ARENA_LO) + 31) // 32 * 32
        assert off + nbytes <= self.ARENA_HI, "SBUF arena overflow at %s: need %d have %d" % (n, nbytes, self.ARENA_HI - off)
        self.arena_ptr = off + nbytes
        t = self.nc.alloc_sbuf_tensor_at(n, list(shape), dt, offset=off)
        return Tl(t, Buf(n))

    def arena_reset(self):
        self.arena_ptr = self.ARENA_LO

    def pool(self, n, shape, dt, name="p"):
        return Rot([self.sb(shape, dt, name) for _ in range(n)])

    def init_psum(self, n=8):
        for i in range(n):
            nm = self.name("ps")
            t = self.st.enter_context(self.nc.psum_tensor(nm, [128, 512], F32))
            self.ps.append(Tl(t, Buf(nm)))

    def psum(self):
        p = self.ps[self.psi % len(self.ps)]
        self.psi += 1
        return p

    def dram_buf(self, name):
        return Buf(name)


class Rot:
    def __init__(self, items):
        self.items = items
        self.i = 0

    def next(self):
        it = self.items[self.i % len(self.items)]
        self.i += 1
        return it


def mm_group(K, ps, pairs, reads):
    n = len(pairs)

    def mk(i, l, r, o):
        return lambda e: e.matmul(o, l, r, start=(i == 0), stop=(i == n - 1))
    fns = [mk(i, l, r, o) for i, (l, r, o) in enumerate(pairs)]
    K.S.op("pe", fns, reads=reads, writes=[ps.b])


def rstd_from_sq(K, C, sq, sq_reads, nk, ncols, inv_n, extra=None):
    ps = K.psum()
    pairs = [(C["ones_bf"].t[:, :], sq.t[:, k, 0:ncols], ps.t[:, 0:ncols]) for k in range(nk)]
    mm_group(K, ps, pairs, reads=[C["ones_bf"].b, sq.b])
    r = C["rstd_pool"].next()
    K.S.op("act", lambda e: e.activation(out=r.t[:, 0:ncols], in_=ps.t[:, 0:ncols], func=AF.Sqrt,
                                         bias=C["eps_col"].t[:, 0:1], scale=inv_n),
           reads=[ps.b, C["eps_col"].b], writes=[r.b])
    K.S.op("dve", lambda e: e.reciprocal(out=r.t[:, 0:ncols], in_=r.t[:, 0:ncols]), reads=[r.b], writes=[r.b])
    return r


def pre_norm(K, C, x, xb, g, gcol0, h, hb, s):
    sl = slice(s * 512, (s + 1) * 512)
    sq = C["sq_pool"].next()
    K.S.op("act", lambda e: e.activation(out=sq.t[:, :, :], in_=x.t[:, :, sl], func=AF.Square),
           reads=[xb], writes=[sq.b])
    r = rstd_from_sq(K, C, sq, None, 8, 512, 1.0 / D)
    fns = [(lambda e, k=k: e.scalar_tensor_tensor(
        out=h.t[:, k, sl], in0=x.t[:, k, sl], scalar=g.t[:, gcol0 + k:gcol0 + k + 1], in1=r.t[:, :],
        op0=ALU.mult, op1=ALU.mult)) for k in range(8)]
    K.S.op("dve", fns, reads=[xb, r.b, g.b], writes=[hb])


def post_norm_residual(K, C, y, yb, g, gcol0, alpha, x, xb, s):
    sl = slice(s * 512, (s + 1) * 512)
    sq = C["sq_pool"].next()
    K.S.op("act", lambda e: e.activation(out=sq.t[:, :, :], in_=y.t[:, :, sl], func=AF.Square),
           reads=[yb], writes=[sq.b])
    r = rstd_from_sq(K, C, sq, None, 8, 512, 1.0 / D)
    for k in range(8):
        tmp = C["tmp_pool"].next()
        K.S.op("dve", lambda e, k=k, tmp=tmp: e.scalar_tensor_tensor(
            out=tmp.t[:, :], in0=y.t[:, k, sl], scalar=g.t[:, gcol0 + k:gcol0 + k + 1], in1=r.t[:, :],
            op0=ALU.mult, op1=ALU.mult), reads=[yb, r.b, g.b], writes=[tmp.b])
        K.S.op("dve", lambda e, k=k, tmp=tmp: e.scalar_tensor_tensor(
            out=x.t[:, k, sl], in0=tmp.t[:, :], scalar=float(alpha), in1=x.t[:, k, sl],
            op0=ALU.mult, op1=ALU.add), reads=[tmp.b, xb], writes=[xb])


def load_w(K, slot, src_ap, q="pool"):
    K.S.dma(q, lambda e: e.dma_start(out=slot.t[:], in_=src_ap), writes=[slot.b])


def ffn(K, C, h, hb, wg_ap, wu_ap, wd_ap, y, yb):
    a = C["a"]
    nsub = PASS // 512
    for f in range(NF):
        wg = C["wgu_pool"].next()
        load_w(K, wg, wg_ap[f].rearrange("p (k j) -> p k j", j=128))
        wu = C["wgu_pool"].next()
        load_w(K, wu, wu_ap[f].rearrange("p (k j) -> p k j", j=128))
        for s in range(nsub):
            sl = slice(s * 512, (s + 1) * 512)
            pg = K.psum()
            mm_group(K, pg, [(wg.t[:, k, :], h.t[:, k, sl], pg.t[:, :]) for k in range(8)], reads=[wg.b, hb])
            pu = K.psum()
            mm_group(K, pu, [(wu.t[:, k, :], h.t[:, k, sl], pu.t[:, :]) for k in range(8)], reads=[wu.b, hb])
            sg = C["tmp_pool"].next()
            K.S.op("act", lambda e, sg=sg, pg=pg: e.activation(out=sg.t[:, :], in_=pg.t[:, :], func=AF.Silu),
                   reads=[pg.b], writes=[sg.b])
            ab = a.b[f * nsub + s]
            K.S.op("dve", lambda e, sg=sg, pu=pu, f=f, sl=sl: e.tensor_tensor(
                out=a.t[:, f, sl], in0=sg.t[:, :], in1=pu.t[:, :], op=ALU.mult),
                reads=[sg.b, pu.b], writes=[ab])
    for d in range(8):
        wd = C["wd_pool"].next()
        load_w(K, wd, wd_ap[d].rearrange("p (c j) -> p c j", j=128))
        for s in range(nsub):
            sl = slice(s * 512, (s + 1) * 512)
            py = K.psum()
            mm_group(K, py, [(wd.t[:, c, :], a.t[:, c, sl], py.t[:, :]) for c in range(NF)],
                     reads=[wd.b] + [a.b[c * nsub + s] for c in range(NF)])
            K.S.op("act", lambda e, py=py, d=d, sl=sl: e.activation(out=y.t[:, d, sl], in_=py.t[:, :], func=AF.Copy),
                   reads=[py.b], writes=[yb])


class ATl:
    def __init__(self, t, bufs):
        self.t = t
        self.b = bufs


def common_consts(K):
    C = {}
    ones_bf = K.sb([128, 128], BF16, "ones_bf")
    K.S.op("dve", lambda e: e.memset(ones_bf.t[:, :], 1.0), writes=[ones_bf.b])
    C["ones_bf"] = ones_bf
    eps = K.sb([128, 1], F32, "eps")
    K.S.op("dve", lambda e: e.memset(eps.t[:, :], EPS), writes=[eps.b])
    C["eps_col"] = eps
    C["rstd_pool"] = K.pool(2, [128, 512], F32, "rstd")
    C["sq_pool"] = K.pool(1, [128, 8, 512], BF16, "sq")
    C["tmp_pool"] = K.pool(3, [128, 512], F32, "tmp")
    return C


def ffn_bufs(K, C):
    nsub = PASS // 512
    at = K.sb([128, NF, PASS], BF16, "a")
    C["a"] = ATl(at.t, [Buf("a%d" % i) for i in range(NF * nsub)])
    C["wgu_pool"] = K.pool(4, [128, 8, 128], BF16, "wgu")
    C["wd_pool"] = K.pool(2, [128, NF, 128], BF16, "wd")


OQ, OK_, OV, OCA, OCG, OGQ, OGK, OGV, OGR, OLR, OGT = 0, 512, 1024, 1536, 2048, 2560, 2816, 3072, 3584, 4096, 4112
FM_CHUNKS = ([("q", i, OQ + 128 * i) for i in range(4)] + [("k", i, OK_ + 128 * i) for i in range(4)] +
             [("a", i, OCA + 128 * i) for i in range(4)] + [("g", i, OCG + 128 * i) for i in range(4)] +
             [("gq", i, OGQ + 128 * i) for i in range(2)] + [("r", i, OGR + 128 * i) for i in range(4)] +
             [("lr", 0, OLR)])
NFM = len(FM_CHUNKS)
SB_QT, SB_KT, SB_V, SB_GQT, SB_GV = 0, 128 * TPC, 256 * TPC, 384 * TPC, 448 * TPC
SB_LEN = 576 * TPC
SF_UT, SF_GK, SF_LA = 0, 128 * TPC, 192 * TPC
SF_LEN = 256 * TPC


class XB:
    def __init__(self, U, nunits):
        self.U, self.nunits = U, nunits
        self.NCH = nunits // U
        self.CL = U * TPC

    def shape(self, nrank=4):
        return [self.NCH, nrank, self.CL]

    def fm_pieces(self, ap3, d, off, f0, nf):
        u0 = off // TPC + f0
        u1 = u0 + nf
        out = []
        u = u0
        while u < u1:
            c = u // self.U
            ue = min(u1, (c + 1) * self.U)
            lo = (u - c * self.U) * TPC
            out.append((u - u0, ue - u0, ap3[c, d, lo:lo + (ue - u) * TPC].rearrange("(f t) -> f t", t=TPC)))
            u = ue
        return out

    def tm_all(self, ap3, off, nfeat, tok0, ntok):
        e0 = off + tok0 * nfeat
        c = e0 // self.CL
        lo = e0 - c * self.CL
        assert lo + ntok * nfeat <= self.CL
        return ap3[c, :, lo:lo + ntok * nfeat].rearrange("d (t f) -> t d f", f=nfeat)

    def tm_src(self, ap3, j, off, nfeat, tok0, ntok):
        e0 = off + tok0 * nfeat
        c = e0 // self.CL
        lo = e0 - c * self.CL
        assert lo + ntok * nfeat <= self.CL
        return ap3[c, j, lo:lo + ntok * nfeat].rearrange("(t f) -> t f", f=nfeat)


X1B = XB(64, 576)
X1F = XB(32, 256)
X2B = XB(64, 128)
X2F = XB(32, 256)


def build_A():
    nc = bass.Bass("TRN2", target_bir_lowering=False)
    dt = nc.dram_tensor
    xT = dt("xT", [D, TPC], F32, kind="ExternalInput").ap()
    wg = dt("wg", [NF, 128, 8 * 128], F32, kind="ExternalInput").ap()
    wu = dt("wu", [NF, 128, 8 * 128], F32, kind="ExternalInput").ap()
    wd = dt("wd", [8, 128, NF * 128], F32, kind="ExternalInput").ap()
    wfm = dt("wfm", [NFM, 128, 8 * 128], F32, kind="ExternalInput").ap()
    wtm = dt("wtm", [128, 8 * 1280], F32, kind="ExternalInput").ap()
    walpha = dt("walpha", [16, 256], F32, kind="ExternalInput").ap()
    balpha = dt("balpha", [1, 256], F32, kind="ExternalInput").ap()
    gains = dt("gains", [128, 24], F32, kind="ExternalInput").ap()
    x1T = dt("x1T", [D, TPC], F32, kind="ExternalOutput").ap()
    send_bf = dt("send_bf", [4, SB_LEN], BF16, kind="ExternalOutput").ap()
    send_f = dt("send_f", [4, SF_LEN], F32, kind="ExternalOutput").ap()
    silur = dt("silur", [512, TPC], F32, kind="ExternalOutput").ap()
    with ExitStack() as st:
        K = Ctx(nc, st)
        K.init_psum(8)
        C = common_consts(K)
        ffn_bufs(K, C)
        bb = dict(b_x1=Buf("o_x1", True), b_sbf=Buf("o_sbf", True), b_sf=Buf("o_sf", True), b_sr=Buf("o_sr", True))
        K.outs += list(bb.values())
        emit_A(K, C, dict(xT=xT, wg=wg, wu=wu, wd=wd, wfm=wfm, wtm=wtm, walpha=walpha, balpha=balpha,
                          gains=gains, x1T=x1T, send_bf=send_bf, send_f=send_f, silur=silur, **bb))
        K.S.wait_all_on("sp", K.outs)
        K.S.emit()
    return nc


def emit_A(K, C, io):
    S = K.S
    nsub = PASS // 512
    g = K.sb([128, 24], F32, "gains")
    S.dma("sp", lambda e: e.dma_start(out=g.t[:, :], in_=io["gains"]), writes=[g.b])
    wtm = K.sb([128, 8, 1280], BF16, "wtm")
    load_w(K, wtm, io["wtm"].rearrange("p (k j) -> p k j", j=1280))
    wal = K.sb([16, 256], F32, "walpha")
    S.dma("sp", lambda e: e.dma_start(out=wal.t[:, :], in_=io["walpha"]), writes=[wal.b])
    bal = K.sb([1, 256], F32, "balpha")
    S.dma("sp", lambda e: e.dma_start(out=bal.t[:, :], in_=io["balpha"]), writes=[bal.b])
    ones_f = K.sb([1, 128], F32, "ones_f")
    S.op("dve", lambda e: e.memset(ones_f.t[:, :], 1.0), writes=[ones_f.b])

    x = K.sb([128, 8, PASS], F32, "x")
    h = K.sb([128, 8, PASS], BF16, "h")
    y = K.sb([128, 8, PASS], F32, "y")
    lrT = K.sb([16, PASS], F32, "lrT")
    wfm_pool = K.pool(2, [128, 8, 128], BF16, "wfm")
    st_bf = K.pool(4, [128, 512], BF16, "st_bf")
    st_f = K.pool(3, [128, 512], F32, "st_f")
    st_la = K.pool(2, [128, 256], F32, "st_la")
    xv = io["xT"].rearrange("(k p) t -> p k t", p=128)
    x1v = io["x1T"].rearrange("(k p) t -> p k t", p=128)
    sbf, sf = io["send_bf"], io["send_f"]
    o_x1, o_sbf, o_sf, o_sr = io["b_x1"], io["b_sbf"], io["b_sf"], io["b_sr"]
    b_xin = io.get("b_xin")

    def fm_out(xb, ap3, p, off, o, tsl, outbuf, pf0=0, nf=128):
        for (fa, fb, dst) in xb.fm_pieces(ap3, p, off, 0, nf):
            S.dma("sp", lambda e, dst=dst, fa=fa, fb=fb: e.dma_start(out=dst[:, tsl], in_=o.t[pf0 + fa:pf0 + fb, :]),
                  reads=[o.b], outs=[outbuf])

    for ps_ in range(TPC // PASS):
        t0 = ps_ * PASS
        S.dma("sp", lambda e, t0=t0: e.dma_start(out=x.t[:, :, :], in_=xv[:, :, t0:t0 + PASS]), writes=[x.b],
              reads=[b_xin] if b_xin is not None else [])
        for s in range(nsub):
            pre_norm(K, C, x, x.b, g, 0, h, h.b, s)
        ffn(K, C, h, h.b, io["wg"], io["wu"], io["wd"], y, y.b)
        for s in range(nsub):
            post_norm_residual(K, C, y, y.b, g, 16, 0.5, x, x.b, s)
        S.dma("sp", lambda e, t0=t0: e.dma_start(out=x1v[:, :, t0:t0 + PASS], in_=x.t[:, :, :]),
              reads=[x.b], outs=[o_x1])
        for s in range(nsub):
            pre_norm(K, C, x, x.b, g, 8, h, h.b, s)
        for ci, (kind, i, col) in enumerate(FM_CHUNKS):
            w = wfm_pool.next()
            load_w(K, w, io["wfm"][ci].rearrange("p (k j) -> p k j", j=128))
            for s in range(nsub):
                sl = slice(s * 512, (s + 1) * 512)
                tsl = slice(t0 + s * 512, t0 + (s + 1) * 512)
                pp = K.psum()
                mm_group(K, pp, [(w.t[:, k, :], h.t[:, k, sl], pp.t[:, :]) for k in range(8)], reads=[w.b, h.b])
                if kind in ("q", "k"):
                    o = st_bf.next()
                    sc = 0.125 if kind == "q" else 1.0
                    S.op("act", lambda e, o=o, pp=pp, sc=sc: e.activation(out=o.t[:, :], in_=pp.t[:, :], func=AF.Copy, scale=sc),
                         reads=[pp.b], writes=[o.b])
                    off = SB_QT if kind == "q" else SB_KT
                    fm_out(X1B, sbf, i, off, o, tsl, o_sbf)
                elif kind == "a":
                    S.op("act", lambda e, pp=pp, i=i, sl=sl: e.activation(out=y.t[:, i, sl], in_=pp.t[:, :], func=AF.Copy),
                         reads=[pp.b], writes=[y.b])
                elif kind == "g":
                    sg = C["tmp_pool"].next()
                    S.op("act", lambda e, sg=sg, pp=pp: e.activation(out=sg.t[:, :], in_=pp.t[:, :], func=AF.Sigmoid),
                         reads=[pp.b], writes=[sg.b])
                    o = st_f.next()
                    S.op("dve", lambda e, o=o, sg=sg, i=i, sl=sl: e.tensor_tensor(out=o.t[:, :], in0=y.t[:, i, sl], in1=sg.t[:, :], op=ALU.mult),
                         reads=[sg.b, y.b], writes=[o.b])
                    fm_out(X1F, sf, i, SF_UT, o, tsl, o_sf)
                elif kind == "gq":
                    o = st_bf.next()
                    S.op("act", lambda e, o=o, pp=pp: e.activation(out=o.t[:, :], in_=pp.t[:, :], func=AF.Copy, scale=0.125),
                         reads=[pp.b], writes=[o.b])
                    for half in range(2):
                        fm_out(X1B, sbf, 2 * i + half, SB_GQT, o, tsl, o_sbf, pf0=64 * half, nf=64)
                elif kind == "r":
                    o = st_f.next()
                    S.op("act", lambda e, o=o, pp=pp: e.activation(out=o.t[:, :], in_=pp.t[:, :], func=AF.Silu),
                         reads=[pp.b], writes=[o.b])
                    dst = io["silur"][128 * i:128 * i + 128, tsl]
                    S.dma("sp", lambda e, o=o, dst=dst: e.dma_start(out=dst, in_=o.t[:, :]), reads=[o.b], outs=[o_sr])
                elif kind == "lr":
                    S.op("act", lambda e, pp=pp, sl=sl: e.activation(out=lrT.t[:, sl], in_=pp.t[0:16, :], func=AF.Copy),
                         reads=[pp.b], writes=[lrT.b])
        for tb in range(PASS // 128):
            bsl = slice(tb * 128, (tb + 1) * 128)
            gsl = slice(t0 + tb * 128, t0 + (tb + 1) * 128)
            pp = K.psum()
            mm_group(K, pp, [(h.t[:, k, bsl], wtm.t[:, k, 0:512], pp.t[:, :]) for k in range(8)], reads=[wtm.b, h.b])
            o = st_bf.next()
            S.op("act", lambda e, o=o, pp=pp: e.activation(out=o.t[:, :], in_=pp.t[:, :], func=AF.Copy), reads=[pp.b], writes=[o.b])
            S.dma("sp", lambda e, o=o, gsl=gsl: e.dma_start(out=X1B.tm_all(sbf, SB_V, 128, gsl.start, 128), in_=o.t[:, :].rearrange("t (p f) -> t p f", f=128)),
                  reads=[o.b], outs=[o_sbf])
            pp = K.psum()
            mm_group(K, pp, [(h.t[:, k, bsl], wtm.t[:, k, 768:1280], pp.t[:, :]) for k in range(8)], reads=[wtm.b, h.b])
            o = st_bf.next()
            S.op("dve", lambda e, o=o, pp=pp: e.tensor_copy(out=o.t[:, :], in_=pp.t[:, :]), reads=[pp.b], writes=[o.b])
            S.dma("sp", lambda e, o=o, gsl=gsl: e.dma_start(out=X1B.tm_all(sbf, SB_GV, 128, gsl.start, 128), in_=o.t[:, :].rearrange("t (p f) -> t p f", f=128)),
                  reads=[o.b], outs=[o_sbf])
            pp = K.psum()
            mm_group(K, pp, [(h.t[:, k, bsl], wtm.t[:, k, 512:768], pp.t[:, 0:256]) for k in range(8)], reads=[wtm.b, h.b])
            o = st_la.next()
            S.op("act", lambda e, o=o, pp=pp: e.activation(out=o.t[:, :], in_=pp.t[:, 0:256], func=AF.Copy), reads=[pp.b], writes=[o.b])
            S.dma("sp", lambda e, o=o, gsl=gsl: e.dma_start(out=X1F.tm_all(sf, SF_GK, 64, gsl.start, 128), in_=o.t[:, :].rearrange("t (p f) -> t p f", f=64)),
                  reads=[o.b], outs=[o_sf])
            pp = K.psum()
            mm_group(K, pp, [(lrT.t[:, bsl], wal.t[:, :], pp.t[:, 0:256]), (ones_f.t[:, :], bal.t[:, :], pp.t[:, 0:256])],
                     reads=[lrT.b, wal.b, ones_f.b, bal.b])
            o = st_la.next()
            S.op("act", lambda e, o=o, pp=pp: e.activation(out=o.t[:, :], in_=pp.t[:, 0:256], func=AF.Exp, scale=-1.0), reads=[pp.b], writes=[o.b])
            S.op("act", lambda e, o=o: e.activation(out=o.t[:, :], in_=o.t[:, :], func=AF.Ln, bias=1.0), reads=[o.b], writes=[o.b])
            S.op("dve", lambda e, o=o: e.tensor_scalar(out=o.t[:, :], in0=o.t[:, :], scalar1=-1.0 / 16.0, scalar2=None, op0=ALU.mult),
                 reads=[o.b], writes=[o.b])
            S.dma("sp", lambda e, o=o, gsl=gsl: e.dma_start(out=X1F.tm_all(sf, SF_LA, 64, gsl.start, 128), in_=o.t[:, :].rearrange("t (p f) -> t p f", f=64)),
                  reads=[o.b], outs=[o_sf])


def _c(a):
    return np.ascontiguousarray(a)


def lay_fm_w(W, ncols_chunks):
    Kc = W.shape[0] // 128
    n = W.shape[1] // 128
    return _c(W.reshape(Kc, 128, n, 128).transpose(2, 1, 0, 3).reshape(n, 128, Kc * 128))


def lay_pk(v):
    return _c(v.reshape(-1, 128).T)


def prep_A_weights(inp, l):
    w_in = inp["w_in"][l]
    cols = []
    for (kind, i, col) in FM_CHUNKS:
        blk = np.zeros((D, 128), np.float32)
        n = 16 if kind == "lr" else 128
        blk[:, :n] = w_in[:, col:col + n]
        cols.append(blk)
    wfm = lay_fm_w(np.concatenate(cols, axis=1), NFM)
    tm = np.concatenate([w_in[:, OV:OV + 512], w_in[:, OGK:OGK + 256], w_in[:, OGV:OGV + 512]], axis=1)
    wtm = _c(tm.reshape(8, 128, 1280).transpose(1, 0, 2).reshape(128, 8 * 1280))
    gains = np.concatenate([lay_pk(inp["norm_pre"][l, 0]), lay_pk(inp["norm_pre"][l, 1]), lay_pk(inp["norm_post"][l, 0])], axis=1)
    return dict(wg=lay_fm_w(inp["ffn1_w_gate"][l], NF), wu=lay_fm_w(inp["ffn1_w_up"][l], NF),
                wd=lay_fm_w(inp["ffn1_w_down"][l], 8), wfm=wfm, wtm=wtm,
                walpha=_c(inp["gla_w_alpha"][l]), balpha=_c(inp["gla_b_alpha"][l][None, :]), gains=_c(gains))


RB_LEN = 128 * TPC
RF_CY, RF_GN = 0, 128 * TPC
RF_LEN = 256 * TPC


def build_C(dbg=0):
    nc = bass.Bass("TRN2", target_bir_lowering=False)
    dt = nc.dram_tensor
    io = dict(
        x1T=dt("x1T", [D, TPC], F32, kind="ExternalInput").ap(),
        recv_bf=dt("recv_bf", [4, RB_LEN], BF16, kind="ExternalInput").ap(),
        recv_f=dt("recv_f", [4, RF_LEN], F32, kind="ExternalInput").ap(),
        silur=dt("silur", [512, TPC], F32, kind="ExternalInput").ap(),
        wgt=dt("wgt", [24, 128, 8 * 128], F32, kind="ExternalInput").ap(),
        wbr=dt("wbr", [24, 128, 4 * 128], F32, kind="ExternalInput").ap(),
        wo=dt("wo", [8, 128, 8 * 128], F32, kind="ExternalInput").ap(),
        wg=dt("wg", [NF, 128, 8 * 128], F32, kind="ExternalInput").ap(),
        wu=dt("wu", [NF, 128, 8 * 128], F32, kind="ExternalInput").ap(),
        wd=dt("wd", [8, 128, NF * 128], F32, kind="ExternalInput").ap(),
        gains=dt("gains", [128, 40], F32, kind="ExternalInput").ap(),
        x3T=dt("x3T", [D, TPC], F32, kind="ExternalOutput").ap(),
    )
    with ExitStack() as st:
        K = Ctx(nc, st)
        K.init_psum(8)
        C = common_consts(K)
        ffn_bufs(K, C)
        r2bf_, r2f_ = io["recv_bf"], io["recv_f"]
        io["recv_bf"] = lambda p, a, b: r2bf_[p:p + 1, a:b]
        io["recv_f"] = lambda p, a, b: r2f_[p:p + 1, a:b]
        io["b_x3"] = Buf("o_x3", True)
        K.outs.append(io["b_x3"])
        emit_C(K, C, io, dbg)
        K.S.wait_all_on("sp", K.outs)
        K.S.emit()
    return nc


def emit_C(K, C, io, dbg=0):
    S = K.S
    nsub = PASS // 512
    T = TPC
    g = K.sb([128, 40], F32, "gainsC")
    S.dma("sp", lambda e: e.dma_start(out=g.t[:, :], in_=io["gains"]), writes=[g.b])
    x = K.sb([128, 8, PASS], F32, "x")
    h = K.sb([128, 8, PASS], BF16, "h")
    y = K.sb([128, 8, PASS], F32, "y")
    a = C["a"]
    sr_pool = K.pool(1, [128, 4, 512], F32, "sr")
    wgt_pool = K.pool(2, [128, 8, 128], BF16, "wgt")
    wbr_pool = K.pool(2, [128, 4, 128], BF16, "wbr")
    acc = [K.sb([128, 512], F32, "acc") for _ in range(nsub)]
    ln_mean = K.sb([128, 512], F32, "ln_mean")
    ln_msq = K.sb([128, 512], F32, "ln_msq")
    xv = io["x1T"].rearrange("(k p) t -> p k t", p=128)
    ov = io["x3T"].rearrange("(k p) t -> p k t", p=128)
    r2bf, r2f = io["recv_bf"], io["recv_f"]
    sr_v = io["silur"].rearrange("(c f) t -> f c t", f=128)
    o_x3 = io["b_x3"]
    rd2 = [io[k] for k in ("b_r2bf", "b_r2f") if io.get(k) is not None]
    rdx = [io[k] for k in ("b_x1",) if io.get(k) is not None]
    rds = [io[k] for k in ("b_sr",) if io.get(k) is not None]

    def ab(f0, f1, s):
        return [a.b[f * nsub + s] for f in range(f0, f1)]

    for ps_ in range(TPC // PASS):
        t0 = ps_ * PASS
        S.dma("sp", lambda e, t0=t0: e.dma_start(out=x.t[:, :, :], in_=xv[:, :, t0:t0 + PASS]), writes=[x.b], reads=rdx)
        for s in range(nsub):
            pre_norm(K, C, x, x.b, g, 0, h, h.b, s)
        for s in range(nsub):
            sl = slice(s * 512, (s + 1) * 512)
            tsl = slice(t0 + s * 512, t0 + (s + 1) * 512)
            for p in range(4):
                S.dma("sp", lambda e, sl=sl, tsl=tsl, p=p: e.dma_start(out=y.t[:, p, sl], in_=r2f[0, p].rearrange("(f t) -> f t", t=T)[:, tsl]),
                      writes=[y.b], reads=rd2)
                S.dma("sp", lambda e, sl=sl, tsl=tsl, p=p: e.dma_start(out=y.t[:, 4 + p, sl], in_=r2f[1, p].rearrange("(f t) -> f t", t=T)[:, tsl]),
                      writes=[y.b], reads=rd2)
                S.dma("sp", lambda e, sl=sl, tsl=tsl, p=p: e.dma_start(out=a.t[:, 8 + p, sl], in_=r2bf[p].rearrange("(f t) -> f t", t=T)[:, tsl]),
                      writes=ab(8 + p, 9 + p, s), reads=rd2)
            sr = sr_pool.next()
            S.dma("sp", lambda e, sr=sr, tsl=tsl: e.dma_start(out=sr.t[:, :, :], in_=sr_v[:, :, tsl]), writes=[sr.b], reads=rds)
            sq = C["sq_pool"].next()
            S.op("act", lambda e, sq=sq, sl=sl: e.activation(out=sq.t[:, 0:4, :], in_=y.t[:, 0:4, sl], func=AF.Copy),
                 reads=[y.b], writes=[sq.b])
            S.op("act", lambda e, sq=sq, sl=sl: e.activation(out=sq.t[:, 4:8, :], in_=y.t[:, 0:4, sl], func=AF.Square),
                 reads=[y.b], writes=[sq.b])
            p1 = K.psum()
            mm_group(K, p1, [(C["ones_bf"].t[:, :], sq.t[:, c, :], p1.t[:, :]) for c in range(4)], reads=[C["ones_bf"].b, sq.b])
            p2 = K.psum()
            mm_group(K, p2, [(C["ones_bf"].t[:, :], sq.t[:, 4 + c, :], p2.t[:, :]) for c in range(4)], reads=[C["ones_bf"].b, sq.b])
            mean = ln_mean
            S.op("act", lambda e, mean=mean, p1=p1: e.activation(out=mean.t[:, :], in_=p1.t[:, :], func=AF.Copy, scale=1.0 / 512),
                 reads=[p1.b], writes=[mean.b])
            msq = ln_msq
            S.op("dve", lambda e, mean=mean, msq=msq: e.tensor_tensor(out=msq.t[:, :], in0=mean.t[:, :], in1=mean.t[:, :], op=ALU.mult),
                 reads=[mean.b], writes=[msq.b])
            S.op("dve", lambda e, msq=msq, p2=p2: e.scalar_tensor_tensor(out=msq.t[:, :], in0=p2.t[:, :], scalar=1.0 / 512, in1=msq.t[:, :],
                                                                       op0=ALU.mult, op1=ALU.subtract),
                 reads=[p2.b, msq.b], writes=[msq.b])
            rs = C["rstd_pool"].next()
            S.op("act", lambda e, rs=rs, msq=msq: e.activation(out=rs.t[:, :], in_=msq.t[:, :], func=AF.Sqrt, bias=C["eps_col"].t[:, 0:1], scale=1.0),
                 reads=[msq.b, C["eps_col"].b], writes=[rs.b])
            S.op("dve", lambda e, rs=rs: e.reciprocal(out=rs.t[:, :], in_=rs.t[:, :]), reads=[rs.b], writes=[rs.b])
            for c in range(4):
                t = C["tmp_pool"].next()
                S.op("dve", lambda e, t=t, c=c, sl=sl, mean=mean: e.tensor_tensor(out=t.t[:, :], in0=y.t[:, c, sl], in1=mean.t[:, :], op=ALU.subtract),
                     reads=[y.b, mean.b], writes=[t.b])
                S.op("dve", lambda e, t=t, rs=rs: e.tensor_tensor(out=t.t[:, :], in0=t.t[:, :], in1=rs.t[:, :], op=ALU.mult),
                     reads=[t.b, rs.b], writes=[t.b])
                S.op("act", lambda e, t=t, c=c, sl=sl: e.activation(out=a.t[:, c, sl], in_=t.t[:, :], func=AF.Silu,
                                                                      scale=g.t[:, 32 + c:33 + c], bias=g.t[:, 36 + c:37 + c]),
                     reads=[t.b, g.b], writes=ab(c, c + 1, s))
                S.op(POOL_ENG, lambda e, c=c, sl=sl, sr=sr: e.tensor_tensor(out=a.t[:, 4 + c, sl], in0=y.t[:, 4 + c, sl], in1=sr.t[:, c, :], op=ALU.mult),
                     reads=[y.b, sr.b], writes=ab(4 + c, 5 + c, s))
        base = {0: 8, 1: 0, 2: 4}
        for d in range(8):
            for gi in range(3):
                wb = wbr_pool.next()
                load_w(K, wb, io["wbr"][gi * 8 + d].rearrange("p (k j) -> p k j", j=128))
                wgt = wgt_pool.next()
                load_w(K, wgt, io["wgt"][gi * 8 + d].rearrange("p (k j) -> p k j", j=128))
                for s in range(nsub):
                    sl = slice(s * 512, (s + 1) * 512)
                    pb = K.psum()
                    mm_group(K, pb, [(wb.t[:, kc, :], a.t[:, base[gi] + kc, sl], pb.t[:, :]) for kc in range(4)],
                             reads=[wb.b] + ab(base[gi], base[gi] + 4, s))
                    pl = K.psum()
                    mm_group(K, pl, [(wgt.t[:, k, :], h.t[:, k, sl], pl.t[:, :]) for k in range(8)], reads=[wgt.b, h.b])
                    sig = C["tmp_pool"].next()
                    S.op("act", lambda e, sig=sig, pl=pl: e.activation(out=sig.t[:, :], in_=pl.t[:, :], func=AF.Sigmoid),
                         reads=[pl.b], writes=[sig.b])
                    if gi == 0:
                        S.op("dve", lambda e, sig=sig, pb=pb, s=s: e.tensor_tensor(out=acc[s].t[:, :], in0=sig.t[:, :], in1=pb.t[:, :], op=ALU.mult),
                             reads=[sig.b, pb.b], writes=[acc[s].b])
                    else:
                        S.op("dve", lambda e, sig=sig, pb=pb: e.tensor_tensor(out=sig.t[:, :], in0=sig.t[:, :], in1=pb.t[:, :], op=ALU.mult),
                             reads=[sig.b, pb.b], writes=[sig.b])
                        if gi == 1:
                            S.op(POOL_ENG, lambda e, sig=sig, s=s: e.tensor_tensor(out=acc[s].t[:, :], in0=acc[s].t[:, :], in1=sig.t[:, :], op=ALU.add),
                                 reads=[sig.b, acc[s].b], writes=[acc[s].b])
                        else:
                            S.op(POOL_ENG, lambda e, sig=sig, s=s, d=d, sl=sl: e.tensor_tensor(out=a.t[:, 12 + d, sl], in0=acc[s].t[:, :], in1=sig.t[:, :], op=ALU.add),
                                 reads=[sig.b, acc[s].b], writes=ab(12 + d, 13 + d, s))
        for do in range(8):
            wo = C["wgu_pool"].next()
            load_w(K, wo, io["wo"][do].rearrange("p (k j) -> p k j", j=128))
            for s in range(nsub):
                sl = slice(s * 512, (s + 1) * 512)
                pp = K.psum()
                mm_group(K, pp, [(wo.t[:, k, :], a.t[:, 12 + k, sl], pp.t[:, :]) for k in range(8)], reads=[wo.b] + ab(12, 20, s))
                S.op("act", lambda e, pp=pp, do=do, sl=sl: e.activation(out=y.t[:, do, sl], in_=pp.t[:, :], func=AF.Copy),
                     reads=[pp.b], writes=[y.b])
        if dbg == 2:
            S.dma("sp", lambda e, t0=t0: e.dma_start(out=ov[:, :, t0:t0 + PASS], in_=y.t[:, :, :]), reads=[y.b], outs=[o_x3])
            continue
        for s in range(nsub):
            post_norm_residual(K, C, y, y.b, g, 8, 1.0, x, x.b, s)
        if dbg == 1:
            S.dma("sp", lambda e, t0=t0: e.dma_start(out=ov[:, :, t0:t0 + PASS], in_=x.t[:, :, :]), reads=[x.b], outs=[o_x3])
            continue
        for s in range(nsub):
            pre_norm(K, C, x, x.b, g, 16, h, h.b, s)
        ffn(K, C, h, h.b, io["wg"], io["wu"], io["wd"], y, y.b)
        for s in range(nsub):
            post_norm_residual(K, C, y, y.b, g, 24, 0.5, x, x.b, s)
        S.dma("sp", lambda e, t0=t0: e.dma_start(out=ov[:, :, t0:t0 + PASS], in_=x.t[:, :, :]), reads=[x.b], outs=[o_x3])


def prep_C_weights(inp, l):
    w_in = inp["w_in"][l]
    wgt = lay_fm_w(w_in[:, OGT:OGT + 3072], 24)
    wb = inp["w_branch"][l]
    wbr = np.concatenate([lay_fm_w(wb[gi], 8) for gi in range(3)], axis=0)
    gains = np.concatenate([lay_pk(inp["norm_pre"][l, 1]), lay_pk(inp["norm_post"][l, 1]), lay_pk(inp["norm_pre"][l, 2]),
                            lay_pk(inp["norm_post"][l, 2]), lay_pk(inp["conv_ln_g"][l]), lay_pk(inp["conv_ln_b"][l])], axis=1)
    return dict(wgt=wgt, wbr=_c(wbr), wo=lay_fm_w(inp["w_out"][l], 8), wg=lay_fm_w(inp["ffn2_w_gate"][l], NF),
                wu=lay_fm_w(inp["ffn2_w_up"][l], NF), wd=lay_fm_w(inp["ffn2_w_down"][l], 8), gains=_c(gains))


NQT = SEQ // 512
NBLK = SEQ // 128
S2B_LEN = 128 * TPC
S2F_LEN = 256 * TPC


def build_B():
    nc = bass.Bass("TRN2", target_bir_lowering=False)
    dt = nc.dram_tensor
    io = dict(
        rbf=dt("rbf", [4, SB_LEN], BF16, kind="ExternalInput").ap(),
        rf=dt("rf", [4, SF_LEN], F32, kind="ExternalInput").ap(),
        bw=dt("bw", [128, 33], F32, kind="ExternalInput").ap(),
        s2bf=dt("s2bf", [4, S2B_LEN], BF16, kind="ExternalOutput").ap(),
        s2f=dt("s2f", [4, S2F_LEN], F32, kind="ExternalOutput").ap(),
    )
    with ExitStack() as st:
        K = Ctx(nc, st)
        K.init_psum(8)
        rbf_, rf_ = io["rbf"], io["rf"]
        io["rbf"] = lambda j, a, b: rbf_[j:j + 1, a:b]
        io["rf"] = lambda j, a, b: rf_[j:j + 1, a:b]
        io["b_s2bf"], io["b_s2f"] = Buf("o_s2bf", True), Buf("o_s2f", True)
        K.outs += [io["b_s2bf"], io["b_s2f"]]
        emit_B(K, io)
        K.S.wait_all_on("sp", K.outs)
        K.S.emit()
    return nc


def emit_B(K, io):
    S = K.S
    T = TPC
    rbf, rf = io["rbf"], io["rf"]
    o_bfs, o_cys, o_gns = io["b_s2bf"], io["b_s2cy"], io["b_s2gn"]
    coll2 = io.get("coll2")
    N2 = 128 * T
    STQ = "pool" if io.get("late_picks") is not None else "sp"
    rd_qkv = [io[k] for k in ("b_rbf",) if io.get(k) is not None]
    rd_g = [io[k] for k in ("b_rbf2",) if io.get(k) is not None]
    rd_f = [io[k] for k in ("b_rf",) if io.get(k) is not None]
    ones_bf = K.sb([128, 128], BF16, "ones_bf")
    S.op("dve", lambda e: e.memset(ones_bf.t[:, :], 1.0), writes=[ones_bf.b])
    negones = K.sb([128, 1], BF16, "negones")
    S.op("dve", lambda e: e.memset(negones.t[:, :], -1.0), writes=[negones.b])
    eps = K.sb([128, 1], F32, "eps")
    S.op("dve", lambda e: e.memset(eps.t[:, :], EPS), writes=[eps.b])
    tri2 = K.sb([128, 128], BF16, "tri2")
    S.op("pool", lambda e: e.memset(tri2.t[:, :], -1.0), writes=[tri2.b])
    S.op("pool", lambda e: e.affine_select(out=tri2.t[:, :], in_=tri2.t[:, :], pattern=[[-1, 128]], compare_op=ALU.is_ge,
                                            fill=0.0, base=0, channel_multiplier=1), reads=[tri2.b], writes=[tri2.b])
    S.op("pool", lambda e: e.memset(tri2.t[0:1, :], 1.0), reads=[tri2.b], writes=[tri2.b])
    tris = K.sb([128, 128], BF16, "tris")
    S.op("pool", lambda e: e.memset(tris.t[:, :], 1.0), writes=[tris.b])
    S.op("pool", lambda e: e.affine_select(out=tris.t[:, :], in_=tris.t[:, :], pattern=[[-1, 128]], compare_op=ALU.is_gt,
                                            fill=0.0, base=0, channel_multiplier=1), reads=[tris.b], writes=[tris.b])
    S.op("pool", lambda e: e.memset(tris.t[64:128, 0:64], 0.0), reads=[tris.b], writes=[tris.b])
    cind = K.sb([128, 2], BF16, "cind")
    S.op("pool", lambda e: e.memset(cind.t[:, :], 0.0), writes=[cind.b])
    S.op("pool", lambda e: e.memset(cind.t[0:64, 0:1], 1.0), reads=[cind.b], writes=[cind.b])
    S.op("pool", lambda e: e.memset(cind.t[64:128, 1:2], 1.0), reads=[cind.b], writes=[cind.b])
    bw = K.sb([128, 33], F32, "bw")
    S.dma("sp", lambda e: e.dma_start(out=bw.t[:, :], in_=io["bw"]), writes=[bw.b])
    qT = K.sb([128, SEQ], BF16, "qT")
    qT2 = K.sb([128, SEQ], BF16, "qT2")
    S.op("pool", lambda e: e.memset(qT.t[64:128, :], 0.0), writes=[qT.b])
    S.op("pool", lambda e: e.memset(qT2.t[0:64, :], 0.0), writes=[qT2.b])
    qz = [qT, qT2]
    KB = [Buf("kT_j%d" % j) for j in range(4)]
    VB = [Buf("v_j%d" % j) for j in range(4)]
    QB = [[Buf("q%d_j%d" % (h, j)) for j in range(4)] for h in range(2)]
    GB = [Buf("gla_j%d" % j) for j in range(4)]
    UB = [Buf("u_j%d" % j) for j in range(4)]

    def ksrc(b):
        return sorted({min(127 * b, SEQ - 1) // T, min(127 * b + 126, SEQ - 1) // T})
    NB2 = (SEQ + 126) // 127
    kT = K.sb([128, NB2 * 128], BF16, "kT2")
    v = K.sb([128, NB2, 128], BF16, "v2")
    S.op("pool", lambda e: e.memset(kT.t[:, :], 0.0), writes=[kT.b])
    S.op("pool", lambda e: e.memset(v.t[:, :, :], 0.0), writes=[v.b])
    kT3 = kT.t[:, :].rearrange("f (b s) -> f b s", s=128)

    def key_runs(ta, tb):
        runs = []
        t = ta
        while t < tb:
            b, s = t // 127, t % 127
            if s == 0 and tb - t >= 127:
                nb = (tb - t) // 127
                runs.append((b, nb, 0, 127, t - ta))
                t += nb * 127
            else:
                n = min(127 - s, tb - t)
                runs.append((b, 1, s, n, t - ta))
                t += n
        return runs
    gqT = K.sb([64, SEQ], BF16, "gqT")
    gv = K.sb([128, NBLK, 128], BF16, "gv")
    gk = K.sb([128, NBLK, 64], F32, "gk")
    la = K.sb([128, NBLK, 64], F32, "la")
    u = K.sb([128, 32 + SEQ], F32, "u")
    S.op("pool", lambda e: e.memset(u.t[:, 0:32], 0.0), writes=[u.b])
    def ld(out_ap, in_ap, jbuf, after=(), rdb=None):
        S.dma("sp", lambda e: e.dma_start(out=out_ap, in_=in_ap), reads=list(rdb) + list(after), outs=[jbuf])

    for phase_, j in [(0, 0), (0, 1), (0, 2), (0, 3), (1, 0), (1, 1), (1, 2), (1, 3)]:
        ts = slice(j * T, (j + 1) * T)
        if phase_ == 1 and j == 0 and io.get("late_picks") is not None:
            io["late_picks"]()
        for (fa, fb, sap) in (X1B.fm_pieces(rbf, j, SB_KT, 0, 128) if phase_ == 0 else []):
            for (b0, nb, s0, ns, off) in key_runs(j * T, (j + 1) * T):
                if nb > 1 or ns == 127:
                    dst = kT3[fa:fb, b0:b0 + nb, 1:128]
                    srcv = sap[:, off:off + nb * 127].rearrange("f (b s) -> f b s", s=127)
                else:
                    dst = kT3[fa:fb, b0, 1 + s0:1 + s0 + ns]
                    srcv = sap[:, off:off + ns]
                ld(dst, srcv, KB[j], after=[kT.b], rdb=rd_qkv)
        for (fa, fb, sap) in (X1B.fm_pieces(rbf, j, SB_QT, 0, 128) if phase_ == 0 else []):
            tq = qT if fa < 64 else qT2
            assert (fa < 64) == (fb <= 64)
            ld(tq.t[fa:fb, ts], sap, QB[0 if fa < 64 else 1][j], after=[tq.b], rdb=rd_qkv)
        for (fa, fb, sap) in (X1B.fm_pieces(rbf, j, SB_GQT, 0, 64) if phase_ == 1 else []):
            ld(gqT.t[fa:fb, ts], sap, GB[j], rdb=rd_g)
        for (fa, fb, sap) in (X1F.fm_pieces(rf, j, SF_UT, 0, 128) if phase_ == 1 else []):
            ld(u.t[fa:fb, 32 + j * T:32 + (j + 1) * T], sap, UB[j], after=[u.b], rdb=rd_f)
        for tok0 in (0, 1024):
            kb0 = j * 16 + tok0 // 128
            vsrc = X1B.tm_src(rbf, j, SB_V, 128, tok0, 1024)
            for (b0, nb, s0, ns, off) in (key_runs(j * T + tok0, j * T + tok0 + 1024) if phase_ == 0 else []):
                if nb > 1 or ns == 127:
                    dst = v.t[1:128, b0:b0 + nb, :]
                    srcv = vsrc[off:off + nb * 127, :].rearrange("(b s) f -> s b f", s=127)
                else:
                    dst = v.t[1 + s0:1 + s0 + ns, b0, :]
                    srcv = vsrc[off:off + ns, :]
                ld(dst, srcv, VB[j], after=[v.b], rdb=rd_qkv)
            if phase_ == 0:
                continue
            ld(gv.t[:, kb0:kb0 + 8, :], X1B.tm_src(rbf, j, SB_GV, 128, tok0, 1024).rearrange("(k t) f -> t k f", t=128), GB[j], rdb=rd_g)
            ld(gk.t[:, kb0:kb0 + 8, :], X1F.tm_src(rf, j, SF_GK, 64, tok0, 1024).rearrange("(k t) f -> t k f", t=128), GB[j], rdb=rd_f)
            ld(la.t[:, kb0:kb0 + 8, :], X1F.tm_src(rf, j, SF_LA, 64, tok0, 1024).rearrange("(k t) f -> t k f", t=128), GB[j], rdb=rd_f)
    ps_z = Rot(K.ps[0:3] + K.ps[5:7])
    ps_o = K.ps[3:5]
    ps_g = K.ps[7]
    e_pool = K.pool(3, [128, 512], F32, "e")
    sp_pool = K.pool(5, [128, 512], BF16, "sp")
    w_pool = K.pool(3, [128, 512], BF16, "w")
    osb_pool = K.pool(2, [128, 512], BF16, "osb")

    def conv_gen():
        acc_pool = K.pool(2, [128, 512], F32, "cacc")

        def ub_of(pc):
            js = {(pc * 512) // T, max(pc * 512 - 30, 0) // T, (pc * 512 + 511) // T}
            return [u.b] + [UB[j] for j in sorted(js)]
        for pc in range(SEQ // 512):
            acc = acc_pool.next()
            c0 = 2 + pc * 512
            S.op("dve", lambda e, acc=acc, c0=c0: e.tensor_scalar(out=acc.t[:, :], in0=u.t[:, c0:c0 + 512], scalar1=bw.t[:, 0:1], scalar2=bw.t[:, 31:32],
                                                                    op0=ALU.mult, op1=ALU.add), reads=ub_of(pc) + [bw.b], writes=[acc.b])
            yield
            for k in range(1, 31):
                S.op("dve", lambda e, acc=acc, c0=c0, k=k: e.scalar_tensor_tensor(out=acc.t[:, :], in0=u.t[:, c0 + k:c0 + k + 512], scalar=bw.t[:, k:k + 1],
                                                                                  in1=acc.t[:, :], op0=ALU.mult, op1=ALU.add),
                     reads=ub_of(pc) + [bw.b, acc.b], writes=[acc.b])
                yield
            cols = slice((pc % 4) * 512, (pc % 4 + 1) * 512)
            dst = io["s2f"][pc // 4, 0].rearrange("(f t) -> f t", t=T)[:, cols]
            S.dma(STQ, lambda e, acc=acc, dst=dst: e.dma_start(out=dst, in_=acc.t[:, :]), reads=[acc.b], outs=[o_cys[pc // 4]])
            if pc % 4 == 3 and coll2 is not None:
                coll2("cy", pc // 4)
            yield

    def gla_gen():
        state = K.sb([64, 128], F32, "gstate")
        S.op("pool", lambda e: e.memset(state.t[:, :], 0.0), writes=[state.b])
        sbf_pool = K.pool(2, [64, 128], BF16, "gstbf")
        labf_pool = K.pool(2, [128, 64], BF16, "labf")
        ed_pool = K.pool(2, [128, 64], F32, "ged")
        kd_pool = K.pool(2, [128, 64], BF16, "gkd")
        lam_pool = K.pool(2, [64, 2], F32, "glam")
        osb_p = K.pool(2, [128, 128], F32, "gosb")
        sq_p = K.pool(2, [128, 128], BF16, "gsq")
        rs_p = K.pool(2, [128, 128], F32, "grs")
        gst_pool = K.pool(2, [128, 512], F32, "gnst")
        pg = ps_g
        r_dte = pg.t[:, 0:64]
        r_lam = pg.t[0:64, 64:66]
        r_st = pg.t[0:64, 128:256]
        r_o = pg.t[:, 256:384]
        r_ss = pg.t[:, 384:512]
        bg = pg.b
        gst = None
        for blk in range(NBLK):
            labf = labf_pool.next()
            S.op("pool", lambda e, labf=labf, blk=blk: e.tensor_copy(out=labf.t[:, :], in_=la.t[:, blk, :]), reads=[GB[blk // 16]], writes=[labf.b])
            yield
            S.op("pe", [lambda e, labf=labf: e.matmul(r_dte, tris.t[:, :], labf.t[:, :], start=True, stop=True)], reads=[tris.b, labf.b], writes=[bg])
            S.op("pe", [lambda e, labf=labf: e.matmul(r_lam, labf.t[:, :], cind.t[:, :], start=True, stop=True)], reads=[cind.b, labf.b], writes=[bg])
            yield
            ed = ed_pool.next()
            S.op("act", lambda e, ed=ed: e.activation(out=ed.t[:, :], in_=r_dte, func=AF.Exp), writes=[ed.b, bg])
            lam = lam_pool.next()
            S.op("act", lambda e, lam=lam: e.activation(out=lam.t[:, :], in_=r_lam, func=AF.Exp), writes=[lam.b, bg])
            yield
            kd = kd_pool.next()
            S.op("dve", lambda e, kd=kd, ed=ed, blk=blk: e.tensor_tensor(out=kd.t[:, :], in0=gk.t[:, blk, :], in1=ed.t[:, :], op=ALU.mult),
                 reads=[GB[blk // 16], ed.b], writes=[kd.b])
            yield
            for c in range(2):
                cs = slice(64 * c, 64 * c + 64)
                S.op("pe", [lambda e, kd=kd, cs=cs, blk=blk: e.matmul(r_st, kd.t[cs, :], gv.t[cs, blk, :], start=True, stop=True)],
                     reads=[kd.b, GB[blk // 16]], writes=[bg])
                yield
                S.op("dve", lambda e, lam=lam, c=c: e.scalar_tensor_tensor(out=state.t[:, :], in0=state.t[:, :], scalar=lam.t[:, c:c + 1], in1=r_st,
                                                                          op0=ALU.mult, op1=ALU.add), reads=[state.b, lam.b], writes=[state.b, bg])
                sbf = sbf_pool.next()
                S.op("dve", lambda e, sbf=sbf: e.tensor_copy(out=sbf.t[:, :], in_=state.t[:, :]), reads=[state.b], writes=[sbf.b])
                yield
                tk = slice(blk * 128 + 64 * c, blk * 128 + 64 * c + 64)
                S.op("pe", [lambda e, sbf=sbf, tk=tk, c=c: e.matmul(pg.t[:, 256 + 64 * c:256 + 64 * c + 64], sbf.t[:, :], gqT.t[:, tk], start=True, stop=True)],
                     reads=[sbf.b, GB[blk // 16]], writes=[bg])
                yield
            osb = osb_p.next()
            S.op("dve", lambda e, osb=osb: e.tensor_copy(out=osb.t[:, :], in_=r_o), writes=[osb.b, bg])
            sq = sq_p.next()
            S.op("pool", lambda e, osb=osb, sq=sq: e.tensor_tensor(out=sq.t[:, :], in0=osb.t[:, :], in1=osb.t[:, :], op=ALU.mult), reads=[osb.b], writes=[sq.b])
            yield
            S.op("pe", [lambda e, sq=sq: e.matmul(r_ss, ones_bf.t[:, :], sq.t[:, :], start=True, stop=True)], reads=[ones_bf.b, sq.b], writes=[bg])
            yield
            rs = rs_p.next()
            S.op("act", lambda e, rs=rs: e.activation(out=rs.t[:, :], in_=r_ss, func=AF.Ln, bias=eps.t[:, 0:1], scale=1.0 / 128), reads=[eps.b], writes=[rs.b, bg])
            S.op("act", lambda e, rs=rs: e.activation(out=rs.t[:, :], in_=rs.t[:, :], func=AF.Exp, scale=-0.5), reads=[rs.b], writes=[rs.b])
            yield
            if blk % 4 == 0:
                gst = gst_pool.next()
            cs4 = slice((blk % 4) * 128, (blk % 4 + 1) * 128)
            S.op("dve", lambda e, gst=gst, cs4=cs4, osb=osb, rs=rs: e.scalar_tensor_tensor(out=gst.t[:, cs4], in0=osb.t[:, :], scalar=bw.t[:, 32:33], in1=rs.t[:, :],
                                                                                        op0=ALU.mult, op1=ALU.mult), reads=[osb.b, rs.b, bw.b], writes=[gst.b])
            if blk % 4 == 3:
                pc = blk // 4
                cols = slice((pc % 4) * 512, (pc % 4 + 1) * 512)
                dst = io["s2f"][pc // 4, 1].rearrange("(f t) -> f t", t=T)[:, cols]
                S.dma(STQ, lambda e, gst=gst, dst=dst: e.dma_start(out=dst, in_=gst.t[:, :]), reads=[gst.b], outs=[o_gns[pc // 4]])
                if pc % 4 == 3 and coll2 is not None:
                    coll2("gn", pc // 4)
            yield

    import os as _os
    _mode = _os.environ.get("SIDE_MODE", "1,2")
    side = [conv_gen(), gla_gen()]
    periods = [int(t) for t in _mode.split(",")]
    SIDE_DELAY = int(_os.environ.get("SIDE_DELAY", "250")) if io.get("b_rbf2") is not None else 0
    tick = [0]

    def step_side(force=False):
        tick[0] += 1
        for gi, gnr in enumerate(list(side)):
            per = periods[min(gi, len(periods) - 1)]
            if per == 0 and not force:
                continue
            if force or tick[0] % max(per, 1) == 0:
                try:
                    next(gnr)
                except StopIteration:
                    side.remove(gnr)

    seq = []
    for qt in range(NQT):
        q0 = qt * 512
        bmax = (q0 + 510) // 127
        bl = list(range(bmax, -1, -1))
        for idx, b in enumerate(bl):
            seq.append((qt, b, idx == 0, idx == len(bl) - 1))
    tiles = []
    for t in seq:
        for hh in range(2):
            tiles.append((hh,) + t)
    w_pool4 = K.pool(4, [128, 512], BF16, "w4")
    st = {}

    def mask_op(buf, qt, b):
        base = 512 * qt - 127 * b + 1
        S.op("pool", lambda e: e.affine_select(out=buf.t[:, :], in_=buf.t[:, :], pattern=[[1, 512]], compare_op=ALU.is_gt,
                                                fill=0.0, base=base, channel_multiplier=-1), reads=[buf.b], writes=[buf.b])

    def is_diag(qt, b):
        return 127 * b + 127 > 512 * qt

    def zfn_of(i):
        hh, qt, b, first, last = tiles[i]
        pz = ps_z.next()
        st[i] = {"pz": pz}
        qs = slice(qt * 512, (qt + 1) * 512)
        ks = slice(b * 128, (b + 1) * 128)
        return ((lambda e: e.matmul(pz.t[:, :], kT.t[:, ks], qz[hh].t[:, qs], start=True, stop=False)),
                [kT.b, qz[hh].b, QB[hh][qt // 4]] + [KB[j] for j in ksrc(b)], pz)

    def act_exp(i):
        pz = st[i]["pz"]
        ee = e_pool.next()
        st[i]["ee"] = ee
        S.op("act", lambda e: e.activation(out=ee.t[:, :], in_=pz.t[:, :], func=AF.Exp), reads=[pz.b], writes=[ee.b])

    def act_ln(i):
        hh, qt, b, first, last = tiles[i]
        ee = st[i]["ee"]
        sp = sp_pool.next()
        st[i]["sp"] = sp
        S.op("act", lambda e: e.activation(out=sp.t[:, :], in_=ee.t[:, :], func=AF.Ln, bias=1.0), reads=[ee.b], writes=[sp.b])
        if is_diag(qt, b):
            mask_op(sp, qt, b)
        if first:
            S.op("dve", lambda e: e.memset(sp.t[0:1, :], 0.0), reads=[sp.b], writes=[sp.b])
        else:
            pzp = st[i - 2]["pz"]
            S.op("dve", lambda e: e.tensor_copy(out=sp.t[0:1, :], in_=pzp.t[0:1, :]), reads=[sp.b], writes=[sp.b, pzp.b])

    def pv(i):
        hh, qt, b, first, last = tiles[i]
        hs = slice(64 * hh, 64 * hh + 64)
        w = st[i]["w"]
        po = ps_o[hh]
        S.op("pe", [lambda e: e.matmul(po.t[:, :], v.t[:, b, :], w.t[:, :], start=first, stop=last)], reads=[v.b, w.b] + [VB[j] for j in ksrc(b)], writes=[po.b])
        if last:
            osb = osb_pool.next()
            S.op("dve", lambda e: e.tensor_copy(out=osb.t[hs, :], in_=po.t[hs, :]), reads=[po.b], writes=[osb.b])
            cols = slice((qt % 4) * 512, (qt % 4 + 1) * 512)
            dst = io["s2bf"][qt // 4].rearrange("(f t) -> f t", t=T)[hs, cols]
            S.dma(STQ, lambda e, dst=dst: e.dma_start(out=dst, in_=osb.t[hs, :]), reads=[osb.b], outs=[o_bfs[qt // 4]])
            if qt % 4 == 3 and hh == 1 and coll2 is not None:
                coll2("bf", qt // 4)
        st.pop(i - 4, None)

    def exp_w(i):
        hh, qt, b, first, last = tiles[i]
        pz = st[i]["pz"]
        w = w_pool4.next()
        st[i]["w"] = w
        S.op("act", lambda e: e.activation(out=w.t[:, :], in_=pz.t[:, :], func=AF.Exp), reads=[pz.b], writes=[w.b])
        if is_diag(qt, b):
            mask_op(w, qt, b)

    n = len(tiles)
    for i in range(min(2, n)):
        zf, zr, pz = zfn_of(i)
        S.op("pe", [zf], reads=zr, writes=[pz.b])
        act_exp(i)
        act_ln(i)
    for j in range(n):
        hh, qt, b, first, last = tiles[j]
        pz, sp = st[j]["pz"], st[j]["sp"]
        fns = [lambda e, pz=pz, sp=sp: e.matmul(pz.t[:, :], tri2.t[:, :], sp.t[:, :], start=False, stop=True)]
        rds, wrs = [tri2.b, sp.b, pz.b], [pz.b]
        if j + 2 < n:
            zf, zr, pz2 = zfn_of(j + 2)
            fns.append(zf)
            rds += zr
            wrs.append(pz2.b)
        S.op("pe", fns, reads=rds, writes=wrs)
        if j + 2 < n:
            act_exp(j + 2)
            act_ln(j + 2)
        if j >= 1:
            exp_w(j - 1)
        if j >= 2:
            pv(j - 2)
        if j >= SIDE_DELAY:
            step_side()
    exp_w(n - 1)
    if n >= 2:
        pv(n - 2)
    pv(n - 1)
    while side:
        step_side(force=True)


def prep_B_weights(inp, l, p):
    cw = inp["conv_w"][l][:, 128 * p:128 * p + 128].T
    cb = inp["conv_b"][l][128 * p:128 * p + 128][:, None]
    gg = inp["gla_norm_g"][l][128 * p:128 * p + 128][:, None]
    return _c(np.concatenate([cw, cb, gg], axis=1).astype(np.float32))


I32 = mybir.dt.int32
GROUPS = [[0, 1, 2, 3], [4, 5, 6, 7]]


def build_fused(L=DEPTH, phases="AXBYC"):
    nc = bass.Bass("TRN2", target_bir_lowering=False)
    dt = nc.dram_tensor

    def ein(name, shape, d=F32):
        return dt(name, list(shape), d, kind="ExternalInput").ap()

    xT = ein("xT", [D, TPC])
    cid = ein("cid", [1, 1], I32)
    bw = ein("bw", [L, 128, 33])
    wg1, wu1, wd1 = ein("wg1", [L, NF, 128, 1024]), ein("wu1", [L, NF, 128, 1024]), ein("wd1", [L, 8, 128, NF * 128])
    wfm, wtm = ein("wfm", [L, NFM, 128, 1024]), ein("wtm", [L, 128, 8 * 1280])
    walpha, balpha, gainsA = ein("walpha", [L, 16, 256]), ein("balpha", [L, 1, 256]), ein("gainsA", [L, 128, 24])
    wgt, wbr, wo = ein("wgt", [L, 24, 128, 1024]), ein("wbr", [L, 24, 128, 512]), ein("wo", [L, 8, 128, 1024])
    wg2, wu2, wd2 = ein("wg2", [L, NF, 128, 1024]), ein("wu2", [L, NF, 128, 1024]), ein("wd2", [L, 8, 128, NF * 128])
    gainsC = ein("gainsC", [L, 128, 40])
    outT = dt("outT", [D, TPC], F32, kind="ExternalOutput").ap()
    x1T = dt("x1T_i", [D, TPC], F32).ap()
    xbuf = dt("xbuf_i", [D, TPC], F32).ap()
    silur = dt("silur_i", [512, TPC], F32).ap()
    send_bf = dt("send_bf_i", X1B.shape(), BF16).ap()
    send_f = dt("send_f_i", X1F.shape(), F32).ap()
    ag_bf = dt("ag_bf_i", X1B.shape(16), BF16).ap()
    ag_f = dt("ag_f_i", X1F.shape(16), F32).ap()
    N2 = 128 * TPC
    s2bf = dt("s2bf_i", [4, N2], BF16).ap()
    s2f = dt("s2f_i", [4, 2, N2], F32).ap()
    ag2_bf = dt("ag2_bf_i", [4, 4, N2], BF16).ap()
    ag2_f = dt("ag2_f_i", [4, 2, 4, N2], F32).ap()
    st_bf = dt("st_bf_i", X1B.shape(), BF16).ap()
    st_f = dt("st_f_i", X1F.shape(), F32).ap()
    st2_bf = dt("st2_bf_i", [4, N2], BF16).ap()
    st2_f = dt("st2_f_i", [2, 4, N2], F32).ap()
    P = {n: Buf(n, True) for n in ("x1", "sr", "sbf", "sf", "agbf", "agf", "s2bf", "s2f", "ag2bf", "ag2f", "xbuf", "out",
                                   "stbf", "stf", "st2bf", "st2f", "agbf2", "stbf2")}
    PB2 = {k: [Buf("%s%d" % (k, j), True) for j in range(4)] for k in ("s2bf", "s2cy", "s2gn")}
    with ExitStack() as st:
        K = Ctx(nc, st)
        S = K.S
        S.cid_ap = cid[0:1, 0:1]
        K.init_psum(8)

        def stat(ap):
            return ap

        def pick(ap16, stage, bsrc, bdst, c0=0, c1=None):
            c1 = ap16.shape[0] if c1 is None else c1
            v4 = ap16[c0:c1].rearrange("c (r d) n -> d c r n", d=4)
            S.dma("sp", lambda e: e.dma_start(out=stage[c0:c1], in_=v4[bass.ds(S.dyn["p"], 1)].rearrange("o c r n -> (o c) r n")),
                  reads=[bsrc], writes=[bdst])

        def gather(src, dst, bsrc, bdst, c0=0, c1=None):
            for c in range(c0, src.shape[0] if c1 is None else c1):
                S.dma("pool", lambda e, c=c: e.collective_compute("AllGather", ALU.bypass, replica_groups=GROUPS,
                                                                  ins=[src[c].opt()], outs=[dst[c].opt()]),
                      reads=[bsrc], outs=[bdst], sembuf=bdst, inc=1)

        for l in range(L):
            last = (l == L - 1)
            K.arena_reset()
            C = common_consts(K)
            ffn_bufs(K, C)
            if "A" in phases:
              emit_A(K, C, dict(xT=xT if l == 0 else xbuf, b_xin=None if l == 0 else P["xbuf"],
                              wg=wg1[l], wu=wu1[l], wd=wd1[l], wfm=wfm[l], wtm=wtm[l], walpha=walpha[l], balpha=balpha[l],
                              gains=gainsA[l], x1T=x1T, send_bf=send_bf, send_f=send_f, silur=silur,
                              b_x1=P["x1"], b_sbf=P["sbf"], b_sf=P["sf"], b_sr=P["sr"]))
            S.barrier()
            if "X" in phases:
                gather(send_bf, ag_bf, P["sbf"], P["agbf"], 0, 6)
                pick(ag_bf, st_bf, P["agbf"], P["stbf"], 0, 6)
                gather(send_bf, ag_bf, P["sbf"], P["agbf2"], 6, None)
                gather(send_f, ag_f, P["sf"], P["agf"])

                def late_picks():
                    pick(ag_bf, st_bf, P["agbf2"], P["stbf2"], 6, None)
                    pick(ag_f, st_f, P["agf"], P["stf"])
            K.arena_reset()

            def coll2(kind, j):
                if kind == "bf":
                    src_, dst_, bs_, bd_ = s2bf[j], ag2_bf[j], PB2["s2bf"][j], P["ag2bf"]
                else:
                    a_ = 0 if kind == "cy" else 1
                    src_, dst_, bs_, bd_ = s2f[j, a_], ag2_f[j, a_], PB2["s2cy" if a_ == 0 else "s2gn"][j], P["ag2f"]
                S.dma("pool", lambda e: e.collective_compute("AllGather", ALU.bypass, replica_groups=GROUPS,
                                                             ins=[src_.opt()], outs=[dst_.opt()]),
                      reads=[bs_], outs=[bd_], sembuf=bd_, inc=1)
            if "B" in phases:
              emit_B(K, dict(rbf=stat(st_bf), rf=stat(st_f), bw=bw[l], s2bf=s2bf, s2f=s2f,
                           b_rbf=P["stbf"], b_rbf2=P["stbf2"], b_rf=P["stf"],
                           b_s2bf=PB2["s2bf"], b_s2cy=PB2["s2cy"], b_s2gn=PB2["s2gn"], coll2=coll2 if "Y" in phases else None,
                           late_picks=late_picks if "X" in phases else None))
            S.barrier()
            if "Y" in phases:
                S.dma("sp", lambda e: e.dma_start(out=st2_bf, in_=ag2_bf[bass.ds(S.dyn["p"], 1)].rearrange("o r n -> (o r) n")),
                      reads=[P["ag2bf"]], writes=[P["st2bf"]])
                S.dma("sp", lambda e: e.dma_start(out=st2_f, in_=ag2_f[bass.ds(S.dyn["p"], 1)].rearrange("o a r n -> (o a) r n")),
                      reads=[P["ag2f"]], writes=[P["st2f"]])
            K.arena_reset()
            C = common_consts(K)
            ffn_bufs(K, C)
            if "C" in phases:
              emit_C(K, C, dict(x1T=x1T, recv_bf=stat(st2_bf), recv_f=stat(st2_f), silur=silur,
                              wgt=wgt[l], wbr=wbr[l], wo=wo[l], wg=wg2[l], wu=wu2[l], wd=wd2[l], gains=gainsC[l],
                              x3T=outT if last else xbuf, b_x3=P["out"] if last else P["xbuf"],
                              b_x1=P["x1"], b_sr=P["sr"], b_r2bf=P["st2bf"], b_r2f=P["st2f"]))
            S.barrier()
        S.wait_all_on("sp", [P["out"]])
        S.emit()
    return nc, K


_FUSED = {}


def kernel(**inputs):
    inp = {k: np.asarray(v) for k, v in inputs.items()}
    x = inp["x"].astype(np.float32, copy=False)
    cores = list(range(NCORES))
    T = TPC
    L = DEPTH
    if "nc" not in _FUSED:
        _FUSED["nc"] = build_fused(L)[0]
    A = [prep_A_weights(inp, l) for l in range(L)]
    Cw = [prep_C_weights(inp, l) for l in range(L)]
    shared = dict(
        wg1=np.stack([a["wg"] for a in A]), wu1=np.stack([a["wu"] for a in A]), wd1=np.stack([a["wd"] for a in A]),
        wfm=np.stack([a["wfm"] for a in A]), wtm=np.stack([a["wtm"] for a in A]),
        walpha=np.stack([a["walpha"] for a in A]), balpha=np.stack([a["balpha"] for a in A]),
        gainsA=np.stack([a["gains"] for a in A]),
        wgt=np.stack([c["wgt"] for c in Cw]), wbr=np.stack([c["wbr"] for c in Cw]), wo=np.stack([c["wo"] for c in Cw]),
        wg2=np.stack([c["wg"] for c in Cw]), wu2=np.stack([c["wu"] for c in Cw]), wd2=np.stack([c["wd"] for c in Cw]),
        gainsC=np.stack([c["gains"] for c in Cw]))
    del A, Cw
    bws = [np.stack([prep_B_weights(inp, l, p) for l in range(L)]) for p in range(4)]
    in_maps = []
    for c in cores:
        m = dict(shared)
        m["xT"] = _c(x[c // 4, (c % 4) * T:(c % 4 + 1) * T, :].T)
        m["cid"] = np.array([[c % 4]], np.int32)
        m["bw"] = bws[c % 4]
        in_maps.append(m)
    res = run_bass_kernel_spmd(_FUSED["nc"], in_maps, core_ids=cores).results
    out = np.empty((BATCH, SEQ, D), np.float32)
    for c in cores:
        out[c // 4, (c % 4) * T:(c % 4 + 1) * T, :] = np.asarray(res[c]["outT"]).T
    return out
```
